# Optimizing a Trainium2 kernel written in Bass

```python
import math
import jax, jax.numpy as jnp
from jax import lax
import numpy as np

D_MODEL = 2048
BATCH = 4
SEQ = 2048
DEPTH = 2
DEC_BATCH = 16
DEC_SEQ = 16
PAST_LEN = 1024

CHUNK = 64
N_BRANCH = 4
BRANCH = D_MODEL // N_BRANCH
EPS = 1e-6
QBLOCK = 128
RW_HEAD = 64
RW_HEADS = BRANCH // RW_HEAD
RW_LORA_W = 64
RW_LORA_A = 64
RW_IN = 3 * BRANCH + RW_LORA_W + RW_LORA_A
RW_GN_EPS = 64e-5
SSM_GROUP = 16
SSM_GROUPS = BRANCH // SSM_GROUP
SSM_STATE = 64
MLA_HEADS = 8
MLA_NOPE = 64
MLA_ROPE = 32
MLA_V = BRANCH // MLA_HEADS
MLA_Q_LORA = 384
MLA_KV_LORA = 256
MLA_IN = MLA_Q_LORA + MLA_KV_LORA + MLA_ROPE
MLA_SCALE = 1.0 / math.sqrt(MLA_NOPE + MLA_ROPE)
ROPE_BASE = 10000.0
SB_HEAD = 64
SB_HEADS = BRANCH // SB_HEAD
SB_IN = 3 * BRANCH
SB_SCALE = 1.0 / math.sqrt(SB_HEAD)
GATE_OFF = RW_IN + BRANCH + MLA_IN + SB_IN
IN_WIDTH = GATE_OFF + N_BRANCH * BRANCH + N_BRANCH * D_MODEL
SPLITS = (RW_IN, RW_IN + BRANCH, RW_IN + BRANCH + MLA_IN, GATE_OFF, GATE_OFF + N_BRANCH * BRANCH)

kernel_name = "hybrid_streaming_gated_branches_step"


def rms_norm(x, g):
    xf = x.astype(jnp.float32)
    y = xf * lax.rsqrt(jnp.mean(xf * xf, -1, keepdims=True) + EPS)
    return (y * g.astype(jnp.float32)).astype(x.dtype)


def rope(x, pos):
    half = x.shape[-1] // 2
    inv = ROPE_BASE ** (-jnp.arange(half, dtype=jnp.float32) / half)
    ang = pos.astype(jnp.float32)[:, None] * inv
    cos = jnp.cos(ang)[None, :, None, :]
    sin = jnp.sin(ang)[None, :, None, :]
    x1 = x[..., :half].astype(jnp.float32)
    x2 = x[..., half:].astype(jnp.float32)
    return jnp.concatenate([x1 * cos - x2 * sin, x1 * sin + x2 * cos], -1).astype(x.dtype)


def sweep_query_blocks(attend, qs, q_pos):
    T = q_pos.shape[0]
    if T <= QBLOCK or T % QBLOCK:
        return attend(qs, q_pos)
    nb = T // QBLOCK
    qs_b = tuple(jnp.moveaxis(q.reshape(q.shape[0], nb, QBLOCK, *q.shape[2:]), 1, 0) for q in qs)
    pos_b = q_pos.reshape(nb, QBLOCK)
    out = lax.map(lambda a: attend(a[0], a[1]), (qs_b, pos_b))
    out = jnp.moveaxis(out, 0, 1)
    return out.reshape(out.shape[0], T, *out.shape[3:])


def rwkv7_mixer(p_rw, shift_prev, wkv0, mu, w0, w2, a0, a2, k_k, k_a, r_k, lnx_g, lnx_b):
    Bn, T, _ = p_rw.shape
    f32 = jnp.float32
    prev = jnp.concatenate([shift_prev.astype(p_rw.dtype), p_rw[:, :-1]], axis=1)
    ps = (p_rw + (prev - p_rw) * mu).astype(f32)
    r, k, v, wd, ad = jnp.split(ps, [BRANCH, 2 * BRANCH, 3 * BRANCH, 3 * BRANCH + RW_LORA_W], axis=-1)
    log_w = -jax.nn.softplus(-(w0 + jnp.tanh(wd) @ w2)) - 0.5
    decay = jnp.exp(-jnp.exp(log_w))
    a = jax.nn.sigmoid(a0 + ad @ a2)
    heads = lambda t: t.reshape(Bn, T, RW_HEADS, RW_HEAD)
    kk = heads(k * k_k)
    kk = kk / jnp.maximum(jnp.sqrt(jnp.sum(kk * kk, -1, keepdims=True)), 1e-12)
    k = k * (1.0 + (a - 1.0) * k_a)
    r_h, w_h, k_h, v_h, a_h = (heads(t) for t in (r, decay, k, v, a))
    rem = -kk
    wr = kk * a_h

    def step(S, inp):
        r_t, w_t, k_t, v_t, rem_t, wr_t = inp
        S = (S * w_t[:, :, None, :]
             + jnp.einsum('bhvk,bhk->bhv', S, rem_t)[..., None] * wr_t[:, :, None, :]
             + v_t[..., None] * k_t[:, :, None, :])
        return S, jnp.einsum('bhvk,bhk->bhv', S, r_t)

    tm = lambda t: jnp.moveaxis(t, 1, 0)
    S_fin, ys = lax.scan(step, wkv0.astype(f32), tuple(tm(t) for t in (r_h, w_h, k_h, v_h, rem, wr)))
    y = jnp.moveaxis(ys, 0, 1)
    yc = y - jnp.mean(y, -1, keepdims=True)
    y = yc * lax.rsqrt(jnp.mean(yc * yc, -1, keepdims=True) + RW_GN_EPS)
    y = y.reshape(Bn, T, BRANCH) * lnx_g + lnx_b
    bonus = jnp.sum(r_h * k_h * r_k, -1, keepdims=True) * v_h
    y = y + bonus.reshape(Bn, T, BRANCH)
    return y.astype(p_rw.dtype), p_rw[:, -1:], S_fin


def s5_mixer(u, h0_re, h0_im, lam_re, lam_im, log_dt, b_re, b_im, c_re, c_im, d_skip, w_glu, b_glu):
    Bn, T, _ = u.shape
    f32 = jnp.float32
    lam_re, lam_im, b_re, b_im, c_re, c_im = (t.astype(f32) for t in (lam_re, lam_im, b_re, b_im, c_re, c_im))
    uf = u.astype(f32).reshape(Bn, T, SSM_GROUPS, SSM_GROUP)
    dt = jnp.exp(log_dt.astype(f32))[:, None]
    mag = jnp.exp(lam_re * dt)
    ang = lam_im * dt
    lb_re, lb_im = mag * jnp.cos(ang), mag * jnp.sin(ang)
    nr, ni = lb_re - 1.0, lb_im
    den = lam_re * lam_re + lam_im * lam_im
    f_re = (nr * lam_re + ni * lam_im) / den
    f_im = (ni * lam_re - nr * lam_im) / den
    bb_re = f_re[..., None] * b_re - f_im[..., None] * b_im
    bb_im = f_re[..., None] * b_im + f_im[..., None] * b_re
    x_re = jnp.einsum('btgc,gpc->btgp', uf, bb_re)
    x_im = jnp.einsum('btgc,gpc->btgp', uf, bb_im)
    a_re = jnp.broadcast_to(lb_re, x_re.shape)
    a_im = jnp.broadcast_to(lb_im, x_im.shape)

    def combine(e1, e2):
        a1r, a1i, b1r, b1i = e1
        a2r, a2i, b2r, b2i = e2
        return (a1r * a2r - a1i * a2i, a1r * a2i + a1i * a2r,
                a2r * b1r - a2i * b1i + b2r, a2r * b1i + a2i * b1r + b2i)

    Ar, Ai, Hr, Hi = lax.associative_scan(combine, (a_re, a_im, x_re, x_im), axis=1)
    h0r = h0_re.astype(f32)[:, None]
    h0i = h0_im.astype(f32)[:, None]
    hr = Ar * h0r - Ai * h0i + Hr
    hi = Ar * h0i + Ai * h0r + Hi
    y = (jnp.einsum('btgp,gcp->btgc', hr, c_re) - jnp.einsum('btgp,gcp->btgc', hi, c_im)
         + d_skip.astype(f32) * uf).reshape(Bn, T, BRANCH)
    g = jax.nn.gelu(y)
    out = g * jax.nn.sigmoid(g @ w_glu.astype(f32) + b_glu.astype(f32))
    return out.astype(u.dtype), hr[:, -1], hi[:, -1]


def mla_mixer(p_mla, pos, k_pos, ckv_past, kpe_past, q_norm_g, w_q_up, kv_norm_g, w_kv_up):
    Bn, T, _ = p_mla.shape
    f32 = jnp.float32
    q_lat, kv_lat, k_pe = jnp.split(p_mla, [MLA_Q_LORA, MLA_Q_LORA + MLA_KV_LORA], axis=-1)
    q = (rms_norm(q_lat, q_norm_g) @ w_q_up).reshape(Bn, T, MLA_HEADS, MLA_NOPE + MLA_ROPE)
    q_nope = q[..., :MLA_NOPE]
    q_pe = rope(q[..., MLA_NOPE:], pos)
    c_kv = rms_norm(kv_lat, kv_norm_g)
    k_pe = rope(k_pe[:, :, None, :], pos)[:, :, 0, :]
    if ckv_past is None:
        ckv_all, kpe_all = c_kv, k_pe
    else:
        ckv_all = jnp.concatenate([ckv_past.astype(c_kv.dtype), c_kv], axis=1)
        kpe_all = jnp.concatenate([kpe_past.astype(k_pe.dtype), k_pe], axis=1)
    S = ckv_all.shape[1]
    kv = (ckv_all @ w_kv_up).reshape(Bn, S, MLA_HEADS, MLA_NOPE + MLA_V)
    k_nope = kv[..., :MLA_NOPE].astype(f32)
    v = kv[..., MLA_NOPE:].astype(f32)
    kpe_f = kpe_all.astype(f32)
    k_chunk = k_pos // CHUNK

    def attend(qs, qp):
        qn, qr = qs
        s = (jnp.einsum('bqhd,bkhd->bhqk', qn.astype(f32), k_nope)
             + jnp.einsum('bqhr,bkr->bhqk', qr.astype(f32), kpe_f)) * MLA_SCALE
        mask = k_chunk[None, :] <= (qp // CHUNK)[:, None]
        s = jnp.where(mask, s, -1e30)
        w = jax.nn.softmax(s, axis=-1)
        return jnp.einsum('bhqk,bkhd->bqhd', w, v).astype(p_mla.dtype)

    o = sweep_query_blocks(attend, (q_nope, q_pe), pos)
    return o.reshape(Bn, T, MLA_HEADS * MLA_V), c_kv, k_pe


def sb_mixer(p_sb, pos, k_pos, k_past, v_past):
    Bn, T, _ = p_sb.shape
    f32 = jnp.float32
    q, k, v = (t.reshape(Bn, T, SB_HEADS, SB_HEAD) for t in jnp.split(p_sb, 3, axis=-1))
    if k_past is None:
        k_all, v_all = k, v
    else:
        k_all = jnp.concatenate([k_past.astype(k.dtype), k], axis=1)
        v_all = jnp.concatenate([v_past.astype(v.dtype), v], axis=1)
    kf = k_all.astype(f32)
    vf = v_all.astype(f32)

    def attend(qs, qp):
        (qb,) = qs
        z = jnp.einsum('bqhd,bkhd->bhqk', qb.astype(f32), kf) * SB_SCALE
        vis = k_pos[None, :] < qp[:, None]
        log_1m = jnp.where(vis, jax.nn.log_sigmoid(-z), 0.0)
        later = lax.cumsum(log_1m, axis=3, reverse=True) - log_1m
        A = jnp.where(vis, jnp.exp(jax.nn.log_sigmoid(z) + later), 0.0)
        return jnp.einsum('bhqk,bkhd->bqhd', A, vf).astype(p_sb.dtype)

    o = sweep_query_blocks(attend, (q,), pos)
    return o.reshape(Bn, T, BRANCH), k, v


def mixer_layer(x, pos, shift_prev, wkv0, ssm_re0, ssm_im0, ckv_past, kpe_past, sbk_past, sbv_past, p):
    Bn, T, _ = x.shape
    h = rms_norm(x, p["norm_g"])
    proj = h @ p["w_in"]
    p_rw, p_ssm, p_mla, p_sb, p_gate, p_merge = jnp.split(proj, SPLITS, axis=-1)
    if ckv_past is None:
        k_pos = pos
    else:
        k_pos = jnp.concatenate([jnp.arange(ckv_past.shape[1], dtype=jnp.int32), pos])
    y_rw, shift_new, wkv_new = rwkv7_mixer(p_rw, shift_prev, wkv0, p["rw_mu"], p["rw_w0"], p["rw_w2"],
                                           p["rw_a0"], p["rw_a2"], p["rw_k_k"], p["rw_k_a"], p["rw_r_k"],
                                           p["rw_lnx_g"], p["rw_lnx_b"])
    y_ssm, ssm_re_new, ssm_im_new = s5_mixer(p_ssm, ssm_re0, ssm_im0, p["ssm_lam_re"], p["ssm_lam_im"],
                                             p["ssm_log_dt"], p["ssm_b_re"], p["ssm_b_im"], p["ssm_c_re"],
                                             p["ssm_c_im"], p["ssm_d"], p["ssm_w_glu"], p["ssm_b_glu"])
    y_mla, ckv_new, kpe_new = mla_mixer(p_mla, pos, k_pos, ckv_past, kpe_past, p["mla_q_norm"],
                                        p["mla_w_q_up"], p["mla_kv_norm"], p["mla_w_kv_up"])
    y_sb, sbk_new, sbv_new = sb_mixer(p_sb, pos, k_pos, sbk_past, sbv_past)
    branches = jnp.stack([y_rw, y_ssm, y_mla, y_sb], axis=2)
    gated = branches * jax.nn.silu(p_gate.reshape(Bn, T, N_BRANCH, BRANCH))
    up = jnp.einsum('btnc,ncd->btnd', gated, p["w_branch"])
    merge = jax.nn.sigmoid(p_merge.reshape(Bn, T, N_BRANCH, D_MODEL) + p["b_merge"])
    out = jnp.sum(merge * up, axis=2) @ p["w_out"]
    new_state = (shift_new, wkv_new, ssm_re_new, ssm_im_new, ckv_new, kpe_new, sbk_new, sbv_new)
    return x + out.astype(x.dtype), new_state


def setup_inputs(seed: int = 0) -> dict:
    key = jax.random.key(seed)
    keys = iter(jax.random.split(key, 64))
    f32 = jnp.float32
    L = DEPTH

    def nrm(shape, scale=1.0):
        return jax.random.normal(next(keys), shape, f32) * scale

    def unif(shape, lo, hi):
        return jax.random.uniform(next(keys), shape, f32, lo, hi)

    return {
        "x_prompt": nrm((BATCH, SEQ, D_MODEL)),
        "x_sample": nrm((DEC_BATCH, DEC_SEQ, D_MODEL)),
        "state_rwkv_shift": nrm((L, DEC_BATCH, 1, RW_IN)),
        "state_rwkv_wkv": nrm((L, DEC_BATCH, RW_HEADS, RW_HEAD, RW_HEAD), 0.5),
        "state_ssm_re": nrm((L, DEC_BATCH, SSM_GROUPS, SSM_STATE), 0.5),
        "state_ssm_im": nrm((L, DEC_BATCH, SSM_GROUPS, SSM_STATE), 0.5),
        "cache_mla_ckv": nrm((L, DEC_BATCH, PAST_LEN, MLA_KV_LORA)),
        "cache_mla_kpe": nrm((L, DEC_BATCH, PAST_LEN, MLA_ROPE)),
        "cache_sb_k": nrm((L, DEC_BATCH, PAST_LEN, SB_HEADS, SB_HEAD)),
        "cache_sb_v": nrm((L, DEC_BATCH, PAST_LEN, SB_HEADS, SB_HEAD)),
        "norm_g": 1.0 + nrm((L, D_MODEL), 0.02),
        "w_in": nrm((L, D_MODEL, IN_WIDTH), D_MODEL ** -0.5),
        "rw_mu": unif((L, RW_IN), 0.0, 1.0),
        "rw_w0": unif((L, BRANCH), -6.0, 1.0),
        "rw_w2": nrm((L, RW_LORA_W, BRANCH), 0.1 * RW_LORA_W ** -0.5),
        "rw_a0": nrm((L, BRANCH), 0.1),
        "rw_a2": nrm((L, RW_LORA_A, BRANCH), 0.1 * RW_LORA_A ** -0.5),
        "rw_k_k": 0.85 + nrm((L, BRANCH), 0.02),
        "rw_k_a": 1.0 + nrm((L, BRANCH), 0.02),
        "rw_r_k": nrm((L, RW_HEADS, RW_HEAD), 0.1),
        "rw_lnx_g": 1.0 + nrm((L, BRANCH), 0.02),
        "rw_lnx_b": nrm((L, BRANCH), 0.02),
        "ssm_lam_re": -0.5 + nrm((L, SSM_GROUPS, SSM_STATE), 0.01),
        "ssm_lam_im": jnp.pi * jnp.arange(SSM_STATE, dtype=f32) + nrm((L, SSM_GROUPS, SSM_STATE), 0.01),
        "ssm_log_dt": unif((L, SSM_GROUPS), math.log(0.001), math.log(0.1)),
        "ssm_b_re": nrm((L, SSM_GROUPS, SSM_STATE, SSM_GROUP), (2 * SSM_GROUP) ** -0.5),
        "ssm_b_im": nrm((L, SSM_GROUPS, SSM_STATE, SSM_GROUP), (2 * SSM_GROUP) ** -0.5),
        "ssm_c_re": nrm((L, SSM_GROUPS, SSM_GROUP, SSM_STATE), (2 * SSM_STATE) ** -0.5),
        "ssm_c_im": nrm((L, SSM_GROUPS, SSM_GROUP, SSM_STATE), (2 * SSM_STATE) ** -0.5),
        "ssm_d": nrm((L, SSM_GROUPS, SSM_GROUP), 0.5),
        "ssm_w_glu": nrm((L, BRANCH, BRANCH), BRANCH ** -0.5),
        "ssm_b_glu": nrm((L, BRANCH), 0.02),
        "mla_q_norm": 1.0 + nrm((L, MLA_Q_LORA), 0.02),
        "mla_w_q_up": nrm((L, MLA_Q_LORA, MLA_HEADS * (MLA_NOPE + MLA_ROPE)), MLA_Q_LORA ** -0.5),
        "mla_kv_norm": 1.0 + nrm((L, MLA_KV_LORA), 0.02),
        "mla_w_kv_up": nrm((L, MLA_KV_LORA, MLA_HEADS * (MLA_NOPE + MLA_V)), MLA_KV_LORA ** -0.5),
        "w_branch": nrm((L, N_BRANCH, BRANCH, D_MODEL), BRANCH ** -0.5),
        "b_merge": nrm((L, N_BRANCH, D_MODEL), 0.1),
        "w_out": nrm((L, D_MODEL, D_MODEL), D_MODEL ** -0.5),
        "final_norm_g": 1.0 + nrm((D_MODEL,), 0.02),
    }


def reference(x_prompt, x_sample, state_rwkv_shift, state_rwkv_wkv, state_ssm_re, state_ssm_im,
              cache_mla_ckv, cache_mla_kpe, cache_sb_k, cache_sb_v,
              norm_g, w_in, rw_mu, rw_w0, rw_w2, rw_a0, rw_a2, rw_k_k, rw_k_a, rw_r_k, rw_lnx_g, rw_lnx_b,
              ssm_lam_re, ssm_lam_im, ssm_log_dt, ssm_b_re, ssm_b_im, ssm_c_re, ssm_c_im, ssm_d,
              ssm_w_glu, ssm_b_glu, mla_q_norm, mla_w_q_up, mla_kv_norm, mla_w_kv_up,
              w_branch, b_merge, w_out, final_norm_g):
    f32 = jnp.float32
    Bp = x_prompt.shape[0]
    T_p = x_prompt.shape[1]
    T_s = x_sample.shape[1]
    past = cache_mla_ckv.shape[2]
    pos_p = jnp.arange(T_p, dtype=jnp.int32)
    pos_s = past + jnp.arange(T_s, dtype=jnp.int32)
    xp, xs = x_prompt, x_sample
    new_p, new_s = [], []
    for l in range(DEPTH):
        p = {
            "norm_g": norm_g[l], "w_in": w_in[l],
            "rw_mu": rw_mu[l], "rw_w0": rw_w0[l], "rw_w2": rw_w2[l], "rw_a0": rw_a0[l], "rw_a2": rw_a2[l],
            "rw_k_k": rw_k_k[l], "rw_k_a": rw_k_a[l], "rw_r_k": rw_r_k[l],
            "rw_lnx_g": rw_lnx_g[l], "rw_lnx_b": rw_lnx_b[l],
            "ssm_lam_re": ssm_lam_re[l], "ssm_lam_im": ssm_lam_im[l], "ssm_log_dt": ssm_log_dt[l],
            "ssm_b_re": ssm_b_re[l], "ssm_b_im": ssm_b_im[l], "ssm_c_re": ssm_c_re[l], "ssm_c_im": ssm_c_im[l],
            "ssm_d": ssm_d[l], "ssm_w_glu": ssm_w_glu[l], "ssm_b_glu": ssm_b_glu[l],
            "mla_q_norm": mla_q_norm[l], "mla_w_q_up": mla_w_q_up[l],
            "mla_kv_norm": mla_kv_norm[l], "mla_w_kv_up": mla_w_kv_up[l],
            "w_branch": w_branch[l], "b_merge": b_merge[l], "w_out": w_out[l],
        }
        xp, st_p = mixer_layer(
            xp, pos_p,
            jnp.zeros((Bp, 1, RW_IN), xp.dtype),
            jnp.zeros((Bp, RW_HEADS, RW_HEAD, RW_HEAD), f32),
            jnp.zeros((Bp, SSM_GROUPS, SSM_STATE), f32),
            jnp.zeros((Bp, SSM_GROUPS, SSM_STATE), f32),
            None, None, None, None, p)
        xs, st_s = mixer_layer(
            xs, pos_s, state_rwkv_shift[l], state_rwkv_wkv[l], state_ssm_re[l], state_ssm_im[l],
            cache_mla_ckv[l], cache_mla_kpe[l], cache_sb_k[l], cache_sb_v[l], p)
        new_p.append(st_p)
        new_s.append(st_s)
    y_prompt = rms_norm(xp, final_norm_g)
    y_sample = rms_norm(xs, final_norm_g)
    stk = lambda lst, i: jnp.stack([s[i] for s in lst], axis=0)
    shift_p, wkv_p, ssm_re_p, ssm_im_p = stk(new_p, 0), stk(new_p, 1), stk(new_p, 2), stk(new_p, 3)
    ckv_p, kpe_p, sbk_p, sbv_p = stk(new_p, 4), stk(new_p, 5), stk(new_p, 6), stk(new_p, 7)
    shift_s, wkv_s, ssm_re_s, ssm_im_s = stk(new_s, 0), stk(new_s, 1), stk(new_s, 2), stk(new_s, 3)
    ckv_s, kpe_s, sbk_s, sbv_s = stk(new_s, 4), stk(new_s, 5), stk(new_s, 6), stk(new_s, 7)
    return (y_prompt, y_sample,
            shift_p, wkv_p, ssm_re_p, ssm_im_p, ckv_p, kpe_p, sbk_p, sbv_p,
            shift_s, wkv_s, ssm_re_s, ssm_im_s, ckv_s, kpe_s, sbk_s, sbv_s)
```

```python
import math
from contextlib import ExitStack
import numpy as np
import concourse.bass as bass
import concourse.mybir as mybir
from concourse.bass_utils import run_bass_kernel_spmd

F32 = mybir.dt.float32
BF16 = mybir.dt.bfloat16
I32 = mybir.dt.int32
ALU = mybir.AluOpType
AF = mybir.ActivationFunctionType
AX = mybir.AxisListType

L = 2
D = 2048
TP = 2048
TS = 16
NTOK = TP + 2 * TS
PAST = 1024
NFM = 5248
NTM = 1728
FM_BLOCKS = [512 * i for i in range(10)] + [NFM - 512]
TB = [(0, 512), (512, 512), (1024, 512), (1536, 512), (2048, 32)]
TT = [(128 * i, 128) for i in range(16)] + [(2048, 32)]
SEGS = [(0, 2048), (2048, 16), (2064, 16)]
PI = math.pi
CP_MU, CP_W0, CP_A0, CP_KK, CP_KA, CP_RK, CP_LG, CP_LB = 0, 13, 17, 21, 25, 29, 33, 37
CP_LRE, CP_LIM, CP_LDT, CP_DSK, CP_BGLU, CP_BMG = 41, 57, 73, 89, 105, 109
NCP = 173

EPOCH = 24000
DMA_RING = {"sp": 16, "pool": 8, "act": 4}


class Buf:
    __slots__ = ("name", "w", "r")

    def __init__(self, name="b"):
        self.name = name
        self.w = None
        self.r = []


class Ev:
    __slots__ = ("eng", "seq", "sem", "val", "dma")

    def __init__(self, eng, seq, sem, val, dma):
        self.eng, self.seq, self.sem, self.val, self.dma = eng, seq, sem, val, dma


class Sched:
    def __init__(self, nc, stack):
        self.nc = nc
        self.stack = stack
        self.engobj = {"pe": nc.tensor, "dve": nc.vector, "act": nc.scalar, "pool": nc.gpsimd, "sp": nc.sync}
        self.ops = {e: [] for e in self.engobj}
        self.ccount = {e: 0 for e in self.engobj}
        self.csems = {e: [] for e in self.engobj}
        self.dcount = {q: 0 for q in DMA_RING}
        self.dsems = {q: [stack.enter_context(nc.semaphore(f"d_{q}_{i}")) for i in range(n)] for q, n in DMA_RING.items()}
        self.known = {e: {} for e in self.engobj}
        self.knownd = {e: {} for e in self.engobj}
        self.serial = False
        self.last_ev = None

    def _csem(self, eng, ep):
        while len(self.csems[eng]) <= ep:
            self.csems[eng].append(self.stack.enter_context(self.nc.semaphore(f"c_{eng}_{len(self.csems[eng])}")))
        return self.csems[eng][ep]

    def _need(self, eng, ev, waits):
        if ev.dma:
            k = self.knownd[eng]
            if k.get(id(ev.sem), -1) >= ev.val:
                return
            k[id(ev.sem)] = ev.val
        else:
            k = self.known[eng]
            if k.get(ev.eng, -1) >= ev.seq:
                return
            k[ev.eng] = ev.seq
        waits.append((ev.sem, ev.val))

    def add(self, eng, fn, reads=(), writes=(), dma=False):
        waits = []
        for b in reads:
            ev = b.w
            if ev is not None and not (ev.eng == eng and not ev.dma and eng == "pe"):
                self._need(eng, ev, waits)
        for b in writes:
            ev = b.w
            if ev is not None and not (ev.eng == eng and not ev.dma and eng == "pe"):
                self._need(eng, ev, waits)
            for ev in b.r:
                if ev.eng == eng and not ev.dma:
                    continue
                self._need(eng, ev, waits)
        if self.serial and self.last_ev is not None:
            self._need(eng, self.last_ev, waits)
        if dma:
            q = eng
            i = self.dcount[q]
            self.dcount[q] += 1
            n = DMA_RING[q]
            sem = self.dsems[q][i % n]
            if i >= n:
                pv = 16 * (i // n)
                k = self.knownd[eng]
                if k.get(id(sem), -1) < pv:
                    k[id(sem)] = pv
                    waits.append((sem, pv))
            ev = Ev(eng, i, sem, 16 * (i // n + 1), True)
            inc = 16
        else:
            s = self.ccount[eng]
            self.ccount[eng] += 1
            sem = self._csem(eng, s // EPOCH)
            ev = Ev(eng, s, sem, (s % EPOCH) + 1, False)
            inc = 1
        self.ops[eng].append((fn, waits, sem, inc))
        self.last_ev = ev
        for b in writes:
            b.w = ev
            b.r = []
        for b in reads:
            if b.w is not ev:
                b.r.append(ev)
        return ev

    def emit(self):
        nc = self.nc
        final = []
        for q, n in DMA_RING.items():
            c = self.dcount[q]
            for slot in range(n):
                cnt = (c - slot + n - 1) // n if c > slot else 0
                if cnt > 0:
                    final.append((self.dsems[q][slot], 16 * cnt))
        for e in self.engobj:
            c = self.ccount[e]
            if c > 0 and e != "sp":
                final.append((self.csems[e][(c - 1) // EPOCH], ((c - 1) % EPOCH) + 1))
        ops = self.ops
        with nc.Block() as block:
            def run(eobj, lst, fin=None):
                for fn, waits, sem, inc in lst:
                    for (s, v) in waits:
                        eobj.wait_ge(s, v)
                    fn(eobj).then_inc(sem, inc)
                if fin:
                    for (s, v) in fin:
                        eobj.wait_ge(s, v)

            @block.tensor
            def _(e):
                run(e, ops["pe"])

            @block.vector
            def _(e):
                run(e, ops["dve"])

            @block.scalar
            def _(e):
                run(e, ops["act"])

            @block.gpsimd
            def _(e):
                run(e, ops["pool"])

            @block.sync
            def _(e):
                run(e, ops["sp"], final)


class K:
    def __init__(self, S):
        self.S = S

    def dma(self, q, out, in_, R=(), W=(), **kw):
        self.S.add(q, lambda e: e.dma_start(out=out, in_=in_, **kw), R, W, dma=True)

    def mm(self, out, lhsT, rhs, start=True, stop=True, R=(), W=()):
        self.S.add("pe", lambda e: e.matmul(out, lhsT=lhsT, rhs=rhs, start=start, stop=stop), R, W)

    def tr(self, out, in_, ident, R=(), W=()):
        self.S.add("pe", lambda e: e.transpose(out=out, in_=in_, identity=ident), R, W)

    def act(self, out, in_, func, bias=None, scale=None, R=(), W=()):
        kw = {}
        if bias is not None:
            kw["bias"] = bias
        if scale is not None:
            kw["scale"] = scale
        self.S.add("act", lambda e: e.activation(out=out, in_=in_, func=func, **kw), R, W)

    def tt(self, eng, out, in0, in1, op, R=(), W=()):
        self.S.add(eng, lambda e: e.tensor_tensor(out=out, in0=in0, in1=in1, op=op), R, W)

    def ts(self, eng, out, in0, s1, s2=None, op0=ALU.mult, op1=None, R=(), W=()):
        if op1 is None:
            self.S.add(eng, lambda e: e.tensor_scalar(out=out, in0=in0, scalar1=s1, scalar2=None, op0=op0), R, W)
        else:
            self.S.add(eng, lambda e: e.tensor_scalar(out=out, in0=in0, scalar1=s1, scalar2=s2, op0=op0, op1=op1), R, W)

    def stt(self, eng, out, in0, scalar, in1, op0, op1, R=(), W=()):
        self.S.add(eng, lambda e: e.scalar_tensor_tensor(out=out, in0=in0, scalar=scalar, in1=in1, op0=op0, op1=op1), R, W)

    def cp(self, eng, out, in_, R=(), W=()):
        if eng == "act":
            self.S.add("act", lambda e: e.activation(out=out, in_=in_, func=AF.Copy), R, W)
        else:
            self.S.add(eng, lambda e: e.tensor_copy(out=out, in_=in_), R, W)

    def memset(self, eng, out, val, R=(), W=()):
        self.S.add(eng, lambda e: e.memset(out, val), R, W)

    def red(self, out, in_, R=(), W=()):
        self.S.add("dve", lambda e: e.reduce_sum(out=out, in_=in_, axis=AX.X), R, W)

    def recip(self, out, in_, R=(), W=()):
        self.S.add("dve", lambda e: e.reciprocal(out=out, in_=in_), R, W)

    def scan(self, out, d0, d1, init, R=(), W=()):
        self.S.add("dve", lambda e: e.tensor_tensor_scan(out=out, data0=d0, data1=d1, initial=init, op0=ALU.mult, op1=ALU.add), R, W)

    def iota(self, out, pattern, base, cm, R=(), W=()):
        self.S.add("pool", lambda e: e.iota(out, pattern=pattern, base=base, channel_multiplier=cm, allow_small_or_imprecise_dtypes=True), R, W)

    def asel(self, out, in_, pattern, op, fill, base, cm, R=(), W=()):
        self.S.add("pool", lambda e: e.affine_select(out=out, in_=in_, pattern=pattern, compare_op=op, fill=fill, base=base, channel_multiplier=cm), R, W)

    def barrier(self, olds, news, scratch):
        self.S.add("dve", lambda e: e.memset(scratch, 0.0), (), list(olds) + list(news))


class Region:
    def __init__(self, t, n32):
        self.t = t
        self.n32 = n32
        self.off = 0

    def reset(self):
        self.off = 0

    def take(self, shape, dt=F32, parts=128):
        n = 1
        for s in shape:
            n *= s
        n32 = n if dt != BF16 else (n + 1) // 2
        a = self.t[0:parts, self.off:self.off + n32]
        self.off += n32
        assert self.off <= self.n32, (self.off, self.n32)
        if dt == BF16:
            a = a.bitcast(BF16)
        elif dt == I32:
            a = a.bitcast(I32)
        if len(shape) == 1:
            return a
        names = "abcdefg"[:len(shape)]
        kw = {names[i]: shape[i] for i in range(len(shape) - 1)}
        return a.rearrange("p (" + " ".join(names) + ") -> p " + " ".join(names), **kw)


def build(upto="all", dbg=(), SKIP=()):
    nc = bass.Bass("TRN2", target_bir_lowering=False)
    dbg = set(dbg)

    def din(name, shape, dt=F32):
        return nc.dram_tensor(name, list(shape), dt, kind="ExternalInput").ap()

    def dout(name, shape, dt=F32):
        return nc.dram_tensor(name, list(shape), dt, kind="ExternalOutput").ap()

    def dscr(name, shape, dt=F32):
        kind = "ExternalOutput" if name in dbg else "Internal"
        return nc.dram_tensor(name, list(shape), dt, kind=kind).ap()

    def dbgdump(k, name, ap, R):
        if name in dbg:
            t = nc.dram_tensor(name, list(ap.shape), ap.dtype, kind="ExternalOutput").ap()
            k.dma("sp", t, ap, R=R)

    xin = din("xin", [NTOK, D])
    w_fm = din("w_fm", [L, 11, 128, 8192])
    w_tm = din("w_tm", [L, 128, 16 * NTM])
    w_mg = din("w_mg", [L, 8, 4, 128, 4096])
    w_br = din("w_br", [L, 8, 4, 128, 1024])
    w_out = din("w_out", [L, 4, 128, 8192])
    cpar = din("cpar", [L, 128, NCP])
    norm_g = din("norm_g", [L, D])
    fin_g = din("fin_g", [1, D])
    rw_w2 = din("rw_w2", [L, 64, 512])
    rw_a2 = din("rw_a2", [L, 64, 512])
    ssm_b = din("ssm_b", [L, 2, 2048, 16])
    ssm_c = din("ssm_c", [L, 2, 2048, 16])
    w_glu = din("w_glu", [L, 512, 512])
    qn_g = din("qn_g", [L, 384])
    kvn_g = din("kvn_g", [L, 256])
    wq = din("wq", [L, 384, 768])
    wqs = din("wqs", [L, 384, 768])
    wk = din("wk", [L, 256, 512])
    wv = din("wv", [L, 256, 512])
    st_shift = din("st_shift", [L, 2, 128, 13])
    st_wkv = din("st_wkv", [L, 2, 8, 64, 64])
    st_ssm = din("st_ssm", [L, 2, 2, 128, 16])
    c_ckv = din("c_ckv", [L, 2, PAST, 256])
    c_kpe = din("c_kpe", [L, 2, PAST, 32])
    c_sbk = din("c_sbk", [L, 2, PAST, 512])
    c_sbv = din("c_sbv", [L, 2, PAST, 512])

    o_y = dout("o_y", [NTOK, D])
    o_shift = dout("o_shift", [L, 3, 1664])
    o_wkv = dout("o_wkv", [L, 3, 8, 64, 64])
    o_ssm = dout("o_ssm", [L, 2, 3, 2048])
    o_ckv = dout("o_ckv", [L, NTOK, 256])
    o_kpe = dout("o_kpe", [L, NTOK, 32])
    o_sbk = dout("o_sbk", [L, NTOK, 512])
    o_sbv = dout("o_sbv", [L, NTOK, 512])

    X1 = dscr("X1", [NTOK, D])
    X2 = dscr("X2", [NTOK, D])
    Xs = [xin, X1, X2]
    PF = dscr("PF", [NFM, NTOK])
    HTS = dscr("HTS", [128, 16 * NTOK], BF16)
    QNS = dscr("QNS", [128, 3 * NTOK], BF16)
    YS = dscr("YS", [512, NTOK])
    ACC = dscr("ACC", [D, NTOK], BF16)
    GDBG = dscr("GDBG", [128, 16 * NTOK], BF16)
    QF = dscr("QF", [96, 8 * NTOK], BF16).rearrange("p (h t) -> p h t", h=8)
    KF = dscr("KF", [96, 8 * NTOK], BF16).rearrange("p (h t) -> p h t", h=8)
    KFP = dscr("KFP", [96, 16 * PAST], BF16).rearrange("p (h j t) -> p h j t", h=8, j=2)
    VT = dscr("VT", [NTOK, 512], BF16)
    VTP = dscr("VTP", [2 * PAST, 512], BF16)
    RWS = dscr("RWS", [8 * 512, NTOK])
    YRW = dscr("YRW", [512, NTOK])

    with ExitStack() as st:
        S = Sched(nc, st)
        k = K(S)
        sbt = lambda name, shape, dt: st.enter_context(nc.sbuf_tensor(name, shape, dt))
        pst = lambda name, shape, dt: st.enter_context(nc.psum_tensor(name, shape, dt))
        NBIG = 16 * NTOK // 2
        R1t = sbt("R1", [128, NBIG], F32)
        R2t = sbt("R2", [128, NBIG], F32)
        Wt = sbt("Wr", [128, 8192], F32)
        Mt = sbt("Mr", [128, 9216], F32)
        bR1, bR2 = Buf("R1"), Buf("R2")
        ps = [pst(f"ps{i}", [128, 512], F32) for i in range(8)]
        bps = [Buf(f"ps{i}") for i in range(8)]

        MR = Region(Mt, 9216)
        identf = MR.take([128], F32); b_c = Buf("const")
        identb = MR.take([128], BF16)
        bones = MR.take([128], F32)
        dummy = MR.take([8], F32)
        bdum = Buf("dummy")
        MISC_BASE = MR.off

        k.memset("pool", identf, 0.0, W=[b_c])
        k.asel(identf, identf, [[-1, 128]], ALU.not_equal, 1.0, 0, 1, R=[b_c], W=[b_c])
        k.cp("dve", identb, identf, R=[b_c], W=[b_c])
        k.memset("pool", bones, 0.0, W=[b_c])
        k.memset("pool", bones[0:64, 0:64], 1.0, W=[b_c])
        k.memset("pool", bones[64:128, 64:128], 1.0, W=[b_c])

        def sincos(ang, shape, ws, bsc, want_cos=True):
            P = ang.shape[0]
            u = ws.take(shape, F32, P) if False else None
            t_u = ws.take(shape)[0:P]
            t_i = ws.take(shape, I32)[0:P]
            t_r = ws.take(shape)[0:P]
            t_m = ws.take(shape)[0:P]
            t_s = ws.take(shape)[0:P]
            k.ts("dve", t_u, ang, 1.0 / (2 * PI), R=[bsc], W=[bsc])
            k.cp("dve", t_i, t_u, R=[bsc], W=[bsc])
            k.cp("dve", t_u, t_i, R=[bsc], W=[bsc])
            k.stt("dve", t_r, t_u, -2 * PI, ang, ALU.mult, ALU.add, R=[bsc], W=[bsc])
            k.ts("dve", t_m, t_r, PI, -2 * PI, ALU.is_gt, ALU.mult, R=[bsc], W=[bsc])
            k.tt("dve", t_r, t_r, t_m, ALU.add, R=[bsc], W=[bsc])
            k.ts("dve", t_m, t_r, -PI, 2 * PI, ALU.is_lt, ALU.mult, R=[bsc], W=[bsc])
            k.tt("dve", t_r, t_r, t_m, ALU.add, R=[bsc], W=[bsc])
            k.act(t_s, t_r, AF.Sin, R=[bsc], W=[bsc])
            t_c = None
            if want_cos:
                t_c = ws.take(shape)[0:P]
                k.ts("dve", t_u, t_r, PI / 2, None, ALU.add, R=[bsc], W=[bsc])
                k.ts("dve", t_m, t_u, PI, -2 * PI, ALU.is_gt, ALU.mult, R=[bsc], W=[bsc])
                k.tt("dve", t_u, t_u, t_m, ALU.add, R=[bsc], W=[bsc])
                k.act(t_c, t_u, AF.Sin, R=[bsc], W=[bsc])
            return t_s, t_c

        RC = MR.take([17, 32]); RS = MR.take([17, 32]); b_rope = Buf("rope")
        gq = MR.take([384]); gkv = MR.take([256]); b_gq = Buf("gq")
        cpt = MR.take([NCP]); b_cp = Buf("cp")
        MISC_BASE = MR.off
        ws0 = Region(R1t, NBIG)
        b_s0 = Buf("s0")
        posT = ws0.take([17]); inv16 = ws0.take([16]); p16 = ws0.take([4]); angT = ws0.take([17, 16])
        k.iota(posT, [[128, 17]], 0, 1, W=[b_s0])
        k.iota(p16[:, 0:1], [[0, 1]], 0, 1, R=[b_s0], W=[b_s0])
        k.ts("dve", p16[:, 1:2], p16[:, 0:1], 16.0, -16.0, ALU.is_ge, ALU.mult, R=[b_s0], W=[b_s0])
        k.stt("dve", posT[:, 16:17], p16[:, 0:1], 1024.0, p16[:, 1:2], ALU.add, ALU.add, R=[b_s0], W=[b_s0])
        k.iota(inv16, [[1, 16]], 0, 0, R=[b_s0], W=[b_s0])
        k.act(inv16, inv16, AF.Exp, scale=-math.log(10000.0) / 16.0, R=[b_s0], W=[b_s0])
        k.tt("dve", angT, posT.unsqueeze(2).to_broadcast([128, 17, 16]), inv16.unsqueeze(1).to_broadcast([128, 17, 16]), ALU.mult, R=[b_s0], W=[b_s0])
        sT, cT = sincos(angT, [17, 16], ws0, b_s0)
        k.cp("dve", RC[:, :, 0:16], cT, R=[b_s0], W=[b_rope])
        k.cp("dve", RC[:, :, 16:32], cT, R=[b_s0], W=[b_rope])
        k.ts("dve", RS[:, :, 0:16], sT, -1.0, R=[b_s0], W=[b_rope])
        k.cp("dve", RS[:, :, 16:32], sT, R=[b_s0], W=[b_rope])
        k.barrier([b_s0], [bR1], dummy[0:1, 0:1])

        def wblock(buf_i):
            return Wt[:, 4096 * buf_i:4096 * (buf_i + 1)].bitcast(BF16).rearrange("p (a b) -> p a b", a=16)

        bW = [Buf("W0"), Buf("W1")]
        wcount = [0]

        stgc = [0]

        def load_wblock(src2d, ncols, stgs, bstgs):
            i = wcount[0] % 2
            wcount[0] += 1
            wb = wblock(i)
            for kq in range(4):
                si = stgc[0] % len(stgs)
                stgc[0] += 1
                sg_, bsg_ = stgs[si], bstgs[si]
                k.dma("sp", sg_, src2d[:, 4 * ncols * kq:4 * ncols * (kq + 1)].rearrange("p (kt c) -> p kt c", kt=4), W=[bsg_])
                k.cp("pool", wb[:, 4 * kq:4 * kq + 4, 0:ncols], sg_, R=[bsg_], W=[bW[i]])
            return wb, bW[i]

        bPF, bYS, bHTS, bQNS, bACC, bX1, bX2, bOck, bOkp, bOsk, bOsv, bSCR = (Buf(n) for n in ("PF", "YS", "HTS", "QNS", "ACC", "X1", "X2", "Ock", "Okp", "Osk", "Osv", "SCR"))
        bXs = [Buf("xin"), bX1, bX2]
        DRB = [bPF, bYS, bHTS, bQNS, bACC, bX1, bX2, bOck, bOkp, bOsk, bOsv, bSCR]

        def fence(*bs):
            k.barrier(list(bs), list(bs), dummy[0:1, 0:1])

        psrot = [0]

        def next_ps(lo=0, hi=4):
            i = lo + psrot[0] % (hi - lo)
            psrot[0] += 1
            return ps[i], bps[i]

        evrot = [0]

        def ev_eng():
            evrot[0] += 1
            return "act" if evrot[0] % 2 else "dve"

        for l in range(L):
            RA, RB = (R1t, R2t) if l % 2 == 0 else (R2t, R1t)
            bRA, bRB = (bR1, bR2) if l % 2 == 0 else (bR2, bR1)
            hT = RA[:, :].bitcast(BF16).rearrange("p (a b) -> p a b", a=16)
            b_hT = Buf("hT")
            k.barrier([bRA, bRB], [b_hT], dummy[0:1, 0:1])
            fence(*DRB)
            X = Xs[l]
            WS = Region(RB, NBIG)
            xt = [WS.take([D]), WS.take([D])]
            bxt = [Buf("xt0"), Buf("xt1")]
            sq = WS.take([D]); b_sq = Buf("sq")
            gt = WS.take([D]); b_gt = Buf("gt")
            hb = WS.take([D], BF16); b_hb = Buf("hb")
            ss = WS.take([8]); b_ss = Buf("ss")
            k.barrier([bRB], [bxt[0], bxt[1], b_sq, b_gt, b_hb, b_ss], dummy[0:1, 0:1])
            k.dma("sp", gt, norm_g[l:l + 1, :].partition_broadcast(128), W=[b_gt])
            for ti, (r0, nr) in enumerate(TT):
                xb, bx = xt[ti % 2], bxt[ti % 2]
                k.dma("sp", xb[0:nr, :], X[r0:r0 + nr, :], R=[bXs[l]], W=[bx])
                k.act(sq[0:nr, :], xb[0:nr, :], AF.Square, R=[bx], W=[b_sq])
                k.red(ss[0:nr, 0:1], sq[0:nr, :], R=[b_sq], W=[b_ss])
                k.ts("dve", ss[0:nr, 1:2], ss[0:nr, 0:1], 1.0 / D, 1e-6, ALU.mult, ALU.add, R=[b_ss], W=[b_ss])
                k.act(ss[0:nr, 2:3], ss[0:nr, 1:2], AF.Sqrt, R=[b_ss], W=[b_ss])
                k.recip(ss[0:nr, 3:4], ss[0:nr, 2:3], R=[b_ss], W=[b_ss])
                k.stt("dve", hb[0:nr, :], xb[0:nr, :], ss[0:nr, 3:4], gt[0:nr, :], ALU.mult, ALU.mult, R=[bx, b_ss, b_gt], W=[b_hb])
                for half in range(2):
                    pt, bpt = next_ps(4, 8)
                    ptv = pt[:, :].bitcast(BF16).rearrange("p (a b) -> p a b", a=8)
                    for j in range(8):
                        jj = half * 8 + j
                        k.tr(ptv[:, j, 0:nr], hb[0:nr, 128 * jj:128 * jj + 128], identb[0:nr, 0:nr], R=[b_hb, b_c], W=[bpt])
                    k.cp(ev_eng(), hT[:, 8 * half:8 * half + 8, r0:r0 + nr], ptv[:, :, 0:nr], R=[bpt], W=[b_hT])
            k.dma("sp", HTS, RA[:, :].bitcast(BF16), R=[b_hT, bHTS])
            if upto == "A":
                break

            WS = Region(RB, NBIG)
            stg = [WS.take([NTOK]), WS.take([NTOK])]
            bstg = [Buf("stg0"), Buf("stg1")]
            wst = [WS.take([4, 512]) for _ in range(3)]
            bwst = [Buf(f"wst{i}") for i in range(3)]
            k.barrier([bxt[0], bxt[1], b_sq, b_gt, b_hb, b_ss], bstg + bwst, dummy[0:1, 0:1])
            tiles = []
            for i in range(17):
                tiles.append((128 * i, 128, False))
            for i in range(16):
                tiles.append((2176 + 64 * i, 64, False))
            for i in range(16):
                tiles.append((3200 + 128 * i, 128, True))
            sc = 0
            nxtw = load_wblock(w_fm[l][0], 512, wst, bwst)
            for bi_, c0 in enumerate(FM_BLOCKS):
                ncol = 512
                wb, bw = nxtw
                if bi_ + 1 < len(FM_BLOCKS):
                    nxtw = load_wblock(w_fm[l][bi_ + 1], 512, wst, bwst)
                lo_ = 512 * bi_ if bi_ < 10 else 5120
                for (tc0, M, isg) in tiles:
                    if not (lo_ <= tc0 < c0 + ncol):
                        continue
                    sg, bsg = stg[sc % 2], bstg[sc % 2]
                    sc += 1
                    for (t0, nt) in TB:
                        pp, bp = next_ps(0, 4)
                        for kt in range(16):
                            k.mm(pp[0:M, 0:nt], wb[:, kt, tc0 - c0:tc0 - c0 + M], hT[:, kt, t0:t0 + nt], start=(kt == 0), stop=(kt == 15), R=[bw, b_hT], W=[bp])
                        if isg:
                            k.act(sg[0:M, t0:t0 + nt], pp[0:M, 0:nt], AF.Silu, R=[bp], W=[bsg])
                        else:
                            k.cp(ev_eng(), sg[0:M, t0:t0 + nt], pp[0:M, 0:nt], R=[bp], W=[bsg])
                    k.dma("sp", PF[tc0:tc0 + M, :], sg[0:M, :], R=[bsg, bPF])
            if upto == "B1a":
                break

            WS = Region(RB, NBIG)
            wtm = WS.take([16, NTM], BF16); b_wtm = Buf("wtm")
            qa = WS.take([448]); b_qa = Buf("qa")
            sq2 = WS.take([384]); b_sq2 = Buf("sq2")
            qnb = WS.take([384], BF16); b_qnb = Buf("qnb")
            kvt = WS.take([256]); b_kvt = Buf("kvt")
            ckvt = WS.take([256]); b_ckvt = Buf("ckvt")
            MW = Region(Mt, 9216); MW.off = MISC_BASE
            krt = MW.take([64]); b_krt = Buf("krt")
            sbkt = WS.take([512]); b_sbkt = Buf("sbkt")
            sbvt = WS.take([512]); b_sbvt = Buf("sbvt")
            qnT = WS.take([3, 128], BF16); b_qnT = Buf("qnT")
            ss2 = MW.take([8]); b_ss2 = Buf("ss2")
            WW = Region(Wt, 8192)
            tst = [WW.take([NTM]) for _ in range(4)]
            btst = [Buf(f"tst{i}") for i in range(4)]
            k.barrier(bstg + bwst + [bW[0], bW[1]], [b_wtm, b_qa, b_sq2, b_qnb, b_kvt, b_ckvt, b_krt, b_sbkt, b_sbvt, b_qnT, b_ss2] + btst, dummy[0:1, 0:1])
            for kt in range(16):
                k.dma("sp", tst[kt % 4], w_tm[l][:, NTM * kt:NTM * (kt + 1)], W=[btst[kt % 4]])
                k.cp("pool", wtm[:, kt, :], tst[kt % 4], R=[btst[kt % 4]], W=[b_wtm])
            k.dma("sp", gq, qn_g[l:l + 1, :].partition_broadcast(128), W=[b_gq])
            k.dma("sp", gkv, kvn_g[l:l + 1, :].partition_broadcast(128), W=[b_gq])
            QNSv = QNS.rearrange("p (a t) -> p a t", a=3)
            banks = [(0, 448), (448, 256), (704, 512), (1216, 512)]
            for ti, (r0, nr) in enumerate(TT):
                for bi, (c0, ncol) in enumerate(banks):
                    for kt in range(16):
                        k.mm(ps[bi][0:nr, 0:ncol], hT[:, kt, r0:r0 + nr], wtm[:, kt, c0:c0 + ncol], start=(kt == 0), stop=(kt == 15), R=[b_hT, b_wtm], W=[bps[bi]])
                k.cp("act", qa[0:nr, :], ps[0][0:nr, 0:448], R=[bps[0]], W=[b_qa])
                k.act(sq2[0:nr, :], qa[0:nr, 0:384], AF.Square, R=[b_qa], W=[b_sq2])
                k.red(ss2[0:nr, 0:1], sq2[0:nr, :], R=[b_sq2], W=[b_ss2])
                k.ts("dve", ss2[0:nr, 1:2], ss2[0:nr, 0:1], 1.0 / 384, 1e-6, ALU.mult, ALU.add, R=[b_ss2], W=[b_ss2])
                k.act(ss2[0:nr, 2:3], ss2[0:nr, 1:2], AF.Sqrt, R=[b_ss2], W=[b_ss2])
                k.recip(ss2[0:nr, 3:4], ss2[0:nr, 2:3], R=[b_ss2], W=[b_ss2])
                k.stt("dve", qnb[0:nr, :], qa[0:nr, 0:384], ss2[0:nr, 3:4], gq[0:nr, :], ALU.mult, ALU.mult, R=[b_qa, b_ss2, b_gq], W=[b_qnb])
                pt, bpt = next_ps(4, 8)
                ptv = pt[:, :].bitcast(BF16).rearrange("p (a b) -> p a b", a=8)
                for j in range(3):
                    k.tr(ptv[:, j, 0:nr], qnb[0:nr, 128 * j:128 * j + 128], identb[0:nr, 0:nr], R=[b_qnb, b_c], W=[bpt])
                k.cp("dve", qnT[:, :, 0:nr], ptv[:, 0:3, 0:nr], R=[bpt], W=[b_qnT])
                k.dma("sp", QNSv[:, :, r0:r0 + nr], qnT[:, :, 0:nr], R=[b_qnT, bQNS])
                k.tt("dve", krt[0:nr, 0:32], qa[0:nr, 384:416], RC[0:nr, ti, :], ALU.mult, R=[b_qa, b_rope], W=[b_krt])
                k.tt("dve", krt[0:nr, 32:64], qa[0:nr, 416:448], RS[0:nr, ti, :], ALU.mult, R=[b_qa, b_rope], W=[b_krt])
                k.tt("dve", krt[0:nr, 0:32], krt[0:nr, 0:32], krt[0:nr, 32:64], ALU.add, R=[b_krt], W=[b_krt])
                k.dma("sp", o_kpe[l][r0:r0 + nr, :], krt[0:nr, 0:32], R=[b_krt, bOkp])
                k.cp("act", kvt[0:nr, :], ps[1][0:nr, 0:256], R=[bps[1]], W=[b_kvt])
                k.act(sq2[0:nr, 0:256], kvt[0:nr, :], AF.Square, R=[b_kvt], W=[b_sq2])
                k.red(ss2[0:nr, 4:5], sq2[0:nr, 0:256], R=[b_sq2], W=[b_ss2])
                k.ts("dve", ss2[0:nr, 5:6], ss2[0:nr, 4:5], 1.0 / 256, 1e-6, ALU.mult, ALU.add, R=[b_ss2], W=[b_ss2])
                k.act(ss2[0:nr, 6:7], ss2[0:nr, 5:6], AF.Sqrt, R=[b_ss2], W=[b_ss2])
                k.recip(ss2[0:nr, 7:8], ss2[0:nr, 6:7], R=[b_ss2], W=[b_ss2])
                k.stt("dve", ckvt[0:nr, :], kvt[0:nr, :], ss2[0:nr, 7:8], gkv[0:nr, :], ALU.mult, ALU.mult, R=[b_kvt, b_ss2, b_gq], W=[b_ckvt])
                k.dma("sp", o_ckv[l][r0:r0 + nr, :], ckvt[0:nr, :], R=[b_ckvt, bOck])
                k.cp("act", sbkt[0:nr, :], ps[2][0:nr, :], R=[bps[2]], W=[b_sbkt])
                k.dma("sp", o_sbk[l][r0:r0 + nr, :], sbkt[0:nr, :], R=[b_sbkt, bOsk])
                k.cp("dve", sbvt[0:nr, :], ps[3][0:nr, :], R=[bps[3]], W=[b_sbvt])
                k.dma("sp", o_sbv[l][r0:r0 + nr, :], sbvt[0:nr, :], R=[b_sbvt, bOsv])
            if upto == "B1b":
                break

            gT = RB[:, :].bitcast(BF16).rearrange("p (a b) -> p a b", a=16)
            b_gT = Buf("gT")
            b_ra = Buf("ra0")
            k.barrier([b_hT, b_wtm, b_qa, b_sq2, b_qnb, b_kvt, b_ckvt, b_krt, b_sbkt, b_sbvt, b_qnT, b_ss2] + btst, [b_gT, b_ra, b_cp, bW[0], bW[1]], dummy[0:1, 0:1])
            k.dma("sp", cpt, cpar[l], W=[b_cp])
            fence(bPF, bQNS, bHTS, bOck, bOkp, bOsk, bOsv)
            if "C1" not in SKIP:
                S.serial = "SER" in SKIP
                WS = Region(RA, NBIG)
                MW = Region(Mt, 9216); MW.off = MISC_BASE
                WW = Region(Wt, 8192)
                tau = WS.take([NTOK]); cosT = WS.take([NTOK]); sinT = WS.take([NTOK])
                bA = WS.take([NTOK]); bB = WS.take([NTOK]); bC = WS.take([NTOK]); bD = WS.take([NTOK])
                u32 = WS.take([NTOK])
                b_tau, b_cos, b_sin, b_A, b_B, b_C, b_D, b_u32 = (Buf(n) for n in ("tau", "cos", "sin", "A", "B", "C", "D", "u32"))
                ti = WW.take([NTOK], I32); tmp1 = WW.take([NTOK]); tmp2 = WW.take([NTOK])
                b_ti, b_t1, b_t2 = Buf("ti"), Buf("t1"), Buf("t2")
                BBTre = MW.take([16, 128]); BBTim = MW.take([16, 128]); b_bbt = Buf("bbt")
                CBDre = MW.take([16, 32]); CBDim = MW.take([16, 32]); b_cbd = Buf("cbd")
                sp_ = MW.take([16, 16]); b_sp = Buf("sp")
                HF = MW.take([16, 6]); b_hf = Buf("hf")
                halfpi = MW.take([2])
                h0t = MW.take([4, 16]); b_h0 = Buf("h0")
                k.barrier([b_ra], [b_tau, b_cos, b_sin, b_A, b_B, b_C, b_D, b_u32, b_ti, b_t1, b_t2, b_bbt, b_cbd, b_sp, b_hf, b_h0], dummy[0:1, 0:1])
                P_ = lambda i: sp_[:, i, :]
                lre, lim, ldt = cpt[:, CP_LRE:CP_LRE + 16], cpt[:, CP_LIM:CP_LIM + 16], cpt[:, CP_LDT:CP_LDT + 16]
                dtt, mag, ang, lbre, lbim, fre, fim, th2 = P_(0), P_(1), P_(2), P_(3), P_(4), P_(5), P_(6), P_(7)
                q1, q2, q3, rden = P_(8), P_(9), P_(10), P_(11)
                k.memset("dve", halfpi, PI / 2, W=[b_sp])
                k.act(dtt, ldt, AF.Exp, R=[b_cp], W=[b_sp])
                k.tt("dve", q1, lre, dtt, ALU.mult, R=[b_cp, b_sp], W=[b_sp])
                k.act(mag, q1, AF.Exp, R=[b_sp], W=[b_sp])
                k.tt("dve", ang, lim, dtt, ALU.mult, R=[b_cp, b_sp], W=[b_sp])
                k.ts("dve", th2, ang, 1.0 / (2 * PI), R=[b_sp], W=[b_sp])
                wsA = Region(RA, NBIG); wsA.off = 3 * NTOK
                sA, cA = sincos(ang, [16], wsA, b_A)
                k.tt("dve", lbre, mag, cA, ALU.mult, R=[b_sp, b_A], W=[b_sp])
                k.tt("dve", lbim, mag, sA, ALU.mult, R=[b_sp, b_A], W=[b_sp])
                k.ts("dve", q1, lbre, -1.0, None, ALU.add, R=[b_sp], W=[b_sp])
                k.tt("dve", q2, lre, lre, ALU.mult, R=[b_cp], W=[b_sp])
                k.tt("dve", q3, lim, lim, ALU.mult, R=[b_cp], W=[b_sp])
                k.tt("dve", q2, q2, q3, ALU.add, R=[b_sp], W=[b_sp])
                k.recip(rden, q2, R=[b_sp], W=[b_sp])
                k.tt("dve", q2, q1, lre, ALU.mult, R=[b_sp, b_cp], W=[b_sp])
                k.tt("dve", q3, lbim, lim, ALU.mult, R=[b_sp, b_cp], W=[b_sp])
                k.tt("dve", q2, q2, q3, ALU.add, R=[b_sp], W=[b_sp])
                k.tt("dve", fre, q2, rden, ALU.mult, R=[b_sp], W=[b_sp])
                k.tt("dve", q2, lbim, lre, ALU.mult, R=[b_sp, b_cp], W=[b_sp])
                k.tt("dve", q3, q1, lim, ALU.mult, R=[b_sp, b_cp], W=[b_sp])
                k.tt("dve", q2, q2, q3, ALU.subtract, R=[b_sp], W=[b_sp])
                k.tt("dve", fim, q2, rden, ALU.mult, R=[b_sp], W=[b_sp])
                wsD = Region(RA, NBIG); wsD.off = 6 * NTOK
                bre = wsD.take([16, 16]); bim = wsD.take([16, 16]); cre = wsD.take([16, 16]); cim = wsD.take([16, 16])
                bbre = wsD.take([16, 16]); bbim = wsD.take([16, 16]); tq = wsD.take([16, 16])
                wsC = Region(RA, NBIG); wsC.off = 5 * NTOK
                BBDre = wsC.take([16, 32]); BBDim = wsC.take([16, 32])
                bsrc = lambda a: a.rearrange("(st p) c -> p st c", p=128)
                k.dma("sp", bre, bsrc(ssm_b[l][0]), W=[b_D])
                k.dma("sp", bim, bsrc(ssm_b[l][1]), W=[b_D])
                k.dma("sp", cre, bsrc(ssm_c[l][0]), W=[b_D])
                k.dma("sp", cim, bsrc(ssm_c[l][1]), W=[b_D])
                bc = lambda a: a.unsqueeze(2).to_broadcast([128, 16, 16])
                k.tt("dve", bbre, bre, bc(fre), ALU.mult, R=[b_D, b_sp], W=[b_D])
                k.tt("dve", tq, bim, bc(fim), ALU.mult, R=[b_D, b_sp], W=[b_D])
                k.tt("dve", bbre, bbre, tq, ALU.subtract, R=[b_D], W=[b_D])
                k.tt("dve", bbim, bim, bc(fre), ALU.mult, R=[b_D, b_sp], W=[b_D])
                k.tt("dve", tq, bre, bc(fim), ALU.mult, R=[b_D, b_sp], W=[b_D])
                k.tt("dve", bbim, bbim, tq, ALU.add, R=[b_D], W=[b_D])
                for dst, srcm, sc_ in ((BBDre, bbre, 1.0), (BBDim, bbim, 1.0), (CBDre, cre, 1.0), (CBDim, cim, -1.0)):
                    wb_ = [b_cbd] if dst is CBDre or dst is CBDim else [b_C]
                    k.memset("dve", dst, 0.0, R=[b_D], W=wb_)
                    k.ts("dve", dst[0:64, :, 0:16], srcm[0:64, :, :], sc_, R=[b_D], W=wb_)
                    k.ts("dve", dst[64:128, :, 16:32], srcm[64:128, :, :], sc_, R=[b_D], W=wb_)
                for (BBD, BBT) in ((BBDre, BBTre), (BBDim, BBTim)):
                    for q4 in range(4):
                        pp, bp = next_ps(4, 8)
                        ppv = pp[:, :].rearrange("p (a b) -> p a b", a=4)
                        for j in range(4):
                            k.tr(ppv[0:32, j, :], BBD[:, 4 * q4 + j, :], identf, R=[b_C, b_c], W=[bp])
                        k.cp(ev_eng(), BBT[0:32, 4 * q4:4 * q4 + 4, :], ppv[0:32, :, :], R=[bp], W=[b_bbt])
                k.iota(tau[:, 0:TP], [[1, TP]], 1, 0, W=[b_tau])
                k.iota(tau[:, TP:TP + 16], [[1, 16]], 1, 0, W=[b_tau])
                k.iota(tau[:, TP + 16:TP + 32], [[1, 16]], 1, 0, W=[b_tau])
                for sq_ in range(2):
                    for ri in range(2):
                        k.dma("sp", h0t[:, 2 * sq_ + ri, :], st_ssm[l][sq_][ri], W=[b_h0])
                dsk = cpt[0:32, CP_DSK:CP_DSK + 16]
                for st_ in range(16):
                    th = ang[:, st_:st_ + 1]
                    k.act(cosT, tau, AF.Copy, scale=th, R=[b_tau, b_sp], W=[b_cos])
                    k.act(sinT, tau, AF.Copy, scale=th2[:, st_:st_ + 1], R=[b_tau, b_sp], W=[b_sin])
                    k.cp("act", ti, sinT, R=[b_sin], W=[b_ti])
                    k.cp("act", sinT, ti, R=[b_ti], W=[b_sin])
                    k.stt("dve", cosT, sinT, -2 * PI, cosT, ALU.mult, ALU.add, R=[b_sin, b_cos], W=[b_cos])
                    k.ts("dve", sinT, cosT, PI, -2 * PI, ALU.is_gt, ALU.mult, R=[b_cos], W=[b_sin])
                    k.tt("dve", cosT, cosT, sinT, ALU.add, R=[b_cos, b_sin], W=[b_cos])
                    k.act(sinT, cosT, AF.Sin, R=[b_cos], W=[b_sin])
                    k.act(tmp1, cosT, AF.Abs, R=[b_cos], W=[b_t1])
                    k.act(cosT, tmp1, AF.Sin, bias=halfpi[:, 0:1], scale=-1.0, R=[b_t1, b_sp], W=[b_cos])
                    if st_ == 0:
                        dbgdump(k, "d_cos", cosT, [b_cos]); dbgdump(k, "d_sin", sinT, [b_sin]); dbgdump(k, "d_sp", sp_, [b_sp])
                        dbgdump(k, "d_bbt", BBTre[0:32], [b_bbt]); dbgdump(k, "d_cbd", CBDre, [b_cbd])
                    k.dma("sp", u32[0:32, :], PF[1664 + 32 * st_:1664 + 32 * st_ + 32, :], R=[bPF], W=[b_u32])
                    for (t0, nt) in TB:
                        pr, bpr = next_ps(0, 4)
                        pi_, bpi = next_ps(0, 4)
                        k.mm(pr[:, 0:nt], BBTre[0:32, st_, :], u32[0:32, t0:t0 + nt], R=[b_bbt, b_u32], W=[bpr])
                        k.mm(pi_[:, 0:nt], BBTim[0:32, st_, :], u32[0:32, t0:t0 + nt], R=[b_bbt, b_u32], W=[bpi])
                        sl = slice(t0, t0 + nt)
                        k.tt("dve", bC[:, sl], pr[:, 0:nt], cosT[:, sl], ALU.mult, R=[bpr, b_cos], W=[b_C])
                        k.tt("dve", bD[:, sl], pi_[:, 0:nt], sinT[:, sl], ALU.mult, R=[bpi, b_sin], W=[b_D])
                        k.tt("dve", bA[:, sl], bC[:, sl], bD[:, sl], ALU.add, R=[b_C, b_D], W=[b_A])
                        k.tt("dve", bC[:, sl], pi_[:, 0:nt], cosT[:, sl], ALU.mult, R=[bpi, b_cos], W=[b_C])
                        k.tt("dve", bD[:, sl], pr[:, 0:nt], sinT[:, sl], ALU.mult, R=[bpr, b_sin], W=[b_D])
                        k.tt("dve", bB[:, sl], bC[:, sl], bD[:, sl], ALU.subtract, R=[b_C, b_D], W=[b_B])
                    if st_ == 0:
                        dbgdump(k, "d_zre", bA, [b_A]); dbgdump(k, "d_zim", bB, [b_B])
                    for si, (s0_, sn) in enumerate(SEGS):
                        sl = slice(s0_, s0_ + sn)
                        mg_b = mag[:, st_:st_ + 1].to_broadcast([128, sn])
                        i_re = 0.0 if si == 0 else h0t[:, 2 * (si - 1), st_:st_ + 1]
                        i_im = 0.0 if si == 0 else h0t[:, 2 * (si - 1) + 1, st_:st_ + 1]
                        k.scan(bA[:, sl], mg_b, bA[:, sl], i_re, R=[b_A, b_sp, b_h0], W=[b_A])
                        k.scan(bB[:, sl], mg_b, bB[:, sl], i_im, R=[b_B, b_sp, b_h0], W=[b_B])
                    if st_ == 0:
                        dbgdump(k, "d_qre", bA, [b_A]); dbgdump(k, "d_qim", bB, [b_B])
                    k.tt("dve", bC, bA, cosT, ALU.mult, R=[b_A, b_cos], W=[b_C])
                    k.tt("dve", bD, bB, sinT, ALU.mult, R=[b_B, b_sin], W=[b_D])
                    k.tt("dve", bC, bC, bD, ALU.subtract, R=[b_C, b_D], W=[b_C])
                    k.tt("dve", bD, bB, cosT, ALU.mult, R=[b_B, b_cos], W=[b_D])
                    k.tt("dve", bA, bA, sinT, ALU.mult, R=[b_A, b_sin], W=[b_A])
                    k.tt("dve", bD, bD, bA, ALU.add, R=[b_D, b_A], W=[b_D])
                    for si, (s0_, sn) in enumerate(SEGS):
                        e_ = s0_ + sn - 1
                        k.cp("act", HF[:, st_, si:si + 1], bC[:, e_:e_ + 1], R=[b_C], W=[b_hf])
                        k.cp("act", HF[:, st_, 3 + si:4 + si], bD[:, e_:e_ + 1], R=[b_D], W=[b_hf])
                    if st_ == 0:
                        dbgdump(k, "d_hre", bC, [b_C]); dbgdump(k, "d_him", bD, [b_D])
                    for (t0, nt) in TB:
                        py, bpy = next_ps(4, 8)
                        k.mm(py[0:32, 0:nt], CBDre[:, st_, :], bC[:, t0:t0 + nt], start=True, stop=False, R=[b_cbd, b_C], W=[bpy])
                        k.mm(py[0:32, 0:nt], CBDim[:, st_, :], bD[:, t0:t0 + nt], start=False, stop=True, R=[b_cbd, b_D], W=[bpy])
                        k.stt("dve", bB[0:32, t0:t0 + nt], u32[0:32, t0:t0 + nt], dsk[:, st_:st_ + 1], py[0:32, 0:nt], ALU.mult, ALU.add, R=[b_u32, b_cp, bpy], W=[b_B])
                    k.dma("sp", YS[32 * st_:32 * st_ + 32, :], bB[0:32, :], R=[b_B, bYS])
                for si in range(3):
                    for ri in range(2):
                        k.dma("sp", o_ssm[l][ri][si].rearrange("(st p) -> p st", p=128), HF[:, :, 3 * ri + si], R=[b_hf], allow_slow_non_contiguous=True)
                b_rb = Buf("ra1")
                k.barrier([b_tau, b_cos, b_sin, b_A, b_B, b_C, b_D, b_u32, b_ti, b_t1, b_t2, b_bbt, b_cbd, b_sp, b_hf, b_h0], [b_rb], dummy[0:1, 0:1])
                WS = Region(RA, NBIG)
                WW = Region(Wt, 8192)
                yv = WS.take([4, NTOK]); tv = WS.take([4, NTOK])
                b_yv, b_tv = Buf("yv"), Buf("tv")
                gb = WW.take([4, NTOK], BF16); b_gb = Buf("gb")
                wgl = WW.take([4, 512], BF16); b_wgl = Buf("wgl")
                k.barrier([b_rb], [b_yv, b_tv, b_gb, b_wgl], dummy[0:1, 0:1])
                k.dma("pool", wgl, w_glu[l].rearrange("(kt p) c -> p kt c", p=128), W=[b_wgl])
                fence(bYS)
                k.dma("sp", yv, YS.rearrange("(a p) t -> p a t", p=128), R=[bYS], W=[b_yv])
                k.tt("dve", tv, yv, yv, ALU.mult, R=[b_yv], W=[b_tv])
                k.ts("dve", tv, tv, 0.044715, 1.0, ALU.mult, ALU.add, R=[b_tv], W=[b_tv])
                k.tt("dve", tv, tv, yv, ALU.mult, R=[b_tv, b_yv], W=[b_tv])
                k.act(tv, tv, AF.Sigmoid, scale=2.0 * math.sqrt(2.0 / PI), R=[b_tv], W=[b_tv])
                k.tt("dve", yv, yv, tv, ALU.mult, R=[b_tv, b_yv], W=[b_yv])
                k.cp("act", gb, yv, R=[b_yv], W=[b_gb])
                sgt = tv[:, 0, :]
                gat = tv[:, 1, :]
                for oc in range(4):
                    k.dma("sp", gat, PF[3200 + 128 * (4 + oc):3200 + 128 * (5 + oc), :], R=[bPF], W=[b_tv])
                    for (t0, nt) in TB:
                        pp, bp = next_ps(0, 4)
                        for kt in range(4):
                            k.mm(pp[:, 0:nt], wgl[:, kt, 128 * oc:128 * oc + 128], gb[:, kt, t0:t0 + nt], start=(kt == 0), stop=(kt == 3), R=[b_wgl, b_gb], W=[bp])
                        k.act(sgt[:, t0:t0 + nt], pp[:, 0:nt], AF.Sigmoid, bias=cpt[:, CP_BGLU + oc:CP_BGLU + oc + 1], R=[bp, b_cp], W=[b_tv])
                        k.tt("dve", sgt[:, t0:t0 + nt], sgt[:, t0:t0 + nt], yv[:, oc, t0:t0 + nt], ALU.mult, R=[b_tv, b_yv], W=[b_tv])
                        k.tt("dve", gT[:, 4 + oc, t0:t0 + nt], sgt[:, t0:t0 + nt], gat[:, t0:t0 + nt], ALU.mult, R=[b_tv], W=[b_gT])
                k.barrier([b_yv, b_tv, b_gb, b_wgl], [b_ra, bW[0], bW[1]], dummy[0:1, 0:1])
            if upto == "C1":
                k.dma("sp", GDBG, RB[:, :].bitcast(BF16), R=[b_gT])
                break

            def attn_masks_alloc(WW):
                return WW.take([4, 512]), WW.take([16]), Buf("mask")

            def attn_masks(mk, m16, b_mk, kind):
                k.memset("pool", mk, 1.0, W=[b_mk])
                for j in range(4):
                    if kind == "sb":
                        k.asel(mk[:, j, :], mk[:, j, :], [[1, 512]], ALU.is_gt, 0.0, -128 * j, -1, R=[b_mk], W=[b_mk])
                    else:
                        k.asel(mk[:, j, :].rearrange("p (a b) -> p a b", b=64), mk[:, j, :].rearrange("p (a b) -> p a b", b=64), [[64, 8], [0, 64]], ALU.is_ge, 0.0, 63 - 128 * j, -1, R=[b_mk], W=[b_mk])
                k.memset("pool", m16, 1.0, R=[b_mk], W=[b_mk])
                k.asel(m16[0:16, :], m16[0:16, :], [[1, 16]], ALU.is_gt, 0.0, 0, -1, R=[b_mk], W=[b_mk])

            if "C3" not in SKIP:
                S.serial = "SER" in SKIP
                WS = Region(RA, NBIG)
                WW = Region(Wt, 8192)
                MW = Region(Mt, 9216); MW.off = MISC_BASE
                qT = WS.take([2, NTOK], BF16, 64); kT = WS.take([2, NTOK], BF16, 64)
                vpad = WS.take([16, 2, 128], BF16)
                gate = WS.take([NTOK])
                xoff = WS.off
                tEs, tSPs, tSs, tAs = [], [], [], []
                for _ in range(4):
                    tEs.append(WS.take([512])); tSPs.append(WS.take([512])); tSs.append(WS.take([512])); tAs.append(WS.take([512], BF16))
                WS2 = Region(RA, NBIG); WS2.off = xoff + 2 * 1792
                kTp = WS2.take([2, 2, PAST], BF16, 64); vpadp = WS2.take([2, 8, 2, 128], BF16); vnew = WS2.take([2, 2, 128], BF16, 16)
                b_q, b_k, b_kp, b_v, b_vp, b_vn, b_gate = (Buf(n) for n in ("q", "k", "kp", "v", "vp", "vn", "gate"))
                b_Es = [Buf(f"E{i}") for i in range(4)]; b_SPs = [Buf(f"SP{i}") for i in range(4)]; b_Ss = [Buf(f"S{i}") for i in range(4)]; b_A2s = [Buf(f"A2{i}") for i in range(4)]
                sb01 = b_Es[0:2] + b_SPs[0:2] + b_Ss[0:2] + b_A2s[0:2]
                sb23 = b_Es[2:4] + b_SPs[2:4] + b_Ss[2:4] + b_A2s[2:4]
                smp = [b_kp, b_vp, b_vn]
                mk, m16, b_mk = attn_masks_alloc(WW)
                stg = WW.take([NTOK]); b_st = Buf("st")
                stv = WW.take([16, 128]); b_stv = Buf("stv")
                stp = WW.take([8, 128]); b_stp = Buf("stp")
                triU = MW.take([128]); ones128 = MW.take([128]); b_tri = Buf("tri")
                k.barrier([b_ra, bW[0], bW[1]], [b_q, b_k, b_v, b_gate, b_mk, b_st, b_stv, b_stp, b_tri] + sb01 + sb23, dummy[0:1, 0:1])
                attn_masks(mk, m16, b_mk, "sb")
                k.memset("pool", ones128, 1.0, W=[b_tri])
                k.memset("pool", triU, 1.0, W=[b_tri])
                k.asel(triU, triU, [[-1, 128]], ALU.is_gt, 0.0, 0, 1, R=[b_tri], W=[b_tri])
                k.memset("dve", vpad, 0.0, W=[b_v])
                pzc = [0]

                def run_streams(gens, delays=None):
                    active = [[g_, (delays[i] if delays else 0)] for i, g_ in enumerate(gens)]
                    while active:
                        nxt = []
                        for ent in active:
                            if ent[1] > 0:
                                ent[1] -= 1
                                nxt.append(ent)
                                continue
                            try:
                                next(ent[0])
                                nxt.append(ent)
                            except StopIteration:
                                pass
                        active = nxt

                def sb_tile(st_, lhsK, rhsQ, nk, nq, maskap, first, last_acc, vl, pso, bpso, o_start, o_stop, RK, RQ, RV):
                    tE, tSP, tS, tA = tEs[st_], tSPs[st_], tSs[st_], tAs[st_]
                    b_E, b_SP, b_S, b_A2 = b_Es[st_], b_SPs[st_], b_Ss[st_], b_A2s[st_]
                    pz, bpz = ps[st_ % 2], bps[st_ % 2]
                    pl, bpl = ps[2 + st_], bps[2 + st_]
                    k.mm(pz[0:nk, 0:nq], lhsK, rhsQ, R=[RK, RQ], W=[bpz])
                    yield
                    k.act(tE[0:nk, 0:nq], pz[0:nk, 0:nq], AF.Exp, R=[bpz], W=[b_E])
                    k.act(tSP[0:nk, 0:nq], tE[0:nk, 0:nq], AF.Ln, bias=1.0, R=[b_E], W=[b_SP])
                    if maskap is not None:
                        k.tt("dve", tSP[0:nk, 0:nq], tSP[0:nk, 0:nq], maskap, ALU.mult, R=[b_SP, b_mk], W=[b_SP])
                    yield
                    k.mm(pl[0:nk, 0:nq], triU[0:nk, 0:nk], tSP[0:nk, 0:nq], start=True, stop=first, R=[b_tri, b_SP], W=[bpl])
                    if not first:
                        k.mm(pl[0:nk, 0:nq], ones128[:, 0:nk], tS[:, 0:nq], start=False, stop=True, R=[b_tri, b_S], W=[bpl])
                    yield
                    if not last_acc:
                        if first:
                            if nk < 128:
                                k.memset("pool", tS[:, 0:nq], 0.0, W=[b_S])
                            k.cp("pool", tS[0:nk, 0:nq], tSP[0:nk, 0:nq], R=[b_SP], W=[b_S])
                        else:
                            k.tt("pool", tS[0:nk, 0:nq], tS[0:nk, 0:nq], tSP[0:nk, 0:nq], ALU.add, R=[b_SP, b_S], W=[b_S])
                    k.tt("dve", tSP[0:nk, 0:nq], tSP[0:nk, 0:nq], pl[0:nk, 0:nq], ALU.add, R=[b_SP, bpl], W=[b_SP])
                    yield
                    k.act(tSP[0:nk, 0:nq], tSP[0:nk, 0:nq], AF.Exp, scale=-1.0, R=[b_SP], W=[b_SP])
                    yield
                    if maskap is not None:
                        k.tt("dve", tE[0:nk, 0:nq], tE[0:nk, 0:nq], maskap, ALU.mult, R=[b_E, b_mk], W=[b_E])
                    k.tt("dve", tA[0:nk, 0:nq], tE[0:nk, 0:nq], tSP[0:nk, 0:nq], ALU.mult, R=[b_E, b_SP], W=[b_A2])
                    yield
                    k.mm(pso[:, 0:nq], vl, tA[0:nk, 0:nq], start=o_start, stop=o_stop, R=[RV, b_A2], W=[bpso])

                def prompt_stream(st_, hf, qsb, pso, bpso):
                    q0 = 512 * qsb
                    nkt = 4 * (qsb + 1)
                    for kt in range(nkt - 1, -1, -1):
                        jd = kt - 4 * qsb
                        yield from sb_tile(st_, kT[:, hf, 128 * kt:128 * kt + 128], qT[:, hf, q0:q0 + 512], 128, 512,
                                           mk[:, jd, :] if jd >= 0 else None, kt == nkt - 1, kt == 0,
                                           vpad[:, kt, hf, :], pso, bpso, (hf == 0 and kt == nkt - 1), (hf == 1 and kt == 0), b_k, b_q, b_v)
                        yield

                def sample_stream(st_, hf, j, pso, bpso):
                    q0 = TP + 16 * j
                    yield from sb_tile(st_, kT[:, hf, q0:q0 + 16], qT[:, hf, q0:q0 + 16], 16, 16, m16[0:16, :], True, False,
                                       vnew[:, j, hf, :], pso, bpso, hf == 0, False, b_k, b_q, b_vn)
                    yield
                    for kt in range(7, -1, -1):
                        yield from sb_tile(st_, kTp[:, j, hf, 128 * kt:128 * kt + 128], qT[:, hf, q0:q0 + 16], 128, 16, None, False, kt == 0,
                                           vpadp[:, j, kt, hf, :], pso, bpso, False, (hf == 1 and kt == 0), b_kp, b_q, b_vp)
                        yield

                for hp in range(4):
                    for hf in range(2):
                        h_ = 2 * hp + hf
                        k.dma("sp", stg[0:64, :], PF[2176 + 64 * h_:2176 + 64 * h_ + 64, :], R=[bPF], W=[b_st])
                        k.act(qT[:, hf, :], stg[0:64, :], AF.Copy, scale=0.125, R=[b_st], W=[b_q])
                        k.dma("sp", stg[0:64, :], PF[2688 + 64 * h_:2688 + 64 * h_ + 64, :], R=[bPF], W=[b_st])
                        k.cp("act", kT[:, hf, :], stg[0:64, :], R=[b_st], W=[b_k])
                    k.dma("sp", gate, PF[3200 + 128 * (12 + hp):3200 + 128 * (13 + hp), :], R=[bPF], W=[b_gate])
                    k.dma("sp", stv, o_sbv[l][0:TP, 128 * hp:128 * hp + 128].rearrange("(t p) c -> p t c", p=128), R=[bOsv], W=[b_stv])
                    for hf in range(2):
                        k.cp("dve", vpad[:, :, hf, 64 * hf:64 * hf + 64], stv[:, :, 64 * hf:64 * hf + 64], R=[b_stv], W=[b_v])
                    for (qa, qb) in ((0, 3), (1, 2)):
                        run_streams([prompt_stream(0, 0, qa, ps[6], bps[6]), prompt_stream(1, 1, qa, ps[6], bps[6]),
                                     prompt_stream(2, 0, qb, ps[7], bps[7]), prompt_stream(3, 1, qb, ps[7], bps[7])], delays=[0, 0, 1, 1])
                        for (qq, pb) in ((qa, 6), (qb, 7)):
                            q0 = 512 * qq
                            k.tt("dve", gT[:, 12 + hp, q0:q0 + 512], ps[pb][:, :], gate[:, q0:q0 + 512], ALU.mult, R=[bps[pb], b_gate], W=[b_gT])
                    k.barrier(sb23, smp, dummy[0:1, 0:1])
                    k.memset("dve", vpadp, 0.0, W=[b_vp]); k.memset("dve", vnew, 0.0, W=[b_vn])
                    k.dma("sp", stp[0:16, 0:2, :], o_sbv[l][TP:NTOK, 128 * hp:128 * hp + 128].rearrange("(j p) c -> p j c", p=16), R=[bOsv], W=[b_stp])
                    for hf in range(2):
                        k.cp("dve", vnew[:, :, hf, 64 * hf:64 * hf + 64], stp[0:16, 0:2, 64 * hf:64 * hf + 64], R=[b_stp], W=[b_vn])
                    for j in range(2):
                        k.dma("sp", stp, c_sbv[l][j][:, 128 * hp:128 * hp + 128].rearrange("(t p) c -> p t c", p=128), W=[b_stp])
                        for hf in range(2):
                            k.cp("dve", vpadp[:, j, :, hf, 64 * hf:64 * hf + 64], stp[:, :, 64 * hf:64 * hf + 64], R=[b_stp], W=[b_vp])
                        k.dma("sp", stp, c_sbk[l][j][:, 128 * hp:128 * hp + 128].rearrange("(t p) c -> p t c", p=128), W=[b_stp])
                        for hf in range(2):
                            for q4 in range(2):
                                pt, bpt = next_ps(0, 2)
                                for jj in range(4):
                                    k.tr(pt[0:64, 128 * jj:128 * jj + 128], stp[:, 4 * q4 + jj, 64 * hf:64 * hf + 64], identf, R=[b_stp, b_c], W=[bpt])
                                k.cp(ev_eng(), kTp[:, j, hf, 512 * q4:512 * q4 + 512], pt[0:64, :], R=[bpt], W=[b_kp])
                    for j in range(2):
                        q0 = TP + 16 * j
                        run_streams([sample_stream(0, 0, j, ps[6 + j], bps[6 + j]), sample_stream(1, 1, j, ps[6 + j], bps[6 + j])])
                        k.tt("dve", gT[:, 12 + hp, q0:q0 + 16], ps[6 + j][:, 0:16], gate[:, q0:q0 + 16], ALU.mult, R=[bps[6 + j], b_gate], W=[b_gT])
                    k.barrier(smp, sb23, dummy[0:1, 0:1])
                k.barrier([b_q, b_k, b_v, b_gate, b_mk, b_st, b_stv, b_stp, b_tri] + sb01 + sb23, [b_ra, bW[0], bW[1]], dummy[0:1, 0:1])
            if upto == "C3":
                k.dma("sp", GDBG, RB[:, :].bitcast(BF16), R=[b_gT])
                break

            if "C4" not in SKIP:
                S.serial = "SER" in SKIP
                MLA_SCALE = 1.0 / math.sqrt(96.0)
                WS = Region(RA, NBIG); WW = Region(Wt, 8192)
                qnT_ = WS.take([3, NTOK], BF16); CT = WS.take([NTOK]); ST = WS.take([NTOK])
                ttc = WS.take([96]); tts = WS.take([96])
                qo = [WS.take([512], BF16), WS.take([512], BF16)]
                t1 = WS.take([512]); t2 = WS.take([512])
                wq_ = WW.take([3, 768], BF16); wqs_ = WW.take([3, 768], BF16)
                b_qn, b_ct, b_ttc, b_t12, b_wq = (Buf(n) for n in ("qn", "ct", "ttc", "t12", "wq"))
                b_qo = [Buf("qo0"), Buf("qo1")]
                k.barrier([b_ra, bW[0], bW[1]], [b_qn, b_ct, b_ttc, b_t12, b_wq] + b_qo, dummy[0:1, 0:1])
                k.dma("sp", qnT_, QNS.rearrange("p (a t) -> p a t", a=3), R=[bQNS], W=[b_qn])
                k.dma("pool", wq_, wq[l].rearrange("(kt p) c -> p kt c", p=128), W=[b_wq])
                k.dma("pool", wqs_, wqs[l].rearrange("(kt p) c -> p kt c", p=128), W=[b_wq])
                k.memset("dve", ttc, 0.0, W=[b_ttc]); k.memset("dve", tts, 0.0, W=[b_ttc])
                for ti, (r0, nr) in enumerate(TT):
                    for (tsrc, tdst, RT) in ((ttc, CT, RC), (tts, ST, RS)):
                        k.cp("dve", tsrc[0:nr, 64:96], RT[0:nr, ti, :], R=[b_rope], W=[b_ttc])
                        pt, bpt = next_ps(6, 8)
                        k.tr(pt[0:96, 0:nr], tsrc[0:nr, 0:96], identf[0:nr, 0:nr], R=[b_ttc, b_c], W=[bpt])
                        k.cp("act", tdst[64:96, r0:r0 + nr], pt[64:96, 0:nr], R=[bpt], W=[b_ct])
                qi = 0
                for h_ in range(8):
                    for (t0, nt) in TB:
                        p1, bp1 = next_ps(0, 2)
                        p2, bp2 = next_ps(2, 4)
                        for kt in range(3):
                            k.mm(p1[0:96, 0:nt], wq_[:, kt, 96 * h_:96 * h_ + 96], qnT_[:, kt, t0:t0 + nt], start=(kt == 0), stop=(kt == 2), R=[b_wq, b_qn], W=[bp1])
                        for kt in range(3):
                            k.mm(p2[0:96, 0:nt], wqs_[:, kt, 96 * h_:96 * h_ + 96], qnT_[:, kt, t0:t0 + nt], start=(kt == 0), stop=(kt == 2), R=[b_wq, b_qn], W=[bp2])
                        q_, bq_ = qo[qi % 2], b_qo[qi % 2]
                        qi += 1
                        k.cp("act", q_[0:64, 0:nt], p1[0:64, 0:nt], R=[bp1], W=[bq_])
                        k.tt("dve", t1[64:96, 0:nt], p1[64:96, 0:nt], CT[64:96, t0:t0 + nt], ALU.mult, R=[bp1, b_ct], W=[b_t12])
                        k.tt("dve", t2[64:96, 0:nt], p2[64:96, 0:nt], ST[64:96, t0:t0 + nt], ALU.mult, R=[bp2, b_ct], W=[b_t12])
                        k.tt("dve", q_[64:96, 0:nt], t1[64:96, 0:nt], t2[64:96, 0:nt], ALU.add, R=[b_t12], W=[bq_])
                        k.dma("sp", QF[:, h_, t0:t0 + nt], q_[0:96, 0:nt], R=[bq_, bSCR])
                b_m2 = Buf("m2")
                k.barrier([b_qn, b_ct, b_ttc, b_t12, b_wq] + b_qo, [b_m2], dummy[0:1, 0:1])
                WS = Region(RA, NBIG); WW = Region(Wt, 8192)
                ckvT = WS.take([2, NTOK], BF16); ckvTp = WS.take([2, 2, PAST], BF16)
                kpeT = WS.take([NTOK], BF16); kpeTp = WS.take([2, PAST], BF16)
                ck_st = WS.take([256]); kp_st = WS.take([96]); cks8 = WS.take([8, 256]); kps8 = WS.take([8, 96])
                ko = [WS.take([512], BF16), WS.take([512], BF16)]
                vo = [WS.take([512], BF16), WS.take([512], BF16)]
                wk_ = WW.take([2, 512], BF16); wv_ = WW.take([2, 512], BF16)
                b_ck, b_ckp, b_kp2, b_kpp, b_cst, b_kst, b_c8, b_k8, b_wk = (Buf(n) for n in ("ck", "ckp", "kp2", "kpp", "cst", "kst", "c8", "k8", "wk"))
                b_ko = [Buf("ko0"), Buf("ko1")]; b_vo = [Buf("vo0"), Buf("vo1")]
                k.barrier([b_m2], [b_ck, b_ckp, b_kp2, b_kpp, b_cst, b_kst, b_c8, b_k8, b_wk] + b_ko + b_vo, dummy[0:1, 0:1])
                k.dma("pool", wk_, wk[l].rearrange("(kt p) c -> p kt c", p=128), W=[b_wk])
                k.dma("pool", wv_, wv[l].rearrange("(kt p) c -> p kt c", p=128), W=[b_wk])
                k.memset("dve", kp_st, 0.0, W=[b_kst]); k.memset("dve", kps8, 0.0, W=[b_k8])
                for ti, (r0, nr) in enumerate(TT):
                    k.dma("sp", ck_st[0:nr, :], o_ckv[l][r0:r0 + nr, :], R=[bOck], W=[b_cst])
                    pt, bpt = next_ps(6, 8)
                    for j in range(2):
                        k.tr(pt[:, 128 * j:128 * j + nr], ck_st[0:nr, 128 * j:128 * j + 128], identf[0:nr, 0:nr], R=[b_cst, b_c], W=[bpt])
                    k.cp(ev_eng(), ckvT[:, :, r0:r0 + nr], pt[:, 0:256].rearrange("p (a b) -> p a b", a=2)[:, :, 0:nr], R=[bpt], W=[b_ck])
                    k.dma("sp", kp_st[0:nr, 64:96], o_kpe[l][r0:r0 + nr, :], R=[bOkp], W=[b_kst])
                    pt, bpt = next_ps(6, 8)
                    k.tr(pt[0:96, 0:nr], kp_st[0:nr, 0:96], identf[0:nr, 0:nr], R=[b_kst, b_c], W=[bpt])
                    k.cp("act", kpeT[64:96, r0:r0 + nr], pt[64:96, 0:nr], R=[bpt], W=[b_kp2])
                for j in range(2):
                    k.dma("sp", cks8, c_ckv[l][j].rearrange("(t p) c -> p t c", p=128), W=[b_c8])
                    k.dma("sp", kps8[:, :, 64:96], c_kpe[l][j].rearrange("(t p) c -> p t c", p=128), W=[b_k8])
                    for t8 in range(8):
                        pt, bpt = next_ps(6, 8)
                        for kt in range(2):
                            k.tr(pt[:, 128 * kt:128 * kt + 128], cks8[:, t8, 128 * kt:128 * kt + 128], identf, R=[b_c8, b_c], W=[bpt])
                        k.cp(ev_eng(), ckvTp[:, :, j, 128 * t8:128 * t8 + 128], pt[:, 0:256].rearrange("p (a b) -> p a b", a=2), R=[bpt], W=[b_ckp])
                        pt, bpt = next_ps(6, 8)
                        k.tr(pt[0:96, 0:128], kps8[:, t8, 0:96], identf, R=[b_k8, b_c], W=[bpt])
                        k.cp("act", kpeTp[64:96, j, 128 * t8:128 * t8 + 128], pt[64:96, 0:128], R=[bpt], W=[b_kpp])
                ki = 0
                for h_ in range(8):
                    srcs = [(ckvT[:, :, t0:t0 + nt], kpeT[64:96, t0:t0 + nt], KF[:, h_, t0:t0 + nt], nt, b_ck, b_kp2) for (t0, nt) in TB]
                    for j in range(2):
                        for kb in range(2):
                            srcs.append((ckvTp[:, :, j, 512 * kb:512 * kb + 512], kpeTp[64:96, j, 512 * kb:512 * kb + 512], KFP[:, h_, j, 512 * kb:512 * kb + 512], 512, b_ckp, b_kpp))
                    for (csrc, psrc, dst, nt, bcs, bps_) in srcs:
                        pp, bp = next_ps(0, 4)
                        for kt in range(2):
                            k.mm(pp[0:64, 0:nt], wk_[:, kt, 64 * h_:64 * h_ + 64], csrc[:, kt, :], start=(kt == 0), stop=(kt == 1), R=[b_wk, bcs], W=[bp])
                        k_, bk_ = ko[ki % 2], b_ko[ki % 2]
                        ki += 1
                        k.cp(ev_eng(), k_[0:64, 0:nt], pp[0:64, 0:nt], R=[bp], W=[bk_])
                        k.cp("pool", k_[64:96, 0:nt], psrc, R=[bps_], W=[bk_])
                        k.dma("sp", dst, k_[0:96, 0:nt], R=[bk_, bSCR])
                vi = 0
                vsrcs = [(ckvT[:, :, r0:r0 + nr], VT[r0:r0 + nr, :], nr, b_ck) for (r0, nr) in TT]
                for j in range(2):
                    for t8 in range(8):
                        vsrcs.append((ckvTp[:, :, j, 128 * t8:128 * t8 + 128], VTP[PAST * j + 128 * t8:PAST * j + 128 * t8 + 128, :], 128, b_ckp))
                for (csrc, dst, nr, bcs) in vsrcs:
                    pp, bp = next_ps(0, 4)
                    for kt in range(2):
                        k.mm(pp[0:nr, :], csrc[:, kt, :], wv_[:, kt, :], start=(kt == 0), stop=(kt == 1), R=[b_wk, bcs], W=[bp])
                    v_, bv_ = vo[vi % 2], b_vo[vi % 2]
                    vi += 1
                    k.cp(ev_eng(), v_[0:nr, :], pp[0:nr, :], R=[bp], W=[bv_])
                    k.dma("sp", dst, v_[0:nr, :], R=[bv_, bSCR])
                b_m3 = Buf("m3")
                k.barrier([b_ck, b_ckp, b_kp2, b_kpp, b_cst, b_kst, b_c8, b_k8, b_wk] + b_ko + b_vo, [b_m3], dummy[0:1, 0:1])
                fence(bSCR)
                WS = Region(RA, NBIG); WW = Region(Wt, 8192)
                qf = WS.take([2, NTOK], BF16, 96); kf = WS.take([2, NTOK], BF16, 96); kfp = WS.take([2, 2, PAST], BF16, 96)
                vpad = WS.take([16, 2, 128], BF16); vpadp = WS.take([2, 8, 2, 128], BF16); vnew = WS.take([2, 2, 128], BF16, 16)
                gate = WS.take([NTOK]); opad = WS.take([2, 128], BF16)
                tAs = [WS.take([512], BF16), WS.take([512], BF16), WS.take([512], BF16)]; tR = WS.take([512])
                b_A2s = [Buf("A2a"), Buf("A2b"), Buf("A2c")]
                mk, m16, b_mk = attn_masks_alloc(WW)
                mkb = WW.take([4, 512], BF16)
                tac = [0]
                b_q, b_k, b_kp, b_v, b_vp, b_vn, b_gate, b_A2, b_R, b_op = (Buf(n) for n in ("q", "k", "kp", "v", "vp", "vn", "gate", "A2", "R", "op"))
                k.barrier([b_m3], [b_q, b_k, b_kp, b_v, b_vp, b_vn, b_gate, b_A2, b_R, b_op, b_mk] + b_A2s, dummy[0:1, 0:1])
                attn_masks(mk, m16, b_mk, "mla")
                k.cp("dve", mkb, mk, R=[b_mk], W=[b_mk])
                k.memset("dve", vpad, 0.0, W=[b_v]); k.memset("dve", vpadp, 0.0, W=[b_vp]); k.memset("dve", vnew, 0.0, W=[b_vn])
                k.memset("dve", opad, 0.0, W=[b_op])
                k.memset("dve", opad[:, 0, 0:64], 1.0, W=[b_op]); k.memset("dve", opad[:, 1, 64:128], 1.0, W=[b_op])

                def mla_a(lhsK, rhsQ, nk, nq, RK, RQ):
                    pz, bpz = next_ps(0, 4)
                    k.mm(pz[0:nk, 0:nq], lhsK, rhsQ, R=[RK, RQ], W=[bpz])
                    return pz, bpz

                def mla_b(pzs, nk, nq, maskap, vl, hf, pso, bpso, psd, bpsd, o_start, o_stop, RV):
                    pz, bpz = pzs
                    tA, b_A2 = tAs[tac[0] % 3], b_A2s[tac[0] % 3]
                    tac[0] += 1
                    k.act(tA[0:nk, 0:nq], pz[0:nk, 0:nq], AF.Exp, scale=MLA_SCALE, R=[bpz], W=[b_A2])
                    if maskap is not None:
                        k.tt("dve", tA[0:nk, 0:nq], tA[0:nk, 0:nq], maskap, ALU.mult, R=[b_A2, b_mk], W=[b_A2])
                    k.mm(pso[:, 0:nq], vl, tA[0:nk, 0:nq], start=o_start, stop=o_stop, R=[RV, b_A2], W=[bpso])
                    k.mm(psd[:, 0:nq], opad[0:nk, hf, :], tA[0:nk, 0:nq], start=o_start, stop=o_stop, R=[b_op, b_A2], W=[bpsd])

                def mla_pipe(tiles, pso, bpso, psd, bpsd):
                    n = len(tiles)
                    pend = []
                    for i in range(n + 2):
                        if i < n:
                            t = tiles[i]
                            pend.append(mla_a(t["K"], t["Q"], t["nk"], t["nq"], t["RK"], t["RQ"]))
                        if i >= 2:
                            t = tiles[i - 2]
                            mla_b(pend[i - 2], t["nk"], t["nq"], t["mask"], t["V"], t["hf"], pso, bpso, psd, bpsd, i - 2 == 0, i - 2 == n - 1, t["RV"])

                for hp in range(4):
                    k.dma("sp", qf, QF[:, 2 * hp:2 * hp + 2, :], R=[bSCR], W=[b_q])
                    k.dma("sp", kf, KF[:, 2 * hp:2 * hp + 2, :], R=[bSCR], W=[b_k])
                    k.dma("sp", kfp, KFP[:, 2 * hp:2 * hp + 2, :, :], R=[bSCR], W=[b_kp])
                    k.dma("sp", gate, PF[3200 + 128 * (8 + hp):3200 + 128 * (9 + hp), :], R=[bPF], W=[b_gate])
                    for hf in range(2):
                        c0 = 128 * hp + 64 * hf
                        k.dma("sp", vpad[:, :, hf, 64 * hf:64 * hf + 64], VT[0:TP, c0:c0 + 64].rearrange("(t p) c -> p t c", p=128), R=[bSCR], W=[b_v])
                        k.dma("sp", vnew[:, :, hf, 64 * hf:64 * hf + 64], VT[TP:NTOK, c0:c0 + 64].rearrange("(j p) c -> p j c", p=16), R=[bSCR], W=[b_vn])
                        for j in range(2):
                            k.dma("sp", vpadp[:, j, :, hf, 64 * hf:64 * hf + 64], VTP[PAST * j:PAST * j + PAST, c0:c0 + 64].rearrange("(t p) c -> p t c", p=128), R=[bSCR], W=[b_vp])

                    def finish(pso, bpso, psd, bpsd, q0, nq):
                        k.recip(tR[:, 0:nq], psd[:, 0:nq], R=[bpsd], W=[b_R])
                        k.tt("dve", tR[:, 0:nq], pso[:, 0:nq], tR[:, 0:nq], ALU.mult, R=[bpso, b_R], W=[b_R])
                        k.tt("dve", gT[:, 8 + hp, q0:q0 + nq], tR[:, 0:nq], gate[:, q0:q0 + nq], ALU.mult, R=[b_R, b_gate], W=[b_gT])

                    for qsb in range(4):
                        q0 = 512 * qsb
                        pso, bpso = next_ps(4, 6)
                        psd, bpsd = next_ps(6, 8)
                        nkt = 4 * (qsb + 1)
                        tl = []
                        for hf in range(2):
                            for kt in range(nkt):
                                jd = kt - 4 * qsb
                                tl.append(dict(K=kf[:, hf, 128 * kt:128 * kt + 128], Q=qf[:, hf, q0:q0 + 512], nk=128, nq=512,
                                               mask=mkb[:, jd, :] if jd >= 0 else None, V=vpad[:, kt, hf, :], hf=hf, RK=b_k, RQ=b_q, RV=b_v))
                        mla_pipe(tl, pso, bpso, psd, bpsd)
                        finish(pso, bpso, psd, bpsd, q0, 512)
                    for j in range(2):
                        q0 = TP + 16 * j
                        pso, bpso = next_ps(4, 6)
                        psd, bpsd = next_ps(6, 8)
                        tl = []
                        for hf in range(2):
                            for kt in range(8):
                                tl.append(dict(K=kfp[:, hf, j, 128 * kt:128 * kt + 128], Q=qf[:, hf, q0:q0 + 16], nk=128, nq=16, mask=None,
                                               V=vpadp[:, j, kt, hf, :], hf=hf, RK=b_kp, RQ=b_q, RV=b_vp))
                            tl.append(dict(K=kf[:, hf, q0:q0 + 16], Q=qf[:, hf, q0:q0 + 16], nk=16, nq=16, mask=None,
                                           V=vnew[:, j, hf, :], hf=hf, RK=b_k, RQ=b_q, RV=b_vn))
                        mla_pipe(tl, pso, bpso, psd, bpsd)
                        finish(pso, bpso, psd, bpsd, q0, 16)
                k.barrier([b_q, b_k, b_kp, b_v, b_vp, b_vn, b_gate, b_A2, b_R, b_op, b_mk] + b_A2s, [b_ra, bW[0], bW[1]], dummy[0:1, 0:1])
            if upto == "C4":
                k.dma("sp", GDBG, RB[:, :].bitcast(BF16), R=[b_gT])
                break
            if "C2" not in SKIP:
                S.serial = "SER" in SKIP
                CH = [(64 * c, 64) for c in range(32)] + [(2048, 16), (2064, 16)]
                NCH = len(CH)
                MW = Region(Mt, 9216); MW.off = MISC_BASE
                triS = MW.take([64], F32, 64); triI = MW.take([64], F32, 64); triL = MW.take([64], F32, 64)
                pcs = MW.take([4, NCH]); m01 = MW.take([512]); m01s = MW.take([32])
                negw0 = MW.take([4]); omka = MW.take([4])
                b_rc = Buf("rwconst"); b_pcs = Buf("pcs")
                k.barrier([b_ra, bW[0], bW[1]], [b_rc, b_pcs], dummy[0:1, 0:1])
                for (tri, op_, pat, cm) in ((triS, ALU.is_gt, [[1, 64]], -1), (triI, ALU.is_ge, [[1, 64]], -1), (triL, ALU.is_gt, [[-1, 64]], 1)):
                    k.memset("pool", tri, 1.0, W=[b_rc])
                    k.asel(tri, tri, pat, op_, 0.0, 0, cm, R=[b_rc], W=[b_rc])
                k.memset("pool", m01, 1.0, W=[b_rc])
                k.memset("pool", m01.rearrange("p (a b) -> p a b", b=64)[:, :, 0:1], 0.0, W=[b_rc])
                k.memset("pool", m01s, 1.0, W=[b_rc])
                k.memset("pool", m01s.rearrange("p (a b) -> p a b", b=16)[:, :, 0:1], 0.0, W=[b_rc])
                k.ts("dve", negw0, cpt[:, CP_W0:CP_W0 + 4], -1.0, R=[b_cp], W=[b_rc])
                k.ts("dve", omka, cpt[:, CP_KA:CP_KA + 4], -1.0, 1.0, ALU.mult, ALU.add, R=[b_cp], W=[b_rc])
                WS = Region(RA, NBIG); WW = Region(Wt, 8192)
                names = ["Pr", "Pk", "Pv", "Qr", "Qk", "Qv", "X12", "X12p", "TH12", "LW", "AA", "KK", "K2", "CUM", "E1", "E2", "E3", "E4", "WR", "T1", "T2", "OA", "OB", "OK", "OR", "OBE", "OKE", "OBON"]
                T_ = {n: WS.take([512]) for n in names}
                B_ = {n: Buf(n) for n in names}
                w2p = WW.take([512]); a2p = WW.take([512]); b_w2 = Buf("w2p")
                k.barrier([b_ra], list(B_.values()) + [b_w2], dummy[0:1, 0:1])
                k.memset("dve", w2p, 0.0, W=[b_w2]); k.memset("dve", a2p, 0.0, W=[b_w2])
                k.dma("sp", w2p[0:64, :], rw_w2[l], W=[b_w2])
                k.dma("sp", a2p[64:128, :], rw_a2[l], W=[b_w2])

                def load_shift(dst, bd, prv, bp, row0, tile_j, t0, nt):
                    k.dma("sp", dst[:, 0:nt], PF[row0:row0 + 128, t0:t0 + nt], R=[bPF], W=[bd])
                    if t0 == 0:
                        k.memset("dve", prv[:, 0:1], 0.0, W=[bp])
                        k.dma("sp", prv[:, 1:nt], PF[row0:row0 + 128, 0:nt - 1], R=[bPF], W=[bp])
                    elif t0 < TP:
                        k.dma("sp", prv[:, 0:nt], PF[row0:row0 + 128, t0 - 1:t0 + nt - 1], R=[bPF], W=[bp])
                    else:
                        for j in range(2):
                            k.dma("sp", prv[:, 16 * j:16 * j + 1], st_shift[l][j][:, tile_j:tile_j + 1], W=[bp], allow_slow_non_contiguous=True)
                            k.dma("sp", prv[:, 16 * j + 1:16 * j + 16], PF[row0:row0 + 128, t0 + 16 * j:t0 + 16 * j + 15], R=[bPF], W=[bp])
                    mu = cpt[:, CP_MU + tile_j:CP_MU + tile_j + 1]
                    k.tt("dve", prv[:, 0:nt], prv[:, 0:nt], dst[:, 0:nt], ALU.subtract, R=[bp, bd], W=[bp])
                    k.stt("dve", dst[:, 0:nt], prv[:, 0:nt], mu, dst[:, 0:nt], ALU.mult, ALU.add, R=[bp, bd, b_cp], W=[bd])

                for tbi, (t0, nt) in enumerate(TB):
                    sl = slice(0, nt)
                    load_shift(T_["X12"], B_["X12"], T_["X12p"], B_["X12p"], 1536, 12, t0, nt)
                    k.act(T_["TH12"][:, sl], T_["X12"][:, sl], AF.Tanh, R=[B_["X12"]], W=[B_["TH12"]])
                    msk = m01[:, 0:nt] if nt == 512 else m01s[:, 0:nt]
                    for hp in range(4):
                        load_shift(T_["Pr"], B_["Pr"], T_["Qr"], B_["Qr"], 128 * hp, hp, t0, nt)
                        load_shift(T_["Pk"], B_["Pk"], T_["Qk"], B_["Qk"], 512 + 128 * hp, 4 + hp, t0, nt)
                        load_shift(T_["Pv"], B_["Pv"], T_["Qv"], B_["Qv"], 1024 + 128 * hp, 8 + hp, t0, nt)
                        c1 = lambda off: cpt[:, off + hp:off + hp + 1]
                        pl, bpl = next_ps(0, 4)
                        k.mm(pl[:, sl], w2p[:, 128 * hp:128 * hp + 128], T_["TH12"][:, sl], R=[b_w2, B_["TH12"]], W=[bpl])
                        k.act(T_["E1"][:, sl], pl[:, sl], AF.Exp, bias=negw0[:, hp:hp + 1], scale=-1.0, R=[bpl, b_rc], W=[B_["E1"]])
                        k.act(T_["E1"][:, sl], T_["E1"][:, sl], AF.Ln, bias=1.0, R=[B_["E1"]], W=[B_["E1"]])
                        k.act(T_["E1"][:, sl], T_["E1"][:, sl], AF.Exp, bias=-0.5, scale=-1.0, R=[B_["E1"]], W=[B_["E1"]])
                        k.ts("dve", T_["LW"][:, sl], T_["E1"][:, sl], -1.0, R=[B_["E1"]], W=[B_["LW"]])
                        pa, bpa = next_ps(0, 4)
                        k.mm(pa[:, sl], a2p[:, 128 * hp:128 * hp + 128], T_["X12"][:, sl], R=[b_w2, B_["X12"]], W=[bpa])
                        k.act(T_["AA"][:, sl], pa[:, sl], AF.Sigmoid, bias=c1(CP_A0), R=[bpa, b_cp], W=[B_["AA"]])
                        k.ts("dve", T_["KK"][:, sl], T_["Pk"][:, sl], c1(CP_KK), R=[B_["Pk"], b_cp], W=[B_["KK"]])
                        k.tt("dve", T_["T1"][:, sl], T_["KK"][:, sl], T_["KK"][:, sl], ALU.mult, R=[B_["KK"]], W=[B_["T1"]])
                        pn, bpn = next_ps(4, 8)
                        k.mm(pn[:, sl], bones, T_["T1"][:, sl], R=[b_c, B_["T1"]], W=[bpn])
                        k.act(T_["T2"][:, sl], pn[:, sl], AF.Sqrt, R=[bpn], W=[B_["T2"]])
                        k.ts("dve", T_["T2"][:, sl], T_["T2"][:, sl], 1e-12, None, ALU.max, R=[B_["T2"]], W=[B_["T2"]])
                        k.recip(T_["T2"][:, sl], T_["T2"][:, sl], R=[B_["T2"]], W=[B_["T2"]])
                        k.tt("dve", T_["KK"][:, sl], T_["KK"][:, sl], T_["T2"][:, sl], ALU.mult, R=[B_["KK"], B_["T2"]], W=[B_["KK"]])
                        k.ts("dve", T_["T1"][:, sl], T_["AA"][:, sl], c1(CP_KA), omka[:, hp:hp + 1], ALU.mult, ALU.add, R=[B_["AA"], b_cp, b_rc], W=[B_["T1"]])
                        k.tt("dve", T_["K2"][:, sl], T_["Pk"][:, sl], T_["T1"][:, sl], ALU.mult, R=[B_["Pk"], B_["T1"]], W=[B_["K2"]])
                        k.scan(T_["CUM"][:, sl], msk, T_["LW"][:, sl], 0.0, R=[b_rc, B_["LW"]], W=[B_["CUM"]])
                        csz = 64 if nt == 512 else 16
                        cv = T_["CUM"][:, sl].rearrange("p (a b) -> p a b", b=csz)
                        nch_ = nt // csz
                        ch0 = (t0 // 64) if nt == 512 else 32
                        k.act(pcs[:, hp, ch0:ch0 + nch_], cv[:, :, csz - 1], AF.Exp, R=[B_["CUM"]], W=[b_pcs])
                        k.act(T_["E1"][:, sl], T_["CUM"][:, sl], AF.Exp, R=[B_["CUM"]], W=[B_["E1"]])
                        k.act(T_["E2"][:, sl], T_["CUM"][:, sl], AF.Exp, scale=-1.0, R=[B_["CUM"]], W=[B_["E2"]])
                        k.tt("dve", T_["T1"][:, sl], T_["CUM"][:, sl], T_["LW"][:, sl], ALU.subtract, R=[B_["CUM"], B_["LW"]], W=[B_["T1"]])
                        k.act(T_["E3"][:, sl], T_["T1"][:, sl], AF.Exp, R=[B_["T1"]], W=[B_["E3"]])
                        k.tt("dve", T_["T2"][:, sl].rearrange("p (a b) -> p a b", b=csz), cv[:, :, csz - 1:csz].to_broadcast([128, nch_, csz]), cv, ALU.subtract, R=[B_["CUM"]], W=[B_["T2"]])
                        k.act(T_["E4"][:, sl], T_["T2"][:, sl], AF.Exp, R=[B_["T2"]], W=[B_["E4"]])
                        k.stt("dve", T_["OA"][:, sl], T_["KK"][:, sl], -1.0, T_["E3"][:, sl], ALU.mult, ALU.mult, R=[B_["KK"], B_["E3"]], W=[B_["OA"]])
                        k.tt("dve", T_["WR"][:, sl], T_["KK"][:, sl], T_["AA"][:, sl], ALU.mult, R=[B_["KK"], B_["AA"]], W=[B_["WR"]])
                        k.tt("dve", T_["OB"][:, sl], T_["WR"][:, sl], T_["E2"][:, sl], ALU.mult, R=[B_["WR"], B_["E2"]], W=[B_["OB"]])
                        k.tt("dve", T_["OBE"][:, sl], T_["WR"][:, sl], T_["E4"][:, sl], ALU.mult, R=[B_["WR"], B_["E4"]], W=[B_["OBE"]])
                        k.tt("dve", T_["OK"][:, sl], T_["K2"][:, sl], T_["E2"][:, sl], ALU.mult, R=[B_["K2"], B_["E2"]], W=[B_["OK"]])
                        k.tt("dve", T_["OKE"][:, sl], T_["K2"][:, sl], T_["E4"][:, sl], ALU.mult, R=[B_["K2"], B_["E4"]], W=[B_["OKE"]])
                        k.tt("dve", T_["OR"][:, sl], T_["Pr"][:, sl], T_["E1"][:, sl], ALU.mult, R=[B_["Pr"], B_["E1"]], W=[B_["OR"]])
                        k.stt("dve", T_["T1"][:, sl], T_["Pr"][:, sl], c1(CP_RK), T_["K2"][:, sl], ALU.mult, ALU.mult, R=[B_["Pr"], B_["K2"], b_cp], W=[B_["T1"]])
                        pb, bpb = next_ps(4, 8)
                        k.mm(pb[:, sl], bones, T_["T1"][:, sl], R=[b_c, B_["T1"]], W=[bpb])
                        k.tt("dve", T_["OBON"][:, sl], T_["Pv"][:, sl], pb[:, sl], ALU.mult, R=[B_["Pv"], bpb], W=[B_["OBON"]])
                        for ai, nm in enumerate(["OA", "OB", "OK", "OR", "OBE", "OKE", "Pv", "OBON"]):
                            k.dma("sp", RWS[512 * ai + 128 * hp:512 * ai + 128 * hp + 128, t0:t0 + nt], T_[nm][:, sl], R=[B_[nm], bSCR])
                if True:
                    for si, (s0_, sn) in enumerate(SEGS):
                        e_ = s0_ + sn - 1
                        k.dma("sp", o_shift[l][si].rearrange("(a b) -> a b", b=1), PF[0:1664, e_:e_ + 1], R=[bPF], allow_slow_non_contiguous=True)
                b_r2 = Buf("r2")
                k.barrier(list(B_.values()) + [b_w2], [b_r2], dummy[0:1, 0:1])
                fence(bSCR)
                if upto == "R1":
                    break
                WS = Region(RA, NBIG); WW = Region(Wt, 8192)
                LD = [[WW.take([4, 64]) for _ in range(7)] for _ in range(2)]
                b_LD = [[Buf(f"ld{i}{j}") for j in range(7)] for i in range(2)]
                abd = WW.take([4, 2, 64]); bbd = WW.take([4, 2, 64]); rbd = WW.take([4, 2, 64])
                b_abd, b_bbd, b_rbd = Buf("abd"), Buf("bbd"), Buf("rbd")
                def t64(n):
                    return WS.take([512], F32, 64), Buf(n)
                def t128(n, w=512):
                    return WS.take([w]), Buf(n)
                atok, b_atok = t64("atok"); betok, b_betok = t64("betok"); ketok, b_ketok = t64("ketok"); vtok, b_vtok = t64("vtok")
                N1, b_N1 = t64("N1"); A1, b_A1 = t64("A1"); AakT, b_AakT = t64("AakT"); BrbT, b_BrbT = t64("BrbT"); BrkT, b_BrkT = t64("BrkT")
                PA = [t64("PA0"), t64("PA1")]; PN = [t64("PN0"), t64("PN1")]; ZZ = [t64("Z0"), t64("Z1")]
                W1, b_W1 = t64("W1"); U0, b_U0 = t64("U0"); Ap, b_Ap = t64("Ap")
                GTp, b_GTp = t128("GTp"); Hbd, b_Hbd = t128("Hbd")
                RpT, b_RpT = t128("RpT", 256); Y0d, b_Y0d = t128("Y0d", 256); yout, b_yout = t128("yout", 256)
                Sb = [t128("Sb0"), t128("Sb1")]; SIt, b_SI = t128("SI"); SIn = [t128("SIa"), t128("SIb")]; SF = [t128("SFa"), t128("SFb")]
                SO, b_SO = t128("SO")
                allb = [b for row in b_LD for b in row] + [b_abd, b_bbd, b_rbd, b_atok, b_betok, b_ketok, b_vtok, b_N1, b_A1, b_AakT, b_BrbT, b_BrkT, b_W1, b_U0, b_Ap, b_GTp, b_Hbd, b_RpT, b_Y0d, b_yout, b_SI, b_SO] + [x[1] for x in PA + PN + ZZ + Sb + SIn + SF]
                k.barrier([b_r2], allb, dummy[0:1, 0:1])
                v3 = lambda t, C: t[0:C, :].rearrange("c (h x) -> c h x", h=8)[:, :, 0:C]
                p4 = lambda t: t[:, :].rearrange("q (p x) -> q p x", p=4)
                for t_ in (abd, bbd, rbd):
                    k.memset("dve", t_, 0.0, W=[b_abd, b_bbd, b_rbd])
                k.memset("dve", Sb[0][0], 0.0, W=[Sb[0][1]])
                k.memset("dve", SIt, 0.0, W=[b_SI])
                for j in range(2):
                    for h_ in range(8):
                        p_, hf = h_ // 2, h_ % 2
                        k.dma("sp", p4(SIt)[64 * hf:64 * hf + 64, p_, 64 * hf:64 * hf + 64], st_wkv[l][j][h_], W=[b_SI])
                    pt, bpt = next_ps(0, 8)
                    for p_ in range(4):
                        k.tr(pt[:, 128 * p_:128 * p_ + 128], p4(SIt)[:, p_, :], identf, R=[b_SI, b_c], W=[bpt])
                    k.cp("act", SIn[j][0], pt[:, :], R=[bpt], W=[SIn[j][1]])

                def emit_state(Sbuf, bS, seg):
                    pt, bpt = next_ps(0, 8)
                    for p_ in range(4):
                        k.tr(pt[:, 128 * p_:128 * p_ + 128], p4(Sbuf)[:, p_, :], identf, R=[bS, b_c], W=[bpt])
                    k.cp("act", SO, pt[:, :], R=[bpt], W=[b_SO])
                    for h_ in range(8):
                        p_, hf = h_ // 2, h_ % 2
                        k.dma("sp", o_wkv[l][seg][h_], p4(SO)[64 * hf:64 * hf + 64, p_, 64 * hf:64 * hf + 64], R=[b_SO])

                def ld_chunk(cj):
                    t0_, C_ = CH[cj]
                    for ai in range(7):
                        k.dma("sp", LD[cj % 2][ai][:, :, 0:C_], RWS[512 * ai:512 * ai + 512, t0_:t0_ + C_].rearrange("(p c) t -> c p t", c=128), R=[bSCR], W=[b_LD[cj % 2][ai]])

                ld_chunk(0)
                for ci, (t0, C) in enumerate(CH):
                    ld, bld = LD[ci % 2], b_LD[ci % 2]
                    if ci + 1 < NCH:
                        ld_chunk(ci + 1)
                    at, bt, kt_, rt, be, ke, vv = ld
                    b_at, b_bt, b_kt, b_rt, b_be, b_ke, b_vv = bld
                    for (src, bsrc, dst, bdst) in ((at, b_at, abd, b_abd), (bt, b_bt, bbd, b_bbd), (rt, b_rt, rbd, b_rbd)):
                        k.cp("pool", dst[0:64, :, 0, 0:C], src[0:64, :, 0:C], R=[bsrc], W=[bdst])
                        k.cp("pool", dst[64:128, :, 1, 0:C], src[64:128, :, 0:C], R=[bsrc], W=[bdst])
                    for (src, bsrc, dst, bdst) in ((at, b_at, atok, b_atok), (be, b_be, betok, b_betok), (ke, b_ke, ketok, b_ketok), (vv, b_vv, vtok, b_vtok)):
                        pt, bpt = next_ps(0, 8)
                        for p_ in range(4):
                            k.tr(pt[0:C, 128 * p_:128 * p_ + 128], src[:, p_, 0:C], identf, R=[bsrc, b_c], W=[bpt])
                        k.cp(ev_eng(), dst[0:C, :], pt[0:C, :], R=[bpt], W=[bdst])

                    def mat(dst, bdst, lh, blh, rhbd, brh, mask):
                        pm, bpm = next_ps(0, 8)
                        pv = pm[0:C, :].rearrange("c (p h t) -> c p h t", p=4, h=2)
                        for p_ in range(4):
                            k.mm(pv[:, p_, :, 0:C], lh[:, p_, 0:C], rhbd[:, p_, :, 0:C], R=[blh, brh], W=[bpm])
                        k.tt("dve", v3(dst, C), v3(pm, C), mask[0:C, 0:C].unsqueeze(1).to_broadcast([C, 8, C]), ALU.mult, R=[bpm, b_rc], W=[bdst])
                    mat(N1, b_N1, bt, b_bt, abd, b_abd, triS)
                    mat(A1, b_A1, at, b_at, bbd, b_bbd, triL)
                    mat(AakT, b_AakT, kt_, b_kt, abd, b_abd, triS)
                    mat(BrbT, b_BrbT, bt, b_bt, rbd, b_rbd, triI)
                    mat(BrkT, b_BrkT, kt_, b_kt, rbd, b_rbd, triI)
                    Zc, bZc = ZZ[0]
                    k.tt("dve", v3(Zc, C), v3(N1, C), identf[0:C, 0:C].unsqueeze(1).to_broadcast([C, 8, C]), ALU.add, R=[b_N1, b_c], W=[bZc])
                    curA, bcA, curN, bcN = A1, b_A1, N1, b_N1
                    rounds = int(round(math.log2(C))) - 1
                    for r_ in range(rounds):
                        nA, bnA = PA[r_ % 2]; nN, bnN = PN[r_ % 2]
                        psN, bpsN = next_ps(0, 8); psA, bpsA = next_ps(0, 8)
                        lastr = (r_ == rounds - 1)
                        for h_ in range(8):
                            if not lastr:
                                k.mm(psN[0:C, 64 * h_:64 * h_ + C], curA[0:C, 64 * h_:64 * h_ + C], curN[0:C, 64 * h_:64 * h_ + C], R=[bcA, bcN], W=[bpsN])
                            k.mm(psA[0:C, 64 * h_:64 * h_ + C], curN[0:C, 64 * h_:64 * h_ + C], curA[0:C, 64 * h_:64 * h_ + C], R=[bcA, bcN], W=[bpsA])
                        if not lastr:
                            k.cp("act", v3(nN, C), v3(psN, C), R=[bpsN], W=[bnN])
                        k.cp("dve", v3(nA, C), v3(psA, C), R=[bpsA], W=[bnA])
                        psZ, bpsZ = next_ps(0, 8)
                        for h_ in range(8):
                            k.mm(psZ[0:C, 64 * h_:64 * h_ + C], nA[0:C, 64 * h_:64 * h_ + C], Zc[0:C, 64 * h_:64 * h_ + C], R=[bnA, bZc], W=[bpsZ])
                        Zn, bZn = ZZ[(r_ + 1) % 2]
                        k.tt("dve", v3(Zn, C), v3(psZ, C), v3(Zc, C), ALU.add, R=[bpsZ, bZc], W=[bZn])
                        Zc, bZc = Zn, bZn
                        curA, bcA, curN, bcN = nA, bnA, nN, bnN
                    psW, bpsW = next_ps(0, 8)
                    for h_ in range(8):
                        k.mm(psW[0:C, 64 * h_:64 * h_ + 64], AakT[0:C, 64 * h_:64 * h_ + C], vtok[0:C, 64 * h_:64 * h_ + 64], R=[b_AakT, b_vtok], W=[bpsW])
                    k.cp("act", W1[0:C, :], psW[0:C, :], R=[bpsW], W=[b_W1])
                    psU, bpsU = next_ps(0, 8); psP, bpsP = next_ps(0, 8)
                    for h_ in range(8):
                        k.mm(psU[0:C, 64 * h_:64 * h_ + 64], Zc[0:C, 64 * h_:64 * h_ + C], W1[0:C, 64 * h_:64 * h_ + 64], R=[bZc, b_W1], W=[bpsU])
                        k.mm(psP[0:C, 64 * h_:64 * h_ + 64], Zc[0:C, 64 * h_:64 * h_ + C], atok[0:C, 64 * h_:64 * h_ + 64], R=[bZc, b_atok], W=[bpsP])
                    k.cp("act", U0[0:C, :], psU[0:C, :], R=[bpsU], W=[b_U0])
                    k.cp("dve", Ap[0:C, :], psP[0:C, :], R=[bpsP], W=[b_Ap])
                    psG, bpsG = next_ps(0, 8); psH, bpsH = next_ps(0, 8)
                    for p_ in range(4):
                        cs = slice(128 * p_, 128 * p_ + 128)
                        k.mm(psG[:, cs], Ap[0:C, cs], betok[0:C, cs], R=[b_Ap, b_betok], W=[bpsG])
                        k.mm(psH[:, cs], betok[0:C, cs], U0[0:C, cs], start=True, stop=False, R=[b_betok, b_U0], W=[bpsH])
                        k.mm(psH[:, cs], ketok[0:C, cs], vtok[0:C, cs], start=False, stop=True, R=[b_ketok, b_vtok], W=[bpsH])
                    bo_b = bones.unsqueeze(1).to_broadcast([128, 4, 128])
                    k.tt("dve", p4(GTp), p4(psG), bo_b, ALU.mult, R=[bpsG, b_c], W=[b_GTp])
                    for p_ in range(4):
                        k.stt("dve", p4(GTp)[:, p_, :], identf, pcs[:, p_, ci:ci + 1], p4(GTp)[:, p_, :], ALU.mult, ALU.add, R=[b_c, b_pcs, b_GTp], W=[b_GTp])
                    k.tt("dve", p4(Hbd), p4(psH), bo_b, ALU.mult, R=[bpsH, b_c], W=[b_Hbd])
                    if ci < 32:
                        Scur, bScur = Sb[ci % 2]; Snew, bSnew = Sb[(ci + 1) % 2]
                    else:
                        Scur, bScur = SIn[ci - 32]; Snew, bSnew = SF[ci - 32]
                    psS, bpsS = next_ps(0, 8)
                    for p_ in range(4):
                        cs = slice(128 * p_, 128 * p_ + 128)
                        k.mm(psS[:, cs], GTp[:, cs], Scur[:, cs], R=[b_GTp, bScur], W=[bpsS])
                    k.tt("dve", Snew, psS[:, :], Hbd, ALU.add, R=[bpsS, b_Hbd], W=[bSnew])
                    psR, bpsR = next_ps(0, 8); psY0, bpsY0 = next_ps(0, 8)
                    psRv = psR[:, :].rearrange("q (p h t) -> q p h t", p=4, h=2)
                    psY0v = psY0[:, :].rearrange("q (p h t) -> q p h t", p=4, h=2)
                    Brb4 = BrbT[0:C, :].rearrange("c (p h t) -> c p h t", p=4, h=2)
                    Brk4 = BrkT[0:C, :].rearrange("c (p h t) -> c p h t", p=4, h=2)
                    for p_ in range(4):
                        cs = slice(128 * p_, 128 * p_ + 128)
                        k.mm(psRv[:, p_, :, 0:C], Ap[0:C, cs], Brb4[:, p_, :, 0:C], R=[b_Ap, b_BrbT], W=[bpsR])
                        k.mm(psY0v[:, p_, :, 0:C], U0[0:C, cs], Brb4[:, p_, :, 0:C], start=True, stop=False, R=[b_U0, b_BrbT], W=[bpsY0])
                        k.mm(psY0v[:, p_, :, 0:C], vtok[0:C, cs], Brk4[:, p_, :, 0:C], start=False, stop=True, R=[b_vtok, b_BrkT], W=[bpsY0])
                    R4 = RpT[:, :].rearrange("q (p t) -> q p t", p=4); Y4 = Y0d[:, :].rearrange("q (p t) -> q p t", p=4); yo4 = yout[:, :].rearrange("q (p t) -> q p t", p=4)
                    for hf in range(2):
                        ps_ = slice(64 * hf, 64 * hf + 64)
                        k.tt("dve", R4[ps_, :, 0:C], psRv[ps_, :, hf, 0:C], rt[ps_, :, 0:C], ALU.add, R=[bpsR, b_rt], W=[b_RpT])
                        k.cp("act", Y4[ps_, :, 0:C], psY0v[ps_, :, hf, 0:C], R=[bpsY0], W=[b_Y0d])
                    psY, bpsY = next_ps(0, 8)
                    psYv = psY[:, 0:256].rearrange("q (p t) -> q p t", p=4)
                    for p_ in range(4):
                        cs = slice(128 * p_, 128 * p_ + 128)
                        k.mm(psYv[:, p_, 0:C], Scur[:, cs], R4[:, p_, 0:C], R=[bScur, b_RpT], W=[bpsY])
                    k.tt("dve", yo4[:, :, 0:C], psYv[:, :, 0:C], Y4[:, :, 0:C], ALU.add, R=[bpsY, b_Y0d], W=[b_yout])
                    k.dma("sp", YRW[:, t0:t0 + C].rearrange("(p c) t -> c p t", c=128), yo4[:, :, 0:C], R=[b_yout, bSCR])
                    if ci == 31:
                        emit_state(Sb[0][0], Sb[0][1], 0)
                    elif ci >= 32:
                        emit_state(SF[ci - 32][0], SF[ci - 32][1], 1 + ci - 32)
                b_r3 = Buf("r3")
                k.barrier(allb, [b_r3], dummy[0:1, 0:1])
                fence(bSCR)
                if upto == "R2":
                    break
                WS = Region(RA, NBIG)
                yr = [WS.take([512]), WS.take([512])]; bo = [WS.take([512]), WS.take([512])]; ga = [WS.take([512]), WS.take([512])]
                b_yr = [Buf("yr0"), Buf("yr1")]; b_bo = [Buf("bo0"), Buf("bo1")]; b_ga = [Buf("ga0"), Buf("ga1")]
                yc = WS.take([512]); sqq = WS.take([512]); rs_ = WS.take([512])
                b_yc, b_sqq, b_rs = Buf("yc"), Buf("sqq"), Buf("rs")
                k.barrier([b_r3], b_yr + b_bo + b_ga + [b_yc, b_sqq, b_rs], dummy[0:1, 0:1])
                it = 0
                for hp in range(4):
                    for (t0, nt) in TB:
                        i2 = it % 2
                        it += 1
                        sl = slice(0, nt)
                        k.dma("sp", yr[i2][:, sl], YRW[128 * hp:128 * hp + 128, t0:t0 + nt], R=[bSCR], W=[b_yr[i2]])
                        k.dma("sp", bo[i2][:, sl], RWS[7 * 512 + 128 * hp:7 * 512 + 128 * hp + 128, t0:t0 + nt], R=[bSCR], W=[b_bo[i2]])
                        k.dma("sp", ga[i2][:, sl], PF[3200 + 128 * hp:3200 + 128 * hp + 128, t0:t0 + nt], R=[bPF], W=[b_ga[i2]])
                        pm, bpm = next_ps(0, 4)
                        k.mm(pm[:, sl], bones, yr[i2][:, sl], R=[b_c, b_yr[i2]], W=[bpm])
                        k.stt("dve", yc[:, sl], pm[:, sl], -1.0 / 64, yr[i2][:, sl], ALU.mult, ALU.add, R=[bpm, b_yr[i2]], W=[b_yc])
                        k.tt("dve", sqq[:, sl], yc[:, sl], yc[:, sl], ALU.mult, R=[b_yc], W=[b_sqq])
                        pv, bpv = next_ps(4, 8)
                        k.mm(pv[:, sl], bones, sqq[:, sl], R=[b_c, b_sqq], W=[bpv])
                        k.ts("dve", rs_[:, sl], pv[:, sl], 1.0 / 64, 64e-5, ALU.mult, ALU.add, R=[bpv], W=[b_rs])
                        k.act(rs_[:, sl], rs_[:, sl], AF.Sqrt, R=[b_rs], W=[b_rs])
                        k.recip(rs_[:, sl], rs_[:, sl], R=[b_rs], W=[b_rs])
                        k.tt("dve", yc[:, sl], yc[:, sl], rs_[:, sl], ALU.mult, R=[b_yc, b_rs], W=[b_yc])
                        k.ts("dve", yc[:, sl], yc[:, sl], cpt[:, CP_LG + hp:CP_LG + hp + 1], cpt[:, CP_LB + hp:CP_LB + hp + 1], ALU.mult, ALU.add, R=[b_yc, b_cp], W=[b_yc])
                        k.tt("dve", yc[:, sl], yc[:, sl], bo[i2][:, sl], ALU.add, R=[b_yc, b_bo[i2]], W=[b_yc])
                        k.tt("dve", gT[:, hp, t0:t0 + nt], yc[:, sl], ga[i2][:, sl], ALU.mult, R=[b_yc, b_ga[i2]], W=[b_gT])
                k.barrier(b_yr + b_bo + b_ga + [b_yc, b_sqq, b_rs, b_rc, b_pcs], [b_ra, bW[0], bW[1]], dummy[0:1, 0:1])
            if upto == "C2":
                k.dma("sp", GDBG, RB[:, :].bitcast(BF16), R=[b_gT])
                break
            S.serial = False
            if "C2" in SKIP:
                k.memset("dve", gT[:, 0:4, :], 0.0, W=[b_gT])

            hT2 = RA[:, :].bitcast(BF16).rearrange("p (a b) -> p a b", a=16)
            b_h2 = Buf("hT2")
            WW = Region(Wt, 8192); MW = Region(Mt, 9216); MW.off = MISC_BASE
            Wm = [WW.take([16, 256], BF16), WW.take([16, 256], BF16)]
            Wb = [WW.take([4, 256], BF16), WW.take([4, 256], BF16)]
            acb1 = MW.take([NTOK], BF16)
            acb = [acb1, acb1]
            dst_ = [WW.take([4, 256]) for _ in range(2)]
            b_dst = [Buf(f"dst{i}") for i in range(2)]
            mgts = [MW.take([512]), WW.take([512])]; b_mgts = [Buf("mg0"), Buf("mg1")]
            accs = MW.take([2, NTOK]); mgt = mgts[0]; tmpt = MW.take([512])
            b_Wm = [Buf("Wm0"), Buf("Wm1")]; b_acb1 = Buf("acb"); b_acb = [b_acb1, b_acb1]
            b_accs, b_mgt, b_tmpt = Buf("accs"), Buf("mgt"), Buf("tmpt")
            k.barrier([b_ra, bW[0], bW[1]], [b_h2, b_accs, b_mgt, b_tmpt, b_acb1] + b_Wm + b_dst + b_mgts, dummy[0:1, 0:1])
            dsc = [0]
            mgc = [0]

            def dload(dst_ap, src_ap, bdst):
                i = dsc[0] % 2
                dsc[0] += 1
                k.dma("sp", dst_[i], src_ap.rearrange("p (kt c) -> p kt c", kt=4), W=[b_dst[i]])
                k.cp("pool", dst_ap, dst_[i], R=[b_dst[i]], W=[bdst])
            k.dma("sp", RA[:, :].bitcast(BF16), HTS, R=[bHTS], W=[b_h2])
            wi = 0
            ai = 0

            def dgroup(g):
                dblk_, b_ = g // 4, g % 4
                wm_, wb_, bwm = Wm[g % 2], Wb[g % 2], b_Wm[g % 2]
                for kq in range(4):
                    dload(wm_[:, 4 * kq:4 * kq + 4, :], w_mg[l][dblk_][b_][:, 1024 * kq:1024 * (kq + 1)], bwm)
                dload(wb_, w_br[l][dblk_][b_], bwm)
                return wm_, wb_, bwm

            nxtg = dgroup(0)
            for dblk in range(8):
                for b in range(4):
                    wm_, wb_, bwm = nxtg
                    if 4 * dblk + b + 1 < 32:
                        nxtg = dgroup(4 * dblk + b + 1)
                    for dt in range(2):
                        bcol = cpt[:, CP_BMG + 16 * b + 2 * dblk + dt:CP_BMG + 16 * b + 2 * dblk + dt + 1]
                        for (t0, nt) in TB:
                            pm, bpm = next_ps(0, 4)
                            pu, bpu = next_ps(4, 8)
                            for kt in range(16):
                                k.mm(pm[:, 0:nt], wm_[:, kt, 128 * dt:128 * dt + 128], hT2[:, kt, t0:t0 + nt], start=(kt == 0), stop=(kt == 15), R=[bwm, b_h2], W=[bpm])
                            for kt in range(4):
                                k.mm(pu[:, 0:nt], wb_[:, kt, 128 * dt:128 * dt + 128], gT[:, 4 * b + kt, t0:t0 + nt], start=(kt == 0), stop=(kt == 3), R=[bwm, b_gT], W=[bpu])
                            mg_, bmg_ = mgts[mgc[0] % 2], b_mgts[mgc[0] % 2]
                            mgc[0] += 1
                            k.act(mg_[:, 0:nt], pm[:, 0:nt], AF.Sigmoid, bias=bcol, R=[bpm, b_cp], W=[bmg_])
                            if b == 0:
                                k.tt("dve", accs[:, dt, t0:t0 + nt], mg_[:, 0:nt], pu[:, 0:nt], ALU.mult, R=[bmg_, bpu], W=[b_accs])
                            else:
                                k.tt("dve", tmpt[:, 0:nt], mg_[:, 0:nt], pu[:, 0:nt], ALU.mult, R=[bmg_, bpu], W=[b_tmpt])
                                k.tt("dve", accs[:, dt, t0:t0 + nt], accs[:, dt, t0:t0 + nt], tmpt[:, 0:nt], ALU.add, R=[b_tmpt, b_accs], W=[b_accs])
                for dt in range(2):
                    a_, ba_ = acb[ai % 2], b_acb[ai % 2]
                    ai += 1
                    k.cp("act", a_, accs[:, dt, :], R=[b_accs], W=[ba_])
                    r0 = 256 * dblk + 128 * dt
                    k.dma("sp", ACC[r0:r0 + 128, :], a_, R=[ba_, bACC])
            if upto == "D":
                break
            fence(bACC)
            accT = RA[:, :].bitcast(BF16).rearrange("p (a b) -> p a b", a=16)
            b_aT = Buf("accT")
            WS = Region(RB, NBIG)
            xo = [WS.take([512]), WS.take([512])]
            b_xo = [Buf("xo0"), Buf("xo1")]
            est = [WS.take([4, 512]) for _ in range(3)]
            b_est = [Buf(f"est{i}") for i in range(3)]
            k.barrier([b_h2, b_gT, b_accs, b_mgt, b_tmpt, b_acb1] + b_Wm + b_dst + b_mgts, [b_aT, bW[0], bW[1]] + b_xo + b_est, dummy[0:1, 0:1])
            k.dma("sp", accT, ACC.rearrange("(dt p) t -> p dt t", p=128), R=[bACC], W=[b_aT])
            xi = 0
            nxtw = load_wblock(w_out[l][0], 512, est, b_est)
            for cb in range(4):
                wb, bw = nxtw
                if cb + 1 < 4:
                    nxtw = load_wblock(w_out[l][cb + 1], 512, est, b_est)
                for (r0, nr) in TT:
                    x_, bx_ = xo[xi % 2], b_xo[xi % 2]
                    xi += 1
                    k.dma("sp", x_[0:nr, :], Xs[l][r0:r0 + nr, 512 * cb:512 * cb + 512], R=[bXs[l]], W=[bx_])
                    pp, bp = next_ps(0, 4)
                    for dt in range(16):
                        k.mm(pp[0:nr, :], accT[:, dt, r0:r0 + nr], wb[:, dt, :], start=(dt == 0), stop=(dt == 15), R=[b_aT, bw], W=[bp])
                    k.tt("dve", x_[0:nr, :], x_[0:nr, :], pp[0:nr, :], ALU.add, R=[bx_, bp], W=[bx_])
                    k.dma("sp", Xs[l + 1][r0:r0 + nr, 512 * cb:512 * cb + 512], x_[0:nr, :], R=[bx_, bXs[l + 1]])
            k.barrier([b_aT, bW[0], bW[1]] + b_xo + b_est, [bRA, bRB], dummy[0:1, 0:1])
            if upto == "E":
                break
        else:
            fence(*DRB)
            WS = Region(R1t, NBIG)
            xt = [WS.take([D]), WS.take([D])]
            bxt = [Buf("xt0"), Buf("xt1")]
            sq = WS.take([D]); b_sq = Buf("sq")
            gt = WS.take([D]); b_gt = Buf("gt")
            yo = [WS.take([D]), WS.take([D])]; b_yo = [Buf("yo0"), Buf("yo1")]
            ss = WS.take([8]); b_ss = Buf("ss")
            k.barrier([bR1, bR2], [bxt[0], bxt[1], b_sq, b_gt, b_ss] + b_yo, dummy[0:1, 0:1])
            k.dma("sp", gt, fin_g.partition_broadcast(128), W=[b_gt])
            for ti, (r0, nr) in enumerate(TT):
                xb, bx = xt[ti % 2], bxt[ti % 2]
                y_, by_ = yo[ti % 2], b_yo[ti % 2]
                k.dma("sp", xb[0:nr, :], Xs[L][r0:r0 + nr, :], R=[bXs[L]], W=[bx])
                k.act(sq[0:nr, :], xb[0:nr, :], AF.Square, R=[bx], W=[b_sq])
                k.red(ss[0:nr, 0:1], sq[0:nr, :], R=[b_sq], W=[b_ss])
                k.ts("dve", ss[0:nr, 1:2], ss[0:nr, 0:1], 1.0 / D, 1e-6, ALU.mult, ALU.add, R=[b_ss], W=[b_ss])
                k.act(ss[0:nr, 2:3], ss[0:nr, 1:2], AF.Sqrt, R=[b_ss], W=[b_ss])
                k.recip(ss[0:nr, 3:4], ss[0:nr, 2:3], R=[b_ss], W=[b_ss])
                k.stt("dve", y_[0:nr, :], xb[0:nr, :], ss[0:nr, 3:4], gt[0:nr, :], ALU.mult, ALU.mult, R=[bx, b_ss, b_gt], W=[by_])
                k.dma("sp", o_y[r0:r0 + nr, :], y_[0:nr, :], R=[by_])
        S.emit()
        print("ops", {e: len(v) for e, v in S.ops.items()})
    return nc


def _col(v, ntile):
    return np.ascontiguousarray(np.asarray(v).reshape(ntile, 128).T)


_SHARED = {}


def prep_shared(inp):
    f = lambda a: np.ascontiguousarray(np.asarray(a, dtype=np.float32))
    w_in = np.asarray(inp["w_in"], dtype=np.float32)
    sh = {}
    idx_fm = np.concatenate([np.arange(0, 2176), np.arange(2848, 2848 + 1024), np.arange(4384, 6432)])
    kpe0 = 2176 + 640
    idx_tm = np.concatenate([np.arange(2176, 2176 + 384), np.arange(kpe0, kpe0 + 32),
                             np.arange(kpe0 + 16, kpe0 + 32), np.arange(kpe0, kpe0 + 16),
                             np.arange(2176 + 384, 2176 + 640), np.arange(2848 + 512, 2848 + 1536)])
    assert idx_fm.size == NFM and idx_tm.size == NTM
    def blk(a, nk):
        Ln, _, nc_ = a.shape
        return np.ascontiguousarray(a.reshape(Ln, nk, 128, nc_).transpose(0, 2, 1, 3).reshape(Ln, 128, nk * nc_))
    wfm = w_in[:, :, idx_fm]
    sh["w_fm"] = np.ascontiguousarray(np.stack([blk(wfm[:, :, c0:c0 + 512], 16) for c0 in FM_BLOCKS], 1))
    sh["w_tm"] = blk(w_in[:, :, idx_tm], 16)
    wmg = w_in[:, :, 6432:]
    sh["w_mg"] = np.ascontiguousarray(np.stack([np.stack([blk(wmg[:, :, b * 2048 + 256 * d:b * 2048 + 256 * d + 256], 16) for b in range(4)], 1) for d in range(8)], 1))
    wbr = f(inp["w_branch"])
    sh["w_br"] = np.ascontiguousarray(np.stack([np.stack([blk(wbr[:, b, :, 256 * d:256 * d + 256], 4) for b in range(4)], 1) for d in range(8)], 1))
    wo = f(inp["w_out"])
    sh["w_out"] = np.ascontiguousarray(np.stack([blk(wo[:, :, 512 * c:512 * c + 512], 16) for c in range(4)], 1))
    cp = np.zeros((L, 128, NCP), np.float32)
    for l in range(L):
        cp[l, :, CP_MU:CP_MU + 13] = _col(inp["rw_mu"][l], 13)
        cp[l, :, CP_W0:CP_W0 + 4] = _col(inp["rw_w0"][l], 4)
        cp[l, :, CP_A0:CP_A0 + 4] = _col(inp["rw_a0"][l], 4)
        cp[l, :, CP_KK:CP_KK + 4] = _col(inp["rw_k_k"][l], 4)
        cp[l, :, CP_KA:CP_KA + 4] = _col(inp["rw_k_a"][l], 4)
        cp[l, :, CP_RK:CP_RK + 4] = _col(np.asarray(inp["rw_r_k"][l]).reshape(512), 4)
        cp[l, :, CP_LG:CP_LG + 4] = _col(inp["rw_lnx_g"][l], 4)
        cp[l, :, CP_LB:CP_LB + 4] = _col(inp["rw_lnx_b"][l], 4)
        cp[l, :, CP_LRE:CP_LRE + 16] = _col(np.asarray(inp["ssm_lam_re"][l]).reshape(2048), 16)
        cp[l, :, CP_LIM:CP_LIM + 16] = _col(np.asarray(inp["ssm_lam_im"][l]).reshape(2048), 16)
        cp[l, :, CP_LDT:CP_LDT + 16] = _col(np.repeat(np.asarray(inp["ssm_log_dt"][l]), 64), 16)
        cp[l, 0:32, CP_DSK:CP_DSK + 16] = np.asarray(inp["ssm_d"][l]).reshape(16, 32).T
        cp[l, :, CP_BGLU:CP_BGLU + 4] = _col(inp["ssm_b_glu"][l], 4)
        cp[l, :, CP_BMG:CP_BMG + 64] = _col(np.asarray(inp["b_merge"][l]).reshape(8192), 64)
    sh["cpar"] = cp
    sh["norm_g"] = f(inp["norm_g"])
    sh["fin_g"] = f(inp["final_norm_g"]).reshape(1, D)
    sh["rw_w2"] = f(inp["rw_w2"])
    sh["rw_a2"] = f(inp["rw_a2"])
    sh["ssm_b"] = np.ascontiguousarray(np.stack([f(inp["ssm_b_re"]).reshape(L, 2048, 16), f(inp["ssm_b_im"]).reshape(L, 2048, 16)], 1))
    cre = np.transpose(f(inp["ssm_c_re"]), (0, 1, 3, 2)).reshape(L, 2048, 16)
    cim = np.transpose(f(inp["ssm_c_im"]), (0, 1, 3, 2)).reshape(L, 2048, 16)
    sh["ssm_c"] = np.ascontiguousarray(np.stack([cre, cim], 1))
    sh["w_glu"] = f(inp["ssm_w_glu"])
    sh["qn_g"] = f(inp["mla_q_norm"])
    sh["kvn_g"] = f(inp["mla_kv_norm"])
    wq_ = f(inp["mla_w_q_up"])
    sh["wq"] = wq_
    wq4 = wq_.reshape(L, 384, 8, 96)
    sh["wqs"] = np.ascontiguousarray(np.concatenate([wq4[..., :64], wq4[..., 80:96], wq4[..., 64:80]], -1).reshape(L, 384, 768))
    wkv4 = f(inp["mla_w_kv_up"]).reshape(L, 256, 8, 128)
    sh["wk"] = np.ascontiguousarray(wkv4[..., :64].reshape(L, 256, 512))
    sh["wv"] = np.ascontiguousarray(wkv4[..., 64:].reshape(L, 256, 512))
    return sh


def prep_core(inp, sh, c):
    f = lambda a: np.ascontiguousarray(np.asarray(a, dtype=np.float32))
    b = c % 4
    s0 = 2 * c
    m = dict(sh)
    m["xin"] = np.ascontiguousarray(np.concatenate([f(inp["x_prompt"][b]), f(inp["x_sample"][s0]), f(inp["x_sample"][s0 + 1])], 0))
    shf = f(inp["state_rwkv_shift"])[:, s0:s0 + 2, 0, :]
    m["st_shift"] = np.ascontiguousarray(np.transpose(shf.reshape(L, 2, 13, 128), (0, 1, 3, 2)))
    m["st_wkv"] = f(inp["state_rwkv_wkv"])[:, s0:s0 + 2]
    sre = f(inp["state_ssm_re"])[:, s0:s0 + 2].reshape(L, 2, 16, 128)
    sim = f(inp["state_ssm_im"])[:, s0:s0 + 2].reshape(L, 2, 16, 128)
    m["st_ssm"] = np.ascontiguousarray(np.transpose(np.stack([sre, sim], 2), (0, 1, 2, 4, 3)))
    m["c_ckv"] = f(inp["cache_mla_ckv"])[:, s0:s0 + 2]
    m["c_kpe"] = f(inp["cache_mla_kpe"])[:, s0:s0 + 2]
    m["c_sbk"] = f(inp["cache_sb_k"])[:, s0:s0 + 2].reshape(L, 2, PAST, 512)
    m["c_sbv"] = f(inp["cache_sb_v"])[:, s0:s0 + 2].reshape(L, 2, PAST, 512)
    return m


_NC = {}


def kernel(**inputs):
    if "nc" not in _NC:
        _NC["nc"] = build()
    nc = _NC["nc"]
    sh = prep_shared(inputs)
    in_maps = [prep_core(inputs, sh, c) for c in range(8)]
    res = run_bass_kernel_spmd(nc, in_maps, core_ids=list(range(8))).results
    f = np.float32
    y_prompt = np.stack([res[b]["o_y"][:TP] for b in range(4)], 0).astype(f)
    y_sample = np.stack([res[s // 2]["o_y"][TP + 16 * (s % 2):TP + 16 * (s % 2) + 16] for s in range(16)], 0).astype(f)

    def pr(name, fn):
        return np.stack([np.stack([fn(res[b][name][l]) for b in range(4)], 0) for l in range(L)], 0).astype(f)

    def sa(name, fn):
        return np.stack([np.stack([fn(res[s // 2][name][l], s % 2) for s in range(16)], 0) for l in range(L)], 0).astype(f)

    shift_p = pr("o_shift", lambda a: a[0].reshape(1, 1664))
    wkv_p = pr("o_wkv", lambda a: a[0])
    ssm_re_p = pr("o_ssm", lambda a: a[0, 0].reshape(32, 64))
    ssm_im_p = pr("o_ssm", lambda a: a[1, 0].reshape(32, 64))
    ckv_p = pr("o_ckv", lambda a: a[:TP])
    kpe_p = pr("o_kpe", lambda a: a[:TP])
    sbk_p = pr("o_sbk", lambda a: a[:TP].reshape(TP, 8, 64))
    sbv_p = pr("o_sbv", lambda a: a[:TP].reshape(TP, 8, 64))
    shift_s = sa("o_shift", lambda a, j: a[1 + j].reshape(1, 1664))
    wkv_s = sa("o_wkv", lambda a, j: a[1 + j])
    ssm_re_s = sa("o_ssm", lambda a, j: a[0, 1 + j].reshape(32, 64))
    ssm_im_s = sa("o_ssm", lambda a, j: a[1, 1 + j].reshape(32, 64))
    rows = lambda a, j: a[TP + 16 * j:TP + 16 * j + 16]
    ckv_s = sa("o_ckv", rows)
    kpe_s = sa("o_kpe", rows)
    sbk_s = sa("o_sbk", lambda a, j: rows(a, j).reshape(16, 8, 64))
    sbv_s = sa("o_sbv", lambda a, j: rows(a, j).reshape(16, 8, 64))
    return (y_prompt, y_sample, shift_p, wkv_p, ssm_re_p, ssm_im_p, ckv_p, kpe_p, sbk_p, sbv_p,
            shift_s, wkv_s, ssm_re_s, ssm_im_s, ckv_s, kpe_s, sbk_s, sbv_s)
```

```python
import math
from contextlib import ExitStack
import numpy as np
import concourse.bass as bass
import concourse.mybir as mybir
from concourse.bass_utils import run_bass_kernel_spmd

F32 = mybir.dt.float32
BF16 = mybir.dt.bfloat16
I32 = mybir.dt.int32
ALU = mybir.AluOpType
AF = mybir.ActivationFunctionType
AX = mybir.AxisListType

L = 2
D = 2048
TP = 2048
TS = 16
NTOK = TP + 2 * TS
PAST = 1024
NFM = 5248
NTM = 1728
FM_BLOCKS = [512 * i for i in range(10)] + [NFM - 512]
TB = [(0, 512), (512, 512), (1024, 512), (1536, 512), (2048, 32)]
TT = [(128 * i, 128) for i in range(16)] + [(2048, 32)]
SEGS = [(0, 2048), (2048, 16), (2064, 16)]
PI = math.pi
CP_MU, CP_W0, CP_A0, CP_KK, CP_KA, CP_RK, CP_LG, CP_LB = 0, 13, 17, 21, 25, 29, 33, 37
CP_LRE, CP_LIM, CP_LDT, CP_DSK, CP_BGLU, CP_BMG = 41, 57, 73, 89, 105, 109
NCP = 173

EPOCH = 24000
DMA_RING = {"sp": 16, "pool": 8, "act": 4}


class Buf:
    __slots__ = ("name", "w", "r")

    def __init__(self, name="b"):
        self.name = name
        self.w = None
        self.r = []


class Ev:
    __slots__ = ("eng", "seq", "sem", "val", "dma")

    def __init__(self, eng, seq, sem, val, dma):
        self.eng, self.seq, self.sem, self.val, self.dma = eng, seq, sem, val, dma


class Sched:
    def __init__(self, nc, stack):
        self.nc = nc
        self.stack = stack
        self.engobj = {"pe": nc.tensor, "dve": nc.vector, "act": nc.scalar, "pool": nc.gpsimd, "sp": nc.sync}
        self.ops = {e: [] for e in self.engobj}
        self.ccount = {e: 0 for e in self.engobj}
        self.csems = {e: [] for e in self.engobj}
        self.dcount = {q: 0 for q in DMA_RING}
        self.dsems = {q: [stack.enter_context(nc.semaphore(f"d_{q}_{i}")) for i in range(n)] for q, n in DMA_RING.items()}
        self.known = {e: {} for e in self.engobj}
        self.knownd = {e: {} for e in self.engobj}
        self.serial = False
        self.last_ev = None

    def _csem(self, eng, ep):
        while len(self.csems[eng]) <= ep:
            self.csems[eng].append(self.stack.enter_context(self.nc.semaphore(f"c_{eng}_{len(self.csems[eng])}")))
        return self.csems[eng][ep]

    def _need(self, eng, ev, waits):
        if ev.dma:
            k = self.knownd[eng]
            if k.get(id(ev.sem), -1) >= ev.val:
                return
            k[id(ev.sem)] = ev.val
        else:
            k = self.known[eng]
            if k.get(ev.eng, -1) >= ev.seq:
                return
            k[ev.eng] = ev.seq
        waits.append((ev.sem, ev.val))

    def add(self, eng, fn, reads=(), writes=(), dma=False):
        waits = []
        for b in reads:
            ev = b.w
            if ev is not None and not (ev.eng == eng and not ev.dma and eng == "pe"):
                self._need(eng, ev, waits)
        for b in writes:
            ev = b.w
            if ev is not None and not (ev.eng == eng and not ev.dma and eng == "pe"):
                self._need(eng, ev, waits)
            for ev in b.r:
                if ev.eng == eng and not ev.dma:
                    continue
                self._need(eng, ev, waits)
        if self.serial and self.last_ev is not None:
            self._need(eng, self.last_ev, waits)
        if dma:
            q = eng
            i = self.dcount[q]
            self.dcount[q] += 1
            n = DMA_RING[q]
            sem = self.dsems[q][i % n]
            if i >= n:
                pv = 16 * (i // n)
                k = self.knownd[eng]
                if k.get(id(sem), -1) < pv:
                    k[id(sem)] = pv
                    waits.append((sem, pv))
            ev = Ev(eng, i, sem, 16 * (i // n + 1), True)
            inc = 16
        else:
            s = self.ccount[eng]
            self.ccount[eng] += 1
            sem = self._csem(eng, s // EPOCH)
            ev = Ev(eng, s, sem, (s % EPOCH) + 1, False)
            inc = 1
        self.ops[eng].append((fn, waits, sem, inc))
        self.last_ev = ev
        for b in writes:
            b.w = ev
            b.r = []
        for b in reads:
            if b.w is not ev:
                b.r.append(ev)
        return ev

    def emit(self):
        nc = self.nc
        final = []
        for q, n in DMA_RING.items():
            c = self.dcount[q]
            for slot in range(n):
                cnt = (c - slot + n - 1) // n if c > slot else 0
                if cnt > 0:
                    final.append((self.dsems[q][slot], 16 * cnt))
        for e in self.engobj:
            c = self.ccount[e]
            if c > 0 and e != "sp":
                final.append((self.csems[e][(c - 1) // EPOCH], ((c - 1) % EPOCH) + 1))
        ops = self.ops
        with nc.Block() as block:
            def run(eobj, lst, fin=None):
                for fn, waits, sem, inc in lst:
                    for (s, v) in waits:
                        eobj.wait_ge(s, v)
                    fn(eobj).then_inc(sem, inc)
                if fin:
                    for (s, v) in fin:
                        eobj.wait_ge(s, v)

            @block.tensor
            def _(e):
                run(e, ops["pe"])

            @block.vector
            def _(e):
                run(e, ops["dve"])

            @block.scalar
            def _(e):
                run(e, ops["act"])

            @block.gpsimd
            def _(e):
                run(e, ops["pool"])

            @block.sync
            def _(e):
                run(e, ops["sp"], final)


class K:
    def __init__(self, S):
        self.S = S

    def dma(self, q, out, in_, R=(), W=(), **kw):
        self.S.add(q, lambda e: e.dma_start(out=out, in_=in_, **kw), R, W, dma=True)

    def mm(self, out, lhsT, rhs, start=True, stop=True, R=(), W=()):
        self.S.add("pe", lambda e: e.matmul(out, lhsT=lhsT, rhs=rhs, start=start, stop=stop), R, W)

    def tr(self, out, in_, ident, R=(), W=()):
        self.S.add("pe", lambda e: e.transpose(out=out, in_=in_, identity=ident), R, W)

    def act(self, out, in_, func, bias=None, scale=None, R=(), W=()):
        kw = {}
        if bias is not None:
            kw["bias"] = bias
        if scale is not None:
            kw["scale"] = scale
        self.S.add("act", lambda e: e.activation(out=out, in_=in_, func=func, **kw), R, W)

    def tt(self, eng, out, in0, in1, op, R=(), W=()):
        self.S.add(eng, lambda e: e.tensor_tensor(out=out, in0=in0, in1=in1, op=op), R, W)

    def ts(self, eng, out, in0, s1, s2=None, op0=ALU.mult, op1=None, R=(), W=()):
        if op1 is None:
            self.S.add(eng, lambda e: e.tensor_scalar(out=out, in0=in0, scalar1=s1, scalar2=None, op0=op0), R, W)
        else:
            self.S.add(eng, lambda e: e.tensor_scalar(out=out, in0=in0, scalar1=s1, scalar2=s2, op0=op0, op1=op1), R, W)

    def stt(self, eng, out, in0, scalar, in1, op0, op1, R=(), W=()):
        self.S.add(eng, lambda e: e.scalar_tensor_tensor(out=out, in0=in0, scalar=scalar, in1=in1, op0=op0, op1=op1), R, W)

    def cp(self, eng, out, in_, R=(), W=()):
        if eng == "act":
            self.S.add("act", lambda e: e.activation(out=out, in_=in_, func=AF.Copy), R, W)
        else:
            self.S.add(eng, lambda e: e.tensor_copy(out=out, in_=in_), R, W)

    def memset(self, eng, out, val, R=(), W=()):
        self.S.add(eng, lambda e: e.memset(out, val), R, W)

    def red(self, out, in_, R=(), W=()):
        self.S.add("dve", lambda e: e.reduce_sum(out=out, in_=in_, axis=AX.X), R, W)

    def recip(self, out, in_, R=(), W=()):
        self.S.add("dve", lambda e: e.reciprocal(out=out, in_=in_), R, W)

    def scan(self, out, d0, d1, init, R=(), W=()):
        self.S.add("dve", lambda e: e.tensor_tensor_scan(out=out, data0=d0, data1=d1, initial=init, op0=ALU.mult, op1=ALU.add), R, W)

    def iota(self, out, pattern, base, cm, R=(), W=()):
        self.S.add("pool", lambda e: e.iota(out, pattern=pattern, base=base, channel_multiplier=cm, allow_small_or_imprecise_dtypes=True), R, W)

    def asel(self, out, in_, pattern, op, fill, base, cm, R=(), W=()):
        self.S.add("pool", lambda e: e.affine_select(out=out, in_=in_, pattern=pattern, compare_op=op, fill=fill, base=base, channel_multiplier=cm), R, W)

    def barrier(self, olds, news, scratch):
        self.S.add("dve", lambda e: e.memset(scratch, 0.0), (), list(olds) + list(news))


class Region:
    def __init__(self, t, n32):
        self.t = t
        self.n32 = n32
        self.off = 0

    def reset(self):
        self.off = 0

    def take(self, shape, dt=F32, parts=128):
        n = 1
        for s in shape:
            n *= s
        n32 = n if dt != BF16 else (n + 1) // 2
        a = self.t[0:parts, self.off:self.off + n32]
        self.off += n32
        assert self.off <= self.n32, (self.off, self.n32)
        if dt == BF16:
            a = a.bitcast(BF16)
        elif dt == I32:
            a = a.bitcast(I32)
        if len(shape) == 1:
            return a
        names = "abcdefg"[:len(shape)]
        kw = {names[i]: shape[i] for i in range(len(shape) - 1)}
        return a.rearrange("p (" + " ".join(names) + ") -> p " + " ".join(names), **kw)


def build(upto="all", dbg=(), SKIP=()):
    nc = bass.Bass("TRN2", target_bir_lowering=False)
    dbg = set(dbg)

    def din(name, shape, dt=F32):
        return nc.dram_tensor(name, list(shape), dt, kind="ExternalInput").ap()

    def dout(name, shape, dt=F32):
        return nc.dram_tensor(name, list(shape), dt, kind="ExternalOutput").ap()

    def dscr(name, shape, dt=F32):
        kind = "ExternalOutput" if name in dbg else "Internal"
        return nc.dram_tensor(name, list(shape), dt, kind=kind).ap()

    def dbgdump(k, name, ap, R):
        if name in dbg:
            t = nc.dram_tensor(name, list(ap.shape), ap.dtype, kind="ExternalOutput").ap()
            k.dma("sp", t, ap, R=R)

    xin = din("xin", [NTOK, D])
    w_fm = din("w_fm", [L, 11, 128, 8192])
    w_tm = din("w_tm", [L, 128, 16 * NTM])
    w_mg = din("w_mg", [L, 8, 4, 128, 4096])
    w_br = din("w_br", [L, 8, 4, 128, 1024])
    w_out = din("w_out", [L, 4, 128, 8192])
    cpar = din("cpar", [L, 128, NCP])
    norm_g = din("norm_g", [L, D])
    fin_g = din("fin_g", [1, D])
    rw_w2 = din("rw_w2", [L, 64, 512])
    rw_a2 = din("rw_a2", [L, 64, 512])
    ssm_b = din("ssm_b", [L, 2, 2048, 16])
    ssm_c = din("ssm_c", [L, 2, 2048, 16])
    w_glu = din("w_glu", [L, 512, 512])
    qn_g = din("qn_g", [L, 384])
    kvn_g = din("kvn_g", [L, 256])
    wq = din("wq", [L, 384, 768])
    wqs = din("wqs", [L, 384, 768])
    wk = din("wk", [L, 256, 512])
    wv = din("wv", [L, 256, 512])
    st_shift = din("st_shift", [L, 2, 128, 13])
    st_wkv = din("st_wkv", [L, 2, 8, 64, 64])
    st_ssm = din("st_ssm", [L, 2, 2, 128, 16])
    c_ckv = din("c_ckv", [L, 2, PAST, 256])
    c_kpe = din("c_kpe", [L, 2, PAST, 32])
    c_sbk = din("c_sbk", [L, 2, PAST, 512])
    c_sbv = din("c_sbv", [L, 2, PAST, 512])

    o_y = dout("o_y", [NTOK, D])
    o_shift = dout("o_shift", [L, 3, 1664])
    o_wkv = dout("o_wkv", [L, 3, 8, 64, 64])
    o_ssm = dout("o_ssm", [L, 2, 3, 2048])
    o_ckv = dout("o_ckv", [L, NTOK, 256])
    o_kpe = dout("o_kpe", [L, NTOK, 32])
    o_sbk = dout("o_sbk", [L, NTOK, 512])
    o_sbv = dout("o_sbv", [L, NTOK, 512])

    X1 = dscr("X1", [NTOK, D])
    X2 = dscr("X2", [NTOK, D])
    Xs = [xin, X1, X2]
    PF = dscr("PF", [NFM, NTOK])
    HTS = dscr("HTS", [128, 16 * NTOK], BF16)
    QNS = dscr("QNS", [128, 3 * NTOK], BF16)
    YS = dscr("YS", [512, NTOK])
    ACC = dscr("ACC", [D, NTOK], BF16)
    GDBG = dscr("GDBG", [128, 16 * NTOK], BF16)
    QF = dscr("QF", [96, 8 * NTOK], BF16).rearrange("p (h t) -> p h t", h=8)
    KF = dscr("KF", [96, 8 * NTOK], BF16).rearrange("p (h t) -> p h t", h=8)
    KFP = dscr("KFP", [96, 16 * PAST], BF16).rearrange("p (h j t) -> p h j t", h=8, j=2)
    VT = dscr("VT", [NTOK, 512], BF16)
    VTP = dscr("VTP", [2 * PAST, 512], BF16)
    RWS = dscr("RWS", [8 * 512, NTOK])
    YRW = dscr("YRW", [512, NTOK])

    with ExitStack() as st:
        S = Sched(nc, st)
        k = K(S)
        sbt = lambda name, shape, dt: st.enter_context(nc.sbuf_tensor(name, shape, dt))
        pst = lambda name, shape, dt: st.enter_context(nc.psum_tensor(name, shape, dt))
        NBIG = 16 * NTOK // 2
        R1t = sbt("R1", [128, NBIG], F32)
        R2t = sbt("R2", [128, NBIG], F32)
        Wt = sbt("Wr", [128, 8192], F32)
        Mt = sbt("Mr", [128, 9216], F32)
        bR1, bR2 = Buf("R1"), Buf("R2")
        ps = [pst(f"ps{i}", [128, 512], F32) for i in range(8)]
        bps = [Buf(f"ps{i}") for i in range(8)]

        MR = Region(Mt, 9216)
        identf = MR.take([128], F32); b_c = Buf("const")
        identb = MR.take([128], BF16)
        bones = MR.take([128], F32)
        dummy = MR.take([8], F32)
        bdum = Buf("dummy")
        MISC_BASE = MR.off

        k.memset("pool", identf, 0.0, W=[b_c])
        k.asel(identf, identf, [[-1, 128]], ALU.not_equal, 1.0, 0, 1, R=[b_c], W=[b_c])
        k.cp("dve", identb, identf, R=[b_c], W=[b_c])
        k.memset("pool", bones, 0.0, W=[b_c])
        k.memset("pool", bones[0:64, 0:64], 1.0, W=[b_c])
        k.memset("pool", bones[64:128, 64:128], 1.0, W=[b_c])

        def sincos(ang, shape, ws, bsc, want_cos=True):
            P = ang.shape[0]
            u = ws.take(shape, F32, P) if False else None
            t_u = ws.take(shape)[0:P]
            t_i = ws.take(shape, I32)[0:P]
            t_r = ws.take(shape)[0:P]
            t_m = ws.take(shape)[0:P]
            t_s = ws.take(shape)[0:P]
            k.ts("dve", t_u, ang, 1.0 / (2 * PI), R=[bsc], W=[bsc])
            k.cp("dve", t_i, t_u, R=[bsc], W=[bsc])
            k.cp("dve", t_u, t_i, R=[bsc], W=[bsc])
            k.stt("dve", t_r, t_u, -2 * PI, ang, ALU.mult, ALU.add, R=[bsc], W=[bsc])
            k.ts("dve", t_m, t_r, PI, -2 * PI, ALU.is_gt, ALU.mult, R=[bsc], W=[bsc])
            k.tt("dve", t_r, t_r, t_m, ALU.add, R=[bsc], W=[bsc])
            k.ts("dve", t_m, t_r, -PI, 2 * PI, ALU.is_lt, ALU.mult, R=[bsc], W=[bsc])
            k.tt("dve", t_r, t_r, t_m, ALU.add, R=[bsc], W=[bsc])
            k.act(t_s, t_r, AF.Sin, R=[bsc], W=[bsc])
            t_c = None
            if want_cos:
                t_c = ws.take(shape)[0:P]
                k.ts("dve", t_u, t_r, PI / 2, None, ALU.add, R=[bsc], W=[bsc])
                k.ts("dve", t_m, t_u, PI, -2 * PI, ALU.is_gt, ALU.mult, R=[bsc], W=[bsc])
                k.tt("dve", t_u, t_u, t_m, ALU.add, R=[bsc], W=[bsc])
                k.act(t_c, t_u, AF.Sin, R=[bsc], W=[bsc])
            return t_s, t_c

        RC = MR.take([17, 32]); RS = MR.take([17, 32]); b_rope = Buf("rope")
        gq = MR.take([384]); gkv = MR.take([256]); b_gq = Buf("gq")
        cpt = MR.take([NCP]); b_cp = Buf("cp")
        MISC_BASE = MR.off
        ws0 = Region(R1t, NBIG)
        b_s0 = Buf("s0")
        posT = ws0.take([17]); inv16 = ws0.take([16]); p16 = ws0.take([4]); angT = ws0.take([17, 16])
        k.iota(posT, [[128, 17]], 0, 1, W=[b_s0])
        k.iota(p16[:, 0:1], [[0, 1]], 0, 1, R=[b_s0], W=[b_s0])
        k.ts("dve", p16[:, 1:2], p16[:, 0:1], 16.0, -16.0, ALU.is_ge, ALU.mult, R=[b_s0], W=[b_s0])
        k.stt("dve", posT[:, 16:17], p16[:, 0:1], 1024.0, p16[:, 1:2], ALU.add, ALU.add, R=[b_s0], W=[b_s0])
        k.iota(inv16, [[1, 16]], 0, 0, R=[b_s0], W=[b_s0])
        k.act(inv16, inv16, AF.Exp, scale=-math.log(10000.0) / 16.0, R=[b_s0], W=[b_s0])
        k.tt("dve", angT, posT.unsqueeze(2).to_broadcast([128, 17, 16]), inv16.unsqueeze(1).to_broadcast([128, 17, 16]), ALU.mult, R=[b_s0], W=[b_s0])
        sT, cT = sincos(angT, [17, 16], ws0, b_s0)
        k.cp("dve", RC[:, :, 0:16], cT, R=[b_s0], W=[b_rope])
        k.cp("dve", RC[:, :, 16:32], cT, R=[b_s0], W=[b_rope])
        k.ts("dve", RS[:, :, 0:16], sT, -1.0, R=[b_s0], W=[b_rope])
        k.cp("dve", RS[:, :, 16:32], sT, R=[b_s0], W=[b_rope])
        k.barrier([b_s0], [bR1], dummy[0:1, 0:1])

        def wblock(buf_i):
            return Wt[:, 4096 * buf_i:4096 * (buf_i + 1)].bitcast(BF16).rearrange("p (a b) -> p a b", a=16)

        bW = [Buf("W0"), Buf("W1")]
        wcount = [0]

        stgc = [0]

        def load_wblock(src2d, ncols, stgs, bstgs):
            i = wcount[0] % 2
            wcount[0] += 1
            wb = wblock(i)
            for kq in range(4):
                si = stgc[0] % len(stgs)
                stgc[0] += 1
                sg_, bsg_ = stgs[si], bstgs[si]
                k.dma("sp", sg_, src2d[:, 4 * ncols * kq:4 * ncols * (kq + 1)].rearrange("p (kt c) -> p kt c", kt=4), W=[bsg_])
                k.cp("pool", wb[:, 4 * kq:4 * kq + 4, 0:ncols], sg_, R=[bsg_], W=[bW[i]])
            return wb, bW[i]

        bPF, bYS, bHTS, bQNS, bACC, bX1, bX2, bOck, bOkp, bOsk, bOsv, bSCR = (Buf(n) for n in ("PF", "YS", "HTS", "QNS", "ACC", "X1", "X2", "Ock", "Okp", "Osk", "Osv", "SCR"))
        bXs = [Buf("xin"), bX1, bX2]
        DRB = [bPF, bYS, bHTS, bQNS, bACC, bX1, bX2, bOck, bOkp, bOsk, bOsv, bSCR]

        def fence(*bs):
            k.barrier(list(bs), list(bs), dummy[0:1, 0:1])

        psrot = [0]

        def next_ps(lo=0, hi=4):
            i = lo + psrot[0] % (hi - lo)
            psrot[0] += 1
            return ps[i], bps[i]

        evrot = [0]

        def ev_eng():
            evrot[0] += 1
            return "act" if evrot[0] % 2 else "dve"

        for l in range(L):
            RA, RB = (R1t, R2t) if l % 2 == 0 else (R2t, R1t)
            bRA, bRB = (bR1, bR2) if l % 2 == 0 else (bR2, bR1)
            hT = RA[:, :].bitcast(BF16).rearrange("p (a b) -> p a b", a=16)
            b_hT = Buf("hT")
            k.barrier([bRA, bRB], [b_hT], dummy[0:1, 0:1])
            fence(*DRB)
            X = Xs[l]
            WS = Region(RB, NBIG)
            xt = [WS.take([D]), WS.take([D])]
            bxt = [Buf("xt0"), Buf("xt1")]
            sq = WS.take([D]); b_sq = Buf("sq")
            gt = WS.take([D]); b_gt = Buf("gt")
            hb = WS.take([D], BF16); b_hb = Buf("hb")
            ss = WS.take([8]); b_ss = Buf("ss")
            k.barrier([bRB], [bxt[0], bxt[1], b_sq, b_gt, b_hb, b_ss], dummy[0:1, 0:1])
            k.dma("sp", gt, norm_g[l:l + 1, :].partition_broadcast(128), W=[b_gt])
            for ti, (r0, nr) in enumerate(TT):
                xb, bx = xt[ti % 2], bxt[ti % 2]
                k.dma("sp", xb[0:nr, :], X[r0:r0 + nr, :], R=[bXs[l]], W=[bx])
                k.act(sq[0:nr, :], xb[0:nr, :], AF.Square, R=[bx], W=[b_sq])
                k.red(ss[0:nr, 0:1], sq[0:nr, :], R=[b_sq], W=[b_ss])
                k.ts("dve", ss[0:nr, 1:2], ss[0:nr, 0:1], 1.0 / D, 1e-6, ALU.mult, ALU.add, R=[b_ss], W=[b_ss])
                k.act(ss[0:nr, 2:3], ss[0:nr, 1:2], AF.Sqrt, R=[b_ss], W=[b_ss])
                k.recip(ss[0:nr, 3:4], ss[0:nr, 2:3], R=[b_ss], W=[b_ss])
                k.stt("dve", hb[0:nr, :], xb[0:nr, :], ss[0:nr, 3:4], gt[0:nr, :], ALU.mult, ALU.mult, R=[bx, b_ss, b_gt], W=[b_hb])
                for half in range(2):
                    pt, bpt = next_ps(4, 8)
                    ptv = pt[:, :].bitcast(BF16).rearrange("p (a b) -> p a b", a=8)
                    for j in range(8):
                        jj = half * 8 + j
                        k.tr(ptv[:, j, 0:nr], hb[0:nr, 128 * jj:128 * jj + 128], identb[0:nr, 0:nr], R=[b_hb, b_c], W=[bpt])
                    k.cp(ev_eng(), hT[:, 8 * half:8 * half + 8, r0:r0 + nr], ptv[:, :, 0:nr], R=[bpt], W=[b_hT])
            k.dma("sp", HTS, RA[:, :].bitcast(BF16), R=[b_hT, bHTS])
            if upto == "A":
                break

            WS = Region(RB, NBIG)
            stg = [WS.take([NTOK]), WS.take([NTOK])]
            bstg = [Buf("stg0"), Buf("stg1")]
            wst = [WS.take([4, 512]) for _ in range(3)]
            bwst = [Buf(f"wst{i}") for i in range(3)]
            k.barrier([bxt[0], bxt[1], b_sq, b_gt, b_hb, b_ss], bstg + bwst, dummy[0:1, 0:1])
            tiles = []
            for i in range(17):
                tiles.append((128 * i, 128, False))
            for i in range(16):
                tiles.append((2176 + 64 * i, 64, False))
            for i in range(16):
                tiles.append((3200 + 128 * i, 128, True))
            sc = 0
            nxtw = load_wblock(w_fm[l][0], 512, wst, bwst)
            for bi_, c0 in enumerate(FM_BLOCKS):
                ncol = 512
                wb, bw = nxtw
                if bi_ + 1 < len(FM_BLOCKS):
                    nxtw = load_wblock(w_fm[l][bi_ + 1], 512, wst, bwst)
                lo_ = 512 * bi_ if bi_ < 10 else 5120
                for (tc0, M, isg) in tiles:
                    if not (lo_ <= tc0 < c0 + ncol):
                        continue
                    sg, bsg = stg[sc % 2], bstg[sc % 2]
                    sc += 1
                    for (t0, nt) in TB:
                        pp, bp = next_ps(0, 4)
                        for kt in range(16):
                            k.mm(pp[0:M, 0:nt], wb[:, kt, tc0 - c0:tc0 - c0 + M], hT[:, kt, t0:t0 + nt], start=(kt == 0), stop=(kt == 15), R=[bw, b_hT], W=[bp])
                        if isg:
                            k.act(sg[0:M, t0:t0 + nt], pp[0:M, 0:nt], AF.Silu, R=[bp], W=[bsg])
                        else:
                            k.cp(ev_eng(), sg[0:M, t0:t0 + nt], pp[0:M, 0:nt], R=[bp], W=[bsg])
                    k.dma("sp", PF[tc0:tc0 + M, :], sg[0:M, :], R=[bsg, bPF])
            if upto == "B1a":
                break

            WS = Region(RB, NBIG)
            wtm = WS.take([16, NTM], BF16); b_wtm = Buf("wtm")
            qa = WS.take([448]); b_qa = Buf("qa")
            sq2 = WS.take([384]); b_sq2 = Buf("sq2")
            qnb = WS.take([384], BF16); b_qnb = Buf("qnb")
            kvt = WS.take([256]); b_kvt = Buf("kvt")
            ckvt = WS.take([256]); b_ckvt = Buf("ckvt")
            MW = Region(Mt, 9216); MW.off = MISC_BASE
            krt = MW.take([64]); b_krt = Buf("krt")
            sbkt = WS.take([512]); b_sbkt = Buf("sbkt")
            sbvt = WS.take([512]); b_sbvt = Buf("sbvt")
            qnT = WS.take([3, 128], BF16); b_qnT = Buf("qnT")
            ss2 = MW.take([8]); b_ss2 = Buf("ss2")
            WW = Region(Wt, 8192)
            tst = [WW.take([NTM]) for _ in range(4)]
            btst = [Buf(f"tst{i}") for i in range(4)]
            k.barrier(bstg + bwst + [bW[0], bW[1]], [b_wtm, b_qa, b_sq2, b_qnb, b_kvt, b_ckvt, b_krt, b_sbkt, b_sbvt, b_qnT, b_ss2] + btst, dummy[0:1, 0:1])
            for kt in range(16):
                k.dma("sp", tst[kt % 4], w_tm[l][:, NTM * kt:NTM * (kt + 1)], W=[btst[kt % 4]])
                k.cp("pool", wtm[:, kt, :], tst[kt % 4], R=[btst[kt % 4]], W=[b_wtm])
            k.dma("sp", gq, qn_g[l:l + 1, :].partition_broadcast(128), W=[b_gq])
            k.dma("sp", gkv, kvn_g[l:l + 1, :].partition_broadcast(128), W=[b_gq])
            QNSv = QNS.rearrange("p (a t) -> p a t", a=3)
            banks = [(0, 448), (448, 256), (704, 512), (1216, 512)]
            for ti, (r0, nr) in enumerate(TT):
                for bi, (c0, ncol) in enumerate(banks):
                    for kt in range(16):
                        k.mm(ps[bi][0:nr, 0:ncol], hT[:, kt, r0:r0 + nr], wtm[:, kt, c0:c0 + ncol], start=(kt == 0), stop=(kt == 15), R=[b_hT, b_wtm], W=[bps[bi]])
                k.cp("act", qa[0:nr, :], ps[0][0:nr, 0:448], R=[bps[0]], W=[b_qa])
                k.act(sq2[0:nr, :], qa[0:nr, 0:384], AF.Square, R=[b_qa], W=[b_sq2])
                k.red(ss2[0:nr, 0:1], sq2[0:nr, :], R=[b_sq2], W=[b_ss2])
                k.ts("dve", ss2[0:nr, 1:2], ss2[0:nr, 0:1], 1.0 / 384, 1e-6, ALU.mult, ALU.add, R=[b_ss2], W=[b_ss2])
                k.act(ss2[0:nr, 2:3], ss2[0:nr, 1:2], AF.Sqrt, R=[b_ss2], W=[b_ss2])
                k.recip(ss2[0:nr, 3:4], ss2[0:nr, 2:3], R=[b_ss2], W=[b_ss2])
                k.stt("dve", qnb[0:nr, :], qa[0:nr, 0:384], ss2[0:nr, 3:4], gq[0:nr, :], ALU.mult, ALU.mult, R=[b_qa, b_ss2, b_gq], W=[b_qnb])
                pt, bpt = next_ps(4, 8)
                ptv = pt[:, :].bitcast(BF16).rearrange("p (a b) -> p a b", a=8)
                for j in range(3):
                    k.tr(ptv[:, j, 0:nr], qnb[0:nr, 128 * j:128 * j + 128], identb[0:nr, 0:nr], R=[b_qnb, b_c], W=[bpt])
                k.cp("dve", qnT[:, :, 0:nr], ptv[:, 0:3, 0:nr], R=[bpt], W=[b_qnT])
                k.dma("sp", QNSv[:, :, r0:r0 + nr], qnT[:, :, 0:nr], R=[b_qnT, bQNS])
                k.tt("dve", krt[0:nr, 0:32], qa[0:nr, 384:416], RC[0:nr, ti, :], ALU.mult, R=[b_qa, b_rope], W=[b_krt])
                k.tt("dve", krt[0:nr, 32:64], qa[0:nr, 416:448], RS[0:nr, ti, :], ALU.mult, R=[b_qa, b_rope], W=[b_krt])
                k.tt("dve", krt[0:nr, 0:32], krt[0:nr, 0:32], krt[0:nr, 32:64], ALU.add, R=[b_krt], W=[b_krt])
                k.dma("sp", o_kpe[l][r0:r0 + nr, :], krt[0:nr, 0:32], R=[b_krt, bOkp])
                k.cp("act", kvt[0:nr, :], ps[1][0:nr, 0:256], R=[bps[1]], W=[b_kvt])
                k.act(sq2[0:nr, 0:256], kvt[0:nr, :], AF.Square, R=[b_kvt], W=[b_sq2])
                k.red(ss2[0:nr, 4:5], sq2[0:nr, 0:256], R=[b_sq2], W=[b_ss2])
                k.ts("dve", ss2[0:nr, 5:6], ss2[0:nr, 4:5], 1.0 / 256, 1e-6, ALU.mult, ALU.add, R=[b_ss2], W=[b_ss2])
                k.act(ss2[0:nr, 6:7], ss2[0:nr, 5:6], AF.Sqrt, R=[b_ss2], W=[b_ss2])
                k.recip(ss2[0:nr, 7:8], ss2[0:nr, 6:7], R=[b_ss2], W=[b_ss2])
                k.stt("dve", ckvt[0:nr, :], kvt[0:nr, :], ss2[0:nr, 7:8], gkv[0:nr, :], ALU.mult, ALU.mult, R=[b_kvt, b_ss2, b_gq], W=[b_ckvt])
                k.dma("sp", o_ckv[l][r0:r0 + nr, :], ckvt[0:nr, :], R=[b_ckvt, bOck])
                k.cp("act", sbkt[0:nr, :], ps[2][0:nr, :], R=[bps[2]], W=[b_sbkt])
                k.dma("sp", o_sbk[l][r0:r0 + nr, :], sbkt[0:nr, :], R=[b_sbkt, bOsk])
                k.cp("dve", sbvt[0:nr, :], ps[3][0:nr, :], R=[bps[3]], W=[b_sbvt])
                k.dma("sp", o_sbv[l][r0:r0 + nr, :], sbvt[0:nr, :], R=[b_sbvt, bOsv])
            if upto == "B1b":
                break

            gT = RB[:, :].bitcast(BF16).rearrange("p (a b) -> p a b", a=16)
            b_gT = Buf("gT")
            b_ra = Buf("ra0")
            k.barrier([b_hT, b_wtm, b_qa, b_sq2, b_qnb, b_kvt, b_ckvt, b_krt, b_sbkt, b_sbvt, b_qnT, b_ss2] + btst, [b_gT, b_ra, b_cp, bW[0], bW[1]], dummy[0:1, 0:1])
            k.dma("sp", cpt, cpar[l], W=[b_cp])
            fence(bPF, bQNS, bHTS, bOck, bOkp, bOsk, bOsv)
            if "C1" not in SKIP:
                S.serial = "SER" in SKIP
                WS = Region(RA, NBIG)
                MW = Region(Mt, 9216); MW.off = MISC_BASE
                WW = Region(Wt, 8192)
                tau = WS.take([NTOK]); cosT = WS.take([NTOK]); sinT = WS.take([NTOK])
                bA = WS.take([NTOK]); bB = WS.take([NTOK]); bC = WS.take([NTOK]); bD = WS.take([NTOK])
                u32 = WS.take([NTOK])
                b_tau, b_cos, b_sin, b_A, b_B, b_C, b_D, b_u32 = (Buf(n) for n in ("tau", "cos", "sin", "A", "B", "C", "D", "u32"))
                ti = WW.take([NTOK], I32); tmp1 = WW.take([NTOK]); tmp2 = WW.take([NTOK])
                b_ti, b_t1, b_t2 = Buf("ti"), Buf("t1"), Buf("t2")
                BBTre = MW.take([16, 128]); BBTim = MW.take([16, 128]); b_bbt = Buf("bbt")
                CBDre = MW.take([16, 32]); CBDim = MW.take([16, 32]); b_cbd = Buf("cbd")
                sp_ = MW.take([16, 16]); b_sp = Buf("sp")
                HF = MW.take([16, 6]); b_hf = Buf("hf")
                halfpi = MW.take([2])
                h0t = MW.take([4, 16]); b_h0 = Buf("h0")
                k.barrier([b_ra], [b_tau, b_cos, b_sin, b_A, b_B, b_C, b_D, b_u32, b_ti, b_t1, b_t2, b_bbt, b_cbd, b_sp, b_hf, b_h0], dummy[0:1, 0:1])
                P_ = lambda i: sp_[:, i, :]
                lre, lim, ldt = cpt[:, CP_LRE:CP_LRE + 16], cpt[:, CP_LIM:CP_LIM + 16], cpt[:, CP_LDT:CP_LDT + 16]
                dtt, mag, ang, lbre, lbim, fre, fim, th2 = P_(0), P_(1), P_(2), P_(3), P_(4), P_(5), P_(6), P_(7)
                q1, q2, q3, rden = P_(8), P_(9), P_(10), P_(11)
                k.memset("dve", halfpi, PI / 2, W=[b_sp])
                k.act(dtt, ldt, AF.Exp, R=[b_cp], W=[b_sp])
                k.tt("dve", q1, lre, dtt, ALU.mult, R=[b_cp, b_sp], W=[b_sp])
                k.act(mag, q1, AF.Exp, R=[b_sp], W=[b_sp])
                k.tt("dve", ang, lim, dtt, ALU.mult, R=[b_cp, b_sp], W=[b_sp])
                k.ts("dve", th2, ang, 1.0 / (2 * PI), R=[b_sp], W=[b_sp])
                wsA = Region(RA, NBIG); wsA.off = 3 * NTOK
                sA, cA = sincos(ang, [16], wsA, b_A)
                k.tt("dve", lbre, mag, cA, ALU.mult, R=[b_sp, b_A], W=[b_sp])
                k.tt("dve", lbim, mag, sA, ALU.mult, R=[b_sp, b_A], W=[b_sp])
                k.ts("dve", q1, lbre, -1.0, None, ALU.add, R=[b_sp], W=[b_sp])
                k.tt("dve", q2, lre, lre, ALU.mult, R=[b_cp], W=[b_sp])
                k.tt("dve", q3, lim, lim, ALU.mult, R=[b_cp], W=[b_sp])
                k.tt("dve", q2, q2, q3, ALU.add, R=[b_sp], W=[b_sp])
                k.recip(rden, q2, R=[b_sp], W=[b_sp])
                k.tt("dve", q2, q1, lre, ALU.mult, R=[b_sp, b_cp], W=[b_sp])
                k.tt("dve", q3, lbim, lim, ALU.mult, R=[b_sp, b_cp], W=[b_sp])
                k.tt("dve", q2, q2, q3, ALU.add, R=[b_sp], W=[b_sp])
                k.tt("dve", fre, q2, rden, ALU.mult, R=[b_sp], W=[b_sp])
                k.tt("dve", q2, lbim, lre, ALU.mult, R=[b_sp, b_cp], W=[b_sp])
                k.tt("dve", q3, q1, lim, ALU.mult, R=[b_sp, b_cp], W=[b_sp])
                k.tt("dve", q2, q2, q3, ALU.subtract, R=[b_sp], W=[b_sp])
                k.tt("dve", fim, q2, rden, ALU.mult, R=[b_sp], W=[b_sp])
                wsD = Region(RA, NBIG); wsD.off = 6 * NTOK
                bre = wsD.take([16, 16]); bim = wsD.take([16, 16]); cre = wsD.take([16, 16]); cim = wsD.take([16, 16])
                bbre = wsD.take([16, 16]); bbim = wsD.take([16, 16]); tq = wsD.take([16, 16])
                wsC = Region(RA, NBIG); wsC.off = 5 * NTOK
                BBDre = wsC.take([16, 32]); BBDim = wsC.take([16, 32])
                bsrc = lambda a: a.rearrange("(st p) c -> p st c", p=128)
                k.dma("sp", bre, bsrc(ssm_b[l][0]), W=[b_D])
                k.dma("sp", bim, bsrc(ssm_b[l][1]), W=[b_D])
                k.dma("sp", cre, bsrc(ssm_c[l][0]), W=[b_D])
                k.dma("sp", cim, bsrc(ssm_c[l][1]), W=[b_D])
                bc = lambda a: a.unsqueeze(2).to_broadcast([128, 16, 16])
                k.tt("dve", bbre, bre, bc(fre), ALU.mult, R=[b_D, b_sp], W=[b_D])
                k.tt("dve", tq, bim, bc(fim), ALU.mult, R=[b_D, b_sp], W=[b_D])
                k.tt("dve", bbre, bbre, tq, ALU.subtract, R=[b_D], W=[b_D])
                k.tt("dve", bbim, bim, bc(fre), ALU.mult, R=[b_D, b_sp], W=[b_D])
                k.tt("dve", tq, bre, bc(fim), ALU.mult, R=[b_D, b_sp], W=[b_D])
                k.tt("dve", bbim, bbim, tq, ALU.add, R=[b_D], W=[b_D])
                for dst, srcm, sc_ in ((BBDre, bbre, 1.0), (BBDim, bbim, 1.0), (CBDre, cre, 1.0), (CBDim, cim, -1.0)):
                    wb_ = [b_cbd] if dst is CBDre or dst is CBDim else [b_C]
                    k.memset("dve", dst, 0.0, R=[b_D], W=wb_)
                    k.ts("dve", dst[0:64, :, 0:16], srcm[0:64, :, :], sc_, R=[b_D], W=wb_)
                    k.ts("dve", dst[64:128, :, 16:32], srcm[64:128, :, :], sc_, R=[b_D], W=wb_)
                for (BBD, BBT) in ((BBDre, BBTre), (BBDim, BBTim)):
                    for q4 in range(4):
                        pp, bp = next_ps(4, 8)
                        ppv = pp[:, :].rearrange("p (a b) -> p a b", a=4)
                        for j in range(4):
                            k.tr(ppv[0:32, j, :], BBD[:, 4 * q4 + j, :], identf, R=[b_C, b_c], W=[bp])
                        k.cp(ev_eng(), BBT[0:32, 4 * q4:4 * q4 + 4, :], ppv[0:32, :, :], R=[bp], W=[b_bbt])
                k.iota(tau[:, 0:TP], [[1, TP]], 1, 0, W=[b_tau])
                k.iota(tau[:, TP:TP + 16], [[1, 16]], 1, 0, W=[b_tau])
                k.iota(tau[:, TP + 16:TP + 32], [[1, 16]], 1, 0, W=[b_tau])
                for sq_ in range(2):
                    for ri in range(2):
                        k.dma("sp", h0t[:, 2 * sq_ + ri, :], st_ssm[l][sq_][ri], W=[b_h0])
                dsk = cpt[0:32, CP_DSK:CP_DSK + 16]
                for st_ in range(16):
                    th = ang[:, st_:st_ + 1]
                    k.act(cosT, tau, AF.Copy, scale=th, R=[b_tau, b_sp], W=[b_cos])
                    k.act(sinT, tau, AF.Copy, scale=th2[:, st_:st_ + 1], R=[b_tau, b_sp], W=[b_sin])
                    k.cp("act", ti, sinT, R=[b_sin], W=[b_ti])
                    k.cp("act", sinT, ti, R=[b_ti], W=[b_sin])
                    k.stt("dve", cosT, sinT, -2 * PI, cosT, ALU.mult, ALU.add, R=[b_sin, b_cos], W=[b_cos])
                    k.ts("dve", sinT, cosT, PI, -2 * PI, ALU.is_gt, ALU.mult, R=[b_cos], W=[b_sin])
                    k.tt("dve", cosT, cosT, sinT, ALU.add, R=[b_cos, b_sin], W=[b_cos])
                    k.act(sinT, cosT, AF.Sin, R=[b_cos], W=[b_sin])
                    k.act(tmp1, cosT, AF.Abs, R=[b_cos], W=[b_t1])
                    k.act(cosT, tmp1, AF.Sin, bias=halfpi[:, 0:1], scale=-1.0, R=[b_t1, b_sp], W=[b_cos])
                    if st_ == 0:
                        dbgdump(k, "d_cos", cosT, [b_cos]); dbgdump(k, "d_sin", sinT, [b_sin]); dbgdump(k, "d_sp", sp_, [b_sp])
                        dbgdump(k, "d_bbt", BBTre[0:32], [b_bbt]); dbgdump(k, "d_cbd", CBDre, [b_cbd])
                    k.dma("sp", u32[0:32, :], PF[1664 + 32 * st_:1664 + 32 * st_ + 32, :], R=[bPF], W=[b_u32])
                    for (t0, nt) in TB:
                        pr, bpr = next_ps(0, 4)
                        pi_, bpi = next_ps(0, 4)
                        k.mm(pr[:, 0:nt], BBTre[0:32, st_, :], u32[0:32, t0:t0 + nt], R=[b_bbt, b_u32], W=[bpr])
                        k.mm(pi_[:, 0:nt], BBTim[0:32, st_, :], u32[0:32, t0:t0 + nt], R=[b_bbt, b_u32], W=[bpi])
                        sl = slice(t0, t0 + nt)
                        k.tt("dve", bC[:, sl], pr[:, 0:nt], cosT[:, sl], ALU.mult, R=[bpr, b_cos], W=[b_C])
                        k.tt("dve", bD[:, sl], pi_[:, 0:nt], sinT[:, sl], ALU.mult, R=[bpi, b_sin], W=[b_D])
                        k.tt("dve", bA[:, sl], bC[:, sl], bD[:, sl], ALU.add, R=[b_C, b_D], W=[b_A])
                        k.tt("dve", bC[:, sl], pi_[:, 0:nt], cosT[:, sl], ALU.mult, R=[bpi, b_cos], W=[b_C])
                        k.tt("dve", bD[:, sl], pr[:, 0:nt], sinT[:, sl], ALU.mult, R=[bpr, b_sin], W=[b_D])
                        k.tt("dve", bB[:, sl], bC[:, sl], bD[:, sl], ALU.subtract, R=[b_C, b_D], W=[b_B])
                    if st_ == 0:
                        dbgdump(k, "d_zre", bA, [b_A]); dbgdump(k, "d_zim", bB, [b_B])
                    for si, (s0_, sn) in enumerate(SEGS):
                        sl = slice(s0_, s0_ + sn)
                        mg_b = mag[:, st_:st_ + 1].to_broadcast([128, sn])
                        i_re = 0.0 if si == 0 else h0t[:, 2 * (si - 1), st_:st_ + 1]
                        i_im = 0.0 if si == 0 else h0t[:, 2 * (si - 1) + 1, st_:st_ + 1]
                        k.scan(bA[:, sl], mg_b, bA[:, sl], i_re, R=[b_A, b_sp, b_h0], W=[b_A])
                        k.scan(bB[:, sl], mg_b, bB[:, sl], i_im, R=[b_B, b_sp, b_h0], W=[b_B])
                    if st_ == 0:
                        dbgdump(k, "d_qre", bA, [b_A]); dbgdump(k, "d_qim", bB, [b_B])
                    k.tt("dve", bC, bA, cosT, ALU.mult, R=[b_A, b_cos], W=[b_C])
                    k.tt("dve", bD, bB, sinT, ALU.mult, R=[b_B, b_sin], W=[b_D])
                    k.tt("dve", bC, bC, bD, ALU.subtract, R=[b_C, b_D], W=[b_C])
                    k.tt("dve", bD, bB, cosT, ALU.mult, R=[b_B, b_cos], W=[b_D])
                    k.tt("dve", bA, bA, sinT, ALU.mult, R=[b_A, b_sin], W=[b_A])
                    k.tt("dve", bD, bD, bA, ALU.add, R=[b_D, b_A], W=[b_D])
                    for si, (s0_, sn) in enumerate(SEGS):
                        e_ = s0_ + sn - 1
                        k.cp("act", HF[:, st_, si:si + 1], bC[:, e_:e_ + 1], R=[b_C], W=[b_hf])
                        k.cp("act", HF[:, st_, 3 + si:4 + si], bD[:, e_:e_ + 1], R=[b_D], W=[b_hf])
                    if st_ == 0:
                        dbgdump(k, "d_hre", bC, [b_C]); dbgdump(k, "d_him", bD, [b_D])
                    for (t0, nt) in TB:
                        py, bpy = next_ps(4, 8)
                        k.mm(py[0:32, 0:nt], CBDre[:, st_, :], bC[:, t0:t0 + nt], start=True, stop=False, R=[b_cbd, b_C], W=[bpy])
                        k.mm(py[0:32, 0:nt], CBDim[:, st_, :], bD[:, t0:t0 + nt], start=False, stop=True, R=[b_cbd, b_D], W=[bpy])
                        k.stt("dve", bB[0:32, t0:t0 + nt], u32[0:32, t0:t0 + nt], dsk[:, st_:st_ + 1], py[0:32, 0:nt], ALU.mult, ALU.add, R=[b_u32, b_cp, bpy], W=[b_B])
                    k.dma("sp", YS[32 * st_:32 * st_ + 32, :], bB[0:32, :], R=[b_B, bYS])
                for si in range(3):
                    for ri in range(2):
                        k.dma("sp", o_ssm[l][ri][si].rearrange("(st p) -> p st", p=128), HF[:, :, 3 * ri + si], R=[b_hf], allow_slow_non_contiguous=True)
                b_rb = Buf("ra1")
                k.barrier([b_tau, b_cos, b_sin, b_A, b_B, b_C, b_D, b_u32, b_ti, b_t1, b_t2, b_bbt, b_cbd, b_sp, b_hf, b_h0], [b_rb], dummy[0:1, 0:1])
                WS = Region(RA, NBIG)
                WW = Region(Wt, 8192)
                yv = WS.take([4, NTOK]); tv = WS.take([4, NTOK])
                b_yv, b_tv = Buf("yv"), Buf("tv")
                gb = WW.take([4, NTOK], BF16); b_gb = Buf("gb")
                wgl = WW.take([4, 512], BF16); b_wgl = Buf("wgl")
                k.barrier([b_rb], [b_yv, b_tv, b_gb, b_wgl], dummy[0:1, 0:1])
                k.dma("pool", wgl, w_glu[l].rearrange("(kt p) c -> p kt c", p=128), W=[b_wgl])
                fence(bYS)
                k.dma("sp", yv, YS.rearrange("(a p) t -> p a t", p=128), R=[bYS], W=[b_yv])
                k.tt("dve", tv, yv, yv, ALU.mult, R=[b_yv], W=[b_tv])
                k.ts("dve", tv, tv, 0.044715, 1.0, ALU.mult, ALU.add, R=[b_tv], W=[b_tv])
                k.tt("dve", tv, tv, yv, ALU.mult, R=[b_tv, b_yv], W=[b_tv])
                k.act(tv, tv, AF.Sigmoid, scale=2.0 * math.sqrt(2.0 / PI), R=[b_tv], W=[b_tv])
                k.tt("dve", yv, yv, tv, ALU.mult, R=[b_tv, b_yv], W=[b_yv])
                k.cp("act", gb, yv, R=[b_yv], W=[b_gb])
                sgt = tv[:, 0, :]
                gat = tv[:, 1, :]
                for oc in range(4):
                    k.dma("sp", gat, PF[3200 + 128 * (4 + oc):3200 + 128 * (5 + oc), :], R=[bPF], W=[b_tv])
                    for (t0, nt) in TB:
                        pp, bp = next_ps(0, 4)
                        for kt in range(4):
                            k.mm(pp[:, 0:nt], wgl[:, kt, 128 * oc:128 * oc + 128], gb[:, kt, t0:t0 + nt], start=(kt == 0), stop=(kt == 3), R=[b_wgl, b_gb], W=[bp])
                        k.act(sgt[:, t0:t0 + nt], pp[:, 0:nt], AF.Sigmoid, bias=cpt[:, CP_BGLU + oc:CP_BGLU + oc + 1], R=[bp, b_cp], W=[b_tv])
                        k.tt("dve", sgt[:, t0:t0 + nt], sgt[:, t0:t0 + nt], yv[:, oc, t0:t0 + nt], ALU.mult, R=[b_tv, b_yv], W=[b_tv])
                        k.tt("dve", gT[:, 4 + oc, t0:t0 + nt], sgt[:, t0:t0 + nt], gat[:, t0:t0 + nt], ALU.mult, R=[b_tv], W=[b_gT])
                k.barrier([b_yv, b_tv, b_gb, b_wgl], [b_ra, bW[0], bW[1]], dummy[0:1, 0:1])
            if upto == "C1":
                k.dma("sp", GDBG, RB[:, :].bitcast(BF16), R=[b_gT])
                break

            def attn_masks_alloc(WW):
                return WW.take([4, 512]), WW.take([16]), Buf("mask")

            def attn_masks(mk, m16, b_mk, kind):
                k.memset("pool", mk, 1.0, W=[b_mk])
                for j in range(4):
                    if kind == "sb":
                        k.asel(mk[:, j, :], mk[:, j, :], [[1, 512]], ALU.is_gt, 0.0, -128 * j, -1, R=[b_mk], W=[b_mk])
                    else:
                        k.asel(mk[:, j, :].rearrange("p (a b) -> p a b", b=64), mk[:, j, :].rearrange("p (a b) -> p a b", b=64), [[64, 8], [0, 64]], ALU.is_ge, 0.0, 63 - 128 * j, -1, R=[b_mk], W=[b_mk])
                k.memset("pool", m16, 1.0, R=[b_mk], W=[b_mk])
                k.asel(m16[0:16, :], m16[0:16, :], [[1, 16]], ALU.is_gt, 0.0, 0, -1, R=[b_mk], W=[b_mk])

            if "C3" not in SKIP:
                S.serial = "SER" in SKIP
                WS = Region(RA, NBIG)
                WW = Region(Wt, 8192)
                MW = Region(Mt, 9216); MW.off = MISC_BASE
                qT = WS.take([2, NTOK], BF16, 64); kT = WS.take([2, NTOK], BF16, 64)
                vpad = WS.take([16, 2, 128], BF16)
                gate = WS.take([NTOK])
                xoff = WS.off
                tEs, tSPs, tSs, tAs = [], [], [], []
                for _ in range(4):
                    tEs.append(WS.take([512])); tSPs.append(WS.take([512])); tSs.append(WS.take([512])); tAs.append(WS.take([512], BF16))
                WS2 = Region(RA, NBIG); WS2.off = xoff + 2 * 1792
                kTp = WS2.take([2, 2, PAST], BF16, 64); vpadp = WS2.take([2, 8, 2, 128], BF16); vnew = WS2.take([2, 2, 128], BF16, 16)
                b_q, b_k, b_kp, b_v, b_vp, b_vn, b_gate = (Buf(n) for n in ("q", "k", "kp", "v", "vp", "vn", "gate"))
                b_Es = [Buf(f"E{i}") for i in range(4)]; b_SPs = [Buf(f"SP{i}") for i in range(4)]; b_Ss = [Buf(f"S{i}") for i in range(4)]; b_A2s = [Buf(f"A2{i}") for i in range(4)]
                sb01 = b_Es[0:2] + b_SPs[0:2] + b_Ss[0:2] + b_A2s[0:2]
                sb23 = b_Es[2:4] + b_SPs[2:4] + b_Ss[2:4] + b_A2s[2:4]
                smp = [b_kp, b_vp, b_vn]
                mk, m16, b_mk = attn_masks_alloc(WW)
                stg = WW.take([NTOK]); b_st = Buf("st")
                stv = WW.take([16, 128]); b_stv = Buf("stv")
                stp = WW.take([8, 128]); b_stp = Buf("stp")
                triU = MW.take([128]); ones128 = MW.take([128]); b_tri = Buf("tri")
                k.barrier([b_ra, bW[0], bW[1]], [b_q, b_k, b_v, b_gate, b_mk, b_st, b_stv, b_stp, b_tri] + sb01 + sb23, dummy[0:1, 0:1])
                attn_masks(mk, m16, b_mk, "sb")
                k.memset("pool", ones128, 1.0, W=[b_tri])
                k.memset("pool", triU, 1.0, W=[b_tri])
                k.asel(triU, triU, [[-1, 128]], ALU.is_gt, 0.0, 0, 1, R=[b_tri], W=[b_tri])
                k.memset("dve", vpad, 0.0, W=[b_v])
                pzc = [0]

                def run_streams(gens, delays=None):
                    active = [[g_, (delays[i] if delays else 0)] for i, g_ in enumerate(gens)]
                    while active:
                        nxt = []
                        for ent in active:
                            if ent[1] > 0:
                                ent[1] -= 1
                                nxt.append(ent)
                                continue
                            try:
                                next(ent[0])
                                nxt.append(ent)
                            except StopIteration:
                                pass
                        active = nxt

                def sb_tile(st_, lhsK, rhsQ, nk, nq, maskap, first, last_acc, vl, pso, bpso, o_start, o_stop, RK, RQ, RV):
                    tE, tSP, tS, tA = tEs[st_], tSPs[st_], tSs[st_], tAs[st_]
                    b_E, b_SP, b_S, b_A2 = b_Es[st_], b_SPs[st_], b_Ss[st_], b_A2s[st_]
                    pz, bpz = ps[st_ % 2], bps[st_ % 2]
                    pl, bpl = ps[2 + st_], bps[2 + st_]
                    k.mm(pz[0:nk, 0:nq], lhsK, rhsQ, R=[RK, RQ], W=[bpz])
                    yield
                    k.act(tE[0:nk, 0:nq], pz[0:nk, 0:nq], AF.Exp, R=[bpz], W=[b_E])
                    k.act(tSP[0:nk, 0:nq], tE[0:nk, 0:nq], AF.Ln, bias=1.0, R=[b_E], W=[b_SP])
                    k.cp("act" if st_ % 2 == 0 else "dve", tE[0:nk, 0:nq], pz[0:nk, 0:nq], R=[bpz], W=[b_E])
                    if maskap is not None:
                        k.tt("dve", tSP[0:nk, 0:nq], tSP[0:nk, 0:nq], maskap, ALU.mult, R=[b_SP, b_mk], W=[b_SP])
                    yield
                    k.mm(pl[0:nk, 0:nq], triU[0:nk, 0:nk], tSP[0:nk, 0:nq], start=True, stop=first, R=[b_tri, b_SP], W=[bpl])
                    if not first:
                        k.mm(pl[0:nk, 0:nq], ones128[:, 0:nk], tS[:, 0:nq], start=False, stop=True, R=[b_tri, b_S], W=[bpl])
                    yield
                    k.tt("dve", tE[0:nk, 0:nq], tE[0:nk, 0:nq], tSP[0:nk, 0:nq], ALU.subtract, R=[b_E, b_SP], W=[b_E])
                    k.tt("dve", tE[0:nk, 0:nq], tE[0:nk, 0:nq], pl[0:nk, 0:nq], ALU.subtract, R=[b_E, bpl], W=[b_E])
                    if not last_acc:
                        if first:
                            if nk < 128:
                                k.memset("pool", tS[:, 0:nq], 0.0, W=[b_S])
                            k.cp("pool", tS[0:nk, 0:nq], tSP[0:nk, 0:nq], R=[b_SP], W=[b_S])
                        else:
                            k.tt("pool", tS[0:nk, 0:nq], tS[0:nk, 0:nq], tSP[0:nk, 0:nq], ALU.add, R=[b_SP, b_S], W=[b_S])
                    yield
                    k.act(tA[0:nk, 0:nq], tE[0:nk, 0:nq], AF.Exp, R=[b_E], W=[b_A2])
                    if maskap is not None:
                        k.tt("dve", tA[0:nk, 0:nq], tA[0:nk, 0:nq], maskap, ALU.mult, R=[b_A2, b_mk], W=[b_A2])
                    yield
                    k.mm(pso[:, 0:nq], vl, tA[0:nk, 0:nq], start=o_start, stop=o_stop, R=[RV, b_A2], W=[bpso])

                def prompt_stream(st_, hf, qsb, pso, bpso):
                    q0 = 512 * qsb
                    nkt = 4 * (qsb + 1)
                    for kt in range(nkt - 1, -1, -1):
                        jd = kt - 4 * qsb
                        yield from sb_tile(st_, kT[:, hf, 128 * kt:128 * kt + 128], qT[:, hf, q0:q0 + 512], 128, 512,
                                           mk[:, jd, :] if jd >= 0 else None, kt == nkt - 1, kt == 0,
                                           vpad[:, kt, hf, :], pso, bpso, (hf == 0 and kt == nkt - 1), (hf == 1 and kt == 0), b_k, b_q, b_v)
                        yield

                def sample_stream(st_, hf, j, pso, bpso):
                    q0 = TP + 16 * j
                    yield from sb_tile(st_, kT[:, hf, q0:q0 + 16], qT[:, hf, q0:q0 + 16], 16, 16, m16[0:16, :], True, False,
                                       vnew[:, j, hf, :], pso, bpso, hf == 0, False, b_k, b_q, b_vn)
                    yield
                    for kt in range(7, -1, -1):
                        yield from sb_tile(st_, kTp[:, j, hf, 128 * kt:128 * kt + 128], qT[:, hf, q0:q0 + 16], 128, 16, None, False, kt == 0,
                                           vpadp[:, j, kt, hf, :], pso, bpso, False, (hf == 1 and kt == 0), b_kp, b_q, b_vp)
                        yield

                for hp in range(4):
                    for hf in range(2):
                        h_ = 2 * hp + hf
                        k.dma("sp", stg[0:64, :], PF[2176 + 64 * h_:2176 + 64 * h_ + 64, :], R=[bPF], W=[b_st])
                        k.act(qT[:, hf, :], stg[0:64, :], AF.Copy, scale=0.125, R=[b_st], W=[b_q])
                        k.dma("sp", stg[0:64, :], PF[2688 + 64 * h_:2688 + 64 * h_ + 64, :], R=[bPF], W=[b_st])
                        k.cp("act", kT[:, hf, :], stg[0:64, :], R=[b_st], W=[b_k])
                    k.dma("sp", gate, PF[3200 + 128 * (12 + hp):3200 + 128 * (13 + hp), :], R=[bPF], W=[b_gate])
                    k.dma("sp", stv, o_sbv[l][0:TP, 128 * hp:128 * hp + 128].rearrange("(t p) c -> p t c", p=128), R=[bOsv], W=[b_stv])
                    for hf in range(2):
                        k.cp("dve", vpad[:, :, hf, 64 * hf:64 * hf + 64], stv[:, :, 64 * hf:64 * hf + 64], R=[b_stv], W=[b_v])
                    for (qa, qb) in ((0, 3), (1, 2)):
                        run_streams([prompt_stream(0, 0, qa, ps[6], bps[6]), prompt_stream(1, 1, qa, ps[6], bps[6]),
                                     prompt_stream(2, 0, qb, ps[7], bps[7]), prompt_stream(3, 1, qb, ps[7], bps[7])], delays=[0, 0, 1, 1])
                        for (qq, pb) in ((qa, 6), (qb, 7)):
                            q0 = 512 * qq
                            k.tt("dve", gT[:, 12 + hp, q0:q0 + 512], ps[pb][:, :], gate[:, q0:q0 + 512], ALU.mult, R=[bps[pb], b_gate], W=[b_gT])
                    k.barrier(sb23, smp, dummy[0:1, 0:1])
                    k.memset("dve", vpadp, 0.0, W=[b_vp]); k.memset("dve", vnew, 0.0, W=[b_vn])
                    k.dma("sp", stp[0:16, 0:2, :], o_sbv[l][TP:NTOK, 128 * hp:128 * hp + 128].rearrange("(j p) c -> p j c", p=16), R=[bOsv], W=[b_stp])
                    for hf in range(2):
                        k.cp("dve", vnew[:, :, hf, 64 * hf:64 * hf + 64], stp[0:16, 0:2, 64 * hf:64 * hf + 64], R=[b_stp], W=[b_vn])
                    for j in range(2):
                        k.dma("sp", stp, c_sbv[l][j][:, 128 * hp:128 * hp + 128].rearrange("(t p) c -> p t c", p=128), W=[b_stp])
                        for hf in range(2):
                            k.cp("dve", vpadp[:, j, :, hf, 64 * hf:64 * hf + 64], stp[:, :, 64 * hf:64 * hf + 64], R=[b_stp], W=[b_vp])
                        k.dma("sp", stp, c_sbk[l][j][:, 128 * hp:128 * hp + 128].rearrange("(t p) c -> p t c", p=128), W=[b_stp])
                        for hf in range(2):
                            for q4 in range(2):
                                pt, bpt = next_ps(0, 2)
                                for jj in range(4):
                                    k.tr(pt[0:64, 128 * jj:128 * jj + 128], stp[:, 4 * q4 + jj, 64 * hf:64 * hf + 64], identf, R=[b_stp, b_c], W=[bpt])
                                k.cp(ev_eng(), kTp[:, j, hf, 512 * q4:512 * q4 + 512], pt[0:64, :], R=[bpt], W=[b_kp])
                    for j in range(2):
                        q0 = TP + 16 * j
                        run_streams([sample_stream(0, 0, j, ps[6 + j], bps[6 + j]), sample_stream(1, 1, j, ps[6 + j], bps[6 + j])])
                        k.tt("dve", gT[:, 12 + hp, q0:q0 + 16], ps[6 + j][:, 0:16], gate[:, q0:q0 + 16], ALU.mult, R=[bps[6 + j], b_gate], W=[b_gT])
                    k.barrier(smp, sb23, dummy[0:1, 0:1])
                k.barrier([b_q, b_k, b_v, b_gate, b_mk, b_st, b_stv, b_stp, b_tri] + sb01 + sb23, [b_ra, bW[0], bW[1]], dummy[0:1, 0:1])
            if upto == "C3":
                k.dma("sp", GDBG, RB[:, :].bitcast(BF16), R=[b_gT])
                break

            if "C4" not in SKIP:
                S.serial = "SER" in SKIP
                MLA_SCALE = 1.0 / math.sqrt(96.0)
                WS = Region(RA, NBIG); WW = Region(Wt, 8192)
                qnT_ = WS.take([3, NTOK], BF16); CT = WS.take([NTOK]); ST = WS.take([NTOK])
                ttc = WS.take([96]); tts = WS.take([96])
                qo = [WS.take([512], BF16), WS.take([512], BF16)]
                t1 = WS.take([512]); t2 = WS.take([512])
                wq_ = WW.take([3, 768], BF16); wqs_ = WW.take([3, 768], BF16)
                b_qn, b_ct, b_ttc, b_t12, b_wq = (Buf(n) for n in ("qn", "ct", "ttc", "t12", "wq"))
                b_qo = [Buf("qo0"), Buf("qo1")]
                k.barrier([b_ra, bW[0], bW[1]], [b_qn, b_ct, b_ttc, b_t12, b_wq] + b_qo, dummy[0:1, 0:1])
                k.dma("sp", qnT_, QNS.rearrange("p (a t) -> p a t", a=3), R=[bQNS], W=[b_qn])
                k.dma("pool", wq_, wq[l].rearrange("(kt p) c -> p kt c", p=128), W=[b_wq])
                k.dma("pool", wqs_, wqs[l].rearrange("(kt p) c -> p kt c", p=128), W=[b_wq])
                k.memset("dve", ttc, 0.0, W=[b_ttc]); k.memset("dve", tts, 0.0, W=[b_ttc])
                for ti, (r0, nr) in enumerate(TT):
                    for (tsrc, tdst, RT) in ((ttc, CT, RC), (tts, ST, RS)):
                        k.cp("dve", tsrc[0:nr, 64:96], RT[0:nr, ti, :], R=[b_rope], W=[b_ttc])
                        pt, bpt = next_ps(6, 8)
                        k.tr(pt[0:96, 0:nr], tsrc[0:nr, 0:96], identf[0:nr, 0:nr], R=[b_ttc, b_c], W=[bpt])
                        k.cp("act", tdst[64:96, r0:r0 + nr], pt[64:96, 0:nr], R=[bpt], W=[b_ct])
                qi = 0
                for h_ in range(8):
                    for (t0, nt) in TB:
                        p1, bp1 = next_ps(0, 2)
                        p2, bp2 = next_ps(2, 4)
                        for kt in range(3):
                            k.mm(p1[0:96, 0:nt], wq_[:, kt, 96 * h_:96 * h_ + 96], qnT_[:, kt, t0:t0 + nt], start=(kt == 0), stop=(kt == 2), R=[b_wq, b_qn], W=[bp1])
                        for kt in range(3):
                            k.mm(p2[0:96, 0:nt], wqs_[:, kt, 96 * h_:96 * h_ + 96], qnT_[:, kt, t0:t0 + nt], start=(kt == 0), stop=(kt == 2), R=[b_wq, b_qn], W=[bp2])
                        q_, bq_ = qo[qi % 2], b_qo[qi % 2]
                        qi += 1
                        k.cp("act", q_[0:64, 0:nt], p1[0:64, 0:nt], R=[bp1], W=[bq_])
                        k.tt("dve", t1[64:96, 0:nt], p1[64:96, 0:nt], CT[64:96, t0:t0 + nt], ALU.mult, R=[bp1, b_ct], W=[b_t12])
                        k.tt("dve", t2[64:96, 0:nt], p2[64:96, 0:nt], ST[64:96, t0:t0 + nt], ALU.mult, R=[bp2, b_ct], W=[b_t12])
                        k.tt("dve", q_[64:96, 0:nt], t1[64:96, 0:nt], t2[64:96, 0:nt], ALU.add, R=[b_t12], W=[bq_])
                        k.dma("sp", QF[:, h_, t0:t0 + nt], q_[0:96, 0:nt], R=[bq_, bSCR])
                b_m2 = Buf("m2")
                k.barrier([b_qn, b_ct, b_ttc, b_t12, b_wq] + b_qo, [b_m2], dummy[0:1, 0:1])
                WS = Region(RA, NBIG); WW = Region(Wt, 8192)
                ckvT = WS.take([2, NTOK], BF16); ckvTp = WS.take([2, 2, PAST], BF16)
                kpeT = WS.take([NTOK], BF16); kpeTp = WS.take([2, PAST], BF16)
                ck_st = WS.take([256]); kp_st = WS.take([96]); cks8 = WS.take([8, 256]); kps8 = WS.take([8, 96])
                ko = [WS.take([512], BF16), WS.take([512], BF16)]
                vo = [WS.take([512], BF16), WS.take([512], BF16)]
                wk_ = WW.take([2, 512], BF16); wv_ = WW.take([2, 512], BF16)
                b_ck, b_ckp, b_kp2, b_kpp, b_cst, b_kst, b_c8, b_k8, b_wk = (Buf(n) for n in ("ck", "ckp", "kp2", "kpp", "cst", "kst", "c8", "k8", "wk"))
                b_ko = [Buf("ko0"), Buf("ko1")]; b_vo = [Buf("vo0"), Buf("vo1")]
                k.barrier([b_m2], [b_ck, b_ckp, b_kp2, b_kpp, b_cst, b_kst, b_c8, b_k8, b_wk] + b_ko + b_vo, dummy[0:1, 0:1])
                k.dma("pool", wk_, wk[l].rearrange("(kt p) c -> p kt c", p=128), W=[b_wk])
                k.dma("pool", wv_, wv[l].rearrange("(kt p) c -> p kt c", p=128), W=[b_wk])
                k.memset("dve", kp_st, 0.0, W=[b_kst]); k.memset("dve", kps8, 0.0, W=[b_k8])
                for ti, (r0, nr) in enumerate(TT):
                    k.dma("sp", ck_st[0:nr, :], o_ckv[l][r0:r0 + nr, :], R=[bOck], W=[b_cst])
                    pt, bpt = next_ps(6, 8)
                    for j in range(2):
                        k.tr(pt[:, 128 * j:128 * j + nr], ck_st[0:nr, 128 * j:128 * j + 128], identf[0:nr, 0:nr], R=[b_cst, b_c], W=[bpt])
                    k.cp(ev_eng(), ckvT[:, :, r0:r0 + nr], pt[:, 0:256].rearrange("p (a b) -> p a b", a=2)[:, :, 0:nr], R=[bpt], W=[b_ck])
                    k.dma("sp", kp_st[0:nr, 64:96], o_kpe[l][r0:r0 + nr, :], R=[bOkp], W=[b_kst])
                    pt, bpt = next_ps(6, 8)
                    k.tr(pt[0:96, 0:nr], kp_st[0:nr, 0:96], identf[0:nr, 0:nr], R=[b_kst, b_c], W=[bpt])
                    k.cp("act", kpeT[64:96, r0:r0 + nr], pt[64:96, 0:nr], R=[bpt], W=[b_kp2])
                for j in range(2):
                    k.dma("sp", cks8, c_ckv[l][j].rearrange("(t p) c -> p t c", p=128), W=[b_c8])
                    k.dma("sp", kps8[:, :, 64:96], c_kpe[l][j].rearrange("(t p) c -> p t c", p=128), W=[b_k8])
                    for t8 in range(8):
                        pt, bpt = next_ps(6, 8)
                        for kt in range(2):
                            k.tr(pt[:, 128 * kt:128 * kt + 128], cks8[:, t8, 128 * kt:128 * kt + 128], identf, R=[b_c8, b_c], W=[bpt])
                        k.cp(ev_eng(), ckvTp[:, :, j, 128 * t8:128 * t8 + 128], pt[:, 0:256].rearrange("p (a b) -> p a b", a=2), R=[bpt], W=[b_ckp])
                        pt, bpt = next_ps(6, 8)
                        k.tr(pt[0:96, 0:128], kps8[:, t8, 0:96], identf, R=[b_k8, b_c], W=[bpt])
                        k.cp("act", kpeTp[64:96, j, 128 * t8:128 * t8 + 128], pt[64:96, 0:128], R=[bpt], W=[b_kpp])
                ki = 0
                for h_ in range(8):
                    srcs = [(ckvT[:, :, t0:t0 + nt], kpeT[64:96, t0:t0 + nt], KF[:, h_, t0:t0 + nt], nt, b_ck, b_kp2) for (t0, nt) in TB]
                    for j in range(2):
                        for kb in range(2):
                            srcs.append((ckvTp[:, :, j, 512 * kb:512 * kb + 512], kpeTp[64:96, j, 512 * kb:512 * kb + 512], KFP[:, h_, j, 512 * kb:512 * kb + 512], 512, b_ckp, b_kpp))
                    for (csrc, psrc, dst, nt, bcs, bps_) in srcs:
                        pp, bp = next_ps(0, 4)
                        for kt in range(2):
                            k.mm(pp[0:64, 0:nt], wk_[:, kt, 64 * h_:64 * h_ + 64], csrc[:, kt, :], start=(kt == 0), stop=(kt == 1), R=[b_wk, bcs], W=[bp])
                        k_, bk_ = ko[ki % 2], b_ko[ki % 2]
                        ki += 1
                        k.cp(ev_eng(), k_[0:64, 0:nt], pp[0:64, 0:nt], R=[bp], W=[bk_])
                        k.cp("pool", k_[64:96, 0:nt], psrc, R=[bps_], W=[bk_])
                        k.dma("sp", dst, k_[0:96, 0:nt], R=[bk_, bSCR])
                vi = 0
                vsrcs = [(ckvT[:, :, r0:r0 + nr], VT[r0:r0 + nr, :], nr, b_ck) for (r0, nr) in TT]
                for j in range(2):
                    for t8 in range(8):
                        vsrcs.append((ckvTp[:, :, j, 128 * t8:128 * t8 + 128], VTP[PAST * j + 128 * t8:PAST * j + 128 * t8 + 128, :], 128, b_ckp))
                for (csrc, dst, nr, bcs) in vsrcs:
                    pp, bp = next_ps(0, 4)
                    for kt in range(2):
                        k.mm(pp[0:nr, :], csrc[:, kt, :], wv_[:, kt, :], start=(kt == 0), stop=(kt == 1), R=[b_wk, bcs], W=[bp])
                    v_, bv_ = vo[vi % 2], b_vo[vi % 2]
                    vi += 1
                    k.cp(ev_eng(), v_[0:nr, :], pp[0:nr, :], R=[bp], W=[bv_])
                    k.dma("sp", dst, v_[0:nr, :], R=[bv_, bSCR])
                b_m3 = Buf("m3")
                k.barrier([b_ck, b_ckp, b_kp2, b_kpp, b_cst, b_kst, b_c8, b_k8, b_wk] + b_ko + b_vo, [b_m3], dummy[0:1, 0:1])
                fence(bSCR)
                WS = Region(RA, NBIG); WW = Region(Wt, 8192)
                qf = WS.take([2, NTOK], BF16, 96); kf = WS.take([2, NTOK], BF16, 96); kfp = WS.take([2, 2, PAST], BF16, 96)
                vpad = WS.take([16, 2, 128], BF16); vpadp = WS.take([2, 8, 2, 128], BF16); vnew = WS.take([2, 2, 128], BF16, 16)
                gate = WS.take([NTOK]); opad = WS.take([2, 128], BF16)
                tAs = [WS.take([512], BF16), WS.take([512], BF16), WS.take([512], BF16)]; tR = WS.take([512])
                b_A2s = [Buf("A2a"), Buf("A2b"), Buf("A2c")]
                mk, m16, b_mk = attn_masks_alloc(WW)
                mkb = WW.take([4, 512], BF16)
                tac = [0]
                b_q, b_k, b_kp, b_v, b_vp, b_vn, b_gate, b_A2, b_R, b_op = (Buf(n) for n in ("q", "k", "kp", "v", "vp", "vn", "gate", "A2", "R", "op"))
                k.barrier([b_m3], [b_q, b_k, b_kp, b_v, b_vp, b_vn, b_gate, b_A2, b_R, b_op, b_mk] + b_A2s, dummy[0:1, 0:1])
                attn_masks(mk, m16, b_mk, "mla")
                k.cp("dve", mkb, mk, R=[b_mk], W=[b_mk])
                k.memset("dve", vpad, 0.0, W=[b_v]); k.memset("dve", vpadp, 0.0, W=[b_vp]); k.memset("dve", vnew, 0.0, W=[b_vn])
                k.memset("dve", opad, 0.0, W=[b_op])
                k.memset("dve", opad[:, 0, 0:64], 1.0, W=[b_op]); k.memset("dve", opad[:, 1, 64:128], 1.0, W=[b_op])

                def mla_a(lhsK, rhsQ, nk, nq, RK, RQ):
                    pz, bpz = next_ps(0, 4)
                    k.mm(pz[0:nk, 0:nq], lhsK, rhsQ, R=[RK, RQ], W=[bpz])
                    return pz, bpz

                def mla_b(pzs, nk, nq, maskap, vl, hf, pso, bpso, psd, bpsd, o_start, o_stop, RV):
                    pz, bpz = pzs
                    tA, b_A2 = tAs[tac[0] % 3], b_A2s[tac[0] % 3]
                    tac[0] += 1
                    k.act(tA[0:nk, 0:nq], pz[0:nk, 0:nq], AF.Exp, scale=MLA_SCALE, R=[bpz], W=[b_A2])
                    if maskap is not None:
                        k.tt("dve", tA[0:nk, 0:nq], tA[0:nk, 0:nq], maskap, ALU.mult, R=[b_A2, b_mk], W=[b_A2])
                    k.mm(pso[:, 0:nq], vl, tA[0:nk, 0:nq], start=o_start, stop=o_stop, R=[RV, b_A2], W=[bpso])
                    k.mm(psd[:, 0:nq], opad[0:nk, hf, :], tA[0:nk, 0:nq], start=o_start, stop=o_stop, R=[b_op, b_A2], W=[bpsd])

                def mla_pipe(tiles, pso, bpso, psd, bpsd):
                    n = len(tiles)
                    pend = []
                    for i in range(n + 2):
                        if i < n:
                            t = tiles[i]
                            pend.append(mla_a(t["K"], t["Q"], t["nk"], t["nq"], t["RK"], t["RQ"]))
                        if i >= 2:
                            t = tiles[i - 2]
                            mla_b(pend[i - 2], t["nk"], t["nq"], t["mask"], t["V"], t["hf"], pso, bpso, psd, bpsd, i - 2 == 0, i - 2 == n - 1, t["RV"])

                for hp in range(4):
                    k.dma("sp", qf, QF[:, 2 * hp:2 * hp + 2, :], R=[bSCR], W=[b_q])
                    k.dma("sp", kf, KF[:, 2 * hp:2 * hp + 2, :], R=[bSCR], W=[b_k])
                    k.dma("sp", kfp, KFP[:, 2 * hp:2 * hp + 2, :, :], R=[bSCR], W=[b_kp])
                    k.dma("sp", gate, PF[3200 + 128 * (8 + hp):3200 + 128 * (9 + hp), :], R=[bPF], W=[b_gate])
                    for hf in range(2):
                        c0 = 128 * hp + 64 * hf
                        k.dma("sp", vpad[:, :, hf, 64 * hf:64 * hf + 64], VT[0:TP, c0:c0 + 64].rearrange("(t p) c -> p t c", p=128), R=[bSCR], W=[b_v])
                        k.dma("sp", vnew[:, :, hf, 64 * hf:64 * hf + 64], VT[TP:NTOK, c0:c0 + 64].rearrange("(j p) c -> p j c", p=16), R=[bSCR], W=[b_vn])
                        for j in range(2):
                            k.dma("sp", vpadp[:, j, :, hf, 64 * hf:64 * hf + 64], VTP[PAST * j:PAST * j + PAST, c0:c0 + 64].rearrange("(t p) c -> p t c", p=128), R=[bSCR], W=[b_vp])

                    def finish(pso, bpso, psd, bpsd, q0, nq):
                        k.recip(tR[:, 0:nq], psd[:, 0:nq], R=[bpsd], W=[b_R])
                        k.tt("dve", tR[:, 0:nq], pso[:, 0:nq], tR[:, 0:nq], ALU.mult, R=[bpso, b_R], W=[b_R])
                        k.tt("dve", gT[:, 8 + hp, q0:q0 + nq], tR[:, 0:nq], gate[:, q0:q0 + nq], ALU.mult, R=[b_R, b_gate], W=[b_gT])

                    for qsb in range(4):
                        q0 = 512 * qsb
                        pso, bpso = next_ps(4, 6)
                        psd, bpsd = next_ps(6, 8)
                        nkt = 4 * (qsb + 1)
                        tl = []
                        for hf in range(2):
                            for kt in range(nkt):
                                jd = kt - 4 * qsb
                                tl.append(dict(K=kf[:, hf, 128 * kt:128 * kt + 128], Q=qf[:, hf, q0:q0 + 512], nk=128, nq=512,
                                               mask=mkb[:, jd, :] if jd >= 0 else None, V=vpad[:, kt, hf, :], hf=hf, RK=b_k, RQ=b_q, RV=b_v))
                        mla_pipe(tl, pso, bpso, psd, bpsd)
                        finish(pso, bpso, psd, bpsd, q0, 512)
                    for j in range(2):
                        q0 = TP + 16 * j
                        pso, bpso = next_ps(4, 6)
                        psd, bpsd = next_ps(6, 8)
                        tl = []
                        for hf in range(2):
                            for kt in range(8):
                                tl.append(dict(K=kfp[:, hf, j, 128 * kt:128 * kt + 128], Q=qf[:, hf, q0:q0 + 16], nk=128, nq=16, mask=None,
                                               V=vpadp[:, j, kt, hf, :], hf=hf, RK=b_kp, RQ=b_q, RV=b_vp))
                            tl.append(dict(K=kf[:, hf, q0:q0 + 16], Q=qf[:, hf, q0:q0 + 16], nk=16, nq=16, mask=None,
                                           V=vnew[:, j, hf, :], hf=hf, RK=b_k, RQ=b_q, RV=b_vn))
                        mla_pipe(tl, pso, bpso, psd, bpsd)
                        finish(pso, bpso, psd, bpsd, q0, 16)
                k.barrier([b_q, b_k, b_kp, b_v, b_vp, b_vn, b_gate, b_A2, b_R, b_op, b_mk] + b_A2s, [b_ra, bW[0], bW[1]], dummy[0:1, 0:1])
            if upto == "C4":
                k.dma("sp", GDBG, RB[:, :].bitcast(BF16), R=[b_gT])
                break
            if "C2" not in SKIP:
                S.serial = "SER" in SKIP
                CH = [(64 * c, 64) for c in range(32)] + [(2048, 16), (2064, 16)]
                NCH = len(CH)
                MW = Region(Mt, 9216); MW.off = MISC_BASE
                triS = MW.take([64], F32, 64); triI = MW.take([64], F32, 64); triL = MW.take([64], F32, 64)
                pcs = MW.take([4, NCH]); m01 = MW.take([512]); m01s = MW.take([32])
                negw0 = MW.take([4]); omka = MW.take([4])
                b_rc = Buf("rwconst"); b_pcs = Buf("pcs")
                k.barrier([b_ra, bW[0], bW[1]], [b_rc, b_pcs], dummy[0:1, 0:1])
                for (tri, op_, pat, cm) in ((triS, ALU.is_gt, [[1, 64]], -1), (triI, ALU.is_ge, [[1, 64]], -1), (triL, ALU.is_gt, [[-1, 64]], 1)):
                    k.memset("pool", tri, 1.0, W=[b_rc])
                    k.asel(tri, tri, pat, op_, 0.0, 0, cm, R=[b_rc], W=[b_rc])
                k.memset("pool", m01, 1.0, W=[b_rc])
                k.memset("pool", m01.rearrange("p (a b) -> p a b", b=64)[:, :, 0:1], 0.0, W=[b_rc])
                k.memset("pool", m01s, 1.0, W=[b_rc])
                k.memset("pool", m01s.rearrange("p (a b) -> p a b", b=16)[:, :, 0:1], 0.0, W=[b_rc])
                k.ts("dve", negw0, cpt[:, CP_W0:CP_W0 + 4], -1.0, R=[b_cp], W=[b_rc])
                k.ts("dve", omka, cpt[:, CP_KA:CP_KA + 4], -1.0, 1.0, ALU.mult, ALU.add, R=[b_cp], W=[b_rc])
                WS = Region(RA, NBIG); WW = Region(Wt, 8192)
                names = ["Pr", "Pk", "Pv", "Qr", "Qk", "Qv", "X12", "X12p", "TH12", "LW", "AA", "KK", "K2", "CUM", "E1", "E2", "E3", "E4", "WR", "T1", "T2", "OA", "OB", "OK", "OR", "OBE", "OKE", "OBON"]
                T_ = {n: WS.take([512]) for n in names}
                B_ = {n: Buf(n) for n in names}
                w2p = WW.take([512]); a2p = WW.take([512]); b_w2 = Buf("w2p")
                k.barrier([b_ra], list(B_.values()) + [b_w2], dummy[0:1, 0:1])
                k.memset("dve", w2p, 0.0, W=[b_w2]); k.memset("dve", a2p, 0.0, W=[b_w2])
                k.dma("sp", w2p[0:64, :], rw_w2[l], W=[b_w2])
                k.dma("sp", a2p[64:128, :], rw_a2[l], W=[b_w2])

                def load_shift(dst, bd, prv, bp, row0, tile_j, t0, nt):
                    k.dma("sp", dst[:, 0:nt], PF[row0:row0 + 128, t0:t0 + nt], R=[bPF], W=[bd])
                    if t0 == 0:
                        k.memset("dve", prv[:, 0:1], 0.0, W=[bp])
                        k.dma("sp", prv[:, 1:nt], PF[row0:row0 + 128, 0:nt - 1], R=[bPF], W=[bp])
                    elif t0 < TP:
                        k.dma("sp", prv[:, 0:nt], PF[row0:row0 + 128, t0 - 1:t0 + nt - 1], R=[bPF], W=[bp])
                    else:
                        for j in range(2):
                            k.dma("sp", prv[:, 16 * j:16 * j + 1], st_shift[l][j][:, tile_j:tile_j + 1], W=[bp], allow_slow_non_contiguous=True)
                            k.dma("sp", prv[:, 16 * j + 1:16 * j + 16], PF[row0:row0 + 128, t0 + 16 * j:t0 + 16 * j + 15], R=[bPF], W=[bp])
                    mu = cpt[:, CP_MU + tile_j:CP_MU + tile_j + 1]
                    k.tt("dve", prv[:, 0:nt], prv[:, 0:nt], dst[:, 0:nt], ALU.subtract, R=[bp, bd], W=[bp])
                    k.stt("dve", dst[:, 0:nt], prv[:, 0:nt], mu, dst[:, 0:nt], ALU.mult, ALU.add, R=[bp, bd, b_cp], W=[bd])

                for tbi, (t0, nt) in enumerate(TB):
                    sl = slice(0, nt)
                    load_shift(T_["X12"], B_["X12"], T_["X12p"], B_["X12p"], 1536, 12, t0, nt)
                    k.act(T_["TH12"][:, sl], T_["X12"][:, sl], AF.Tanh, R=[B_["X12"]], W=[B_["TH12"]])
                    msk = m01[:, 0:nt] if nt == 512 else m01s[:, 0:nt]
                    for hp in range(4):
                        load_shift(T_["Pr"], B_["Pr"], T_["Qr"], B_["Qr"], 128 * hp, hp, t0, nt)
                        load_shift(T_["Pk"], B_["Pk"], T_["Qk"], B_["Qk"], 512 + 128 * hp, 4 + hp, t0, nt)
                        load_shift(T_["Pv"], B_["Pv"], T_["Qv"], B_["Qv"], 1024 + 128 * hp, 8 + hp, t0, nt)
                        c1 = lambda off: cpt[:, off + hp:off + hp + 1]
                        pl, bpl = next_ps(0, 4)
                        k.mm(pl[:, sl], w2p[:, 128 * hp:128 * hp + 128], T_["TH12"][:, sl], R=[b_w2, B_["TH12"]], W=[bpl])
                        k.act(T_["E1"][:, sl], pl[:, sl], AF.Exp, bias=negw0[:, hp:hp + 1], scale=-1.0, R=[bpl, b_rc], W=[B_["E1"]])
                        k.act(T_["E1"][:, sl], T_["E1"][:, sl], AF.Ln, bias=1.0, R=[B_["E1"]], W=[B_["E1"]])
                        k.act(T_["E1"][:, sl], T_["E1"][:, sl], AF.Exp, bias=-0.5, scale=-1.0, R=[B_["E1"]], W=[B_["E1"]])
                        k.ts("dve", T_["LW"][:, sl], T_["E1"][:, sl], -1.0, R=[B_["E1"]], W=[B_["LW"]])
                        pa, bpa = next_ps(0, 4)
                        k.mm(pa[:, sl], a2p[:, 128 * hp:128 * hp + 128], T_["X12"][:, sl], R=[b_w2, B_["X12"]], W=[bpa])
                        k.act(T_["AA"][:, sl], pa[:, sl], AF.Sigmoid, bias=c1(CP_A0), R=[bpa, b_cp], W=[B_["AA"]])
                        k.ts("dve", T_["KK"][:, sl], T_["Pk"][:, sl], c1(CP_KK), R=[B_["Pk"], b_cp], W=[B_["KK"]])
                        k.tt("dve", T_["T1"][:, sl], T_["KK"][:, sl], T_["KK"][:, sl], ALU.mult, R=[B_["KK"]], W=[B_["T1"]])
                        pn, bpn = next_ps(4, 8)
                        k.mm(pn[:, sl], bones, T_["T1"][:, sl], R=[b_c, B_["T1"]], W=[bpn])
                        k.act(T_["T2"][:, sl], pn[:, sl], AF.Sqrt, R=[bpn], W=[B_["T2"]])
                        k.ts("dve", T_["T2"][:, sl], T_["T2"][:, sl], 1e-12, None, ALU.max, R=[B_["T2"]], W=[B_["T2"]])
                        k.recip(T_["T2"][:, sl], T_["T2"][:, sl], R=[B_["T2"]], W=[B_["T2"]])
                        k.tt("dve", T_["KK"][:, sl], T_["KK"][:, sl], T_["T2"][:, sl], ALU.mult, R=[B_["KK"], B_["T2"]], W=[B_["KK"]])
                        k.ts("dve", T_["T1"][:, sl], T_["AA"][:, sl], c1(CP_KA), omka[:, hp:hp + 1], ALU.mult, ALU.add, R=[B_["AA"], b_cp, b_rc], W=[B_["T1"]])
                        k.tt("dve", T_["K2"][:, sl], T_["Pk"][:, sl], T_["T1"][:, sl], ALU.mult, R=[B_["Pk"], B_["T1"]], W=[B_["K2"]])
                        k.scan(T_["CUM"][:, sl], msk, T_["LW"][:, sl], 0.0, R=[b_rc, B_["LW"]], W=[B_["CUM"]])
                        csz = 64 if nt == 512 else 16
                        cv = T_["CUM"][:, sl].rearrange("p (a b) -> p a b", b=csz)
                        nch_ = nt // csz
                        ch0 = (t0 // 64) if nt == 512 else 32
                        k.act(pcs[:, hp, ch0:ch0 + nch_], cv[:, :, csz - 1], AF.Exp, R=[B_["CUM"]], W=[b_pcs])
                        k.act(T_["E1"][:, sl], T_["CUM"][:, sl], AF.Exp, R=[B_["CUM"]], W=[B_["E1"]])
                        k.act(T_["E2"][:, sl], T_["CUM"][:, sl], AF.Exp, scale=-1.0, R=[B_["CUM"]], W=[B_["E2"]])
                        k.tt("dve", T_["T1"][:, sl], T_["CUM"][:, sl], T_["LW"][:, sl], ALU.subtract, R=[B_["CUM"], B_["LW"]], W=[B_["T1"]])
                        k.act(T_["E3"][:, sl], T_["T1"][:, sl], AF.Exp, R=[B_["T1"]], W=[B_["E3"]])
                        k.tt("dve", T_["T2"][:, sl].rearrange("p (a b) -> p a b", b=csz), cv[:, :, csz - 1:csz].to_broadcast([128, nch_, csz]), cv, ALU.subtract, R=[B_["CUM"]], W=[B_["T2"]])
                        k.act(T_["E4"][:, sl], T_["T2"][:, sl], AF.Exp, R=[B_["T2"]], W=[B_["E4"]])
                        k.stt("dve", T_["OA"][:, sl], T_["KK"][:, sl], -1.0, T_["E3"][:, sl], ALU.mult, ALU.mult, R=[B_["KK"], B_["E3"]], W=[B_["OA"]])
                        k.tt("dve", T_["WR"][:, sl], T_["KK"][:, sl], T_["AA"][:, sl], ALU.mult, R=[B_["KK"], B_["AA"]], W=[B_["WR"]])
                        k.tt("dve", T_["OB"][:, sl], T_["WR"][:, sl], T_["E2"][:, sl], ALU.mult, R=[B_["WR"], B_["E2"]], W=[B_["OB"]])
                        k.tt("dve", T_["OBE"][:, sl], T_["WR"][:, sl], T_["E4"][:, sl], ALU.mult, R=[B_["WR"], B_["E4"]], W=[B_["OBE"]])
                        k.tt("dve", T_["OK"][:, sl], T_["K2"][:, sl], T_["E2"][:, sl], ALU.mult, R=[B_["K2"], B_["E2"]], W=[B_["OK"]])
                        k.tt("dve", T_["OKE"][:, sl], T_["K2"][:, sl], T_["E4"][:, sl], ALU.mult, R=[B_["K2"], B_["E4"]], W=[B_["OKE"]])
                        k.tt("dve", T_["OR"][:, sl], T_["Pr"][:, sl], T_["E1"][:, sl], ALU.mult, R=[B_["Pr"], B_["E1"]], W=[B_["OR"]])
                        k.stt("dve", T_["T1"][:, sl], T_["Pr"][:, sl], c1(CP_RK), T_["K2"][:, sl], ALU.mult, ALU.mult, R=[B_["Pr"], B_["K2"], b_cp], W=[B_["T1"]])
                        pb, bpb = next_ps(4, 8)
                        k.mm(pb[:, sl], bones, T_["T1"][:, sl], R=[b_c, B_["T1"]], W=[bpb])
                        k.tt("dve", T_["OBON"][:, sl], T_["Pv"][:, sl], pb[:, sl], ALU.mult, R=[B_["Pv"], bpb], W=[B_["OBON"]])
                        for ai, nm in enumerate(["OA", "OB", "OK", "OR", "OBE", "OKE", "Pv", "OBON"]):
                            k.dma("sp", RWS[512 * ai + 128 * hp:512 * ai + 128 * hp + 128, t0:t0 + nt], T_[nm][:, sl], R=[B_[nm], bSCR])
                if True:
                    for si, (s0_, sn) in enumerate(SEGS):
                        e_ = s0_ + sn - 1
                        k.dma("sp", o_shift[l][si].rearrange("(a b) -> a b", b=1), PF[0:1664, e_:e_ + 1], R=[bPF], allow_slow_non_contiguous=True)
                b_r2 = Buf("r2")
                k.barrier(list(B_.values()) + [b_w2], [b_r2], dummy[0:1, 0:1])
                fence(bSCR)
                if upto == "R1":
                    break
                WS = Region(RA, NBIG); WW = Region(Wt, 8192)
                LD = [[WW.take([4, 64]) for _ in range(7)] for _ in range(2)]
                b_LD = [[Buf(f"ld{i}{j}") for j in range(7)] for i in range(2)]
                abd = WW.take([4, 2, 64]); bbd = WW.take([4, 2, 64]); rbd = WW.take([4, 2, 64])
                b_abd, b_bbd, b_rbd = Buf("abd"), Buf("bbd"), Buf("rbd")
                def t64(n):
                    return WS.take([512], F32, 64), Buf(n)
                def t128(n, w=512):
                    return WS.take([w]), Buf(n)
                atok, b_atok = t64("atok"); betok, b_betok = t64("betok"); ketok, b_ketok = t64("ketok"); vtok, b_vtok = t64("vtok")
                N1, b_N1 = t64("N1"); A1, b_A1 = t64("A1"); AakT, b_AakT = t64("AakT"); BrbT, b_BrbT = t64("BrbT"); BrkT, b_BrkT = t64("BrkT")
                PA = [t64("PA0"), t64("PA1")]; PN = [t64("PN0"), t64("PN1")]; ZZ = [t64("Z0"), t64("Z1")]
                W1, b_W1 = t64("W1"); U0, b_U0 = t64("U0"); Ap, b_Ap = t64("Ap")
                GTp, b_GTp = t128("GTp"); Hbd, b_Hbd = t128("Hbd")
                RpT, b_RpT = t128("RpT", 256); Y0d, b_Y0d = t128("Y0d", 256); yout, b_yout = t128("yout", 256)
                Sb = [t128("Sb0"), t128("Sb1")]; SIt, b_SI = t128("SI"); SIn = [t128("SIa"), t128("SIb")]; SF = [t128("SFa"), t128("SFb")]
                SO, b_SO = t128("SO")
                allb = [b for row in b_LD for b in row] + [b_abd, b_bbd, b_rbd, b_atok, b_betok, b_ketok, b_vtok, b_N1, b_A1, b_AakT, b_BrbT, b_BrkT, b_W1, b_U0, b_Ap, b_GTp, b_Hbd, b_RpT, b_Y0d, b_yout, b_SI, b_SO] + [x[1] for x in PA + PN + ZZ + Sb + SIn + SF]
                k.barrier([b_r2], allb, dummy[0:1, 0:1])
                v3 = lambda t, C: t[0:C, :].rearrange("c (h x) -> c h x", h=8)[:, :, 0:C]
                p4 = lambda t: t[:, :].rearrange("q (p x) -> q p x", p=4)
                for t_ in (abd, bbd, rbd):
                    k.memset("dve", t_, 0.0, W=[b_abd, b_bbd, b_rbd])
                k.memset("dve", Sb[0][0], 0.0, W=[Sb[0][1]])
                k.memset("dve", SIt, 0.0, W=[b_SI])
                for j in range(2):
                    for h_ in range(8):
                        p_, hf = h_ // 2, h_ % 2
                        k.dma("sp", p4(SIt)[64 * hf:64 * hf + 64, p_, 64 * hf:64 * hf + 64], st_wkv[l][j][h_], W=[b_SI])
                    pt, bpt = next_ps(0, 8)
                    for p_ in range(4):
                        k.tr(pt[:, 128 * p_:128 * p_ + 128], p4(SIt)[:, p_, :], identf, R=[b_SI, b_c], W=[bpt])
                    k.cp("act", SIn[j][0], pt[:, :], R=[bpt], W=[SIn[j][1]])

                def emit_state(Sbuf, bS, seg):
                    pt, bpt = next_ps(0, 8)
                    for p_ in range(4):
                        k.tr(pt[:, 128 * p_:128 * p_ + 128], p4(Sbuf)[:, p_, :], identf, R=[bS, b_c], W=[bpt])
                    k.cp("act", SO, pt[:, :], R=[bpt], W=[b_SO])
                    for h_ in range(8):
                        p_, hf = h_ // 2, h_ % 2
                        k.dma("sp", o_wkv[l][seg][h_], p4(SO)[64 * hf:64 * hf + 64, p_, 64 * hf:64 * hf + 64], R=[b_SO])

                def ld_chunk(cj):
                    t0_, C_ = CH[cj]
                    for ai in range(7):
                        k.dma("sp", LD[cj % 2][ai][:, :, 0:C_], RWS[512 * ai:512 * ai + 512, t0_:t0_ + C_].rearrange("(p c) t -> c p t", c=128), R=[bSCR], W=[b_LD[cj % 2][ai]])

                ld_chunk(0)
                for ci, (t0, C) in enumerate(CH):
                    ld, bld = LD[ci % 2], b_LD[ci % 2]
                    if ci + 1 < NCH:
                        ld_chunk(ci + 1)
                    at, bt, kt_, rt, be, ke, vv = ld
                    b_at, b_bt, b_kt, b_rt, b_be, b_ke, b_vv = bld
                    for (src, bsrc, dst, bdst) in ((at, b_at, abd, b_abd), (bt, b_bt, bbd, b_bbd), (rt, b_rt, rbd, b_rbd)):
                        k.cp("pool", dst[0:64, :, 0, 0:C], src[0:64, :, 0:C], R=[bsrc], W=[bdst])
                        k.cp("pool", dst[64:128, :, 1, 0:C], src[64:128, :, 0:C], R=[bsrc], W=[bdst])
                    for (src, bsrc, dst, bdst) in ((at, b_at, atok, b_atok), (be, b_be, betok, b_betok), (ke, b_ke, ketok, b_ketok), (vv, b_vv, vtok, b_vtok)):
                        pt, bpt = next_ps(0, 8)
                        for p_ in range(4):
                            k.tr(pt[0:C, 128 * p_:128 * p_ + 128], src[:, p_, 0:C], identf, R=[bsrc, b_c], W=[bpt])
                        k.cp(ev_eng(), dst[0:C, :], pt[0:C, :], R=[bpt], W=[bdst])

                    def mat(dst, bdst, lh, blh, rhbd, brh, mask):
                        pm, bpm = next_ps(0, 8)
                        pv = pm[0:C, :].rearrange("c (p h t) -> c p h t", p=4, h=2)
                        for p_ in range(4):
                            k.mm(pv[:, p_, :, 0:C], lh[:, p_, 0:C], rhbd[:, p_, :, 0:C], R=[blh, brh], W=[bpm])
                        k.tt("dve", v3(dst, C), v3(pm, C), mask[0:C, 0:C].unsqueeze(1).to_broadcast([C, 8, C]), ALU.mult, R=[bpm, b_rc], W=[bdst])
                    mat(N1, b_N1, bt, b_bt, abd, b_abd, triS)
                    mat(A1, b_A1, at, b_at, bbd, b_bbd, triL)
                    mat(AakT, b_AakT, kt_, b_kt, abd, b_abd, triS)
                    mat(BrbT, b_BrbT, bt, b_bt, rbd, b_rbd, triI)
                    mat(BrkT, b_BrkT, kt_, b_kt, rbd, b_rbd, triI)
                    Zc, bZc = ZZ[0]
                    k.tt("dve", v3(Zc, C), v3(N1, C), identf[0:C, 0:C].unsqueeze(1).to_broadcast([C, 8, C]), ALU.add, R=[b_N1, b_c], W=[bZc])
                    curA, bcA, curN, bcN = A1, b_A1, N1, b_N1
                    rounds = int(round(math.log2(C))) - 1
                    for r_ in range(rounds):
                        nA, bnA = PA[r_ % 2]; nN, bnN = PN[r_ % 2]
                        psN, bpsN = next_ps(0, 8); psA, bpsA = next_ps(0, 8)
                        lastr = (r_ == rounds - 1)
                        for h_ in range(8):
                            if not lastr:
                                k.mm(psN[0:C, 64 * h_:64 * h_ + C], curA[0:C, 64 * h_:64 * h_ + C], curN[0:C, 64 * h_:64 * h_ + C], R=[bcA, bcN], W=[bpsN])
                            k.mm(psA[0:C, 64 * h_:64 * h_ + C], curN[0:C, 64 * h_:64 * h_ + C], curA[0:C, 64 * h_:64 * h_ + C], R=[bcA, bcN], W=[bpsA])
                        if not lastr:
                            k.cp("act", v3(nN, C), v3(psN, C), R=[bpsN], W=[bnN])
                        k.cp("dve", v3(nA, C), v3(psA, C), R=[bpsA], W=[bnA])
                        psZ, bpsZ = next_ps(0, 8)
                        for h_ in range(8):
                            k.mm(psZ[0:C, 64 * h_:64 * h_ + C], nA[0:C, 64 * h_:64 * h_ + C], Zc[0:C, 64 * h_:64 * h_ + C], R=[bnA, bZc], W=[bpsZ])
                        Zn, bZn = ZZ[(r_ + 1) % 2]
                        k.tt("dve", v3(Zn, C), v3(psZ, C), v3(Zc, C), ALU.add, R=[bpsZ, bZc], W=[bZn])
                        Zc, bZc = Zn, bZn
                        curA, bcA, curN, bcN = nA, bnA, nN, bnN
                    psW, bpsW = next_ps(0, 8)
                    for h_ in range(8):
                        k.mm(psW[0:C, 64 * h_:64 * h_ + 64], AakT[0:C, 64 * h_:64 * h_ + C], vtok[0:C, 64 * h_:64 * h_ + 64], R=[b_AakT, b_vtok], W=[bpsW])
                    k.cp("act", W1[0:C, :], psW[0:C, :], R=[bpsW], W=[b_W1])
                    psU, bpsU = next_ps(0, 8); psP, bpsP = next_ps(0, 8)
                    for h_ in range(8):
                        k.mm(psU[0:C, 64 * h_:64 * h_ + 64], Zc[0:C, 64 * h_:64 * h_ + C], W1[0:C, 64 * h_:64 * h_ + 64], R=[bZc, b_W1], W=[bpsU])
                        k.mm(psP[0:C, 64 * h_:64 * h_ + 64], Zc[0:C, 64 * h_:64 * h_ + C], atok[0:C, 64 * h_:64 * h_ + 64], R=[bZc, b_atok], W=[bpsP])
                    k.cp("act", U0[0:C, :], psU[0:C, :], R=[bpsU], W=[b_U0])
                    k.cp("dve", Ap[0:C, :], psP[0:C, :], R=[bpsP], W=[b_Ap])
                    psG, bpsG = next_ps(0, 8); psH, bpsH = next_ps(0, 8)
                    for p_ in range(4):
                        cs = slice(128 * p_, 128 * p_ + 128)
                        k.mm(psG[:, cs], Ap[0:C, cs], betok[0:C, cs], R=[b_Ap, b_betok], W=[bpsG])
                        k.mm(psH[:, cs], betok[0:C, cs], U0[0:C, cs], start=True, stop=False, R=[b_betok, b_U0], W=[bpsH])
                        k.mm(psH[:, cs], ketok[0:C, cs], vtok[0:C, cs], start=False, stop=True, R=[b_ketok, b_vtok], W=[bpsH])
                    bo_b = bones.unsqueeze(1).to_broadcast([128, 4, 128])
                    k.tt("dve", p4(GTp), p4(psG), bo_b, ALU.mult, R=[bpsG, b_c], W=[b_GTp])
                    for p_ in range(4):
                        k.stt("dve", p4(GTp)[:, p_, :], identf, pcs[:, p_, ci:ci + 1], p4(GTp)[:, p_, :], ALU.mult, ALU.add, R=[b_c, b_pcs, b_GTp], W=[b_GTp])
                    k.tt("dve", p4(Hbd), p4(psH), bo_b, ALU.mult, R=[bpsH, b_c], W=[b_Hbd])
                    if ci < 32:
                        Scur, bScur = Sb[ci % 2]; Snew, bSnew = Sb[(ci + 1) % 2]
                    else:
                        Scur, bScur = SIn[ci - 32]; Snew, bSnew = SF[ci - 32]
                    psS, bpsS = next_ps(0, 8)
                    for p_ in range(4):
                        cs = slice(128 * p_, 128 * p_ + 128)
                        k.mm(psS[:, cs], GTp[:, cs], Scur[:, cs], R=[b_GTp, bScur], W=[bpsS])
                    k.tt("dve", Snew, psS[:, :], Hbd, ALU.add, R=[bpsS, b_Hbd], W=[bSnew])
                    psR, bpsR = next_ps(0, 8); psY0, bpsY0 = next_ps(0, 8)
                    psRv = psR[:, :].rearrange("q (p h t) -> q p h t", p=4, h=2)
                    psY0v = psY0[:, :].rearrange("q (p h t) -> q p h t", p=4, h=2)
                    Brb4 = BrbT[0:C, :].rearrange("c (p h t) -> c p h t", p=4, h=2)
                    Brk4 = BrkT[0:C, :].rearrange("c (p h t) -> c p h t", p=4, h=2)
                    for p_ in range(4):
                        cs = slice(128 * p_, 128 * p_ + 128)
                        k.mm(psRv[:, p_, :, 0:C], Ap[0:C, cs], Brb4[:, p_, :, 0:C], R=[b_Ap, b_BrbT], W=[bpsR])
                        k.mm(psY0v[:, p_, :, 0:C], U0[0:C, cs], Brb4[:, p_, :, 0:C], start=True, stop=False, R=[b_U0, b_BrbT], W=[bpsY0])
                        k.mm(psY0v[:, p_, :, 0:C], vtok[0:C, cs], Brk4[:, p_, :, 0:C], start=False, stop=True, R=[b_vtok, b_BrkT], W=[bpsY0])
                    R4 = RpT[:, :].rearrange("q (p t) -> q p t", p=4); Y4 = Y0d[:, :].rearrange("q (p t) -> q p t", p=4); yo4 = yout[:, :].rearrange("q (p t) -> q p t", p=4)
                    for hf in range(2):
                        ps_ = slice(64 * hf, 64 * hf + 64)
                        k.tt("dve", R4[ps_, :, 0:C], psRv[ps_, :, hf, 0:C], rt[ps_, :, 0:C], ALU.add, R=[bpsR, b_rt], W=[b_RpT])
                        k.cp("act", Y4[ps_, :, 0:C], psY0v[ps_, :, hf, 0:C], R=[bpsY0], W=[b_Y0d])
                    psY, bpsY = next_ps(0, 8)
                    psYv = psY[:, 0:256].rearrange("q (p t) -> q p t", p=4)
                    for p_ in range(4):
                        cs = slice(128 * p_, 128 * p_ + 128)
                        k.mm(psYv[:, p_, 0:C], Scur[:, cs], R4[:, p_, 0:C], R=[bScur, b_RpT], W=[bpsY])
                    k.tt("dve", yo4[:, :, 0:C], psYv[:, :, 0:C], Y4[:, :, 0:C], ALU.add, R=[bpsY, b_Y0d], W=[b_yout])
                    k.dma("sp", YRW[:, t0:t0 + C].rearrange("(p c) t -> c p t", c=128), yo4[:, :, 0:C], R=[b_yout, bSCR])
                    if ci == 31:
                        emit_state(Sb[0][0], Sb[0][1], 0)
                    elif ci >= 32:
                        emit_state(SF[ci - 32][0], SF[ci - 32][1], 1 + ci - 32)
                b_r3 = Buf("r3")
                k.barrier(allb, [b_r3], dummy[0:1, 0:1])
                fence(bSCR)
                if upto == "R2":
                    break
                WS = Region(RA, NBIG)
                yr = [WS.take([512]), WS.take([512])]; bo = [WS.take([512]), WS.take([512])]; ga = [WS.take([512]), WS.take([512])]
                b_yr = [Buf("yr0"), Buf("yr1")]; b_bo = [Buf("bo0"), Buf("bo1")]; b_ga = [Buf("ga0"), Buf("ga1")]
                yc = WS.take([512]); sqq = WS.take([512]); rs_ = WS.take([512])
                b_yc, b_sqq, b_rs = Buf("yc"), Buf("sqq"), Buf("rs")
                k.barrier([b_r3], b_yr + b_bo + b_ga + [b_yc, b_sqq, b_rs], dummy[0:1, 0:1])
                it = 0
                for hp in range(4):
                    for (t0, nt) in TB:
                        i2 = it % 2
                        it += 1
                        sl = slice(0, nt)
                        k.dma("sp", yr[i2][:, sl], YRW[128 * hp:128 * hp + 128, t0:t0 + nt], R=[bSCR], W=[b_yr[i2]])
                        k.dma("sp", bo[i2][:, sl], RWS[7 * 512 + 128 * hp:7 * 512 + 128 * hp + 128, t0:t0 + nt], R=[bSCR], W=[b_bo[i2]])
                        k.dma("sp", ga[i2][:, sl], PF[3200 + 128 * hp:3200 + 128 * hp + 128, t0:t0 + nt], R=[bPF], W=[b_ga[i2]])
                        pm, bpm = next_ps(0, 4)
                        k.mm(pm[:, sl], bones, yr[i2][:, sl], R=[b_c, b_yr[i2]], W=[bpm])
                        k.stt("dve", yc[:, sl], pm[:, sl], -1.0 / 64, yr[i2][:, sl], ALU.mult, ALU.add, R=[bpm, b_yr[i2]], W=[b_yc])
                        k.tt("dve", sqq[:, sl], yc[:, sl], yc[:, sl], ALU.mult, R=[b_yc], W=[b_sqq])
                        pv, bpv = next_ps(4, 8)
                        k.mm(pv[:, sl], bones, sqq[:, sl], R=[b_c, b_sqq], W=[bpv])
                        k.ts("dve", rs_[:, sl], pv[:, sl], 1.0 / 64, 64e-5, ALU.mult, ALU.add, R=[bpv], W=[b_rs])
                        k.act(rs_[:, sl], rs_[:, sl], AF.Sqrt, R=[b_rs], W=[b_rs])
                        k.recip(rs_[:, sl], rs_[:, sl], R=[b_rs], W=[b_rs])
                        k.tt("dve", yc[:, sl], yc[:, sl], rs_[:, sl], ALU.mult, R=[b_yc, b_rs], W=[b_yc])
                        k.ts("dve", yc[:, sl], yc[:, sl], cpt[:, CP_LG + hp:CP_LG + hp + 1], cpt[:, CP_LB + hp:CP_LB + hp + 1], ALU.mult, ALU.add, R=[b_yc, b_cp], W=[b_yc])
                        k.tt("dve", yc[:, sl], yc[:, sl], bo[i2][:, sl], ALU.add, R=[b_yc, b_bo[i2]], W=[b_yc])
                        k.tt("dve", gT[:, hp, t0:t0 + nt], yc[:, sl], ga[i2][:, sl], ALU.mult, R=[b_yc, b_ga[i2]], W=[b_gT])
                k.barrier(b_yr + b_bo + b_ga + [b_yc, b_sqq, b_rs, b_rc, b_pcs], [b_ra, bW[0], bW[1]], dummy[0:1, 0:1])
            if upto == "C2":
                k.dma("sp", GDBG, RB[:, :].bitcast(BF16), R=[b_gT])
                break
            S.serial = False
            if "C2" in SKIP:
                k.memset("dve", gT[:, 0:4, :], 0.0, W=[b_gT])

            hT2 = RA[:, :].bitcast(BF16).rearrange("p (a b) -> p a b", a=16)
            b_h2 = Buf("hT2")
            WW = Region(Wt, 8192); MW = Region(Mt, 9216); MW.off = MISC_BASE
            Wm = [WW.take([16, 256], BF16), WW.take([16, 256], BF16)]
            Wb = [WW.take([4, 256], BF16), WW.take([4, 256], BF16)]
            acb1 = MW.take([NTOK], BF16)
            acb = [acb1, acb1]
            dst_ = [WW.take([4, 256]) for _ in range(2)]
            b_dst = [Buf(f"dst{i}") for i in range(2)]
            mgts = [MW.take([512]), WW.take([512])]; b_mgts = [Buf("mg0"), Buf("mg1")]
            accs = MW.take([2, NTOK]); mgt = mgts[0]; tmpt = MW.take([512])
            b_Wm = [Buf("Wm0"), Buf("Wm1")]; b_acb1 = Buf("acb"); b_acb = [b_acb1, b_acb1]
            b_accs, b_mgt, b_tmpt = Buf("accs"), Buf("mgt"), Buf("tmpt")
            k.barrier([b_ra, bW[0], bW[1]], [b_h2, b_accs, b_mgt, b_tmpt, b_acb1] + b_Wm + b_dst + b_mgts, dummy[0:1, 0:1])
            dsc = [0]
            mgc = [0]

            def dload(dst_ap, src_ap, bdst):
                i = dsc[0] % 2
                dsc[0] += 1
                k.dma("sp", dst_[i], src_ap.rearrange("p (kt c) -> p kt c", kt=4), W=[b_dst[i]])
                k.cp("pool", dst_ap, dst_[i], R=[b_dst[i]], W=[bdst])
            k.dma("sp", RA[:, :].bitcast(BF16), HTS, R=[bHTS], W=[b_h2])
            wi = 0
            ai = 0

            def dgroup(g):
                dblk_, b_ = g // 4, g % 4
                wm_, wb_, bwm = Wm[g % 2], Wb[g % 2], b_Wm[g % 2]
                for kq in range(4):
                    dload(wm_[:, 4 * kq:4 * kq + 4, :], w_mg[l][dblk_][b_][:, 1024 * kq:1024 * (kq + 1)], bwm)
                dload(wb_, w_br[l][dblk_][b_], bwm)
                return wm_, wb_, bwm

            nxtg = dgroup(0)
            for dblk in range(8):
                for b in range(4):
                    wm_, wb_, bwm = nxtg
                    if 4 * dblk + b + 1 < 32:
                        nxtg = dgroup(4 * dblk + b + 1)
                    for dt in range(2):
                        bcol = cpt[:, CP_BMG + 16 * b + 2 * dblk + dt:CP_BMG + 16 * b + 2 * dblk + dt + 1]
                        for (t0, nt) in TB:
                            pm, bpm = next_ps(0, 4)
                            pu, bpu = next_ps(4, 8)
                            for kt in range(16):
                                k.mm(pm[:, 0:nt], wm_[:, kt, 128 * dt:128 * dt + 128], hT2[:, kt, t0:t0 + nt], start=(kt == 0), stop=(kt == 15), R=[bwm, b_h2], W=[bpm])
                            for kt in range(4):
                                k.mm(pu[:, 0:nt], wb_[:, kt, 128 * dt:128 * dt + 128], gT[:, 4 * b + kt, t0:t0 + nt], start=(kt == 0), stop=(kt == 3), R=[bwm, b_gT], W=[bpu])
                            mg_, bmg_ = mgts[mgc[0] % 2], b_mgts[mgc[0] % 2]
                            mgc[0] += 1
                            k.act(mg_[:, 0:nt], pm[:, 0:nt], AF.Sigmoid, bias=bcol, R=[bpm, b_cp], W=[bmg_])
                            if b == 0:
                                k.tt("dve", accs[:, dt, t0:t0 + nt], mg_[:, 0:nt], pu[:, 0:nt], ALU.mult, R=[bmg_, bpu], W=[b_accs])
                            else:
                                k.tt("dve", tmpt[:, 0:nt], mg_[:, 0:nt], pu[:, 0:nt], ALU.mult, R=[bmg_, bpu], W=[b_tmpt])
                                k.tt("dve", accs[:, dt, t0:t0 + nt], accs[:, dt, t0:t0 + nt], tmpt[:, 0:nt], ALU.add, R=[b_tmpt, b_accs], W=[b_accs])
                for dt in range(2):
                    a_, ba_ = acb[ai % 2], b_acb[ai % 2]
                    ai += 1
                    k.cp("act", a_, accs[:, dt, :], R=[b_accs], W=[ba_])
                    r0 = 256 * dblk + 128 * dt
                    k.dma("sp", ACC[r0:r0 + 128, :], a_, R=[ba_, bACC])
            if upto == "D":
                break
            fence(bACC)
            accT = RA[:, :].bitcast(BF16).rearrange("p (a b) -> p a b", a=16)
            b_aT = Buf("accT")
            WS = Region(RB, NBIG)
            xo = [WS.take([512]), WS.take([512])]
            b_xo = [Buf("xo0"), Buf("xo1")]
            est = [WS.take([4, 512]) for _ in range(3)]
            b_est = [Buf(f"est{i}") for i in range(3)]
            k.barrier([b_h2, b_gT, b_accs, b_mgt, b_tmpt, b_acb1] + b_Wm + b_dst + b_mgts, [b_aT, bW[0], bW[1]] + b_xo + b_est, dummy[0:1, 0:1])
            k.dma("sp", accT, ACC.rearrange("(dt p) t -> p dt t", p=128), R=[bACC], W=[b_aT])
            xi = 0
            nxtw = load_wblock(w_out[l][0], 512, est, b_est)
            for cb in range(4):
                wb, bw = nxtw
                if cb + 1 < 4:
                    nxtw = load_wblock(w_out[l][cb + 1], 512, est, b_est)
                for (r0, nr) in TT:
                    x_, bx_ = xo[xi % 2], b_xo[xi % 2]
                    xi += 1
                    k.dma("sp", x_[0:nr, :], Xs[l][r0:r0 + nr, 512 * cb:512 * cb + 512], R=[bXs[l]], W=[bx_])
                    pp, bp = next_ps(0, 4)
                    for dt in range(16):
                        k.mm(pp[0:nr, :], accT[:, dt, r0:r0 + nr], wb[:, dt, :], start=(dt == 0), stop=(dt == 15), R=[b_aT, bw], W=[bp])
                    k.tt("dve", x_[0:nr, :], x_[0:nr, :], pp[0:nr, :], ALU.add, R=[bx_, bp], W=[bx_])
                    k.dma("sp", Xs[l + 1][r0:r0 + nr, 512 * cb:512 * cb + 512], x_[0:nr, :], R=[bx_, bXs[l + 1]])
            k.barrier([b_aT, bW[0], bW[1]] + b_xo + b_est, [bRA, bRB], dummy[0:1, 0:1])
            if upto == "E":
                break
        else:
            fence(*DRB)
            WS = Region(R1t, NBIG)
            xt = [WS.take([D]), WS.take([D])]
            bxt = [Buf("xt0"), Buf("xt1")]
            sq = WS.take([D]); b_sq = Buf("sq")
            gt = WS.take([D]); b_gt = Buf("gt")
            yo = [WS.take([D]), WS.take([D])]; b_yo = [Buf("yo0"), Buf("yo1")]
            ss = WS.take([8]); b_ss = Buf("ss")
            k.barrier([bR1, bR2], [bxt[0], bxt[1], b_sq, b_gt, b_ss] + b_yo, dummy[0:1, 0:1])
            k.dma("sp", gt, fin_g.partition_broadcast(128), W=[b_gt])
            for ti, (r0, nr) in enumerate(TT):
                xb, bx = xt[ti % 2], bxt[ti % 2]
                y_, by_ = yo[ti % 2], b_yo[ti % 2]
                k.dma("sp", xb[0:nr, :], Xs[L][r0:r0 + nr, :], R=[bXs[L]], W=[bx])
                k.act(sq[0:nr, :], xb[0:nr, :], AF.Square, R=[bx], W=[b_sq])
                k.red(ss[0:nr, 0:1], sq[0:nr, :], R=[b_sq], W=[b_ss])
                k.ts("dve", ss[0:nr, 1:2], ss[0:nr, 0:1], 1.0 / D, 1e-6, ALU.mult, ALU.add, R=[b_ss], W=[b_ss])
                k.act(ss[0:nr, 2:3], ss[0:nr, 1:2], AF.Sqrt, R=[b_ss], W=[b_ss])
                k.recip(ss[0:nr, 3:4], ss[0:nr, 2:3], R=[b_ss], W=[b_ss])
                k.stt("dve", y_[0:nr, :], xb[0:nr, :], ss[0:nr, 3:4], gt[0:nr, :], ALU.mult, ALU.mult, R=[bx, b_ss, b_gt], W=[by_])
                k.dma("sp", o_y[r0:r0 + nr, :], y_[0:nr, :], R=[by_])
        S.emit()
        print("ops", {e: len(v) for e, v in S.ops.items()})
    return nc


def _col(v, ntile):
    return np.ascontiguousarray(np.asarray(v).reshape(ntile, 128).T)


_SHARED = {}


def prep_shared(inp):
    f = lambda a: np.ascontiguousarray(np.asarray(a, dtype=np.float32))
    w_in = np.asarray(inp["w_in"], dtype=np.float32)
    sh = {}
    idx_fm = np.concatenate([np.arange(0, 2176), np.arange(2848, 2848 + 1024), np.arange(4384, 6432)])
    kpe0 = 2176 + 640
    idx_tm = np.concatenate([np.arange(2176, 2176 + 384), np.arange(kpe0, kpe0 + 32),
                             np.arange(kpe0 + 16, kpe0 + 32), np.arange(kpe0, kpe0 + 16),
                             np.arange(2176 + 384, 2176 + 640), np.arange(2848 + 512, 2848 + 1536)])
    assert idx_fm.size == NFM and idx_tm.size == NTM
    def blk(a, nk):
        Ln, _, nc_ = a.shape
        return np.ascontiguousarray(a.reshape(Ln, nk, 128, nc_).transpose(0, 2, 1, 3).reshape(Ln, 128, nk * nc_))
    wfm = w_in[:, :, idx_fm]
    sh["w_fm"] = np.ascontiguousarray(np.stack([blk(wfm[:, :, c0:c0 + 512], 16) for c0 in FM_BLOCKS], 1))
    sh["w_tm"] = blk(w_in[:, :, idx_tm], 16)
    wmg = w_in[:, :, 6432:]
    sh["w_mg"] = np.ascontiguousarray(np.stack([np.stack([blk(wmg[:, :, b * 2048 + 256 * d:b * 2048 + 256 * d + 256], 16) for b in range(4)], 1) for d in range(8)], 1))
    wbr = f(inp["w_branch"])
    sh["w_br"] = np.ascontiguousarray(np.stack([np.stack([blk(wbr[:, b, :, 256 * d:256 * d + 256], 4) for b in range(4)], 1) for d in range(8)], 1))
    wo = f(inp["w_out"])
    sh["w_out"] = np.ascontiguousarray(np.stack([blk(wo[:, :, 512 * c:512 * c + 512], 16) for c in range(4)], 1))
    cp = np.zeros((L, 128, NCP), np.float32)
    for l in range(L):
        cp[l, :, CP_MU:CP_MU + 13] = _col(inp["rw_mu"][l], 13)
        cp[l, :, CP_W0:CP_W0 + 4] = _col(inp["rw_w0"][l], 4)
        cp[l, :, CP_A0:CP_A0 + 4] = _col(inp["rw_a0"][l], 4)
        cp[l, :, CP_KK:CP_KK + 4] = _col(inp["rw_k_k"][l], 4)
        cp[l, :, CP_KA:CP_KA + 4] = _col(inp["rw_k_a"][l], 4)
        cp[l, :, CP_RK:CP_RK + 4] = _col(np.asarray(inp["rw_r_k"][l]).reshape(512), 4)
        cp[l, :, CP_LG:CP_LG + 4] = _col(inp["rw_lnx_g"][l], 4)
        cp[l, :, CP_LB:CP_LB + 4] = _col(inp["rw_lnx_b"][l], 4)
        cp[l, :, CP_LRE:CP_LRE + 16] = _col(np.asarray(inp["ssm_lam_re"][l]).reshape(2048), 16)
        cp[l, :, CP_LIM:CP_LIM + 16] = _col(np.asarray(inp["ssm_lam_im"][l]).reshape(2048), 16)
        cp[l, :, CP_LDT:CP_LDT + 16] = _col(np.repeat(np.asarray(inp["ssm_log_dt"][l]), 64), 16)
        cp[l, 0:32, CP_DSK:CP_DSK + 16] = np.asarray(inp["ssm_d"][l]).reshape(16, 32).T
        cp[l, :, CP_BGLU:CP_BGLU + 4] = _col(inp["ssm_b_glu"][l], 4)
        cp[l, :, CP_BMG:CP_BMG + 64] = _col(np.asarray(inp["b_merge"][l]).reshape(8192), 64)
    sh["cpar"] = cp
    sh["norm_g"] = f(inp["norm_g"])
    sh["fin_g"] = f(inp["final_norm_g"]).reshape(1, D)
    sh["rw_w2"] = f(inp["rw_w2"])
    sh["rw_a2"] = f(inp["rw_a2"])
    sh["ssm_b"] = np.ascontiguousarray(np.stack([f(inp["ssm_b_re"]).reshape(L, 2048, 16), f(inp["ssm_b_im"]).reshape(L, 2048, 16)], 1))
    cre = np.transpose(f(inp["ssm_c_re"]), (0, 1, 3, 2)).reshape(L, 2048, 16)
    cim = np.transpose(f(inp["ssm_c_im"]), (0, 1, 3, 2)).reshape(L, 2048, 16)
    sh["ssm_c"] = np.ascontiguousarray(np.stack([cre, cim], 1))
    sh["w_glu"] = f(inp["ssm_w_glu"])
    sh["qn_g"] = f(inp["mla_q_norm"])
    sh["kvn_g"] = f(inp["mla_kv_norm"])
    wq_ = f(inp["mla_w_q_up"])
    sh["wq"] = wq_
    wq4 = wq_.reshape(L, 384, 8, 96)
    sh["wqs"] = np.ascontiguousarray(np.concatenate([wq4[..., :64], wq4[..., 80:96], wq4[..., 64:80]], -1).reshape(L, 384, 768))
    wkv4 = f(inp["mla_w_kv_up"]).reshape(L, 256, 8, 128)
    sh["wk"] = np.ascontiguousarray(wkv4[..., :64].reshape(L, 256, 512))
    sh["wv"] = np.ascontiguousarray(wkv4[..., 64:].reshape(L, 256, 512))
    return sh


def prep_core(inp, sh, c):
    f = lambda a: np.ascontiguousarray(np.asarray(a, dtype=np.float32))
    b = c % 4
    s0 = 2 * c
    m = dict(sh)
    m["xin"] = np.ascontiguousarray(np.concatenate([f(inp["x_prompt"][b]), f(inp["x_sample"][s0]), f(inp["x_sample"][s0 + 1])], 0))
    shf = f(inp["state_rwkv_shift"])[:, s0:s0 + 2, 0, :]
    m["st_shift"] = np.ascontiguousarray(np.transpose(shf.reshape(L, 2, 13, 128), (0, 1, 3, 2)))
    m["st_wkv"] = f(inp["state_rwkv_wkv"])[:, s0:s0 + 2]
    sre = f(inp["state_ssm_re"])[:, s0:s0 + 2].reshape(L, 2, 16, 128)
    sim = f(inp["state_ssm_im"])[:, s0:s0 + 2].reshape(L, 2, 16, 128)
    m["st_ssm"] = np.ascontiguousarray(np.transpose(np.stack([sre, sim], 2), (0, 1, 2, 4, 3)))
    m["c_ckv"] = f(inp["cache_mla_ckv"])[:, s0:s0 + 2]
    m["c_kpe"] = f(inp["cache_mla_kpe"])[:, s0:s0 + 2]
    m["c_sbk"] = f(inp["cache_sb_k"])[:, s0:s0 + 2].reshape(L, 2, PAST, 512)
    m["c_sbv"] = f(inp["cache_sb_v"])[:, s0:s0 + 2].reshape(L, 2, PAST, 512)
    return m


_NC = {}


def kernel(**inputs):
    if "nc" not in _NC:
        _NC["nc"] = build()
    nc = _NC["nc"]
    sh = prep_shared(inputs)
    in_maps = [prep_core(inputs, sh, c) for c in range(8)]
    res = run_bass_kernel_spmd(nc, in_maps, core_ids=list(range(8))).results
    f = np.float32
    y_prompt = np.stack([res[b]["o_y"][:TP] for b in range(4)], 0).astype(f)
    y_sample = np.stack([res[s // 2]["o_y"][TP + 16 * (s % 2):TP + 16 * (s % 2) + 16] for s in range(16)], 0).astype(f)

    def pr(name, fn):
        return np.stack([np.stack([fn(res[b][name][l]) for b in range(4)], 0) for l in range(L)], 0).astype(f)

    def sa(name, fn):
        return np.stack([np.stack([fn(res[s // 2][name][l], s % 2) for s in range(16)], 0) for l in range(L)], 0).astype(f)

    shift_p = pr("o_shift", lambda a: a[0].reshape(1, 1664))
    wkv_p = pr("o_wkv", lambda a: a[0])
    ssm_re_p = pr("o_ssm", lambda a: a[0, 0].reshape(32, 64))
    ssm_im_p = pr("o_ssm", lambda a: a[1, 0].reshape(32, 64))
    ckv_p = pr("o_ckv", lambda a: a[:TP])
    kpe_p = pr("o_kpe", lambda a: a[:TP])
    sbk_p = pr("o_sbk", lambda a: a[:TP].reshape(TP, 8, 64))
    sbv_p = pr("o_sbv", lambda a: a[:TP].reshape(TP, 8, 64))
    shift_s = sa("o_shift", lambda a, j: a[1 + j].reshape(1, 1664))
    wkv_s = sa("o_wkv", lambda a, j: a[1 + j])
    ssm_re_s = sa("o_ssm", lambda a, j: a[0, 1 + j].reshape(32, 64))
    ssm_im_s = sa("o_ssm", lambda a, j: a[1, 1 + j].reshape(32, 64))
    rows = lambda a, j: a[TP + 16 * j:TP + 16 * j + 16]
    ckv_s = sa("o_ckv", rows)
    kpe_s = sa("o_kpe", rows)
    sbk_s = sa("o_sbk", lambda a, j: rows(a, j).reshape(16, 8, 64))
    sbv_s = sa("o_sbv", lambda a, j: rows(a, j).reshape(16, 8, 64))
    return (y_prompt, y_sample, shift_p, wkv_p, ssm_re_p, ssm_im_p, ckv_p, kpe_p, sbk_p, sbv_p,
            shift_s, wkv_s, ssm_re_s, ssm_im_s, ckv_s, kpe_s, sbk_s, sbv_s)
```

```python
import math
from contextlib import ExitStack
import numpy as np
import concourse.bass as bass
import concourse.mybir as mybir
from concourse.bass_utils import run_bass_kernel_spmd

F32 = mybir.dt.float32
BF16 = mybir.dt.bfloat16
I32 = mybir.dt.int32
ALU = mybir.AluOpType
AF = mybir.ActivationFunctionType
AX = mybir.AxisListType

L = 2
D = 2048
TP = 2048
TS = 16
NTOK = TP + 2 * TS
PAST = 1024
NFM = 5248
NTM = 1728
FM_BLOCKS = [512 * i for i in range(10)] + [NFM - 512]
TB = [(0, 512), (512, 512), (1024, 512), (1536, 512), (2048, 32)]
TT = [(128 * i, 128) for i in range(16)] + [(2048, 32)]
SEGS = [(0, 2048), (2048, 16), (2064, 16)]
PI = math.pi
CP_MU, CP_W0, CP_A0, CP_KK, CP_KA, CP_RK, CP_LG, CP_LB = 0, 13, 17, 21, 25, 29, 33, 37
CP_LRE, CP_LIM, CP_LDT, CP_DSK, CP_BGLU, CP_BMG = 41, 57, 73, 89, 105, 109
NCP = 173

EPOCH = 24000
DMA_RING = {"sp": 16, "pool": 8, "act": 4}


class Buf:
    __slots__ = ("name", "w", "r")

    def __init__(self, name="b"):
        self.name = name
        self.w = None
        self.r = []


class Ev:
    __slots__ = ("eng", "seq", "sem", "val", "dma")

    def __init__(self, eng, seq, sem, val, dma):
        self.eng, self.seq, self.sem, self.val, self.dma = eng, seq, sem, val, dma


class Sched:
    def __init__(self, nc, stack):
        self.nc = nc
        self.stack = stack
        self.engobj = {"pe": nc.tensor, "dve": nc.vector, "act": nc.scalar, "pool": nc.gpsimd, "sp": nc.sync}
        self.ops = {e: [] for e in self.engobj}
        self.ccount = {e: 0 for e in self.engobj}
        self.csems = {e: [] for e in self.engobj}
        self.dcount = {q: 0 for q in DMA_RING}
        self.dsems = {q: [stack.enter_context(nc.semaphore(f"d_{q}_{i}")) for i in range(n)] for q, n in DMA_RING.items()}
        self.known = {e: {} for e in self.engobj}
        self.knownd = {e: {} for e in self.engobj}
        self.serial = False
        self.last_ev = None

    def _csem(self, eng, ep):
        while len(self.csems[eng]) <= ep:
            self.csems[eng].append(self.stack.enter_context(self.nc.semaphore(f"c_{eng}_{len(self.csems[eng])}")))
        return self.csems[eng][ep]

    def _need(self, eng, ev, waits):
        if ev.dma:
            k = self.knownd[eng]
            if k.get(id(ev.sem), -1) >= ev.val:
                return
            k[id(ev.sem)] = ev.val
        else:
            k = self.known[eng]
            if k.get(ev.eng, -1) >= ev.seq:
                return
            k[ev.eng] = ev.seq
        waits.append((ev.sem, ev.val))

    def add(self, eng, fn, reads=(), writes=(), dma=False, sig=True):
        waits = []
        for b in reads:
            ev = b.w
            if ev is not None and not (ev.eng == eng and not ev.dma and eng == "pe"):
                self._need(eng, ev, waits)
        for b in writes:
            ev = b.w
            if ev is not None and not (ev.eng == eng and not ev.dma and eng == "pe"):
                self._need(eng, ev, waits)
            for ev in b.r:
                if ev.eng == eng and not ev.dma:
                    continue
                self._need(eng, ev, waits)
        if self.serial and self.last_ev is not None:
            self._need(eng, self.last_ev, waits)
        if dma:
            q = eng
            i = self.dcount[q]
            self.dcount[q] += 1
            n = DMA_RING[q]
            sem = self.dsems[q][i % n]
            if i >= n:
                pv = 16 * (i // n)
                k = self.knownd[eng]
                if k.get(id(sem), -1) < pv:
                    k[id(sem)] = pv
                    waits.append((sem, pv))
            ev = Ev(eng, i, sem, 16 * (i // n + 1), True)
            inc = 16
        else:
            s = self.ccount[eng]
            if sig:
                self.ccount[eng] += 1
            sem = self._csem(eng, s // EPOCH)
            ev = Ev(eng, s, sem, (s % EPOCH) + 1, False)
            inc = 1 if sig else 0
        self.ops[eng].append((fn, waits, sem, inc))
        self.last_ev = ev
        for b in writes:
            b.w = ev
            b.r = []
        for b in reads:
            if b.w is not ev:
                b.r.append(ev)
        return ev

    def emit(self):
        nc = self.nc
        final = []
        for q, n in DMA_RING.items():
            c = self.dcount[q]
            for slot in range(n):
                cnt = (c - slot + n - 1) // n if c > slot else 0
                if cnt > 0:
                    final.append((self.dsems[q][slot], 16 * cnt))
        for e in self.engobj:
            c = self.ccount[e]
            if c > 0 and e != "sp":
                final.append((self.csems[e][(c - 1) // EPOCH], ((c - 1) % EPOCH) + 1))
        ops = self.ops
        with nc.Block() as block:
            def run(eobj, lst, fin=None):
                for fn, waits, sem, inc in lst:
                    for (s, v) in waits:
                        eobj.wait_ge(s, v)
                    ins = fn(eobj)
                    if inc:
                        ins.then_inc(sem, inc)
                if fin:
                    for (s, v) in fin:
                        eobj.wait_ge(s, v)

            @block.tensor
            def _(e):
                run(e, ops["pe"])

            @block.vector
            def _(e):
                run(e, ops["dve"])

            @block.scalar
            def _(e):
                run(e, ops["act"])

            @block.gpsimd
            def _(e):
                run(e, ops["pool"])

            @block.sync
            def _(e):
                run(e, ops["sp"], final)


class K:
    def __init__(self, S):
        self.S = S

    def dma(self, q, out, in_, R=(), W=(), **kw):
        self.S.add(q, lambda e: e.dma_start(out=out, in_=in_, **kw), R, W, dma=True)

    def mm(self, out, lhsT, rhs, start=True, stop=True, R=(), W=()):
        self.S.add("pe", lambda e: e.matmul(out, lhsT=lhsT, rhs=rhs, start=start, stop=stop), R, W, sig=bool(stop))

    def tr(self, out, in_, ident, R=(), W=()):
        self.S.add("pe", lambda e: e.transpose(out=out, in_=in_, identity=ident), R, W)

    def act(self, out, in_, func, bias=None, scale=None, R=(), W=()):
        kw = {}
        if bias is not None:
            kw["bias"] = bias
        if scale is not None:
            kw["scale"] = scale
        self.S.add("act", lambda e: e.activation(out=out, in_=in_, func=func, **kw), R, W)

    def tt(self, eng, out, in0, in1, op, R=(), W=()):
        self.S.add(eng, lambda e: e.tensor_tensor(out=out, in0=in0, in1=in1, op=op), R, W)

    def ts(self, eng, out, in0, s1, s2=None, op0=ALU.mult, op1=None, R=(), W=()):
        if op1 is None:
            self.S.add(eng, lambda e: e.tensor_scalar(out=out, in0=in0, scalar1=s1, scalar2=None, op0=op0), R, W)
        else:
            self.S.add(eng, lambda e: e.tensor_scalar(out=out, in0=in0, scalar1=s1, scalar2=s2, op0=op0, op1=op1), R, W)

    def stt(self, eng, out, in0, scalar, in1, op0, op1, R=(), W=()):
        self.S.add(eng, lambda e: e.scalar_tensor_tensor(out=out, in0=in0, scalar=scalar, in1=in1, op0=op0, op1=op1), R, W)

    def cp(self, eng, out, in_, R=(), W=()):
        if eng == "act":
            self.S.add("act", lambda e: e.activation(out=out, in_=in_, func=AF.Copy), R, W)
        else:
            self.S.add(eng, lambda e: e.tensor_copy(out=out, in_=in_), R, W)

    def memset(self, eng, out, val, R=(), W=()):
        self.S.add(eng, lambda e: e.memset(out, val), R, W)

    def red(self, out, in_, R=(), W=()):
        self.S.add("dve", lambda e: e.reduce_sum(out=out, in_=in_, axis=AX.X), R, W)

    def recip(self, out, in_, R=(), W=()):
        self.S.add("dve", lambda e: e.reciprocal(out=out, in_=in_), R, W)

    def scan(self, out, d0, d1, init, R=(), W=()):
        self.S.add("dve", lambda e: e.tensor_tensor_scan(out=out, data0=d0, data1=d1, initial=init, op0=ALU.mult, op1=ALU.add), R, W)

    def iota(self, out, pattern, base, cm, R=(), W=()):
        self.S.add("pool", lambda e: e.iota(out, pattern=pattern, base=base, channel_multiplier=cm, allow_small_or_imprecise_dtypes=True), R, W)

    def asel(self, out, in_, pattern, op, fill, base, cm, R=(), W=()):
        self.S.add("pool", lambda e: e.affine_select(out=out, in_=in_, pattern=pattern, compare_op=op, fill=fill, base=base, channel_multiplier=cm), R, W)

    def barrier(self, olds, news, scratch):
        self.S.add("dve", lambda e: e.memset(scratch, 0.0), (), list(olds) + list(news))


class Region:
    def __init__(self, t, n32):
        self.t = t
        self.n32 = n32
        self.off = 0

    def reset(self):
        self.off = 0

    def take(self, shape, dt=F32, parts=128):
        n = 1
        for s in shape:
            n *= s
        n32 = n if dt != BF16 else (n + 1) // 2
        a = self.t[0:parts, self.off:self.off + n32]
        self.off += n32
        assert self.off <= self.n32, (self.off, self.n32)
        if dt == BF16:
            a = a.bitcast(BF16)
        elif dt == I32:
            a = a.bitcast(I32)
        if len(shape) == 1:
            return a
        names = "abcdefg"[:len(shape)]
        kw = {names[i]: shape[i] for i in range(len(shape) - 1)}
        return a.rearrange("p (" + " ".join(names) + ") -> p " + " ".join(names), **kw)


def build(upto="all", dbg=(), SKIP=()):
    nc = bass.Bass("TRN2", target_bir_lowering=False)
    dbg = set(dbg)

    def din(name, shape, dt=F32):
        return nc.dram_tensor(name, list(shape), dt, kind="ExternalInput").ap()

    def dout(name, shape, dt=F32):
        return nc.dram_tensor(name, list(shape), dt, kind="ExternalOutput").ap()

    def dscr(name, shape, dt=F32):
        kind = "ExternalOutput" if name in dbg else "Internal"
        return nc.dram_tensor(name, list(shape), dt, kind=kind).ap()

    def dbgdump(k, name, ap, R):
        if name in dbg:
            t = nc.dram_tensor(name, list(ap.shape), ap.dtype, kind="ExternalOutput").ap()
            k.dma("sp", t, ap, R=R)

    xin = din("xin", [NTOK, D])
    w_fm = din("w_fm", [L, 11, 128, 8192])
    w_tm = din("w_tm", [L, 128, 16 * NTM])
    w_mg = din("w_mg", [L, 8, 4, 128, 4096])
    w_br = din("w_br", [L, 8, 4, 128, 1024])
    w_out = din("w_out", [L, 4, 128, 8192])
    cpar = din("cpar", [L, 128, NCP])
    norm_g = din("norm_g", [L, D])
    fin_g = din("fin_g", [1, D])
    rw_w2 = din("rw_w2", [L, 64, 512])
    rw_a2 = din("rw_a2", [L, 64, 512])
    ssm_b = din("ssm_b", [L, 2, 2048, 16])
    ssm_c = din("ssm_c", [L, 2, 2048, 16])
    w_glu = din("w_glu", [L, 512, 512])
    qn_g = din("qn_g", [L, 384])
    kvn_g = din("kvn_g", [L, 256])
    wq = din("wq", [L, 384, 768])
    wqs = din("wqs", [L, 384, 768])
    wk = din("wk", [L, 256, 512])
    wv = din("wv", [L, 256, 512])
    st_shift = din("st_shift", [L, 2, 128, 13])
    st_wkv = din("st_wkv", [L, 2, 8, 64, 64])
    st_ssm = din("st_ssm", [L, 2, 2, 128, 16])
    c_ckv = din("c_ckv", [L, 2, PAST, 256])
    c_kpe = din("c_kpe", [L, 2, PAST, 32])
    c_sbk = din("c_sbk", [L, 2, PAST, 512])
    c_sbv = din("c_sbv", [L, 2, PAST, 512])

    o_y = dout("o_y", [NTOK, D])
    o_shift = dout("o_shift", [L, 3, 1664])
    o_wkv = dout("o_wkv", [L, 3, 8, 64, 64])
    o_ssm = dout("o_ssm", [L, 2, 3, 2048])
    o_ckv = dout("o_ckv", [L, NTOK, 256])
    o_kpe = dout("o_kpe", [L, NTOK, 32])
    o_sbk = dout("o_sbk", [L, NTOK, 512])
    o_sbv = dout("o_sbv", [L, NTOK, 512])

    X1 = dscr("X1", [NTOK, D])
    X2 = dscr("X2", [NTOK, D])
    Xs = [xin, X1, X2]
    PF = dscr("PF", [NFM, NTOK])
    HTS = dscr("HTS", [128, 16 * NTOK], BF16)
    QNS = dscr("QNS", [128, 3 * NTOK], BF16)
    YS = dscr("YS", [512, NTOK])
    ACC = dscr("ACC", [D, NTOK], BF16)
    GDBG = dscr("GDBG", [128, 16 * NTOK], BF16)
    QF = dscr("QF", [96, 8 * NTOK], BF16).rearrange("p (h t) -> p h t", h=8)
    KF = dscr("KF", [96, 8 * NTOK], BF16).rearrange("p (h t) -> p h t", h=8)
    KFP = dscr("KFP", [96, 16 * PAST], BF16).rearrange("p (h j t) -> p h j t", h=8, j=2)
    VT = dscr("VT", [NTOK, 512], BF16)
    VTP = dscr("VTP", [2 * PAST, 512], BF16)
    RWS = dscr("RWS", [8 * 512, NTOK])
    YRW = dscr("YRW", [512, NTOK])

    with ExitStack() as st:
        S = Sched(nc, st)
        k = K(S)
        sbt = lambda name, shape, dt: st.enter_context(nc.sbuf_tensor(name, shape, dt))
        pst = lambda name, shape, dt: st.enter_context(nc.psum_tensor(name, shape, dt))
        NBIG = 16 * NTOK // 2
        R1t = sbt("R1", [128, NBIG], F32)
        R2t = sbt("R2", [128, NBIG], F32)
        Wt = sbt("Wr", [128, 8192], F32)
        Mt = sbt("Mr", [128, 9216], F32)
        bR1, bR2 = Buf("R1"), Buf("R2")
        ps = [pst(f"ps{i}", [128, 512], F32) for i in range(8)]
        bps = [Buf(f"ps{i}") for i in range(8)]

        MR = Region(Mt, 9216)
        identf = MR.take([128], F32); b_c = Buf("const")
        identb = MR.take([128], BF16)
        bones = MR.take([128], F32)
        dummy = MR.take([8], F32)
        bdum = Buf("dummy")
        MISC_BASE = MR.off

        k.memset("pool", identf, 0.0, W=[b_c])
        k.asel(identf, identf, [[-1, 128]], ALU.not_equal, 1.0, 0, 1, R=[b_c], W=[b_c])
        k.cp("dve", identb, identf, R=[b_c], W=[b_c])
        k.memset("pool", bones, 0.0, W=[b_c])
        k.memset("pool", bones[0:64, 0:64], 1.0, W=[b_c])
        k.memset("pool", bones[64:128, 64:128], 1.0, W=[b_c])

        def sincos(ang, shape, ws, bsc, want_cos=True):
            P = ang.shape[0]
            u = ws.take(shape, F32, P) if False else None
            t_u = ws.take(shape)[0:P]
            t_i = ws.take(shape, I32)[0:P]
            t_r = ws.take(shape)[0:P]
            t_m = ws.take(shape)[0:P]
            t_s = ws.take(shape)[0:P]
            k.ts("dve", t_u, ang, 1.0 / (2 * PI), R=[bsc], W=[bsc])
            k.cp("dve", t_i, t_u, R=[bsc], W=[bsc])
            k.cp("dve", t_u, t_i, R=[bsc], W=[bsc])
            k.stt("dve", t_r, t_u, -2 * PI, ang, ALU.mult, ALU.add, R=[bsc], W=[bsc])
            k.ts("dve", t_m, t_r, PI, -2 * PI, ALU.is_gt, ALU.mult, R=[bsc], W=[bsc])
            k.tt("dve", t_r, t_r, t_m, ALU.add, R=[bsc], W=[bsc])
            k.ts("dve", t_m, t_r, -PI, 2 * PI, ALU.is_lt, ALU.mult, R=[bsc], W=[bsc])
            k.tt("dve", t_r, t_r, t_m, ALU.add, R=[bsc], W=[bsc])
            k.act(t_s, t_r, AF.Sin, R=[bsc], W=[bsc])
            t_c = None
            if want_cos:
                t_c = ws.take(shape)[0:P]
                k.ts("dve", t_u, t_r, PI / 2, None, ALU.add, R=[bsc], W=[bsc])
                k.ts("dve", t_m, t_u, PI, -2 * PI, ALU.is_gt, ALU.mult, R=[bsc], W=[bsc])
                k.tt("dve", t_u, t_u, t_m, ALU.add, R=[bsc], W=[bsc])
                k.act(t_c, t_u, AF.Sin, R=[bsc], W=[bsc])
            return t_s, t_c

        RC = MR.take([17, 32]); RS = MR.take([17, 32]); b_rope = Buf("rope")
        gq = MR.take([384]); gkv = MR.take([256]); b_gq = Buf("gq")
        cpt = MR.take([NCP]); b_cp = Buf("cp")
        MISC_BASE = MR.off
        ws0 = Region(R1t, NBIG)
        b_s0 = Buf("s0")
        posT = ws0.take([17]); inv16 = ws0.take([16]); p16 = ws0.take([4]); angT = ws0.take([17, 16])
        k.iota(posT, [[128, 17]], 0, 1, W=[b_s0])
        k.iota(p16[:, 0:1], [[0, 1]], 0, 1, R=[b_s0], W=[b_s0])
        k.ts("dve", p16[:, 1:2], p16[:, 0:1], 16.0, -16.0, ALU.is_ge, ALU.mult, R=[b_s0], W=[b_s0])
        k.stt("dve", posT[:, 16:17], p16[:, 0:1], 1024.0, p16[:, 1:2], ALU.add, ALU.add, R=[b_s0], W=[b_s0])
        k.iota(inv16, [[1, 16]], 0, 0, R=[b_s0], W=[b_s0])
        k.act(inv16, inv16, AF.Exp, scale=-math.log(10000.0) / 16.0, R=[b_s0], W=[b_s0])
        k.tt("dve", angT, posT.unsqueeze(2).to_broadcast([128, 17, 16]), inv16.unsqueeze(1).to_broadcast([128, 17, 16]), ALU.mult, R=[b_s0], W=[b_s0])
        sT, cT = sincos(angT, [17, 16], ws0, b_s0)
        k.cp("dve", RC[:, :, 0:16], cT, R=[b_s0], W=[b_rope])
        k.cp("dve", RC[:, :, 16:32], cT, R=[b_s0], W=[b_rope])
        k.ts("dve", RS[:, :, 0:16], sT, -1.0, R=[b_s0], W=[b_rope])
        k.cp("dve", RS[:, :, 16:32], sT, R=[b_s0], W=[b_rope])
        k.barrier([b_s0], [bR1], dummy[0:1, 0:1])

        def wblock(buf_i):
            return Wt[:, 4096 * buf_i:4096 * (buf_i + 1)].bitcast(BF16).rearrange("p (a b) -> p a b", a=16)

        bW = [Buf("W0"), Buf("W1")]
        wcount = [0]

        stgc = [0]

        def load_wblock(src2d, ncols, stgs, bstgs):
            i = wcount[0] % 2
            wcount[0] += 1
            wb = wblock(i)
            for kq in range(4):
                si = stgc[0] % len(stgs)
                stgc[0] += 1
                sg_, bsg_ = stgs[si], bstgs[si]
                k.dma("sp", sg_, src2d[:, 4 * ncols * kq:4 * ncols * (kq + 1)].rearrange("p (kt c) -> p kt c", kt=4), W=[bsg_])
                k.cp("pool", wb[:, 4 * kq:4 * kq + 4, 0:ncols], sg_, R=[bsg_], W=[bW[i]])
            return wb, bW[i]

        bPF, bYS, bHTS, bQNS, bACC, bX1, bX2, bOck, bOkp, bOsk, bOsv, bSCR = (Buf(n) for n in ("PF", "YS", "HTS", "QNS", "ACC", "X1", "X2", "Ock", "Okp", "Osk", "Osv", "SCR"))
        bXs = [Buf("xin"), bX1, bX2]
        DRB = [bPF, bYS, bHTS, bQNS, bACC, bX1, bX2, bOck, bOkp, bOsk, bOsv, bSCR]

        def fence(*bs):
            k.barrier(list(bs), list(bs), dummy[0:1, 0:1])

        psrot = [0]

        def next_ps(lo=0, hi=4):
            i = lo + psrot[0] % (hi - lo)
            psrot[0] += 1
            return ps[i], bps[i]

        evrot = [0]

        def ev_eng():
            evrot[0] += 1
            return "act" if evrot[0] % 2 else "dve"

        for l in range(L):
            RA, RB = (R1t, R2t) if l % 2 == 0 else (R2t, R1t)
            bRA, bRB = (bR1, bR2) if l % 2 == 0 else (bR2, bR1)
            hT = RA[:, :].bitcast(BF16).rearrange("p (a b) -> p a b", a=16)
            b_hT = Buf("hT")
            k.barrier([bRA, bRB], [b_hT], dummy[0:1, 0:1])
            fence(*DRB)
            X = Xs[l]
            WS = Region(RB, NBIG)
            xt = [WS.take([D]), WS.take([D])]
            bxt = [Buf("xt0"), Buf("xt1")]
            sq = WS.take([D]); b_sq = Buf("sq")
            gt = WS.take([D]); b_gt = Buf("gt")
            hb = WS.take([D], BF16); b_hb = Buf("hb")
            ss = WS.take([8]); b_ss = Buf("ss")
            k.barrier([bRB], [bxt[0], bxt[1], b_sq, b_gt, b_hb, b_ss], dummy[0:1, 0:1])
            k.dma("sp", gt, norm_g[l:l + 1, :].partition_broadcast(128), W=[b_gt])
            for ti, (r0, nr) in enumerate(TT):
                xb, bx = xt[ti % 2], bxt[ti % 2]
                k.dma("sp", xb[0:nr, :], X[r0:r0 + nr, :], R=[bXs[l]], W=[bx])
                k.act(sq[0:nr, :], xb[0:nr, :], AF.Square, R=[bx], W=[b_sq])
                k.red(ss[0:nr, 0:1], sq[0:nr, :], R=[b_sq], W=[b_ss])
                k.ts("dve", ss[0:nr, 1:2], ss[0:nr, 0:1], 1.0 / D, 1e-6, ALU.mult, ALU.add, R=[b_ss], W=[b_ss])
                k.act(ss[0:nr, 2:3], ss[0:nr, 1:2], AF.Sqrt, R=[b_ss], W=[b_ss])
                k.recip(ss[0:nr, 3:4], ss[0:nr, 2:3], R=[b_ss], W=[b_ss])
                k.stt("dve", hb[0:nr, :], xb[0:nr, :], ss[0:nr, 3:4], gt[0:nr, :], ALU.mult, ALU.mult, R=[bx, b_ss, b_gt], W=[b_hb])
                for half in range(2):
                    pt, bpt = next_ps(4, 8)
                    ptv = pt[:, :].bitcast(BF16).rearrange("p (a b) -> p a b", a=8)
                    for j in range(8):
                        jj = half * 8 + j
                        k.tr(ptv[:, j, 0:nr], hb[0:nr, 128 * jj:128 * jj + 128], identb[0:nr, 0:nr], R=[b_hb, b_c], W=[bpt])
                    k.cp(ev_eng(), hT[:, 8 * half:8 * half + 8, r0:r0 + nr], ptv[:, :, 0:nr], R=[bpt], W=[b_hT])
            k.dma("sp", HTS, RA[:, :].bitcast(BF16), R=[b_hT, bHTS])
            if upto == "A":
                break

            WS = Region(RB, NBIG)
            stg = [WS.take([NTOK]), WS.take([NTOK])]
            bstg = [Buf("stg0"), Buf("stg1")]
            wst = [WS.take([4, 512]) for _ in range(3)]
            bwst = [Buf(f"wst{i}") for i in range(3)]
            k.barrier([bxt[0], bxt[1], b_sq, b_gt, b_hb, b_ss], bstg + bwst, dummy[0:1, 0:1])
            tiles = []
            for i in range(17):
                tiles.append((128 * i, 128, False))
            for i in range(16):
                tiles.append((2176 + 64 * i, 64, False))
            for i in range(16):
                tiles.append((3200 + 128 * i, 128, True))
            sc = 0
            nxtw = load_wblock(w_fm[l][0], 512, wst, bwst)
            for bi_, c0 in enumerate(FM_BLOCKS):
                ncol = 512
                wb, bw = nxtw
                if bi_ + 1 < len(FM_BLOCKS):
                    nxtw = load_wblock(w_fm[l][bi_ + 1], 512, wst, bwst)
                lo_ = 512 * bi_ if bi_ < 10 else 5120
                for (tc0, M, isg) in tiles:
                    if not (lo_ <= tc0 < c0 + ncol):
                        continue
                    sg, bsg = stg[sc % 2], bstg[sc % 2]
                    sc += 1
                    for (t0, nt) in TB:
                        pp, bp = next_ps(0, 4)
                        for kt in range(16):
                            k.mm(pp[0:M, 0:nt], wb[:, kt, tc0 - c0:tc0 - c0 + M], hT[:, kt, t0:t0 + nt], start=(kt == 0), stop=(kt == 15), R=[bw, b_hT], W=[bp])
                        if isg:
                            k.act(sg[0:M, t0:t0 + nt], pp[0:M, 0:nt], AF.Silu, R=[bp], W=[bsg])
                        else:
                            k.cp(ev_eng(), sg[0:M, t0:t0 + nt], pp[0:M, 0:nt], R=[bp], W=[bsg])
                    k.dma("sp", PF[tc0:tc0 + M, :], sg[0:M, :], R=[bsg, bPF])
            if upto == "B1a":
                break

            WS = Region(RB, NBIG)
            wtm = WS.take([16, NTM], BF16); b_wtm = Buf("wtm")
            qa = WS.take([448]); b_qa = Buf("qa")
            sq2 = WS.take([384]); b_sq2 = Buf("sq2")
            qnb = WS.take([384], BF16); b_qnb = Buf("qnb")
            kvt = WS.take([256]); b_kvt = Buf("kvt")
            ckvt = WS.take([256]); b_ckvt = Buf("ckvt")
            MW = Region(Mt, 9216); MW.off = MISC_BASE
            krt = MW.take([64]); b_krt = Buf("krt")
            sbkt = WS.take([512]); b_sbkt = Buf("sbkt")
            sbvt = WS.take([512]); b_sbvt = Buf("sbvt")
            qnT = WS.take([3, 128], BF16); b_qnT = Buf("qnT")
            ss2 = MW.take([8]); b_ss2 = Buf("ss2")
            WW = Region(Wt, 8192)
            tst = [WW.take([NTM]) for _ in range(4)]
            btst = [Buf(f"tst{i}") for i in range(4)]
            k.barrier(bstg + bwst + [bW[0], bW[1]], [b_wtm, b_qa, b_sq2, b_qnb, b_kvt, b_ckvt, b_krt, b_sbkt, b_sbvt, b_qnT, b_ss2] + btst, dummy[0:1, 0:1])
            for kt in range(16):
                k.dma("sp", tst[kt % 4], w_tm[l][:, NTM * kt:NTM * (kt + 1)], W=[btst[kt % 4]])
                k.cp("pool", wtm[:, kt, :], tst[kt % 4], R=[btst[kt % 4]], W=[b_wtm])
            k.dma("sp", gq, qn_g[l:l + 1, :].partition_broadcast(128), W=[b_gq])
            k.dma("sp", gkv, kvn_g[l:l + 1, :].partition_broadcast(128), W=[b_gq])
            QNSv = QNS.rearrange("p (a t) -> p a t", a=3)
            banks = [(0, 448), (448, 256), (704, 512), (1216, 512)]
            for ti, (r0, nr) in enumerate(TT):
                for bi, (c0, ncol) in enumerate(banks):
                    for kt in range(16):
                        k.mm(ps[bi][0:nr, 0:ncol], hT[:, kt, r0:r0 + nr], wtm[:, kt, c0:c0 + ncol], start=(kt == 0), stop=(kt == 15), R=[b_hT, b_wtm], W=[bps[bi]])
                k.cp("act", qa[0:nr, :], ps[0][0:nr, 0:448], R=[bps[0]], W=[b_qa])
                k.act(sq2[0:nr, :], qa[0:nr, 0:384], AF.Square, R=[b_qa], W=[b_sq2])
                k.red(ss2[0:nr, 0:1], sq2[0:nr, :], R=[b_sq2], W=[b_ss2])
                k.ts("dve", ss2[0:nr, 1:2], ss2[0:nr, 0:1], 1.0 / 384, 1e-6, ALU.mult, ALU.add, R=[b_ss2], W=[b_ss2])
                k.act(ss2[0:nr, 2:3], ss2[0:nr, 1:2], AF.Sqrt, R=[b_ss2], W=[b_ss2])
                k.recip(ss2[0:nr, 3:4], ss2[0:nr, 2:3], R=[b_ss2], W=[b_ss2])
                k.stt("dve", qnb[0:nr, :], qa[0:nr, 0:384], ss2[0:nr, 3:4], gq[0:nr, :], ALU.mult, ALU.mult, R=[b_qa, b_ss2, b_gq], W=[b_qnb])
                pt, bpt = next_ps(4, 8)
                ptv = pt[:, :].bitcast(BF16).rearrange("p (a b) -> p a b", a=8)
                for j in range(3):
                    k.tr(ptv[:, j, 0:nr], qnb[0:nr, 128 * j:128 * j + 128], identb[0:nr, 0:nr], R=[b_qnb, b_c], W=[bpt])
                k.cp("dve", qnT[:, :, 0:nr], ptv[:, 0:3, 0:nr], R=[bpt], W=[b_qnT])
                k.dma("sp", QNSv[:, :, r0:r0 + nr], qnT[:, :, 0:nr], R=[b_qnT, bQNS])
                k.tt("dve", krt[0:nr, 0:32], qa[0:nr, 384:416], RC[0:nr, ti, :], ALU.mult, R=[b_qa, b_rope], W=[b_krt])
                k.tt("dve", krt[0:nr, 32:64], qa[0:nr, 416:448], RS[0:nr, ti, :], ALU.mult, R=[b_qa, b_rope], W=[b_krt])
                k.tt("dve", krt[0:nr, 0:32], krt[0:nr, 0:32], krt[0:nr, 32:64], ALU.add, R=[b_krt], W=[b_krt])
                k.dma("sp", o_kpe[l][r0:r0 + nr, :], krt[0:nr, 0:32], R=[b_krt, bOkp])
                k.cp("act", kvt[0:nr, :], ps[1][0:nr, 0:256], R=[bps[1]], W=[b_kvt])
                k.act(sq2[0:nr, 0:256], kvt[0:nr, :], AF.Square, R=[b_kvt], W=[b_sq2])
                k.red(ss2[0:nr, 4:5], sq2[0:nr, 0:256], R=[b_sq2], W=[b_ss2])
                k.ts("dve", ss2[0:nr, 5:6], ss2[0:nr, 4:5], 1.0 / 256, 1e-6, ALU.mult, ALU.add, R=[b_ss2], W=[b_ss2])
                k.act(ss2[0:nr, 6:7], ss2[0:nr, 5:6], AF.Sqrt, R=[b_ss2], W=[b_ss2])
                k.recip(ss2[0:nr, 7:8], ss2[0:nr, 6:7], R=[b_ss2], W=[b_ss2])
                k.stt("dve", ckvt[0:nr, :], kvt[0:nr, :], ss2[0:nr, 7:8], gkv[0:nr, :], ALU.mult, ALU.mult, R=[b_kvt, b_ss2, b_gq], W=[b_ckvt])
                k.dma("sp", o_ckv[l][r0:r0 + nr, :], ckvt[0:nr, :], R=[b_ckvt, bOck])
                k.cp("act", sbkt[0:nr, :], ps[2][0:nr, :], R=[bps[2]], W=[b_sbkt])
                k.dma("sp", o_sbk[l][r0:r0 + nr, :], sbkt[0:nr, :], R=[b_sbkt, bOsk])
                k.cp("dve", sbvt[0:nr, :], ps[3][0:nr, :], R=[bps[3]], W=[b_sbvt])
                k.dma("sp", o_sbv[l][r0:r0 + nr, :], sbvt[0:nr, :], R=[b_sbvt, bOsv])
            if upto == "B1b":
                break

            gT = RB[:, :].bitcast(BF16).rearrange("p (a b) -> p a b", a=16)
            b_gT = Buf("gT")
            b_ra = Buf("ra0")
            k.barrier([b_hT, b_wtm, b_qa, b_sq2, b_qnb, b_kvt, b_ckvt, b_krt, b_sbkt, b_sbvt, b_qnT, b_ss2] + btst, [b_gT, b_ra, b_cp, bW[0], bW[1]], dummy[0:1, 0:1])
            k.dma("sp", cpt, cpar[l], W=[b_cp])
            fence(bPF, bQNS, bHTS, bOck, bOkp, bOsk, bOsv)
            if "C1" not in SKIP:
                S.serial = "SER" in SKIP
                WS = Region(RA, NBIG)
                MW = Region(Mt, 9216); MW.off = MISC_BASE
                WW = Region(Wt, 8192)
                tau = WS.take([NTOK]); cosT = WS.take([NTOK]); sinT = WS.take([NTOK])
                bA = WS.take([NTOK]); bB = WS.take([NTOK]); bC = WS.take([NTOK]); bD = WS.take([NTOK])
                u32 = WS.take([NTOK])
                b_tau, b_cos, b_sin, b_A, b_B, b_C, b_D, b_u32 = (Buf(n) for n in ("tau", "cos", "sin", "A", "B", "C", "D", "u32"))
                ti = WW.take([NTOK], I32); tmp1 = WW.take([NTOK]); tmp2 = WW.take([NTOK])
                b_ti, b_t1, b_t2 = Buf("ti"), Buf("t1"), Buf("t2")
                BBTre = MW.take([16, 128]); BBTim = MW.take([16, 128]); b_bbt = Buf("bbt")
                CBDre = MW.take([16, 32]); CBDim = MW.take([16, 32]); b_cbd = Buf("cbd")
                sp_ = MW.take([16, 16]); b_sp = Buf("sp")
                HF = MW.take([16, 6]); b_hf = Buf("hf")
                halfpi = MW.take([2])
                h0t = MW.take([4, 16]); b_h0 = Buf("h0")
                k.barrier([b_ra], [b_tau, b_cos, b_sin, b_A, b_B, b_C, b_D, b_u32, b_ti, b_t1, b_t2, b_bbt, b_cbd, b_sp, b_hf, b_h0], dummy[0:1, 0:1])
                P_ = lambda i: sp_[:, i, :]
                lre, lim, ldt = cpt[:, CP_LRE:CP_LRE + 16], cpt[:, CP_LIM:CP_LIM + 16], cpt[:, CP_LDT:CP_LDT + 16]
                dtt, mag, ang, lbre, lbim, fre, fim, th2 = P_(0), P_(1), P_(2), P_(3), P_(4), P_(5), P_(6), P_(7)
                q1, q2, q3, rden = P_(8), P_(9), P_(10), P_(11)
                k.memset("dve", halfpi, PI / 2, W=[b_sp])
                k.act(dtt, ldt, AF.Exp, R=[b_cp], W=[b_sp])
                k.tt("dve", q1, lre, dtt, ALU.mult, R=[b_cp, b_sp], W=[b_sp])
                k.act(mag, q1, AF.Exp, R=[b_sp], W=[b_sp])
                k.tt("dve", ang, lim, dtt, ALU.mult, R=[b_cp, b_sp], W=[b_sp])
                k.ts("dve", th2, ang, 1.0 / (2 * PI), R=[b_sp], W=[b_sp])
                wsA = Region(RA, NBIG); wsA.off = 3 * NTOK
                sA, cA = sincos(ang, [16], wsA, b_A)
                k.tt("dve", lbre, mag, cA, ALU.mult, R=[b_sp, b_A], W=[b_sp])
                k.tt("dve", lbim, mag, sA, ALU.mult, R=[b_sp, b_A], W=[b_sp])
                k.ts("dve", q1, lbre, -1.0, None, ALU.add, R=[b_sp], W=[b_sp])
                k.tt("dve", q2, lre, lre, ALU.mult, R=[b_cp], W=[b_sp])
                k.tt("dve", q3, lim, lim, ALU.mult, R=[b_cp], W=[b_sp])
                k.tt("dve", q2, q2, q3, ALU.add, R=[b_sp], W=[b_sp])
                k.recip(rden, q2, R=[b_sp], W=[b_sp])
                k.tt("dve", q2, q1, lre, ALU.mult, R=[b_sp, b_cp], W=[b_sp])
                k.tt("dve", q3, lbim, lim, ALU.mult, R=[b_sp, b_cp], W=[b_sp])
                k.tt("dve", q2, q2, q3, ALU.add, R=[b_sp], W=[b_sp])
                k.tt("dve", fre, q2, rden, ALU.mult, R=[b_sp], W=[b_sp])
                k.tt("dve", q2, lbim, lre, ALU.mult, R=[b_sp, b_cp], W=[b_sp])
                k.tt("dve", q3, q1, lim, ALU.mult, R=[b_sp, b_cp], W=[b_sp])
                k.tt("dve", q2, q2, q3, ALU.subtract, R=[b_sp], W=[b_sp])
                k.tt("dve", fim, q2, rden, ALU.mult, R=[b_sp], W=[b_sp])
                wsD = Region(RA, NBIG); wsD.off = 6 * NTOK
                bre = wsD.take([16, 16]); bim = wsD.take([16, 16]); cre = wsD.take([16, 16]); cim = wsD.take([16, 16])
                bbre = wsD.take([16, 16]); bbim = wsD.take([16, 16]); tq = wsD.take([16, 16])
                wsC = Region(RA, NBIG); wsC.off = 5 * NTOK
                BBDre = wsC.take([16, 32]); BBDim = wsC.take([16, 32])
                bsrc = lambda a: a.rearrange("(st p) c -> p st c", p=128)
                k.dma("sp", bre, bsrc(ssm_b[l][0]), W=[b_D])
                k.dma("sp", bim, bsrc(ssm_b[l][1]), W=[b_D])
                k.dma("sp", cre, bsrc(ssm_c[l][0]), W=[b_D])
                k.dma("sp", cim, bsrc(ssm_c[l][1]), W=[b_D])
                bc = lambda a: a.unsqueeze(2).to_broadcast([128, 16, 16])
                k.tt("dve", bbre, bre, bc(fre), ALU.mult, R=[b_D, b_sp], W=[b_D])
                k.tt("dve", tq, bim, bc(fim), ALU.mult, R=[b_D, b_sp], W=[b_D])
                k.tt("dve", bbre, bbre, tq, ALU.subtract, R=[b_D], W=[b_D])
                k.tt("dve", bbim, bim, bc(fre), ALU.mult, R=[b_D, b_sp], W=[b_D])
                k.tt("dve", tq, bre, bc(fim), ALU.mult, R=[b_D, b_sp], W=[b_D])
                k.tt("dve", bbim, bbim, tq, ALU.add, R=[b_D], W=[b_D])
                for dst, srcm, sc_ in ((BBDre, bbre, 1.0), (BBDim, bbim, 1.0), (CBDre, cre, 1.0), (CBDim, cim, -1.0)):
                    wb_ = [b_cbd] if dst is CBDre or dst is CBDim else [b_C]
                    k.memset("dve", dst, 0.0, R=[b_D], W=wb_)
                    k.ts("dve", dst[0:64, :, 0:16], srcm[0:64, :, :], sc_, R=[b_D], W=wb_)
                    k.ts("dve", dst[64:128, :, 16:32], srcm[64:128, :, :], sc_, R=[b_D], W=wb_)
                for (BBD, BBT) in ((BBDre, BBTre), (BBDim, BBTim)):
                    for q4 in range(4):
                        pp, bp = next_ps(4, 8)
                        ppv = pp[:, :].rearrange("p (a b) -> p a b", a=4)
                        for j in range(4):
                            k.tr(ppv[0:32, j, :], BBD[:, 4 * q4 + j, :], identf, R=[b_C, b_c], W=[bp])
                        k.cp(ev_eng(), BBT[0:32, 4 * q4:4 * q4 + 4, :], ppv[0:32, :, :], R=[bp], W=[b_bbt])
                k.iota(tau[:, 0:TP], [[1, TP]], 1, 0, W=[b_tau])
                k.iota(tau[:, TP:TP + 16], [[1, 16]], 1, 0, W=[b_tau])
                k.iota(tau[:, TP + 16:TP + 32], [[1, 16]], 1, 0, W=[b_tau])
                for sq_ in range(2):
                    for ri in range(2):
                        k.dma("sp", h0t[:, 2 * sq_ + ri, :], st_ssm[l][sq_][ri], W=[b_h0])
                dsk = cpt[0:32, CP_DSK:CP_DSK + 16]
                for st_ in range(16):
                    th = ang[:, st_:st_ + 1]
                    k.act(cosT, tau, AF.Copy, scale=th, R=[b_tau, b_sp], W=[b_cos])
                    k.act(sinT, tau, AF.Copy, scale=th2[:, st_:st_ + 1], R=[b_tau, b_sp], W=[b_sin])
                    k.cp("act", ti, sinT, R=[b_sin], W=[b_ti])
                    k.cp("act", sinT, ti, R=[b_ti], W=[b_sin])
                    k.stt("dve", cosT, sinT, -2 * PI, cosT, ALU.mult, ALU.add, R=[b_sin, b_cos], W=[b_cos])
                    k.ts("dve", sinT, cosT, PI, -2 * PI, ALU.is_gt, ALU.mult, R=[b_cos], W=[b_sin])
                    k.tt("dve", cosT, cosT, sinT, ALU.add, R=[b_cos, b_sin], W=[b_cos])
                    k.act(sinT, cosT, AF.Sin, R=[b_cos], W=[b_sin])
                    k.act(tmp1, cosT, AF.Abs, R=[b_cos], W=[b_t1])
                    k.act(cosT, tmp1, AF.Sin, bias=halfpi[:, 0:1], scale=-1.0, R=[b_t1, b_sp], W=[b_cos])
                    if st_ == 0:
                        dbgdump(k, "d_cos", cosT, [b_cos]); dbgdump(k, "d_sin", sinT, [b_sin]); dbgdump(k, "d_sp", sp_, [b_sp])
                        dbgdump(k, "d_bbt", BBTre[0:32], [b_bbt]); dbgdump(k, "d_cbd", CBDre, [b_cbd])
                    k.dma("sp", u32[0:32, :], PF[1664 + 32 * st_:1664 + 32 * st_ + 32, :], R=[bPF], W=[b_u32])
                    for (t0, nt) in TB:
                        pr, bpr = next_ps(0, 4)
                        pi_, bpi = next_ps(0, 4)
                        k.mm(pr[:, 0:nt], BBTre[0:32, st_, :], u32[0:32, t0:t0 + nt], R=[b_bbt, b_u32], W=[bpr])
                        k.mm(pi_[:, 0:nt], BBTim[0:32, st_, :], u32[0:32, t0:t0 + nt], R=[b_bbt, b_u32], W=[bpi])
                        sl = slice(t0, t0 + nt)
                        k.tt("dve", bC[:, sl], pr[:, 0:nt], cosT[:, sl], ALU.mult, R=[bpr, b_cos], W=[b_C])
                        k.tt("dve", bD[:, sl], pi_[:, 0:nt], sinT[:, sl], ALU.mult, R=[bpi, b_sin], W=[b_D])
                        k.tt("dve", bA[:, sl], bC[:, sl], bD[:, sl], ALU.add, R=[b_C, b_D], W=[b_A])
                        k.tt("dve", bC[:, sl], pi_[:, 0:nt], cosT[:, sl], ALU.mult, R=[bpi, b_cos], W=[b_C])
                        k.tt("dve", bD[:, sl], pr[:, 0:nt], sinT[:, sl], ALU.mult, R=[bpr, b_sin], W=[b_D])
                        k.tt("dve", bB[:, sl], bC[:, sl], bD[:, sl], ALU.subtract, R=[b_C, b_D], W=[b_B])
                    if st_ == 0:
                        dbgdump(k, "d_zre", bA, [b_A]); dbgdump(k, "d_zim", bB, [b_B])
                    for si, (s0_, sn) in enumerate(SEGS):
                        sl = slice(s0_, s0_ + sn)
                        mg_b = mag[:, st_:st_ + 1].to_broadcast([128, sn])
                        i_re = 0.0 if si == 0 else h0t[:, 2 * (si - 1), st_:st_ + 1]
                        i_im = 0.0 if si == 0 else h0t[:, 2 * (si - 1) + 1, st_:st_ + 1]
                        k.scan(bA[:, sl], mg_b, bA[:, sl], i_re, R=[b_A, b_sp, b_h0], W=[b_A])
                        k.scan(bB[:, sl], mg_b, bB[:, sl], i_im, R=[b_B, b_sp, b_h0], W=[b_B])
                    if st_ == 0:
                        dbgdump(k, "d_qre", bA, [b_A]); dbgdump(k, "d_qim", bB, [b_B])
                    k.tt("dve", bC, bA, cosT, ALU.mult, R=[b_A, b_cos], W=[b_C])
                    k.tt("dve", bD, bB, sinT, ALU.mult, R=[b_B, b_sin], W=[b_D])
                    k.tt("dve", bC, bC, bD, ALU.subtract, R=[b_C, b_D], W=[b_C])
                    k.tt("dve", bD, bB, cosT, ALU.mult, R=[b_B, b_cos], W=[b_D])
                    k.tt("dve", bA, bA, sinT, ALU.mult, R=[b_A, b_sin], W=[b_A])
                    k.tt("dve", bD, bD, bA, ALU.add, R=[b_D, b_A], W=[b_D])
                    for si, (s0_, sn) in enumerate(SEGS):
                        e_ = s0_ + sn - 1
                        k.cp("act", HF[:, st_, si:si + 1], bC[:, e_:e_ + 1], R=[b_C], W=[b_hf])
                        k.cp("act", HF[:, st_, 3 + si:4 + si], bD[:, e_:e_ + 1], R=[b_D], W=[b_hf])
                    if st_ == 0:
                        dbgdump(k, "d_hre", bC, [b_C]); dbgdump(k, "d_him", bD, [b_D])
                    for (t0, nt) in TB:
                        py, bpy = next_ps(4, 8)
                        k.mm(py[0:32, 0:nt], CBDre[:, st_, :], bC[:, t0:t0 + nt], start=True, stop=False, R=[b_cbd, b_C], W=[bpy])
                        k.mm(py[0:32, 0:nt], CBDim[:, st_, :], bD[:, t0:t0 + nt], start=False, stop=True, R=[b_cbd, b_D], W=[bpy])
                        k.stt("dve", bB[0:32, t0:t0 + nt], u32[0:32, t0:t0 + nt], dsk[:, st_:st_ + 1], py[0:32, 0:nt], ALU.mult, ALU.add, R=[b_u32, b_cp, bpy], W=[b_B])
                    k.dma("sp", YS[32 * st_:32 * st_ + 32, :], bB[0:32, :], R=[b_B, bYS])
                for si in range(3):
                    for ri in range(2):
                        k.dma("sp", o_ssm[l][ri][si].rearrange("(st p) -> p st", p=128), HF[:, :, 3 * ri + si], R=[b_hf], allow_slow_non_contiguous=True)
                b_rb = Buf("ra1")
                k.barrier([b_tau, b_cos, b_sin, b_A, b_B, b_C, b_D, b_u32, b_ti, b_t1, b_t2, b_bbt, b_cbd, b_sp, b_hf, b_h0], [b_rb], dummy[0:1, 0:1])
                WS = Region(RA, NBIG)
                WW = Region(Wt, 8192)
                yv = WS.take([4, NTOK]); tv = WS.take([4, NTOK])
                b_yv, b_tv = Buf("yv"), Buf("tv")
                gb = WW.take([4, NTOK], BF16); b_gb = Buf("gb")
                wgl = WW.take([4, 512], BF16); b_wgl = Buf("wgl")
                k.barrier([b_rb], [b_yv, b_tv, b_gb, b_wgl], dummy[0:1, 0:1])
                k.dma("pool", wgl, w_glu[l].rearrange("(kt p) c -> p kt c", p=128), W=[b_wgl])
                fence(bYS)
                k.dma("sp", yv, YS.rearrange("(a p) t -> p a t", p=128), R=[bYS], W=[b_yv])
                k.tt("dve", tv, yv, yv, ALU.mult, R=[b_yv], W=[b_tv])
                k.ts("dve", tv, tv, 0.044715, 1.0, ALU.mult, ALU.add, R=[b_tv], W=[b_tv])
                k.tt("dve", tv, tv, yv, ALU.mult, R=[b_tv, b_yv], W=[b_tv])
                k.act(tv, tv, AF.Sigmoid, scale=2.0 * math.sqrt(2.0 / PI), R=[b_tv], W=[b_tv])
                k.tt("dve", yv, yv, tv, ALU.mult, R=[b_tv, b_yv], W=[b_yv])
                k.cp("act", gb, yv, R=[b_yv], W=[b_gb])
                sgt = tv[:, 0, :]
                gat = tv[:, 1, :]
                for oc in range(4):
                    k.dma("sp", gat, PF[3200 + 128 * (4 + oc):3200 + 128 * (5 + oc), :], R=[bPF], W=[b_tv])
                    for (t0, nt) in TB:
                        pp, bp = next_ps(0, 4)
                        for kt in range(4):
                            k.mm(pp[:, 0:nt], wgl[:, kt, 128 * oc:128 * oc + 128], gb[:, kt, t0:t0 + nt], start=(kt == 0), stop=(kt == 3), R=[b_wgl, b_gb], W=[bp])
                        k.act(sgt[:, t0:t0 + nt], pp[:, 0:nt], AF.Sigmoid, bias=cpt[:, CP_BGLU + oc:CP_BGLU + oc + 1], R=[bp, b_cp], W=[b_tv])
                        k.tt("dve", sgt[:, t0:t0 + nt], sgt[:, t0:t0 + nt], yv[:, oc, t0:t0 + nt], ALU.mult, R=[b_tv, b_yv], W=[b_tv])
                        k.tt("dve", gT[:, 4 + oc, t0:t0 + nt], sgt[:, t0:t0 + nt], gat[:, t0:t0 + nt], ALU.mult, R=[b_tv], W=[b_gT])
                k.barrier([b_yv, b_tv, b_gb, b_wgl], [b_ra, bW[0], bW[1]], dummy[0:1, 0:1])
            if upto == "C1":
                k.dma("sp", GDBG, RB[:, :].bitcast(BF16), R=[b_gT])
                break

            def attn_masks_alloc(WW):
                return WW.take([4, 512]), WW.take([16]), Buf("mask")

            def attn_masks(mk, m16, b_mk, kind):
                k.memset("pool", mk, 1.0, W=[b_mk])
                for j in range(4):
                    if kind == "sb":
                        k.asel(mk[:, j, :], mk[:, j, :], [[1, 512]], ALU.is_gt, 0.0, -128 * j, -1, R=[b_mk], W=[b_mk])
                    else:
                        k.asel(mk[:, j, :].rearrange("p (a b) -> p a b", b=64), mk[:, j, :].rearrange("p (a b) -> p a b", b=64), [[64, 8], [0, 64]], ALU.is_ge, 0.0, 63 - 128 * j, -1, R=[b_mk], W=[b_mk])
                k.memset("pool", m16, 1.0, R=[b_mk], W=[b_mk])
                k.asel(m16[0:16, :], m16[0:16, :], [[1, 16]], ALU.is_gt, 0.0, 0, -1, R=[b_mk], W=[b_mk])

            if "C3" not in SKIP:
                S.serial = "SER" in SKIP
                WS = Region(RA, NBIG)
                WW = Region(Wt, 8192)
                MW = Region(Mt, 9216); MW.off = MISC_BASE
                qT = WS.take([2, NTOK], BF16, 64); kT = WS.take([2, NTOK], BF16, 64)
                vpad = WS.take([16, 2, 128], BF16)
                gate = WS.take([NTOK])
                xoff = WS.off
                tEs, tSPs, tSs, tAs = [], [], [], []
                for _ in range(4):
                    tEs.append(WS.take([512])); tSPs.append(WS.take([512])); tSs.append(WS.take([512])); tAs.append(WS.take([512], BF16))
                WS2 = Region(RA, NBIG); WS2.off = xoff + 2 * 1792
                kTp = WS2.take([2, 2, PAST], BF16, 64); vpadp = WS2.take([2, 8, 2, 128], BF16); vnew = WS2.take([2, 2, 128], BF16, 16)
                b_q, b_k, b_kp, b_v, b_vp, b_vn, b_gate = (Buf(n) for n in ("q", "k", "kp", "v", "vp", "vn", "gate"))
                b_Es = [Buf(f"E{i}") for i in range(4)]; b_SPs = [Buf(f"SP{i}") for i in range(4)]; b_Ss = [Buf(f"S{i}") for i in range(4)]; b_A2s = [Buf(f"A2{i}") for i in range(4)]
                sb01 = b_Es[0:2] + b_SPs[0:2] + b_Ss[0:2] + b_A2s[0:2]
                sb23 = b_Es[2:4] + b_SPs[2:4] + b_Ss[2:4] + b_A2s[2:4]
                smp = [b_kp, b_vp, b_vn]
                mk, m16, b_mk = attn_masks_alloc(WW)
                stg = WW.take([NTOK]); b_st = Buf("st")
                stv = WW.take([16, 128]); b_stv = Buf("stv")
                stp = WW.take([8, 128]); b_stp = Buf("stp")
                triU = MW.take([128]); ones128 = MW.take([128]); b_tri = Buf("tri")
                k.barrier([b_ra, bW[0], bW[1]], [b_q, b_k, b_v, b_gate, b_mk, b_st, b_stv, b_stp, b_tri] + sb01 + sb23, dummy[0:1, 0:1])
                attn_masks(mk, m16, b_mk, "sb")
                k.memset("pool", ones128, 1.0, W=[b_tri])
                k.memset("pool", triU, 1.0, W=[b_tri])
                k.asel(triU, triU, [[-1, 128]], ALU.is_gt, 0.0, 0, 1, R=[b_tri], W=[b_tri])
                k.memset("dve", vpad, 0.0, W=[b_v])
                pzc = [0]

                def run_streams(gens, delays=None):
                    active = [[g_, (delays[i] if delays else 0)] for i, g_ in enumerate(gens)]
                    while active:
                        nxt = []
                        for ent in active:
                            if ent[1] > 0:
                                ent[1] -= 1
                                nxt.append(ent)
                                continue
                            try:
                                next(ent[0])
                                nxt.append(ent)
                            except StopIteration:
                                pass
                        active = nxt

                def sb_tile(st_, lhsK, rhsQ, nk, nq, maskap, first, last_acc, vl, pso, bpso, o_start, o_stop, RK, RQ, RV):
                    tE, tSP, tS, tA = tEs[st_], tSPs[st_], tSs[st_], tAs[st_]
                    b_E, b_SP, b_S, b_A2 = b_Es[st_], b_SPs[st_], b_Ss[st_], b_A2s[st_]
                    pz, bpz = ps[st_ % 2], bps[st_ % 2]
                    pl, bpl = ps[2 + st_], bps[2 + st_]
                    k.mm(pz[0:nk, 0:nq], lhsK, rhsQ, R=[RK, RQ], W=[bpz])
                    yield
                    k.act(tE[0:nk, 0:nq], pz[0:nk, 0:nq], AF.Exp, R=[bpz], W=[b_E])
                    k.act(tSP[0:nk, 0:nq], tE[0:nk, 0:nq], AF.Ln, bias=1.0, R=[b_E], W=[b_SP])
                    k.cp("act", tE[0:nk, 0:nq], pz[0:nk, 0:nq], R=[bpz], W=[b_E])
                    if maskap is not None:
                        k.tt("dve", tSP[0:nk, 0:nq], tSP[0:nk, 0:nq], maskap, ALU.mult, R=[b_SP, b_mk], W=[b_SP])
                    yield
                    k.mm(pl[0:nk, 0:nq], triU[0:nk, 0:nk], tSP[0:nk, 0:nq], start=True, stop=first, R=[b_tri, b_SP], W=[bpl])
                    if not first:
                        k.mm(pl[0:nk, 0:nq], ones128[:, 0:nk], tS[:, 0:nq], start=False, stop=True, R=[b_tri, b_S], W=[bpl])
                    yield
                    k.tt("dve", tE[0:nk, 0:nq], tE[0:nk, 0:nq], tSP[0:nk, 0:nq], ALU.subtract, R=[b_E, b_SP], W=[b_E])
                    k.tt("dve", tE[0:nk, 0:nq], tE[0:nk, 0:nq], pl[0:nk, 0:nq], ALU.subtract, R=[b_E, bpl], W=[b_E])
                    if not last_acc:
                        if first:
                            if nk < 128:
                                k.memset("pool", tS[:, 0:nq], 0.0, W=[b_S])
                            k.cp("pool", tS[0:nk, 0:nq], tSP[0:nk, 0:nq], R=[b_SP], W=[b_S])
                        else:
                            k.tt("pool", tS[0:nk, 0:nq], tS[0:nk, 0:nq], tSP[0:nk, 0:nq], ALU.add, R=[b_SP, b_S], W=[b_S])
                    yield
                    k.act(tA[0:nk, 0:nq], tE[0:nk, 0:nq], AF.Exp, R=[b_E], W=[b_A2])
                    if maskap is not None:
                        k.tt("dve", tA[0:nk, 0:nq], tA[0:nk, 0:nq], maskap, ALU.mult, R=[b_A2, b_mk], W=[b_A2])
                    yield
                    k.mm(pso[:, 0:nq], vl, tA[0:nk, 0:nq], start=o_start, stop=o_stop, R=[RV, b_A2], W=[bpso])

                def prompt_stream(st_, hf, qsb, pso, bpso):
                    q0 = 512 * qsb
                    nkt = 4 * (qsb + 1)
                    for kt in range(nkt - 1, -1, -1):
                        jd = kt - 4 * qsb
                        yield from sb_tile(st_, kT[:, hf, 128 * kt:128 * kt + 128], qT[:, hf, q0:q0 + 512], 128, 512,
                                           mk[:, jd, :] if jd >= 0 else None, kt == nkt - 1, kt == 0,
                                           vpad[:, kt, hf, :], pso, bpso, (hf == 0 and kt == nkt - 1), (hf == 1 and kt == 0), b_k, b_q, b_v)
                        yield

                def sample_stream(st_, hf, j, pso, bpso):
                    q0 = TP + 16 * j
                    yield from sb_tile(st_, kT[:, hf, q0:q0 + 16], qT[:, hf, q0:q0 + 16], 16, 16, m16[0:16, :], True, False,
                                       vnew[:, j, hf, :], pso, bpso, hf == 0, False, b_k, b_q, b_vn)
                    yield
                    for kt in range(7, -1, -1):
                        yield from sb_tile(st_, kTp[:, j, hf, 128 * kt:128 * kt + 128], qT[:, hf, q0:q0 + 16], 128, 16, None, False, kt == 0,
                                           vpadp[:, j, kt, hf, :], pso, bpso, False, (hf == 1 and kt == 0), b_kp, b_q, b_vp)
                        yield

                for hp in range(4):
                    for hf in range(2):
                        h_ = 2 * hp + hf
                        k.dma("sp", stg[0:64, :], PF[2176 + 64 * h_:2176 + 64 * h_ + 64, :], R=[bPF], W=[b_st])
                        k.act(qT[:, hf, :], stg[0:64, :], AF.Copy, scale=0.125, R=[b_st], W=[b_q])
                        k.dma("sp", stg[0:64, :], PF[2688 + 64 * h_:2688 + 64 * h_ + 64, :], R=[bPF], W=[b_st])
                        k.cp("act", kT[:, hf, :], stg[0:64, :], R=[b_st], W=[b_k])
                    k.dma("sp", gate, PF[3200 + 128 * (12 + hp):3200 + 128 * (13 + hp), :], R=[bPF], W=[b_gate])
                    k.dma("sp", stv, o_sbv[l][0:TP, 128 * hp:128 * hp + 128].rearrange("(t p) c -> p t c", p=128), R=[bOsv], W=[b_stv])
                    for hf in range(2):
                        k.cp("dve", vpad[:, :, hf, 64 * hf:64 * hf + 64], stv[:, :, 64 * hf:64 * hf + 64], R=[b_stv], W=[b_v])
                    for (qa, qb) in ((0, 3), (1, 2)):
                        run_streams([prompt_stream(0, 0, qa, ps[6], bps[6]), prompt_stream(1, 1, qa, ps[6], bps[6]),
                                     prompt_stream(2, 0, qb, ps[7], bps[7]), prompt_stream(3, 1, qb, ps[7], bps[7])], delays=[0, 0, 1, 1])
                        for (qq, pb) in ((qa, 6), (qb, 7)):
                            q0 = 512 * qq
                            k.tt("dve", gT[:, 12 + hp, q0:q0 + 512], ps[pb][:, :], gate[:, q0:q0 + 512], ALU.mult, R=[bps[pb], b_gate], W=[b_gT])
                    k.barrier(sb23, smp, dummy[0:1, 0:1])
                    k.memset("dve", vpadp, 0.0, W=[b_vp]); k.memset("dve", vnew, 0.0, W=[b_vn])
                    k.dma("sp", stp[0:16, 0:2, :], o_sbv[l][TP:NTOK, 128 * hp:128 * hp + 128].rearrange("(j p) c -> p j c", p=16), R=[bOsv], W=[b_stp])
                    for hf in range(2):
                        k.cp("dve", vnew[:, :, hf, 64 * hf:64 * hf + 64], stp[0:16, 0:2, 64 * hf:64 * hf + 64], R=[b_stp], W=[b_vn])
                    for j in range(2):
                        k.dma("sp", stp, c_sbv[l][j][:, 128 * hp:128 * hp + 128].rearrange("(t p) c -> p t c", p=128), W=[b_stp])
                        for hf in range(2):
                            k.cp("dve", vpadp[:, j, :, hf, 64 * hf:64 * hf + 64], stp[:, :, 64 * hf:64 * hf + 64], R=[b_stp], W=[b_vp])
                        k.dma("sp", stp, c_sbk[l][j][:, 128 * hp:128 * hp + 128].rearrange("(t p) c -> p t c", p=128), W=[b_stp])
                        for hf in range(2):
                            for q4 in range(2):
                                pt, bpt = next_ps(0, 2)
                                for jj in range(4):
                                    k.tr(pt[0:64, 128 * jj:128 * jj + 128], stp[:, 4 * q4 + jj, 64 * hf:64 * hf + 64], identf, R=[b_stp, b_c], W=[bpt])
                                k.cp(ev_eng(), kTp[:, j, hf, 512 * q4:512 * q4 + 512], pt[0:64, :], R=[bpt], W=[b_kp])
                    for j in range(2):
                        q0 = TP + 16 * j
                        run_streams([sample_stream(0, 0, j, ps[6 + j], bps[6 + j]), sample_stream(1, 1, j, ps[6 + j], bps[6 + j])])
                        k.tt("dve", gT[:, 12 + hp, q0:q0 + 16], ps[6 + j][:, 0:16], gate[:, q0:q0 + 16], ALU.mult, R=[bps[6 + j], b_gate], W=[b_gT])
                    k.barrier(smp, sb23, dummy[0:1, 0:1])
                k.barrier([b_q, b_k, b_v, b_gate, b_mk, b_st, b_stv, b_stp, b_tri] + sb01 + sb23, [b_ra, bW[0], bW[1]], dummy[0:1, 0:1])
            if upto == "C3":
                k.dma("sp", GDBG, RB[:, :].bitcast(BF16), R=[b_gT])
                break

            if "C4" not in SKIP:
                S.serial = "SER" in SKIP
                MLA_SCALE = 1.0 / math.sqrt(96.0)
                WS = Region(RA, NBIG); WW = Region(Wt, 8192)
                qnT_ = WS.take([3, NTOK], BF16); CT = WS.take([NTOK]); ST = WS.take([NTOK])
                ttc = WS.take([96]); tts = WS.take([96])
                qo = [WS.take([512], BF16), WS.take([512], BF16)]
                t1 = WS.take([512]); t2 = WS.take([512])
                wq_ = WW.take([3, 768], BF16); wqs_ = WW.take([3, 768], BF16)
                b_qn, b_ct, b_ttc, b_t12, b_wq = (Buf(n) for n in ("qn", "ct", "ttc", "t12", "wq"))
                b_qo = [Buf("qo0"), Buf("qo1")]
                k.barrier([b_ra, bW[0], bW[1]], [b_qn, b_ct, b_ttc, b_t12, b_wq] + b_qo, dummy[0:1, 0:1])
                k.dma("sp", qnT_, QNS.rearrange("p (a t) -> p a t", a=3), R=[bQNS], W=[b_qn])
                k.dma("pool", wq_, wq[l].rearrange("(kt p) c -> p kt c", p=128), W=[b_wq])
                k.dma("pool", wqs_, wqs[l].rearrange("(kt p) c -> p kt c", p=128), W=[b_wq])
                k.memset("dve", ttc, 0.0, W=[b_ttc]); k.memset("dve", tts, 0.0, W=[b_ttc])
                for ti, (r0, nr) in enumerate(TT):
                    for (tsrc, tdst, RT) in ((ttc, CT, RC), (tts, ST, RS)):
                        k.cp("dve", tsrc[0:nr, 64:96], RT[0:nr, ti, :], R=[b_rope], W=[b_ttc])
                        pt, bpt = next_ps(6, 8)
                        k.tr(pt[0:96, 0:nr], tsrc[0:nr, 0:96], identf[0:nr, 0:nr], R=[b_ttc, b_c], W=[bpt])
                        k.cp("act", tdst[64:96, r0:r0 + nr], pt[64:96, 0:nr], R=[bpt], W=[b_ct])
                qi = 0
                for h_ in range(8):
                    for (t0, nt) in TB:
                        p1, bp1 = next_ps(0, 2)
                        p2, bp2 = next_ps(2, 4)
                        for kt in range(3):
                            k.mm(p1[0:96, 0:nt], wq_[:, kt, 96 * h_:96 * h_ + 96], qnT_[:, kt, t0:t0 + nt], start=(kt == 0), stop=(kt == 2), R=[b_wq, b_qn], W=[bp1])
                        for kt in range(3):
                            k.mm(p2[0:96, 0:nt], wqs_[:, kt, 96 * h_:96 * h_ + 96], qnT_[:, kt, t0:t0 + nt], start=(kt == 0), stop=(kt == 2), R=[b_wq, b_qn], W=[bp2])
                        q_, bq_ = qo[qi % 2], b_qo[qi % 2]
                        qi += 1
                        k.cp("act", q_[0:64, 0:nt], p1[0:64, 0:nt], R=[bp1], W=[bq_])
                        k.tt("dve", t1[64:96, 0:nt], p1[64:96, 0:nt], CT[64:96, t0:t0 + nt], ALU.mult, R=[bp1, b_ct], W=[b_t12])
                        k.tt("dve", t2[64:96, 0:nt], p2[64:96, 0:nt], ST[64:96, t0:t0 + nt], ALU.mult, R=[bp2, b_ct], W=[b_t12])
                        k.tt("dve", q_[64:96, 0:nt], t1[64:96, 0:nt], t2[64:96, 0:nt], ALU.add, R=[b_t12], W=[bq_])
                        k.dma("sp", QF[:, h_, t0:t0 + nt], q_[0:96, 0:nt], R=[bq_, bSCR])
                b_m2 = Buf("m2")
                k.barrier([b_qn, b_ct, b_ttc, b_t12, b_wq] + b_qo, [b_m2], dummy[0:1, 0:1])
                WS = Region(RA, NBIG); WW = Region(Wt, 8192)
                ckvT = WS.take([2, NTOK], BF16); ckvTp = WS.take([2, 2, PAST], BF16)
                kpeT = WS.take([NTOK], BF16); kpeTp = WS.take([2, PAST], BF16)
                ck_st = WS.take([256]); kp_st = WS.take([96]); cks8 = WS.take([8, 256]); kps8 = WS.take([8, 96])
                ko = [WS.take([512], BF16), WS.take([512], BF16)]
                vo = [WS.take([512], BF16), WS.take([512], BF16)]
                wk_ = WW.take([2, 512], BF16); wv_ = WW.take([2, 512], BF16)
                b_ck, b_ckp, b_kp2, b_kpp, b_cst, b_kst, b_c8, b_k8, b_wk = (Buf(n) for n in ("ck", "ckp", "kp2", "kpp", "cst", "kst", "c8", "k8", "wk"))
                b_ko = [Buf("ko0"), Buf("ko1")]; b_vo = [Buf("vo0"), Buf("vo1")]
                k.barrier([b_m2], [b_ck, b_ckp, b_kp2, b_kpp, b_cst, b_kst, b_c8, b_k8, b_wk] + b_ko + b_vo, dummy[0:1, 0:1])
                k.dma("pool", wk_, wk[l].rearrange("(kt p) c -> p kt c", p=128), W=[b_wk])
                k.dma("pool", wv_, wv[l].rearrange("(kt p) c -> p kt c", p=128), W=[b_wk])
                k.memset("dve", kp_st, 0.0, W=[b_kst]); k.memset("dve", kps8, 0.0, W=[b_k8])
                for ti, (r0, nr) in enumerate(TT):
                    k.dma("sp", ck_st[0:nr, :], o_ckv[l][r0:r0 + nr, :], R=[bOck], W=[b_cst])
                    pt, bpt = next_ps(6, 8)
                    for j in range(2):
                        k.tr(pt[:, 128 * j:128 * j + nr], ck_st[0:nr, 128 * j:128 * j + 128], identf[0:nr, 0:nr], R=[b_cst, b_c], W=[bpt])
                    k.cp(ev_eng(), ckvT[:, :, r0:r0 + nr], pt[:, 0:256].rearrange("p (a b) -> p a b", a=2)[:, :, 0:nr], R=[bpt], W=[b_ck])
                    k.dma("sp", kp_st[0:nr, 64:96], o_kpe[l][r0:r0 + nr, :], R=[bOkp], W=[b_kst])
                    pt, bpt = next_ps(6, 8)
                    k.tr(pt[0:96, 0:nr], kp_st[0:nr, 0:96], identf[0:nr, 0:nr], R=[b_kst, b_c], W=[bpt])
                    k.cp("act", kpeT[64:96, r0:r0 + nr], pt[64:96, 0:nr], R=[bpt], W=[b_kp2])
                for j in range(2):
                    k.dma("sp", cks8, c_ckv[l][j].rearrange("(t p) c -> p t c", p=128), W=[b_c8])
                    k.dma("sp", kps8[:, :, 64:96], c_kpe[l][j].rearrange("(t p) c -> p t c", p=128), W=[b_k8])
                    for t8 in range(8):
                        pt, bpt = next_ps(6, 8)
                        for kt in range(2):
                            k.tr(pt[:, 128 * kt:128 * kt + 128], cks8[:, t8, 128 * kt:128 * kt + 128], identf, R=[b_c8, b_c], W=[bpt])
                        k.cp(ev_eng(), ckvTp[:, :, j, 128 * t8:128 * t8 + 128], pt[:, 0:256].rearrange("p (a b) -> p a b", a=2), R=[bpt], W=[b_ckp])
                        pt, bpt = next_ps(6, 8)
                        k.tr(pt[0:96, 0:128], kps8[:, t8, 0:96], identf, R=[b_k8, b_c], W=[bpt])
                        k.cp("act", kpeTp[64:96, j, 128 * t8:128 * t8 + 128], pt[64:96, 0:128], R=[bpt], W=[b_kpp])
                ki = 0
                for h_ in range(8):
                    srcs = [(ckvT[:, :, t0:t0 + nt], kpeT[64:96, t0:t0 + nt], KF[:, h_, t0:t0 + nt], nt, b_ck, b_kp2) for (t0, nt) in TB]
                    for j in range(2):
                        for kb in range(2):
                            srcs.append((ckvTp[:, :, j, 512 * kb:512 * kb + 512], kpeTp[64:96, j, 512 * kb:512 * kb + 512], KFP[:, h_, j, 512 * kb:512 * kb + 512], 512, b_ckp, b_kpp))
                    for (csrc, psrc, dst, nt, bcs, bps_) in srcs:
                        pp, bp = next_ps(0, 4)
                        for kt in range(2):
                            k.mm(pp[0:64, 0:nt], wk_[:, kt, 64 * h_:64 * h_ + 64], csrc[:, kt, :], start=(kt == 0), stop=(kt == 1), R=[b_wk, bcs], W=[bp])
                        k_, bk_ = ko[ki % 2], b_ko[ki % 2]
                        ki += 1
                        k.cp(ev_eng(), k_[0:64, 0:nt], pp[0:64, 0:nt], R=[bp], W=[bk_])
                        k.cp("pool", k_[64:96, 0:nt], psrc, R=[bps_], W=[bk_])
                        k.dma("sp", dst, k_[0:96, 0:nt], R=[bk_, bSCR])
                vi = 0
                vsrcs = [(ckvT[:, :, r0:r0 + nr], VT[r0:r0 + nr, :], nr, b_ck) for (r0, nr) in TT]
                for j in range(2):
                    for t8 in range(8):
                        vsrcs.append((ckvTp[:, :, j, 128 * t8:128 * t8 + 128], VTP[PAST * j + 128 * t8:PAST * j + 128 * t8 + 128, :], 128, b_ckp))
                for (csrc, dst, nr, bcs) in vsrcs:
                    pp, bp = next_ps(0, 4)
                    for kt in range(2):
                        k.mm(pp[0:nr, :], csrc[:, kt, :], wv_[:, kt, :], start=(kt == 0), stop=(kt == 1), R=[b_wk, bcs], W=[bp])
                    v_, bv_ = vo[vi % 2], b_vo[vi % 2]
                    vi += 1
                    k.cp(ev_eng(), v_[0:nr, :], pp[0:nr, :], R=[bp], W=[bv_])
                    k.dma("sp", dst, v_[0:nr, :], R=[bv_, bSCR])
                b_m3 = Buf("m3")
                k.barrier([b_ck, b_ckp, b_kp2, b_kpp, b_cst, b_kst, b_c8, b_k8, b_wk] + b_ko + b_vo, [b_m3], dummy[0:1, 0:1])
                fence(bSCR)
                WS = Region(RA, NBIG); WW = Region(Wt, 8192)
                qf = WS.take([2, NTOK], BF16, 96); kf = WS.take([2, NTOK], BF16, 96); kfp = WS.take([2, 2, PAST], BF16, 96)
                vpad = WS.take([16, 2, 128], BF16); vpadp = WS.take([2, 8, 2, 128], BF16); vnew = WS.take([2, 2, 128], BF16, 16)
                gate = WS.take([NTOK]); opad = WS.take([2, 128], BF16)
                tAs = [WS.take([512], BF16), WS.take([512], BF16), WS.take([512], BF16)]; tR = WS.take([512])
                b_A2s = [Buf("A2a"), Buf("A2b"), Buf("A2c")]
                mk, m16, b_mk = attn_masks_alloc(WW)
                mkb = WW.take([4, 512], BF16)
                tac = [0]
                b_q, b_k, b_kp, b_v, b_vp, b_vn, b_gate, b_A2, b_R, b_op = (Buf(n) for n in ("q", "k", "kp", "v", "vp", "vn", "gate", "A2", "R", "op"))
                k.barrier([b_m3], [b_q, b_k, b_kp, b_v, b_vp, b_vn, b_gate, b_A2, b_R, b_op, b_mk] + b_A2s, dummy[0:1, 0:1])
                attn_masks(mk, m16, b_mk, "mla")
                k.cp("dve", mkb, mk, R=[b_mk], W=[b_mk])
                k.memset("dve", vpad, 0.0, W=[b_v]); k.memset("dve", vpadp, 0.0, W=[b_vp]); k.memset("dve", vnew, 0.0, W=[b_vn])
                k.memset("dve", opad, 0.0, W=[b_op])
                k.memset("dve", opad[:, 0, 0:64], 1.0, W=[b_op]); k.memset("dve", opad[:, 1, 64:128], 1.0, W=[b_op])

                def mla_a(lhsK, rhsQ, nk, nq, RK, RQ):
                    pz, bpz = next_ps(0, 4)
                    k.mm(pz[0:nk, 0:nq], lhsK, rhsQ, R=[RK, RQ], W=[bpz])
                    return pz, bpz

                def mla_b(pzs, nk, nq, maskap, vl, hf, pso, bpso, psd, bpsd, o_start, o_stop, RV):
                    pz, bpz = pzs
                    tA, b_A2 = tAs[tac[0] % 3], b_A2s[tac[0] % 3]
                    tac[0] += 1
                    k.act(tA[0:nk, 0:nq], pz[0:nk, 0:nq], AF.Exp, scale=MLA_SCALE, R=[bpz], W=[b_A2])
                    if maskap is not None:
                        k.tt("dve", tA[0:nk, 0:nq], tA[0:nk, 0:nq], maskap, ALU.mult, R=[b_A2, b_mk], W=[b_A2])
                    k.mm(pso[:, 0:nq], vl, tA[0:nk, 0:nq], start=o_start, stop=o_stop, R=[RV, b_A2], W=[bpso])
                    k.mm(psd[:, 0:nq], opad[0:nk, hf, :], tA[0:nk, 0:nq], start=o_start, stop=o_stop, R=[b_op, b_A2], W=[bpsd])

                def mla_pipe(tiles, pso, bpso, psd, bpsd):
                    n = len(tiles)
                    pend = []
                    for i in range(n + 2):
                        if i < n:
                            t = tiles[i]
                            pend.append(mla_a(t["K"], t["Q"], t["nk"], t["nq"], t["RK"], t["RQ"]))
                        if i >= 2:
                            t = tiles[i - 2]
                            mla_b(pend[i - 2], t["nk"], t["nq"], t["mask"], t["V"], t["hf"], pso, bpso, psd, bpsd, i - 2 == 0, i - 2 == n - 1, t["RV"])

                for hp in range(4):
                    k.dma("sp", qf, QF[:, 2 * hp:2 * hp + 2, :], R=[bSCR], W=[b_q])
                    k.dma("sp", kf, KF[:, 2 * hp:2 * hp + 2, :], R=[bSCR], W=[b_k])
                    k.dma("sp", kfp, KFP[:, 2 * hp:2 * hp + 2, :, :], R=[bSCR], W=[b_kp])
                    k.dma("sp", gate, PF[3200 + 128 * (8 + hp):3200 + 128 * (9 + hp), :], R=[bPF], W=[b_gate])
                    for hf in range(2):
                        c0 = 128 * hp + 64 * hf
                        k.dma("sp", vpad[:, :, hf, 64 * hf:64 * hf + 64], VT[0:TP, c0:c0 + 64].rearrange("(t p) c -> p t c", p=128), R=[bSCR], W=[b_v])
                        k.dma("sp", vnew[:, :, hf, 64 * hf:64 * hf + 64], VT[TP:NTOK, c0:c0 + 64].rearrange("(j p) c -> p j c", p=16), R=[bSCR], W=[b_vn])
                        for j in range(2):
                            k.dma("sp", vpadp[:, j, :, hf, 64 * hf:64 * hf + 64], VTP[PAST * j:PAST * j + PAST, c0:c0 + 64].rearrange("(t p) c -> p t c", p=128), R=[bSCR], W=[b_vp])

                    def finish(pso, bpso, psd, bpsd, q0, nq):
                        k.recip(tR[:, 0:nq], psd[:, 0:nq], R=[bpsd], W=[b_R])
                        k.tt("dve", tR[:, 0:nq], pso[:, 0:nq], tR[:, 0:nq], ALU.mult, R=[bpso, b_R], W=[b_R])
                        k.tt("dve", gT[:, 8 + hp, q0:q0 + nq], tR[:, 0:nq], gate[:, q0:q0 + nq], ALU.mult, R=[b_R, b_gate], W=[b_gT])

                    for qsb in range(4):
                        q0 = 512 * qsb
                        pso, bpso = next_ps(4, 6)
                        psd, bpsd = next_ps(6, 8)
                        nkt = 4 * (qsb + 1)
                        tl = []
                        for hf in range(2):
                            for kt in range(nkt):
                                jd = kt - 4 * qsb
                                tl.append(dict(K=kf[:, hf, 128 * kt:128 * kt + 128], Q=qf[:, hf, q0:q0 + 512], nk=128, nq=512,
                                               mask=mkb[:, jd, :] if jd >= 0 else None, V=vpad[:, kt, hf, :], hf=hf, RK=b_k, RQ=b_q, RV=b_v))
                        mla_pipe(tl, pso, bpso, psd, bpsd)
                        finish(pso, bpso, psd, bpsd, q0, 512)
                    for j in range(2):
                        q0 = TP + 16 * j
                        pso, bpso = next_ps(4, 6)
                        psd, bpsd = next_ps(6, 8)
                        tl = []
                        for hf in range(2):
                            for kt in range(8):
                                tl.append(dict(K=kfp[:, hf, j, 128 * kt:128 * kt + 128], Q=qf[:, hf, q0:q0 + 16], nk=128, nq=16, mask=None,
                                               V=vpadp[:, j, kt, hf, :], hf=hf, RK=b_kp, RQ=b_q, RV=b_vp))
                            tl.append(dict(K=kf[:, hf, q0:q0 + 16], Q=qf[:, hf, q0:q0 + 16], nk=16, nq=16, mask=None,
                                           V=vnew[:, j, hf, :], hf=hf, RK=b_k, RQ=b_q, RV=b_vn))
                        mla_pipe(tl, pso, bpso, psd, bpsd)
                        finish(pso, bpso, psd, bpsd, q0, 16)
                k.barrier([b_q, b_k, b_kp, b_v, b_vp, b_vn, b_gate, b_A2, b_R, b_op, b_mk] + b_A2s, [b_ra, bW[0], bW[1]], dummy[0:1, 0:1])
            if upto == "C4":
                k.dma("sp", GDBG, RB[:, :].bitcast(BF16), R=[b_gT])
                break
            if "C2" not in SKIP:
                S.serial = "SER" in SKIP
                CH = [(64 * c, 64) for c in range(32)] + [(2048, 16), (2064, 16)]
                NCH = len(CH)
                MW = Region(Mt, 9216); MW.off = MISC_BASE
                triS = MW.take([64], F32, 64); triI = MW.take([64], F32, 64); triL = MW.take([64], F32, 64)
                pcs = MW.take([4, NCH]); m01 = MW.take([512]); m01s = MW.take([32])
                negw0 = MW.take([4]); omka = MW.take([4])
                b_rc = Buf("rwconst"); b_pcs = Buf("pcs")
                k.barrier([b_ra, bW[0], bW[1]], [b_rc, b_pcs], dummy[0:1, 0:1])
                for (tri, op_, pat, cm) in ((triS, ALU.is_gt, [[1, 64]], -1), (triI, ALU.is_ge, [[1, 64]], -1), (triL, ALU.is_gt, [[-1, 64]], 1)):
                    k.memset("pool", tri, 1.0, W=[b_rc])
                    k.asel(tri, tri, pat, op_, 0.0, 0, cm, R=[b_rc], W=[b_rc])
                k.memset("pool", m01, 1.0, W=[b_rc])
                k.memset("pool", m01.rearrange("p (a b) -> p a b", b=64)[:, :, 0:1], 0.0, W=[b_rc])
                k.memset("pool", m01s, 1.0, W=[b_rc])
                k.memset("pool", m01s.rearrange("p (a b) -> p a b", b=16)[:, :, 0:1], 0.0, W=[b_rc])
                k.ts("dve", negw0, cpt[:, CP_W0:CP_W0 + 4], -1.0, R=[b_cp], W=[b_rc])
                k.ts("dve", omka, cpt[:, CP_KA:CP_KA + 4], -1.0, 1.0, ALU.mult, ALU.add, R=[b_cp], W=[b_rc])
                WS = Region(RA, NBIG); WW = Region(Wt, 8192)
                names = ["Pr", "Pk", "Pv", "Qr", "Qk", "Qv", "X12", "X12p", "TH12", "LW", "AA", "KK", "K2", "CUM", "E1", "E2", "E3", "E4", "WR", "T1", "T2", "OA", "OB", "OK", "OR", "OBE", "OKE", "OBON"]
                T_ = {n: WS.take([512]) for n in names}
                B_ = {n: Buf(n) for n in names}
                w2p = WW.take([512]); a2p = WW.take([512]); b_w2 = Buf("w2p")
                k.barrier([b_ra], list(B_.values()) + [b_w2], dummy[0:1, 0:1])
                k.memset("dve", w2p, 0.0, W=[b_w2]); k.memset("dve", a2p, 0.0, W=[b_w2])
                k.dma("sp", w2p[0:64, :], rw_w2[l], W=[b_w2])
                k.dma("sp", a2p[64:128, :], rw_a2[l], W=[b_w2])

                def load_shift(dst, bd, prv, bp, row0, tile_j, t0, nt):
                    k.dma("sp", dst[:, 0:nt], PF[row0:row0 + 128, t0:t0 + nt], R=[bPF], W=[bd])
                    if t0 == 0:
                        k.memset("dve", prv[:, 0:1], 0.0, W=[bp])
                        k.dma("sp", prv[:, 1:nt], PF[row0:row0 + 128, 0:nt - 1], R=[bPF], W=[bp])
                    elif t0 < TP:
                        k.dma("sp", prv[:, 0:nt], PF[row0:row0 + 128, t0 - 1:t0 + nt - 1], R=[bPF], W=[bp])
                    else:
                        for j in range(2):
                            k.dma("sp", prv[:, 16 * j:16 * j + 1], st_shift[l][j][:, tile_j:tile_j + 1], W=[bp], allow_slow_non_contiguous=True)
                            k.dma("sp", prv[:, 16 * j + 1:16 * j + 16], PF[row0:row0 + 128, t0 + 16 * j:t0 + 16 * j + 15], R=[bPF], W=[bp])
                    mu = cpt[:, CP_MU + tile_j:CP_MU + tile_j + 1]
                    k.tt("dve", prv[:, 0:nt], prv[:, 0:nt], dst[:, 0:nt], ALU.subtract, R=[bp, bd], W=[bp])
                    k.stt("dve", dst[:, 0:nt], prv[:, 0:nt], mu, dst[:, 0:nt], ALU.mult, ALU.add, R=[bp, bd, b_cp], W=[bd])

                for tbi, (t0, nt) in enumerate(TB):
                    sl = slice(0, nt)
                    load_shift(T_["X12"], B_["X12"], T_["X12p"], B_["X12p"], 1536, 12, t0, nt)
                    k.act(T_["TH12"][:, sl], T_["X12"][:, sl], AF.Tanh, R=[B_["X12"]], W=[B_["TH12"]])
                    msk = m01[:, 0:nt] if nt == 512 else m01s[:, 0:nt]
                    for hp in range(4):
                        load_shift(T_["Pr"], B_["Pr"], T_["Qr"], B_["Qr"], 128 * hp, hp, t0, nt)
                        load_shift(T_["Pk"], B_["Pk"], T_["Qk"], B_["Qk"], 512 + 128 * hp, 4 + hp, t0, nt)
                        load_shift(T_["Pv"], B_["Pv"], T_["Qv"], B_["Qv"], 1024 + 128 * hp, 8 + hp, t0, nt)
                        c1 = lambda off: cpt[:, off + hp:off + hp + 1]
                        pl, bpl = next_ps(0, 4)
                        k.mm(pl[:, sl], w2p[:, 128 * hp:128 * hp + 128], T_["TH12"][:, sl], R=[b_w2, B_["TH12"]], W=[bpl])
                        k.act(T_["E1"][:, sl], pl[:, sl], AF.Exp, bias=negw0[:, hp:hp + 1], scale=-1.0, R=[bpl, b_rc], W=[B_["E1"]])
                        k.act(T_["E1"][:, sl], T_["E1"][:, sl], AF.Ln, bias=1.0, R=[B_["E1"]], W=[B_["E1"]])
                        k.act(T_["E1"][:, sl], T_["E1"][:, sl], AF.Exp, bias=-0.5, scale=-1.0, R=[B_["E1"]], W=[B_["E1"]])
                        k.ts("dve", T_["LW"][:, sl], T_["E1"][:, sl], -1.0, R=[B_["E1"]], W=[B_["LW"]])
                        pa, bpa = next_ps(0, 4)
                        k.mm(pa[:, sl], a2p[:, 128 * hp:128 * hp + 128], T_["X12"][:, sl], R=[b_w2, B_["X12"]], W=[bpa])
                        k.act(T_["AA"][:, sl], pa[:, sl], AF.Sigmoid, bias=c1(CP_A0), R=[bpa, b_cp], W=[B_["AA"]])
                        k.ts("dve", T_["KK"][:, sl], T_["Pk"][:, sl], c1(CP_KK), R=[B_["Pk"], b_cp], W=[B_["KK"]])
                        k.tt("dve", T_["T1"][:, sl], T_["KK"][:, sl], T_["KK"][:, sl], ALU.mult, R=[B_["KK"]], W=[B_["T1"]])
                        pn, bpn = next_ps(4, 8)
                        k.mm(pn[:, sl], bones, T_["T1"][:, sl], R=[b_c, B_["T1"]], W=[bpn])
                        k.act(T_["T2"][:, sl], pn[:, sl], AF.Sqrt, R=[bpn], W=[B_["T2"]])
                        k.ts("dve", T_["T2"][:, sl], T_["T2"][:, sl], 1e-12, None, ALU.max, R=[B_["T2"]], W=[B_["T2"]])
                        k.recip(T_["T2"][:, sl], T_["T2"][:, sl], R=[B_["T2"]], W=[B_["T2"]])
                        k.tt("dve", T_["KK"][:, sl], T_["KK"][:, sl], T_["T2"][:, sl], ALU.mult, R=[B_["KK"], B_["T2"]], W=[B_["KK"]])
                        k.ts("dve", T_["T1"][:, sl], T_["AA"][:, sl], c1(CP_KA), omka[:, hp:hp + 1], ALU.mult, ALU.add, R=[B_["AA"], b_cp, b_rc], W=[B_["T1"]])
                        k.tt("dve", T_["K2"][:, sl], T_["Pk"][:, sl], T_["T1"][:, sl], ALU.mult, R=[B_["Pk"], B_["T1"]], W=[B_["K2"]])
                        k.scan(T_["CUM"][:, sl], msk, T_["LW"][:, sl], 0.0, R=[b_rc, B_["LW"]], W=[B_["CUM"]])
                        csz = 64 if nt == 512 else 16
                        cv = T_["CUM"][:, sl].rearrange("p (a b) -> p a b", b=csz)
                        nch_ = nt // csz
                        ch0 = (t0 // 64) if nt == 512 else 32
                        k.act(pcs[:, hp, ch0:ch0 + nch_], cv[:, :, csz - 1], AF.Exp, R=[B_["CUM"]], W=[b_pcs])
                        k.act(T_["E1"][:, sl], T_["CUM"][:, sl], AF.Exp, R=[B_["CUM"]], W=[B_["E1"]])
                        k.act(T_["E2"][:, sl], T_["CUM"][:, sl], AF.Exp, scale=-1.0, R=[B_["CUM"]], W=[B_["E2"]])
                        k.tt("dve", T_["T1"][:, sl], T_["CUM"][:, sl], T_["LW"][:, sl], ALU.subtract, R=[B_["CUM"], B_["LW"]], W=[B_["T1"]])
                        k.act(T_["E3"][:, sl], T_["T1"][:, sl], AF.Exp, R=[B_["T1"]], W=[B_["E3"]])
                        k.tt("dve", T_["T2"][:, sl].rearrange("p (a b) -> p a b", b=csz), cv[:, :, csz - 1:csz].to_broadcast([128, nch_, csz]), cv, ALU.subtract, R=[B_["CUM"]], W=[B_["T2"]])
                        k.act(T_["E4"][:, sl], T_["T2"][:, sl], AF.Exp, R=[B_["T2"]], W=[B_["E4"]])
                        k.stt("dve", T_["OA"][:, sl], T_["KK"][:, sl], -1.0, T_["E3"][:, sl], ALU.mult, ALU.mult, R=[B_["KK"], B_["E3"]], W=[B_["OA"]])
                        k.tt("dve", T_["WR"][:, sl], T_["KK"][:, sl], T_["AA"][:, sl], ALU.mult, R=[B_["KK"], B_["AA"]], W=[B_["WR"]])
                        k.tt("dve", T_["OB"][:, sl], T_["WR"][:, sl], T_["E2"][:, sl], ALU.mult, R=[B_["WR"], B_["E2"]], W=[B_["OB"]])
                        k.tt("dve", T_["OBE"][:, sl], T_["WR"][:, sl], T_["E4"][:, sl], ALU.mult, R=[B_["WR"], B_["E4"]], W=[B_["OBE"]])
                        k.tt("dve", T_["OK"][:, sl], T_["K2"][:, sl], T_["E2"][:, sl], ALU.mult, R=[B_["K2"], B_["E2"]], W=[B_["OK"]])
                        k.tt("dve", T_["OKE"][:, sl], T_["K2"][:, sl], T_["E4"][:, sl], ALU.mult, R=[B_["K2"], B_["E4"]], W=[B_["OKE"]])
                        k.tt("dve", T_["OR"][:, sl], T_["Pr"][:, sl], T_["E1"][:, sl], ALU.mult, R=[B_["Pr"], B_["E1"]], W=[B_["OR"]])
                        k.stt("dve", T_["T1"][:, sl], T_["Pr"][:, sl], c1(CP_RK), T_["K2"][:, sl], ALU.mult, ALU.mult, R=[B_["Pr"], B_["K2"], b_cp], W=[B_["T1"]])
                        pb, bpb = next_ps(4, 8)
                        k.mm(pb[:, sl], bones, T_["T1"][:, sl], R=[b_c, B_["T1"]], W=[bpb])
                        k.tt("dve", T_["OBON"][:, sl], T_["Pv"][:, sl], pb[:, sl], ALU.mult, R=[B_["Pv"], bpb], W=[B_["OBON"]])
                        for ai, nm in enumerate(["OA", "OB", "OK", "OR", "OBE", "OKE", "Pv", "OBON"]):
                            k.dma("sp", RWS[512 * ai + 128 * hp:512 * ai + 128 * hp + 128, t0:t0 + nt], T_[nm][:, sl], R=[B_[nm], bSCR])
                if True:
                    for si, (s0_, sn) in enumerate(SEGS):
                        e_ = s0_ + sn - 1
                        k.dma("sp", o_shift[l][si].rearrange("(a b) -> a b", b=1), PF[0:1664, e_:e_ + 1], R=[bPF], allow_slow_non_contiguous=True)
                b_r2 = Buf("r2")
                k.barrier(list(B_.values()) + [b_w2], [b_r2], dummy[0:1, 0:1])
                fence(bSCR)
                if upto == "R1":
                    break
                WS = Region(RA, NBIG); WW = Region(Wt, 8192)
                LD = [[WW.take([4, 64]) for _ in range(7)] for _ in range(2)]
                b_LD = [[Buf(f"ld{i}{j}") for j in range(7)] for i in range(2)]
                abd = WW.take([4, 2, 64]); bbd = WW.take([4, 2, 64]); rbd = WW.take([4, 2, 64])
                b_abd, b_bbd, b_rbd = Buf("abd"), Buf("bbd"), Buf("rbd")
                def t64(n):
                    return WS.take([512], F32, 64), Buf(n)
                def t128(n, w=512):
                    return WS.take([w]), Buf(n)
                atok, b_atok = t64("atok"); betok, b_betok = t64("betok"); ketok, b_ketok = t64("ketok"); vtok, b_vtok = t64("vtok")
                N1, b_N1 = t64("N1"); A1, b_A1 = t64("A1"); AakT, b_AakT = t64("AakT"); BrbT, b_BrbT = t64("BrbT"); BrkT, b_BrkT = t64("BrkT")
                PA = [t64("PA0"), t64("PA1")]; PN = [t64("PN0"), t64("PN1")]; ZZ = [t64("Z0"), t64("Z1")]
                W1, b_W1 = t64("W1"); U0, b_U0 = t64("U0"); Ap, b_Ap = t64("Ap")
                GTp, b_GTp = t128("GTp"); Hbd, b_Hbd = t128("Hbd")
                RpT, b_RpT = t128("RpT", 256); Y0d, b_Y0d = t128("Y0d", 256); yout, b_yout = t128("yout", 256)
                Sb = [t128("Sb0"), t128("Sb1")]; SIt, b_SI = t128("SI"); SIn = [t128("SIa"), t128("SIb")]; SF = [t128("SFa"), t128("SFb")]
                SO, b_SO = t128("SO")
                allb = [b for row in b_LD for b in row] + [b_abd, b_bbd, b_rbd, b_atok, b_betok, b_ketok, b_vtok, b_N1, b_A1, b_AakT, b_BrbT, b_BrkT, b_W1, b_U0, b_Ap, b_GTp, b_Hbd, b_RpT, b_Y0d, b_yout, b_SI, b_SO] + [x[1] for x in PA + PN + ZZ + Sb + SIn + SF]
                k.barrier([b_r2], allb, dummy[0:1, 0:1])
                v3 = lambda t, C: t[0:C, :].rearrange("c (h x) -> c h x", h=8)[:, :, 0:C]
                p4 = lambda t: t[:, :].rearrange("q (p x) -> q p x", p=4)
                for t_ in (abd, bbd, rbd):
                    k.memset("dve", t_, 0.0, W=[b_abd, b_bbd, b_rbd])
                k.memset("dve", Sb[0][0], 0.0, W=[Sb[0][1]])
                k.memset("dve", SIt, 0.0, W=[b_SI])
                for j in range(2):
                    for h_ in range(8):
                        p_, hf = h_ // 2, h_ % 2
                        k.dma("sp", p4(SIt)[64 * hf:64 * hf + 64, p_, 64 * hf:64 * hf + 64], st_wkv[l][j][h_], W=[b_SI])
                    pt, bpt = next_ps(0, 8)
                    for p_ in range(4):
                        k.tr(pt[:, 128 * p_:128 * p_ + 128], p4(SIt)[:, p_, :], identf, R=[b_SI, b_c], W=[bpt])
                    k.cp("act", SIn[j][0], pt[:, :], R=[bpt], W=[SIn[j][1]])

                def emit_state(Sbuf, bS, seg):
                    pt, bpt = next_ps(0, 8)
                    for p_ in range(4):
                        k.tr(pt[:, 128 * p_:128 * p_ + 128], p4(Sbuf)[:, p_, :], identf, R=[bS, b_c], W=[bpt])
                    k.cp("act", SO, pt[:, :], R=[bpt], W=[b_SO])
                    for h_ in range(8):
                        p_, hf = h_ // 2, h_ % 2
                        k.dma("sp", o_wkv[l][seg][h_], p4(SO)[64 * hf:64 * hf + 64, p_, 64 * hf:64 * hf + 64], R=[b_SO])

                def ld_chunk(cj):
                    t0_, C_ = CH[cj]
                    for ai in range(7):
                        k.dma("sp", LD[cj % 2][ai][:, :, 0:C_], RWS[512 * ai:512 * ai + 512, t0_:t0_ + C_].rearrange("(p c) t -> c p t", c=128), R=[bSCR], W=[b_LD[cj % 2][ai]])

                ld_chunk(0)
                for ci, (t0, C) in enumerate(CH):
                    ld, bld = LD[ci % 2], b_LD[ci % 2]
                    if ci + 1 < NCH:
                        ld_chunk(ci + 1)
                    at, bt, kt_, rt, be, ke, vv = ld
                    b_at, b_bt, b_kt, b_rt, b_be, b_ke, b_vv = bld
                    for (src, bsrc, dst, bdst) in ((at, b_at, abd, b_abd), (bt, b_bt, bbd, b_bbd), (rt, b_rt, rbd, b_rbd)):
                        k.cp("pool", dst[0:64, :, 0, 0:C], src[0:64, :, 0:C], R=[bsrc], W=[bdst])
                        k.cp("pool", dst[64:128, :, 1, 0:C], src[64:128, :, 0:C], R=[bsrc], W=[bdst])
                    for (src, bsrc, dst, bdst) in ((at, b_at, atok, b_atok), (be, b_be, betok, b_betok), (ke, b_ke, ketok, b_ketok), (vv, b_vv, vtok, b_vtok)):
                        pt, bpt = next_ps(0, 8)
                        for p_ in range(4):
                            k.tr(pt[0:C, 128 * p_:128 * p_ + 128], src[:, p_, 0:C], identf, R=[bsrc, b_c], W=[bpt])
                        k.cp(ev_eng(), dst[0:C, :], pt[0:C, :], R=[bpt], W=[bdst])

                    def mat(dst, bdst, lh, blh, rhbd, brh, mask):
                        pm, bpm = next_ps(0, 8)
                        pv = pm[0:C, :].rearrange("c (p h t) -> c p h t", p=4, h=2)
                        for p_ in range(4):
                            k.mm(pv[:, p_, :, 0:C], lh[:, p_, 0:C], rhbd[:, p_, :, 0:C], R=[blh, brh], W=[bpm])
                        k.tt("dve", v3(dst, C), v3(pm, C), mask[0:C, 0:C].unsqueeze(1).to_broadcast([C, 8, C]), ALU.mult, R=[bpm, b_rc], W=[bdst])
                    mat(N1, b_N1, bt, b_bt, abd, b_abd, triS)
                    mat(A1, b_A1, at, b_at, bbd, b_bbd, triL)
                    mat(AakT, b_AakT, kt_, b_kt, abd, b_abd, triS)
                    mat(BrbT, b_BrbT, bt, b_bt, rbd, b_rbd, triI)
                    mat(BrkT, b_BrkT, kt_, b_kt, rbd, b_rbd, triI)
                    Zc, bZc = ZZ[0]
                    k.tt("dve", v3(Zc, C), v3(N1, C), identf[0:C, 0:C].unsqueeze(1).to_broadcast([C, 8, C]), ALU.add, R=[b_N1, b_c], W=[bZc])
                    curA, bcA, curN, bcN = A1, b_A1, N1, b_N1
                    rounds = int(round(math.log2(C))) - 1
                    for r_ in range(rounds):
                        nA, bnA = PA[r_ % 2]; nN, bnN = PN[r_ % 2]
                        psN, bpsN = next_ps(0, 8); psA, bpsA = next_ps(0, 8)
                        for h_ in range(8):
                            k.mm(psN[0:C, 64 * h_:64 * h_ + C], curA[0:C, 64 * h_:64 * h_ + C], curN[0:C, 64 * h_:64 * h_ + C], R=[bcA, bcN], W=[bpsN])
                            k.mm(psA[0:C, 64 * h_:64 * h_ + C], curN[0:C, 64 * h_:64 * h_ + C], curA[0:C, 64 * h_:64 * h_ + C], R=[bcA, bcN], W=[bpsA])
                        k.cp("act", v3(nN, C), v3(psN, C), R=[bpsN], W=[bnN])
                        k.cp("dve", v3(nA, C), v3(psA, C), R=[bpsA], W=[bnA])
                        psZ, bpsZ = next_ps(0, 8)
                        for h_ in range(8):
                            k.mm(psZ[0:C, 64 * h_:64 * h_ + C], nA[0:C, 64 * h_:64 * h_ + C], Zc[0:C, 64 * h_:64 * h_ + C], R=[bnA, bZc], W=[bpsZ])
                        Zn, bZn = ZZ[(r_ + 1) % 2]
                        k.tt("dve", v3(Zn, C), v3(psZ, C), v3(Zc, C), ALU.add, R=[bpsZ, bZc], W=[bZn])
                        Zc, bZc = Zn, bZn
                        curA, bcA, curN, bcN = nA, bnA, nN, bnN
                    psW, bpsW = next_ps(0, 8)
                    for h_ in range(8):
                        k.mm(psW[0:C, 64 * h_:64 * h_ + 64], AakT[0:C, 64 * h_:64 * h_ + C], vtok[0:C, 64 * h_:64 * h_ + 64], R=[b_AakT, b_vtok], W=[bpsW])
                    k.cp("act", W1[0:C, :], psW[0:C, :], R=[bpsW], W=[b_W1])
                    psU, bpsU = next_ps(0, 8); psP, bpsP = next_ps(0, 8)
                    for h_ in range(8):
                        k.mm(psU[0:C, 64 * h_:64 * h_ + 64], Zc[0:C, 64 * h_:64 * h_ + C], W1[0:C, 64 * h_:64 * h_ + 64], R=[bZc, b_W1], W=[bpsU])
                        k.mm(psP[0:C, 64 * h_:64 * h_ + 64], Zc[0:C, 64 * h_:64 * h_ + C], atok[0:C, 64 * h_:64 * h_ + 64], R=[bZc, b_atok], W=[bpsP])
                    k.cp("act", U0[0:C, :], psU[0:C, :], R=[bpsU], W=[b_U0])
                    k.cp("dve", Ap[0:C, :], psP[0:C, :], R=[bpsP], W=[b_Ap])
                    psG, bpsG = next_ps(0, 8); psH, bpsH = next_ps(0, 8)
                    for p_ in range(4):
                        cs = slice(128 * p_, 128 * p_ + 128)
                        k.mm(psG[:, cs], Ap[0:C, cs], betok[0:C, cs], R=[b_Ap, b_betok], W=[bpsG])
                        k.mm(psH[:, cs], betok[0:C, cs], U0[0:C, cs], start=True, stop=False, R=[b_betok, b_U0], W=[bpsH])
                        k.mm(psH[:, cs], ketok[0:C, cs], vtok[0:C, cs], start=False, stop=True, R=[b_ketok, b_vtok], W=[bpsH])
                    bo_b = bones.unsqueeze(1).to_broadcast([128, 4, 128])
                    k.tt("dve", p4(GTp), p4(psG), bo_b, ALU.mult, R=[bpsG, b_c], W=[b_GTp])
                    for p_ in range(4):
                        k.stt("dve", p4(GTp)[:, p_, :], identf, pcs[:, p_, ci:ci + 1], p4(GTp)[:, p_, :], ALU.mult, ALU.add, R=[b_c, b_pcs, b_GTp], W=[b_GTp])
                    k.tt("dve", p4(Hbd), p4(psH), bo_b, ALU.mult, R=[bpsH, b_c], W=[b_Hbd])
                    if ci < 32:
                        Scur, bScur = Sb[ci % 2]; Snew, bSnew = Sb[(ci + 1) % 2]
                    else:
                        Scur, bScur = SIn[ci - 32]; Snew, bSnew = SF[ci - 32]
                    psS, bpsS = next_ps(0, 8)
                    for p_ in range(4):
                        cs = slice(128 * p_, 128 * p_ + 128)
                        k.mm(psS[:, cs], GTp[:, cs], Scur[:, cs], R=[b_GTp, bScur], W=[bpsS])
                    k.tt("dve", Snew, psS[:, :], Hbd, ALU.add, R=[bpsS, b_Hbd], W=[bSnew])
                    psR, bpsR = next_ps(0, 8); psY0, bpsY0 = next_ps(0, 8)
                    psRv = psR[:, :].rearrange("q (p h t) -> q p h t", p=4, h=2)
                    psY0v = psY0[:, :].rearrange("q (p h t) -> q p h t", p=4, h=2)
                    Brb4 = BrbT[0:C, :].rearrange("c (p h t) -> c p h t", p=4, h=2)
                    Brk4 = BrkT[0:C, :].rearrange("c (p h t) -> c p h t", p=4, h=2)
                    for p_ in range(4):
                        cs = slice(128 * p_, 128 * p_ + 128)
                        k.mm(psRv[:, p_, :, 0:C], Ap[0:C, cs], Brb4[:, p_, :, 0:C], R=[b_Ap, b_BrbT], W=[bpsR])
                        k.mm(psY0v[:, p_, :, 0:C], U0[0:C, cs], Brb4[:, p_, :, 0:C], start=True, stop=False, R=[b_U0, b_BrbT], W=[bpsY0])
                        k.mm(psY0v[:, p_, :, 0:C], vtok[0:C, cs], Brk4[:, p_, :, 0:C], start=False, stop=True, R=[b_vtok, b_BrkT], W=[bpsY0])
                    R4 = RpT[:, :].rearrange("q (p t) -> q p t", p=4); Y4 = Y0d[:, :].rearrange("q (p t) -> q p t", p=4); yo4 = yout[:, :].rearrange("q (p t) -> q p t", p=4)
                    for hf in range(2):
                        ps_ = slice(64 * hf, 64 * hf + 64)
                        k.tt("dve", R4[ps_, :, 0:C], psRv[ps_, :, hf, 0:C], rt[ps_, :, 0:C], ALU.add, R=[bpsR, b_rt], W=[b_RpT])
                        k.cp("act", Y4[ps_, :, 0:C], psY0v[ps_, :, hf, 0:C], R=[bpsY0], W=[b_Y0d])
                    psY, bpsY = next_ps(0, 8)
                    psYv = psY[:, 0:256].rearrange("q (p t) -> q p t", p=4)
                    for p_ in range(4):
                        cs = slice(128 * p_, 128 * p_ + 128)
                        k.mm(psYv[:, p_, 0:C], Scur[:, cs], R4[:, p_, 0:C], R=[bScur, b_RpT], W=[bpsY])
                    k.tt("dve", yo4[:, :, 0:C], psYv[:, :, 0:C], Y4[:, :, 0:C], ALU.add, R=[bpsY, b_Y0d], W=[b_yout])
                    k.dma("sp", YRW[:, t0:t0 + C].rearrange("(p c) t -> c p t", c=128), yo4[:, :, 0:C], R=[b_yout, bSCR])
                    if ci == 31:
                        emit_state(Sb[0][0], Sb[0][1], 0)
                    elif ci >= 32:
                        emit_state(SF[ci - 32][0], SF[ci - 32][1], 1 + ci - 32)
                b_r3 = Buf("r3")
                k.barrier(allb, [b_r3], dummy[0:1, 0:1])
                fence(bSCR)
                if upto == "R2":
                    break
                WS = Region(RA, NBIG)
                yr = [WS.take([512]), WS.take([512])]; bo = [WS.take([512]), WS.take([512])]; ga = [WS.take([512]), WS.take([512])]
                b_yr = [Buf("yr0"), Buf("yr1")]; b_bo = [Buf("bo0"), Buf("bo1")]; b_ga = [Buf("ga0"), Buf("ga1")]
                yc = WS.take([512]); sqq = WS.take([512]); rs_ = WS.take([512])
                b_yc, b_sqq, b_rs = Buf("yc"), Buf("sqq"), Buf("rs")
                k.barrier([b_r3], b_yr + b_bo + b_ga + [b_yc, b_sqq, b_rs], dummy[0:1, 0:1])
                it = 0
                for hp in range(4):
                    for (t0, nt) in TB:
                        i2 = it % 2
                        it += 1
                        sl = slice(0, nt)
                        k.dma("sp", yr[i2][:, sl], YRW[128 * hp:128 * hp + 128, t0:t0 + nt], R=[bSCR], W=[b_yr[i2]])
                        k.dma("sp", bo[i2][:, sl], RWS[7 * 512 + 128 * hp:7 * 512 + 128 * hp + 128, t0:t0 + nt], R=[bSCR], W=[b_bo[i2]])
                        k.dma("sp", ga[i2][:, sl], PF[3200 + 128 * hp:3200 + 128 * hp + 128, t0:t0 + nt], R=[bPF], W=[b_ga[i2]])
                        pm, bpm = next_ps(0, 4)
                        k.mm(pm[:, sl], bones, yr[i2][:, sl], R=[b_c, b_yr[i2]], W=[bpm])
                        k.stt("dve", yc[:, sl], pm[:, sl], -1.0 / 64, yr[i2][:, sl], ALU.mult, ALU.add, R=[bpm, b_yr[i2]], W=[b_yc])
                        k.tt("dve", sqq[:, sl], yc[:, sl], yc[:, sl], ALU.mult, R=[b_yc], W=[b_sqq])
                        pv, bpv = next_ps(4, 8)
                        k.mm(pv[:, sl], bones, sqq[:, sl], R=[b_c, b_sqq], W=[bpv])
                        k.ts("dve", rs_[:, sl], pv[:, sl], 1.0 / 64, 64e-5, ALU.mult, ALU.add, R=[bpv], W=[b_rs])
                        k.act(rs_[:, sl], rs_[:, sl], AF.Sqrt, R=[b_rs], W=[b_rs])
                        k.recip(rs_[:, sl], rs_[:, sl], R=[b_rs], W=[b_rs])
                        k.tt("dve", yc[:, sl], yc[:, sl], rs_[:, sl], ALU.mult, R=[b_yc, b_rs], W=[b_yc])
                        k.ts("dve", yc[:, sl], yc[:, sl], cpt[:, CP_LG + hp:CP_LG + hp + 1], cpt[:, CP_LB + hp:CP_LB + hp + 1], ALU.mult, ALU.add, R=[b_yc, b_cp], W=[b_yc])
                        k.tt("dve", yc[:, sl], yc[:, sl], bo[i2][:, sl], ALU.add, R=[b_yc, b_bo[i2]], W=[b_yc])
                        k.tt("dve", gT[:, hp, t0:t0 + nt], yc[:, sl], ga[i2][:, sl], ALU.mult, R=[b_yc, b_ga[i2]], W=[b_gT])
                k.barrier(b_yr + b_bo + b_ga + [b_yc, b_sqq, b_rs, b_rc, b_pcs], [b_ra, bW[0], bW[1]], dummy[0:1, 0:1])
            if upto == "C2":
                k.dma("sp", GDBG, RB[:, :].bitcast(BF16), R=[b_gT])
                break
            S.serial = False
            if "C2" in SKIP:
                k.memset("dve", gT[:, 0:4, :], 0.0, W=[b_gT])

            hT2 = RA[:, :].bitcast(BF16).rearrange("p (a b) -> p a b", a=16)
            b_h2 = Buf("hT2")
            WW = Region(Wt, 8192); MW = Region(Mt, 9216); MW.off = MISC_BASE
            Wm = [WW.take([16, 256], BF16), WW.take([16, 256], BF16)]
            Wb = [WW.take([4, 256], BF16), WW.take([4, 256], BF16)]
            acb1 = MW.take([NTOK], BF16)
            acb = [acb1, acb1]
            dst_ = [WW.take([4, 256]) for _ in range(2)]
            b_dst = [Buf(f"dst{i}") for i in range(2)]
            mgts = [MW.take([512]), WW.take([512])]; b_mgts = [Buf("mg0"), Buf("mg1")]
            accs = MW.take([2, NTOK]); mgt = mgts[0]; tmpt = MW.take([512])
            b_Wm = [Buf("Wm0"), Buf("Wm1")]; b_acb1 = Buf("acb"); b_acb = [b_acb1, b_acb1]
            b_accs, b_mgt, b_tmpt = Buf("accs"), Buf("mgt"), Buf("tmpt")
            k.barrier([b_ra, bW[0], bW[1]], [b_h2, b_accs, b_mgt, b_tmpt, b_acb1] + b_Wm + b_dst + b_mgts, dummy[0:1, 0:1])
            dsc = [0]
            mgc = [0]

            def dload(dst_ap, src_ap, bdst):
                i = dsc[0] % 2
                dsc[0] += 1
                k.dma("sp", dst_[i], src_ap.rearrange("p (kt c) -> p kt c", kt=4), W=[b_dst[i]])
                k.cp("pool", dst_ap, dst_[i], R=[b_dst[i]], W=[bdst])
            k.dma("sp", RA[:, :].bitcast(BF16), HTS, R=[bHTS], W=[b_h2])
            wi = 0
            ai = 0

            def dgroup(g):
                dblk_, b_ = g // 4, g % 4
                wm_, wb_, bwm = Wm[g % 2], Wb[g % 2], b_Wm[g % 2]
                for kq in range(4):
                    dload(wm_[:, 4 * kq:4 * kq + 4, :], w_mg[l][dblk_][b_][:, 1024 * kq:1024 * (kq + 1)], bwm)
                dload(wb_, w_br[l][dblk_][b_], bwm)
                return wm_, wb_, bwm

            nxtg = dgroup(0)
            for dblk in range(8):
                for b in range(4):
                    wm_, wb_, bwm = nxtg
                    if 4 * dblk + b + 1 < 32:
                        nxtg = dgroup(4 * dblk + b + 1)
                    for dt in range(2):
                        bcol = cpt[:, CP_BMG + 16 * b + 2 * dblk + dt:CP_BMG + 16 * b + 2 * dblk + dt + 1]
                        for (t0, nt) in TB:
                            pm, bpm = next_ps(0, 4)
                            pu, bpu = next_ps(4, 8)
                            for kt in range(16):
                                k.mm(pm[:, 0:nt], wm_[:, kt, 128 * dt:128 * dt + 128], hT2[:, kt, t0:t0 + nt], start=(kt == 0), stop=(kt == 15), R=[bwm, b_h2], W=[bpm])
                            for kt in range(4):
                                k.mm(pu[:, 0:nt], wb_[:, kt, 128 * dt:128 * dt + 128], gT[:, 4 * b + kt, t0:t0 + nt], start=(kt == 0), stop=(kt == 3), R=[bwm, b_gT], W=[bpu])
                            mg_, bmg_ = mgts[mgc[0] % 2], b_mgts[mgc[0] % 2]
                            mgc[0] += 1
                            k.act(mg_[:, 0:nt], pm[:, 0:nt], AF.Sigmoid, bias=bcol, R=[bpm, b_cp], W=[bmg_])
                            if b == 0:
                                k.tt("dve", accs[:, dt, t0:t0 + nt], mg_[:, 0:nt], pu[:, 0:nt], ALU.mult, R=[bmg_, bpu], W=[b_accs])
                            else:
                                k.tt("dve", tmpt[:, 0:nt], mg_[:, 0:nt], pu[:, 0:nt], ALU.mult, R=[bmg_, bpu], W=[b_tmpt])
                                k.tt("dve", accs[:, dt, t0:t0 + nt], accs[:, dt, t0:t0 + nt], tmpt[:, 0:nt], ALU.add, R=[b_tmpt, b_accs], W=[b_accs])
                for dt in range(2):
                    a_, ba_ = acb[ai % 2], b_acb[ai % 2]
                    ai += 1
                    k.cp("act", a_, accs[:, dt, :], R=[b_accs], W=[ba_])
                    r0 = 256 * dblk + 128 * dt
                    k.dma("sp", ACC[r0:r0 + 128, :], a_, R=[ba_, bACC])
            if upto == "D":
                break
            fence(bACC)
            accT = RA[:, :].bitcast(BF16).rearrange("p (a b) -> p a b", a=16)
            b_aT = Buf("accT")
            WS = Region(RB, NBIG)
            xo = [WS.take([512]), WS.take([512])]
            b_xo = [Buf("xo0"), Buf("xo1")]
            est = [WS.take([4, 512]) for _ in range(3)]
            b_est = [Buf(f"est{i}") for i in range(3)]
            k.barrier([b_h2, b_gT, b_accs, b_mgt, b_tmpt, b_acb1] + b_Wm + b_dst + b_mgts, [b_aT, bW[0], bW[1]] + b_xo + b_est, dummy[0:1, 0:1])
            k.dma("sp", accT, ACC.rearrange("(dt p) t -> p dt t", p=128), R=[bACC], W=[b_aT])
            xi = 0
            nxtw = load_wblock(w_out[l][0], 512, est, b_est)
            for cb in range(4):
                wb, bw = nxtw
                if cb + 1 < 4:
                    nxtw = load_wblock(w_out[l][cb + 1], 512, est, b_est)
                for (r0, nr) in TT:
                    x_, bx_ = xo[xi % 2], b_xo[xi % 2]
                    xi += 1
                    k.dma("sp", x_[0:nr, :], Xs[l][r0:r0 + nr, 512 * cb:512 * cb + 512], R=[bXs[l]], W=[bx_])
                    pp, bp = next_ps(0, 4)
                    for dt in range(16):
                        k.mm(pp[0:nr, :], accT[:, dt, r0:r0 + nr], wb[:, dt, :], start=(dt == 0), stop=(dt == 15), R=[b_aT, bw], W=[bp])
                    k.tt("dve", x_[0:nr, :], x_[0:nr, :], pp[0:nr, :], ALU.add, R=[bx_, bp], W=[bx_])
                    k.dma("sp", Xs[l + 1][r0:r0 + nr, 512 * cb:512 * cb + 512], x_[0:nr, :], R=[bx_, bXs[l + 1]])
            k.barrier([b_aT, bW[0], bW[1]] + b_xo + b_est, [bRA, bRB], dummy[0:1, 0:1])
            if upto == "E":
                break
        else:
            fence(*DRB)
            WS = Region(R1t, NBIG)
            xt = [WS.take([D]), WS.take([D])]
            bxt = [Buf("xt0"), Buf("xt1")]
            sq = WS.take([D]); b_sq = Buf("sq")
            gt = WS.take([D]); b_gt = Buf("gt")
            yo = [WS.take([D]), WS.take([D])]; b_yo = [Buf("yo0"), Buf("yo1")]
            ss = WS.take([8]); b_ss = Buf("ss")
            k.barrier([bR1, bR2], [bxt[0], bxt[1], b_sq, b_gt, b_ss] + b_yo, dummy[0:1, 0:1])
            k.dma("sp", gt, fin_g.partition_broadcast(128), W=[b_gt])
            for ti, (r0, nr) in enumerate(TT):
                xb, bx = xt[ti % 2], bxt[ti % 2]
                y_, by_ = yo[ti % 2], b_yo[ti % 2]
                k.dma("sp", xb[0:nr, :], Xs[L][r0:r0 + nr, :], R=[bXs[L]], W=[bx])
                k.act(sq[0:nr, :], xb[0:nr, :], AF.Square, R=[bx], W=[b_sq])
                k.red(ss[0:nr, 0:1], sq[0:nr, :], R=[b_sq], W=[b_ss])
                k.ts("dve", ss[0:nr, 1:2], ss[0:nr, 0:1], 1.0 / D, 1e-6, ALU.mult, ALU.add, R=[b_ss], W=[b_ss])
                k.act(ss[0:nr, 2:3], ss[0:nr, 1:2], AF.Sqrt, R=[b_ss], W=[b_ss])
                k.recip(ss[0:nr, 3:4], ss[0:nr, 2:3], R=[b_ss], W=[b_ss])
                k.stt("dve", y_[0:nr, :], xb[0:nr, :], ss[0:nr, 3:4], gt[0:nr, :], ALU.mult, ALU.mult, R=[bx, b_ss, b_gt], W=[by_])
                k.dma("sp", o_y[r0:r0 + nr, :], y_[0:nr, :], R=[by_])
        S.emit()
        print("ops", {e: len(v) for e, v in S.ops.items()})
    return nc


def _col(v, ntile):
    return np.ascontiguousarray(np.asarray(v).reshape(ntile, 128).T)


_SHARED = {}


def prep_shared(inp):
    f = lambda a: np.ascontiguousarray(np.asarray(a, dtype=np.float32))
    w_in = np.asarray(inp["w_in"], dtype=np.float32)
    sh = {}
    idx_fm = np.concatenate([np.arange(0, 2176), np.arange(2848, 2848 + 1024), np.arange(4384, 6432)])
    kpe0 = 2176 + 640
    idx_tm = np.concatenate([np.arange(2176, 2176 + 384), np.arange(kpe0, kpe0 + 32),
                             np.arange(kpe0 + 16, kpe0 + 32), np.arange(kpe0, kpe0 + 16),
                             np.arange(2176 + 384, 2176 + 640), np.arange(2848 + 512, 2848 + 1536)])
    assert idx_fm.size == NFM and idx_tm.size == NTM
    def blk(a, nk):
        Ln, _, nc_ = a.shape
        return np.ascontiguousarray(a.reshape(Ln, nk, 128, nc_).transpose(0, 2, 1, 3).reshape(Ln, 128, nk * nc_))
    wfm = w_in[:, :, idx_fm]
    sh["w_fm"] = np.ascontiguousarray(np.stack([blk(wfm[:, :, c0:c0 + 512], 16) for c0 in FM_BLOCKS], 1))
    sh["w_tm"] = blk(w_in[:, :, idx_tm], 16)
    wmg = w_in[:, :, 6432:]
    sh["w_mg"] = np.ascontiguousarray(np.stack([np.stack([blk(wmg[:, :, b * 2048 + 256 * d:b * 2048 + 256 * d + 256], 16) for b in range(4)], 1) for d in range(8)], 1))
    wbr = f(inp["w_branch"])
    sh["w_br"] = np.ascontiguousarray(np.stack([np.stack([blk(wbr[:, b, :, 256 * d:256 * d + 256], 4) for b in range(4)], 1) for d in range(8)], 1))
    wo = f(inp["w_out"])
    sh["w_out"] = np.ascontiguousarray(np.stack([blk(wo[:, :, 512 * c:512 * c + 512], 16) for c in range(4)], 1))
    cp = np.zeros((L, 128, NCP), np.float32)
    for l in range(L):
        cp[l, :, CP_MU:CP_MU + 13] = _col(inp["rw_mu"][l], 13)
        cp[l, :, CP_W0:CP_W0 + 4] = _col(inp["rw_w0"][l], 4)
        cp[l, :, CP_A0:CP_A0 + 4] = _col(inp["rw_a0"][l], 4)
        cp[l, :, CP_KK:CP_KK + 4] = _col(inp["rw_k_k"][l], 4)
        cp[l, :, CP_KA:CP_KA + 4] = _col(inp["rw_k_a"][l], 4)
        cp[l, :, CP_RK:CP_RK + 4] = _col(np.asarray(inp["rw_r_k"][l]).reshape(512), 4)
        cp[l, :, CP_LG:CP_LG + 4] = _col(inp["rw_lnx_g"][l], 4)
        cp[l, :, CP_LB:CP_LB + 4] = _col(inp["rw_lnx_b"][l], 4)
        cp[l, :, CP_LRE:CP_LRE + 16] = _col(np.asarray(inp["ssm_lam_re"][l]).reshape(2048), 16)
        cp[l, :, CP_LIM:CP_LIM + 16] = _col(np.asarray(inp["ssm_lam_im"][l]).reshape(2048), 16)
        cp[l, :, CP_LDT:CP_LDT + 16] = _col(np.repeat(np.asarray(inp["ssm_log_dt"][l]), 64), 16)
        cp[l, 0:32, CP_DSK:CP_DSK + 16] = np.asarray(inp["ssm_d"][l]).reshape(16, 32).T
        cp[l, :, CP_BGLU:CP_BGLU + 4] = _col(inp["ssm_b_glu"][l], 4)
        cp[l, :, CP_BMG:CP_BMG + 64] = _col(np.asarray(inp["b_merge"][l]).reshape(8192), 64)
    sh["cpar"] = cp
    sh["norm_g"] = f(inp["norm_g"])
    sh["fin_g"] = f(inp["final_norm_g"]).reshape(1, D)
    sh["rw_w2"] = f(inp["rw_w2"])
    sh["rw_a2"] = f(inp["rw_a2"])
    sh["ssm_b"] = np.ascontiguousarray(np.stack([f(inp["ssm_b_re"]).reshape(L, 2048, 16), f(inp["ssm_b_im"]).reshape(L, 2048, 16)], 1))
    cre = np.transpose(f(inp["ssm_c_re"]), (0, 1, 3, 2)).reshape(L, 2048, 16)
    cim = np.transpose(f(inp["ssm_c_im"]), (0, 1, 3, 2)).reshape(L, 2048, 16)
    sh["ssm_c"] = np.ascontiguousarray(np.stack([cre, cim], 1))
    sh["w_glu"] = f(inp["ssm_w_glu"])
    sh["qn_g"] = f(inp["mla_q_norm"])
    sh["kvn_g"] = f(inp["mla_kv_norm"])
    wq_ = f(inp["mla_w_q_up"])
    sh["wq"] = wq_
    wq4 = wq_.reshape(L, 384, 8, 96)
    sh["wqs"] = np.ascontiguousarray(np.concatenate([wq4[..., :64], wq4[..., 80:96], wq4[..., 64:80]], -1).reshape(L, 384, 768))
    wkv4 = f(inp["mla_w_kv_up"]).reshape(L, 256, 8, 128)
    sh["wk"] = np.ascontiguousarray(wkv4[..., :64].reshape(L, 256, 512))
    sh["wv"] = np.ascontiguousarray(wkv4[..., 64:].reshape(L, 256, 512))
    return sh


def prep_core(inp, sh, c):
    f = lambda a: np.ascontiguousarray(np.asarray(a, dtype=np.float32))
    b = c % 4
    s0 = 2 * c
    m = dict(sh)
    m["xin"] = np.ascontiguousarray(np.concatenate([f(inp["x_prompt"][b]), f(inp["x_sample"][s0]), f(inp["x_sample"][s0 + 1])], 0))
    shf = f(inp["state_rwkv_shift"])[:, s0:s0 + 2, 0, :]
    m["st_shift"] = np.ascontiguousarray(np.transpose(shf.reshape(L, 2, 13, 128), (0, 1, 3, 2)))
    m["st_wkv"] = f(inp["state_rwkv_wkv"])[:, s0:s0 + 2]
    sre = f(inp["state_ssm_re"])[:, s0:s0 + 2].reshape(L, 2, 16, 128)
    sim = f(inp["state_ssm_im"])[:, s0:s0 + 2].reshape(L, 2, 16, 128)
    m["st_ssm"] = np.ascontiguousarray(np.transpose(np.stack([sre, sim], 2), (0, 1, 2, 4, 3)))
    m["c_ckv"] = f(inp["cache_mla_ckv"])[:, s0:s0 + 2]
    m["c_kpe"] = f(inp["cache_mla_kpe"])[:, s0:s0 + 2]
    m["c_sbk"] = f(inp["cache_sb_k"])[:, s0:s0 + 2].reshape(L, 2, PAST, 512)
    m["c_sbv"] = f(inp["cache_sb_v"])[:, s0:s0 + 2].reshape(L, 2, PAST, 512)
    return m


_NC = {}


def kernel(**inputs):
    if "nc" not in _NC:
        _NC["nc"] = build()
    nc = _NC["nc"]
    sh = prep_shared(inputs)
    in_maps = [prep_core(inputs, sh, c) for c in range(8)]
    res = run_bass_kernel_spmd(nc, in_maps, core_ids=list(range(8))).results
    f = np.float32
    y_prompt = np.stack([res[b]["o_y"][:TP] for b in range(4)], 0).astype(f)
    y_sample = np.stack([res[s // 2]["o_y"][TP + 16 * (s % 2):TP + 16 * (s % 2) + 16] for s in range(16)], 0).astype(f)

    def pr(name, fn):
        return np.stack([np.stack([fn(res[b][name][l]) for b in range(4)], 0) for l in range(L)], 0).astype(f)

    def sa(name, fn):
        return np.stack([np.stack([fn(res[s // 2][name][l], s % 2) for s in range(16)], 0) for l in range(L)], 0).astype(f)

    shift_p = pr("o_shift", lambda a: a[0].reshape(1, 1664))
    wkv_p = pr("o_wkv", lambda a: a[0])
    ssm_re_p = pr("o_ssm", lambda a: a[0, 0].reshape(32, 64))
    ssm_im_p = pr("o_ssm", lambda a: a[1, 0].reshape(32, 64))
    ckv_p = pr("o_ckv", lambda a: a[:TP])
    kpe_p = pr("o_kpe", lambda a: a[:TP])
    sbk_p = pr("o_sbk", lambda a: a[:TP].reshape(TP, 8, 64))
    sbv_p = pr("o_sbv", lambda a: a[:TP].reshape(TP, 8, 64))
    shift_s = sa("o_shift", lambda a, j: a[1 + j].reshape(1, 1664))
    wkv_s = sa("o_wkv", lambda a, j: a[1 + j])
    ssm_re_s = sa("o_ssm", lambda a, j: a[0, 1 + j].reshape(32, 64))
    ssm_im_s = sa("o_ssm", lambda a, j: a[1, 1 + j].reshape(32, 64))
    rows = lambda a, j: a[TP + 16 * j:TP + 16 * j + 16]
    ckv_s = sa("o_ckv", rows)
    kpe_s = sa("o_kpe", rows)
    sbk_s = sa("o_sbk", lambda a, j: rows(a, j).reshape(16, 8, 64))
    sbv_s = sa("o_sbv", lambda a, j: rows(a, j).reshape(16, 8, 64))
    return (y_prompt, y_sample, shift_p, wkv_p, ssm_re_p, ssm_im_p, ckv_p, kpe_p, sbk_p, sbv_p,
            shift_s, wkv_s, ssm_re_s, ssm_im_s, ckv_s, kpe_s, sbk_s, sbv_s)
```
